# Optimizing a Trainium2 kernel written in Bass

```python
import math
import jax, jax.numpy as jnp
from jax import lax
import numpy as np

D_MODEL = 1024
BATCH = 8
SEQ = 4096
DEPTH = 2

CTX_LEN = 256
GRID_W = 64
EPS = 1e-6
N_BRANCH = 3
ATT_HEADS = 8
ATT_HD = 64
ATT_VD = 2 * ATT_HD
Q_BLOCK = 128
ROPE_THETA = 10000.0
HG_HEADS = 8
HG_DK = 128
HG_DV = 128
GLA_CHUNK = 32
SSM_D_INNER = 2 * D_MODEL
SSM_HEADDIM = 64
SSM_HEADS = SSM_D_INNER // SSM_HEADDIM
SSM_GROUPS = 4
SSM_STATE = 128
SSM_CONV = 5
SSD_CHUNK = 64
SSM_XBC = SSM_D_INNER + 2 * SSM_GROUPS * SSM_STATE
FFN_HIDDEN = -(-8 * D_MODEL // (3 * 256)) * 256

ATT_QK = ATT_HEADS * 2 * ATT_HD
ATT_V = ATT_HEADS * ATT_VD
HG_K = HG_HEADS * HG_DK
HG_V = HG_HEADS * HG_DV
STATE_SPLITS = (ATT_QK, ATT_V, HG_K, HG_K, HG_V, SSM_XBC, SSM_HEADS, SSM_HEADS)
QUERY_SPLITS = (ATT_QK, HG_K, HG_V, SSM_D_INNER, N_BRANCH * D_MODEL)
ALL_SPLITS = STATE_SPLITS + QUERY_SPLITS
N_STATE_COLS = sum(STATE_SPLITS)
N_IN_COLS = sum(ALL_SPLITS)

kernel_name = 'hybrid_gated_diffattn_hgrn2_ssd_dit'


def rmsnorm(x, g):
    xf = x.astype(jnp.float32)
    y = xf * lax.rsqrt(jnp.mean(xf * xf, axis=-1, keepdims=True) + EPS)
    return (y * g.astype(jnp.float32)).astype(x.dtype)


def adaln(x, g, shift, scale):
    return rmsnorm(x, g) * (1 + scale) + shift


def split_cols(t, sizes):
    return jnp.split(t, np.cumsum(sizes)[:-1].tolist(), axis=-1)


def flip_seq(t, rev):
    return jnp.flip(t, axis=1) if rev else t


def rope_axis(t, pos):
    half = t.shape[-1] // 2
    freqs = ROPE_THETA ** (-jnp.arange(half, dtype=jnp.float32) / half)
    ang = pos[:, None] * freqs[None, :]
    shape = (1, pos.shape[0]) + (1,) * (t.ndim - 3) + (half,)
    cos, sin = jnp.cos(ang).reshape(shape), jnp.sin(ang).reshape(shape)
    t1, t2 = t[..., :half], t[..., half:]
    return jnp.concatenate([t1 * cos - t2 * sin, t2 * cos + t1 * sin], axis=-1)


def rope_2d(t, row, col):
    half = t.shape[-1] // 2
    return jnp.concatenate([rope_axis(t[..., :half], row), rope_axis(t[..., half:], col)], axis=-1).astype(t.dtype)


def diff_attn(q, k, v, lam):
    Bn, Lq = q.shape[:2]
    qb = q.reshape((Bn, Lq // Q_BLOCK, Q_BLOCK) + q.shape[2:]).swapaxes(0, 1)
    scale = ATT_HD ** -0.5

    def block(q_blk):
        s = jnp.einsum('bqhcd,bkhcd->bhcqk', q_blk, k).astype(jnp.float32) * scale
        p = jax.nn.softmax(s, axis=-1)
        w = (p[:, :, 0] - lam * p[:, :, 1]).astype(v.dtype)
        return jnp.einsum('bhqk,bkhe->bqhe', w, v)

    o = lax.map(block, qb)
    return o.swapaxes(0, 1).reshape((Bn, Lq) + o.shape[3:])


def hgrn_gates(f_raw, lb):
    fr = f_raw.astype(jnp.float32)
    log_f = jnp.log(lb + (1.0 - lb) * jax.nn.sigmoid(fr))
    k = (1.0 - lb) * jax.nn.sigmoid(-fr)
    return log_f, k


def gla_chunked(q, k, v, log_f, s0):
    Bn, L, H, _ = q.shape
    nc = L // GLA_CHUNK

    def chunks(t):
        return t.astype(jnp.float32).reshape(Bn, nc, GLA_CHUNK, H, t.shape[-1]).transpose(1, 0, 3, 2, 4)

    lower = jnp.tril(jnp.ones((GLA_CHUNK, GLA_CHUNK), dtype=bool))[:, :, None]

    def step(S, inp):
        qc, kc, vc, lfc = inp
        b = jnp.cumsum(lfc, axis=2)
        diff = b[:, :, :, None, :] - b[:, :, None, :, :]
        seg = jnp.where(lower, jnp.exp(jnp.where(lower, diff, 0.0)), 0.0)
        scores = jnp.einsum('bhtd,bhsd,bhtsd->bhts', qc, kc, seg)
        o = scores @ vc + jnp.einsum('bhtd,bhde->bhte', qc * jnp.exp(b), S)
        b_end = b[:, :, -1:, :]
        S = jnp.exp(b_end[:, :, 0, :, None]) * S + jnp.einsum('bhsd,bhse->bhde', kc * jnp.exp(b_end - b), vc)
        return S, o

    S, o = lax.scan(step, s0, (chunks(q), chunks(k), chunks(v), chunks(log_f)))
    return o.transpose(1, 0, 3, 2, 4).reshape(Bn, L, H, -1).astype(v.dtype), S


def gla_state(k, v, log_f):
    b = jnp.cumsum(log_f, axis=1)
    w = jnp.exp(b[:, -1:] - b)
    return jnp.einsum('blhd,blhe->bhde', k.astype(jnp.float32) * w, v.astype(jnp.float32))


def seg_decay(a):
    T = a.shape[-1]
    strict = jnp.tril(jnp.ones((T, T), dtype=bool), -1)
    lower = jnp.tril(jnp.ones((T, T), dtype=bool))
    cs = jnp.cumsum(jnp.where(strict, a[..., :, None], 0.0), axis=-2)
    return jnp.where(lower, jnp.exp(jnp.where(lower, cs, 0.0)), 0.0)


def ssd_chunked(xdt, a, Bm, Cm, s0):
    out_dtype = Cm.dtype
    Bn, L, H, P = xdt.shape
    G, N = Bm.shape[2], Bm.shape[3]
    R = H // G
    nc = L // SSD_CHUNK
    T = SSD_CHUNK
    xs = xdt.reshape(Bn, nc, T, G, R, P).swapaxes(0, 1)
    As = a.reshape(Bn, nc, T, G, R).transpose(1, 0, 3, 4, 2)
    Bs = Bm.astype(jnp.float32).reshape(Bn, nc, T, G, N).swapaxes(0, 1)
    Cs = Cm.astype(jnp.float32).reshape(Bn, nc, T, G, N).swapaxes(0, 1)

    def step(S, inp):
        xc, ac, bc, cc = inp
        acum = jnp.cumsum(ac, axis=-1)
        Lm = seg_decay(ac)
        y = (jnp.einsum('btgn,bsgn,bgrts,bsgrp->btgrp', cc, bc, Lm, xc)
             + jnp.einsum('btgn,bgrpn,bgrt->btgrp', cc, S, jnp.exp(acum)))
        S = (jnp.exp(acum[..., -1])[..., None, None] * S
             + jnp.einsum('bsgn,bgrs,bsgrp->bgrpn', bc, jnp.exp(acum[..., -1:] - acum), xc))
        return S, y

    S, y = lax.scan(step, s0.reshape(Bn, G, R, P, N), (xs, As, Bs, Cs))
    return y.swapaxes(0, 1).reshape(Bn, L, H, P).astype(out_dtype), S.reshape(Bn, H, P, N)


def ssd_state(xdt, a, Bm):
    Bn, L, H, P = xdt.shape
    G, N = Bm.shape[2], Bm.shape[3]
    acum = jnp.cumsum(a, axis=1)
    w = jnp.exp(acum[:, -1:] - acum)
    xw = (xdt * w[..., None]).reshape(Bn, L, G, H // G, P)
    return jnp.einsum('blgrp,blgn->bgrpn', xw, Bm.astype(jnp.float32)).reshape(Bn, H, P, N)


def dwconv(u, w, b):
    pad = SSM_CONV // 2
    y = lax.conv_general_dilated(u, w.astype(u.dtype)[:, None, :], window_strides=(1,),
                                 padding=[(pad, pad)], dimension_numbers=('NWC', 'WIO', 'NWC'),
                                 feature_group_count=u.shape[-1])
    return y + b.astype(u.dtype)


def ssm_conv_split(u, w, b):
    Bn, L = u.shape[:2]
    xs, bm, cm = split_cols(jax.nn.silu(dwconv(u, w, b)), (SSM_D_INNER, SSM_GROUPS * SSM_STATE, SSM_GROUPS * SSM_STATE))
    return (xs.reshape(Bn, L, SSM_HEADS, SSM_HEADDIM), bm.reshape(Bn, L, SSM_GROUPS, SSM_STATE),
            cm.reshape(Bn, L, SSM_GROUPS, SSM_STATE))


def att_post(o, g, lam_init):
    return (rmsnorm(o, g) * (1.0 - lam_init)).reshape(o.shape[0], o.shape[1], -1)


def hg_post(o, gate, g):
    return (rmsnorm(o, g) * jax.nn.silu(gate).reshape(o.shape)).reshape(o.shape[0], o.shape[1], -1)


def ssm_post(y, xs, z, d_skip, g):
    Bn, L = y.shape[:2]
    y = (y + d_skip[:, None] * xs).reshape(Bn, L, SSM_D_INNER)
    return rmsnorm(y * jax.nn.silu(z), g)


def merge_branches(o_att, o_hg, o_ssm, gates, w_b_att, w_b_hg, w_b_ssm, w_out):
    g_att, g_hg, g_ssm = jnp.split(jax.nn.sigmoid(gates), N_BRANCH, axis=-1)
    m = g_att * (o_att @ w_b_att) + g_hg * (o_hg @ w_b_hg) + g_ssm * (o_ssm @ w_b_ssm)
    return m @ w_out


def ffn(h, w1, w3, w2):
    return (jax.nn.silu(h @ w1) * (h @ w3)) @ w2


def layer(x, xc, mod, mod_c, row, col, l, last, att_lambda, lb, w_in, norm1_g, att_norm_g,
          hg_norm_g, conv_w, conv_b, dt_bias, a_log, d_skip, ssm_norm_g, w_b_att, w_b_hg,
          w_b_ssm, w_out, norm2_g, ffn_w1, ffn_w3, ffn_w2):
    f32 = jnp.float32
    Bn, L, _ = x.shape
    Lc = xc.shape[1]
    sh_m, sc_m, g_m, sh_f, sc_f, g_f = jnp.split(mod[:, None, :], 6, axis=-1)
    shc_m, scc_m, gc_m, shc_f, scc_f, gc_f = jnp.split(mod_c, 6, axis=-1)

    lam_init = 0.8 - 0.6 * math.exp(-0.3 * l)
    lq1, lk1, lq2, lk2 = att_lambda.astype(f32)
    lam = jnp.exp(jnp.sum(lq1 * lk1)) - jnp.exp(jnp.sum(lq2 * lk2)) + lam_init

    h = adaln(x, norm1_g, sh_m, sc_m)
    hc = adaln(xc, norm1_g, shc_m, scc_m)
    ak, av, hf_f, hf_b, hi, xbc, dt_f, dt_b, aq, hq, hgate, z, gates = split_cols(h @ w_in, ALL_SPLITS)
    if last:
        cc = split_cols(hc @ w_in[:, :N_STATE_COLS], STATE_SPLITS)
    else:
        cc = split_cols(hc @ w_in, ALL_SPLITS)
    akc, avc, hfc_f, hfc_b, hic, xbcc, dtc_f, dtc_b = cc[:8]

    k_ctx = akc.reshape(Bn, Lc, ATT_HEADS, 2, ATT_HD)
    v_ctx = avc.reshape(Bn, Lc, ATT_HEADS, ATT_VD)
    q_lat = rope_2d(aq.reshape(Bn, L, ATT_HEADS, 2, ATT_HD), row, col)
    k_all = jnp.concatenate([k_ctx, rope_2d(ak.reshape(Bn, L, ATT_HEADS, 2, ATT_HD), row, col)], axis=1)
    v_all = jnp.concatenate([v_ctx, av.reshape(Bn, L, ATT_HEADS, ATT_VD)], axis=1)
    o_att = diff_attn(q_lat, k_all, v_all, lam)

    q_hg = jax.nn.silu(hq).reshape(Bn, L, HG_HEADS, HG_DK)
    v_hg = hi.reshape(Bn, L, HG_HEADS, HG_DV)
    vc_hg = hic.reshape(Bn, Lc, HG_HEADS, HG_DV)
    o_dirs, oc_dirs = [], []
    for d, (f_lat, f_ctx) in enumerate(((hf_f, hfc_f), (hf_b, hfc_b))):
        rev = d == 1
        lb_d = lb[d].reshape(HG_HEADS, HG_DK)
        lf, kf = hgrn_gates(f_lat.reshape(Bn, L, HG_HEADS, HG_DK), lb_d)
        lfc, kfc = hgrn_gates(f_ctx.reshape(Bn, Lc, HG_HEADS, HG_DK), lb_d)
        if last:
            s_ctx = gla_state(flip_seq(kfc, rev), flip_seq(vc_hg, rev), flip_seq(lfc, rev))
        else:
            qc_hg = jax.nn.silu(cc[9]).reshape(Bn, Lc, HG_HEADS, HG_DK)
            oc, s_ctx = gla_chunked(flip_seq(qc_hg, rev), flip_seq(kfc, rev), flip_seq(vc_hg, rev),
                                    flip_seq(lfc, rev), jnp.zeros((Bn, HG_HEADS, HG_DK, HG_DV), f32))
            oc_dirs.append(flip_seq(oc, rev))
        o, _ = gla_chunked(flip_seq(q_hg, rev), flip_seq(kf, rev), flip_seq(v_hg, rev), flip_seq(lf, rev), s_ctx)
        o_dirs.append(flip_seq(o, rev))

    xs, bm, cm = ssm_conv_split(xbc, conv_w, conv_b)
    xsc, bmc, cmc = ssm_conv_split(xbcc, conv_w, conv_b)
    y_dirs, yc_dirs = [], []
    for d, (dt_lat, dt_ctx) in enumerate(((dt_f, dtc_f), (dt_b, dtc_b))):
        rev = d == 1
        A = -jnp.exp(a_log[d].astype(f32))
        dt = jax.nn.softplus(dt_lat.astype(f32) + dt_bias[d])
        dtc = jax.nn.softplus(dt_ctx.astype(f32) + dt_bias[d])
        xdt, a = xs.astype(f32) * dt[..., None], dt * A
        xdtc, ac = xsc.astype(f32) * dtc[..., None], dtc * A
        if last:
            s_ctx = ssd_state(flip_seq(xdtc, rev), flip_seq(ac, rev), flip_seq(bmc, rev))
        else:
            yc, s_ctx = ssd_chunked(flip_seq(xdtc, rev), flip_seq(ac, rev), flip_seq(bmc, rev), flip_seq(cmc, rev),
                                    jnp.zeros((Bn, SSM_HEADS, SSM_HEADDIM, SSM_STATE), f32))
            yc_dirs.append(flip_seq(yc, rev))
        y, _ = ssd_chunked(flip_seq(xdt, rev), flip_seq(a, rev), flip_seq(bm, rev), flip_seq(cm, rev), s_ctx)
        y_dirs.append(flip_seq(y, rev))

    mix = merge_branches(att_post(o_att, att_norm_g, lam_init),
                         hg_post(o_dirs[0] + o_dirs[1], hgate, hg_norm_g),
                         ssm_post(y_dirs[0] + y_dirs[1], xs, z, d_skip, ssm_norm_g),
                         gates, w_b_att, w_b_hg, w_b_ssm, w_out)
    x = x + g_m * mix
    x = x + g_f * ffn(adaln(x, norm2_g, sh_f, sc_f), ffn_w1, ffn_w3, ffn_w2)
    if last:
        return x, None

    oc_att = diff_attn(cc[8].reshape(Bn, Lc, ATT_HEADS, 2, ATT_HD), k_ctx, v_ctx, lam)
    mix_c = merge_branches(att_post(oc_att, att_norm_g, lam_init),
                           hg_post(oc_dirs[0] + oc_dirs[1], cc[10], hg_norm_g),
                           ssm_post(yc_dirs[0] + yc_dirs[1], xsc, cc[11], d_skip, ssm_norm_g),
                           cc[12], w_b_att, w_b_hg, w_b_ssm, w_out)
    xc = xc + gc_m * mix_c
    xc = xc + gc_f * ffn(adaln(xc, norm2_g, shc_f, scc_f), ffn_w1, ffn_w3, ffn_w2)
    return x, xc


def setup_inputs(seed: int = 0) -> dict:
    key = jax.random.key(seed)
    k = jax.random.split(key, 32)
    D = D_MODEL

    def nrm(i, shape, scale):
        return scale * jax.random.normal(k[i], shape, jnp.float32)

    dt = jnp.exp(jax.random.uniform(k[14], (DEPTH, 2, SSM_HEADS), jnp.float32, math.log(1e-3), math.log(1e-1)))
    return {
        'x': nrm(0, (BATCH, SEQ, D), 1.0),
        'c': nrm(1, (BATCH, D), 1.0),
        'ctx': nrm(2, (BATCH, CTX_LEN, D), 1.0),
        'c_ctx': nrm(3, (D,), 1.0),
        'w_ada': nrm(4, (DEPTH, D, 6 * D), 0.5 * D ** -0.5),
        'b_ada': nrm(5, (DEPTH, 6 * D), 0.01),
        'norm1_g': 1.0 + nrm(6, (DEPTH, D), 0.02),
        'w_in': nrm(7, (DEPTH, D, N_IN_COLS), D ** -0.5),
        'att_lambda': nrm(8, (DEPTH, 4, ATT_HD), 0.1),
        'att_norm_g': 1.0 + nrm(9, (DEPTH, ATT_VD), 0.02),
        'hg_lb_logits': nrm(10, (2, DEPTH, HG_K), 0.5),
        'hg_norm_g': 1.0 + nrm(11, (DEPTH, HG_DV), 0.02),
        'ssm_conv_w': nrm(12, (DEPTH, SSM_CONV, SSM_XBC), SSM_CONV ** -0.5),
        'ssm_conv_b': nrm(13, (DEPTH, SSM_XBC), 0.01),
        'ssm_dt_bias': dt + jnp.log(-jnp.expm1(-dt)),
        'ssm_a_log': jnp.log(jax.random.uniform(k[15], (DEPTH, 2, SSM_HEADS), jnp.float32, 1.0, 16.0)),
        'ssm_d': 1.0 + nrm(16, (DEPTH, SSM_HEADS), 0.1),
        'ssm_norm_g': 1.0 + nrm(17, (DEPTH, SSM_D_INNER), 0.02),
        'w_branch_att': nrm(18, (DEPTH, ATT_V, D), ATT_V ** -0.5),
        'w_branch_hg': nrm(19, (DEPTH, HG_V, D), HG_V ** -0.5),
        'w_branch_ssm': nrm(20, (DEPTH, SSM_D_INNER, D), SSM_D_INNER ** -0.5),
        'w_out': nrm(21, (DEPTH, D, D), D ** -0.5),
        'norm2_g': 1.0 + nrm(22, (DEPTH, D), 0.02),
        'ffn_w1': nrm(23, (DEPTH, D, FFN_HIDDEN), D ** -0.5),
        'ffn_w3': nrm(24, (DEPTH, D, FFN_HIDDEN), D ** -0.5),
        'ffn_w2': nrm(25, (DEPTH, FFN_HIDDEN, D), FFN_HIDDEN ** -0.5),
        'final_g': 1.0 + nrm(26, (D,), 0.02),
    }


def reference(x, c, ctx, c_ctx, w_ada, b_ada, norm1_g, w_in, att_lambda, att_norm_g, hg_lb_logits,
              hg_norm_g, ssm_conv_w, ssm_conv_b, ssm_dt_bias, ssm_a_log, ssm_d, ssm_norm_g,
              w_branch_att, w_branch_hg, w_branch_ssm, w_out, norm2_g, ffn_w1, ffn_w3, ffn_w2, final_g):
    L = x.shape[1]
    rows = L // GRID_W
    row = jnp.repeat(jnp.arange(rows, dtype=jnp.float32), GRID_W)
    col = jnp.broadcast_to(jnp.arange(GRID_W, dtype=jnp.float32), (rows, GRID_W)).reshape(-1)
    lb_w = jax.nn.softmax(hg_lb_logits.astype(jnp.float32), axis=1)
    lb = jnp.cumsum(lb_w, axis=1) - lb_w[:, :1]
    c_act = jax.nn.silu(c)
    cc_act = jax.nn.silu(c_ctx)
    xc = ctx
    for l in range(DEPTH):
        mod = c_act @ w_ada[l] + b_ada[l]
        mod_c = cc_act @ w_ada[l] + b_ada[l]
        x, xc = layer(x, xc, mod, mod_c, row, col, l, l == DEPTH - 1, att_lambda[l], lb[:, l], w_in[l],
                      norm1_g[l], att_norm_g[l], hg_norm_g[l], ssm_conv_w[l], ssm_conv_b[l], ssm_dt_bias[l],
                      ssm_a_log[l], ssm_d[l], ssm_norm_g[l], w_branch_att[l], w_branch_hg[l], w_branch_ssm[l],
                      w_out[l], norm2_g[l], ffn_w1[l], ffn_w3[l], ffn_w2[l])
    return rmsnorm(x, final_g)
```

```python
import math
from contextlib import ExitStack
import numpy as np
import ml_dtypes
import concourse.bass as bass
import concourse.mybir as mybir
from concourse.bass_utils import run_bass_kernel_spmd

F32 = mybir.dt.float32
BF16 = mybir.dt.bfloat16
AF = mybir.ActivationFunctionType
ALU = mybir.AluOpType

D = 1024
LC = 256
L = 4096
T = LC + L
NT = T // 128
NCH = T // 32
DEPTH = 2
EPS = 1e-6
NIN = 16448
FFH = 2816
BLOCKS = [(0, 256)] + [(256 + 512 * i, 512) for i in range(8)]
O_AK, O_AV, O_HFF, O_HFB, O_HI, O_XBC, O_DTF, O_DTB, O_AQ, O_HQ, O_HGATE, O_Z, O_GATES = (
    0, 1024, 2048, 3072, 4096, 5120, 8192, 8224, 8256, 9280, 10304, 11328, 13376)


class _Stop(Exception):
    pass


class Sched:
    def __init__(self, nc, es):
        self.nc = nc
        self.eng = {'pe': nc.tensor, 'act': nc.scalar, 'dve': nc.vector, 'pool': nc.gpsimd, 'sp': nc.sync}
        self.semh = {}
        self.cnt = {}
        for k in ['pe', 'act', 'dve', 'pool']:
            self.semh[k] = es.enter_context(nc.semaphore('s_' + k))
            self.cnt[k] = 0
        self.dpool = {'sp': [], 'pool': []}
        for q, n in (('sp', 24), ('pool', 16)):
            for i in range(n):
                nm = f'd_{q}{i}'
                self.semh[nm] = es.enter_context(nc.semaphore(nm))
                self.cnt[nm] = 0
                self.dpool[q].append(nm)
        self.drr = {'sp': 0, 'pool': 0}
        self.seen = {k: {} for k in self.eng}
        self.res = {}
        self.nops = 0
        self.psum = set()

    def _keys(self, ap, sub):
        nm = ap.tensor.name
        if nm in self.psum:
            return [(nm, None)]
        if sub is not None and nm in sub:
            v = sub[nm]
            if isinstance(v, (list, tuple)):
                return [(nm, x) for x in v]
            return [(nm, v)]
        return [(nm, None)]

    def _deps(self, rkeys, wkeys, eng=None):
        deps = []
        for k in rkeys:
            st = self.res.get(k)
            if st is not None and st[0] is not None:
                deps.append(st[0])
        for k in wkeys:
            st = self.res.get(k)
            if st is not None:
                if st[0] is not None and st[0][0] != eng:
                    deps.append(st[0])
                deps.extend(e for e in st[1] if e[0] != eng)
        return deps

    def _wait(self, eng, deps):
        need = {}
        for (k, v) in deps:
            if eng == 'pe' and k == 'pe':
                continue
            if self.seen[eng].get(k, 0) >= v:
                continue
            if need.get(k, 0) < v:
                need[k] = v
        for k, v in need.items():
            self.eng[eng].wait_ge(self.semh[k], v)
            self.seen[eng][k] = v

    def _record(self, ev, rkeys, wkeys):
        for k in rkeys:
            st = self.res.setdefault(k, [None, []])
            st[1] = [e for e in st[1] if e[0] != ev[0]] + [ev]
        for k in wkeys:
            self.res[k] = [ev, []]

    def op(self, eng, fn, ins, outs, sub=None):
        rkeys = [k for a in ins if a is not None and hasattr(a, 'tensor') for k in self._keys(a, sub)]
        wkeys = [k for a in outs for k in self._keys(a, sub)]
        wkeys += [k for k in rkeys if k[0] in self.psum and k not in wkeys]
        self._wait(eng, self._deps(rkeys, wkeys, eng))
        inst = fn(self.eng[eng])
        self.cnt[eng] += 1
        inst.then_inc(self.semh[eng], 1)
        self._record((eng, self.cnt[eng]), rkeys, wkeys)
        self.nops += 1

    def dma(self, q, out, in_, sub=None, track_out=True, track_in=True):
        rkeys = self._keys(in_, sub) if track_in else []
        wkeys = self._keys(out, sub) if track_out else []
        pool = self.dpool[q]
        nm = pool[self.drr[q] % len(pool)]
        self.drr[q] += 1
        deps = self._deps(rkeys, wkeys)
        if self.cnt[nm] > 0:
            deps.append((nm, self.cnt[nm]))
        self._wait(q, deps)
        self.eng[q].dma_start(out=out, in_=in_).then_inc(self.semh[nm], 16)
        self.cnt[nm] += 16
        self._record((nm, self.cnt[nm]), rkeys, wkeys)
        self.nops += 1

    def load(self, out, in_, sub=None):
        self.dma('sp', out, in_, sub=sub, track_in=False)

    def store(self, out, in_, sub=None):
        self.dma('pool', out, in_, sub=sub, track_out=False)

    def barrier(self):
        allev = [(k, v) for k, v in self.cnt.items() if v > 0]
        for e in self.eng:
            self._wait(e, allev)
        self.res = {}


def _bc_mid(a, n):
    ap = [list(x) for x in a.ap]
    return bass.AP(a.tensor, a.offset, [ap[0], [0, n]] + ap[1:])


def _bc_last(a, n):
    ap = [list(x) for x in a.ap]
    return bass.AP(a.tensor, a.offset, ap + [[0, n]])


def _rev(a):
    ap = [list(x) for x in a.ap]
    assert len(ap) == 2 and ap[1][0] == 1
    return bass.AP(a.tensor, a.offset + ap[1][1] - 1, [ap[0], [-1, ap[1][1]]])


def _strided(a, start, step, n):
    ap = [list(x) for x in a.ap]
    return bass.AP(a.tensor, a.offset + start, [ap[0], [step, n]])


def _pbcast(dram_ap_1d_offset_tensor, offset, n):
    return bass.AP(dram_ap_1d_offset_tensor, offset, [[0, 128], [1, n]])


def build_program(stop_after=None, dump=None):
    nc = bass.Bass("TRN2", target_bir_lowering=False)
    es = ExitStack()
    S = Sched(nc, es)

    def din(name, shape, dt=F32):
        return nc.dram_tensor(name, list(shape), dt, kind="ExternalInput").ap()

    def dscr(name, shape, dt):
        if dump is not None and name in dump:
            return nc.dram_tensor(name, list(shape), dt, kind="ExternalOutput").ap()
        return nc.dram_tensor(name, list(shape), dt).ap()

    xT_in = din("xT", [D, T])
    c2_in = din("c2", [D, 2])
    w_ada = din("w_ada", [DEPTH, D, 6 * D])
    b_adaT = din("b_adaT", [DEPTH, 128, 48])
    g1T = din("g1T", [DEPTH, 128, 8])
    g2T = din("g2T", [DEPTH, 128, 8])
    gfT = din("gfT", [128, 8])
    w_in = din("w_in", [DEPTH, D, NIN])
    w_rot = din("w_rot", [DEPTH, D, 2048])
    cosT = din("cosT", [128, T])
    sinT = din("sinT", [128, T])
    att_lam = din("att_lam", [DEPTH, 256])
    att_gT = din("att_gT", [128, DEPTH])
    hg_lbT = din("hg_lbT", [128, 2, DEPTH, 8])
    hg_gT = din("hg_gT", [128, DEPTH])
    conv_wT = din("conv_wT", [DEPTH, 128, 24, 5])
    conv_bT = din("conv_bT", [DEPTH, 128, 24])
    conv_b = din("conv_b", [DEPTH, 3072])
    dt_bias = din("dt_bias", [DEPTH, 64])
    a_log = din("a_log", [DEPTH, 64])
    ssm_d = din("ssm_d", [DEPTH, 32])
    ssm_gT = din("ssm_gT", [DEPTH, 128, 16])
    w_b_att = din("w_b_att", [DEPTH, 1024, D])
    w_b_hg = din("w_b_hg", [DEPTH, 1024, D])
    w_b_ssm = din("w_b_ssm", [DEPTH, 2048, D])
    w_out = din("w_out", [DEPTH, D, D])
    ffn_w1 = din("ffn_w1", [DEPTH, D, FFH])
    ffn_w3 = din("ffn_w3", [DEPTH, D, FFH])
    ffn_w2 = din("ffn_w2", [DEPTH, FFH, D])
    cst_in = din("cst", [128, 6 * 128 + 64 + 4 + 256])
    m0_in = din("m0", [128, T], BF16)
    yT = nc.dram_tensor("yT", [D, L], F32, kind="ExternalOutput").ap()

    XT = dscr("XT", [D, T], F32)
    QT = dscr("QT", [1024, T], BF16)
    KT = dscr("KT", [1024, T], BF16)
    AV = dscr("AV", [T, 1024], BF16)
    HFF = dscr("HFF", [1024, T], F32)
    HFB = dscr("HFB", [1024, T], F32)
    HQ = dscr("HQ", [1024, T], F32)
    HI = dscr("HI", [T, 1024], BF16)
    HGATE = dscr("HGATE", [1024, T], F32)
    XBC = dscr("XBC", [3072, T], BF16)
    DTR = dscr("DTR", [T, 64], F32)
    ZZ = dscr("ZZ", [T, 2048], F32)
    GATES = dscr("GATES", [3072, T], F32)
    XS = dscr("XS", [T, 2048], F32)
    BTOK = dscr("BTOK", [T, 512], BF16)
    BTT = dscr("BTT", [512, T], BF16)
    CTT = dscr("CTT", [512, T], BF16)
    YF = dscr("YF", [T, 2048], F32)
    BR = dscr("BR", [4096, T], BF16)

    uid = [0]

    def sb(ctx, name, shape, dt):
        uid[0] += 1
        return ctx.enter_context(nc.sbuf_tensor(f"{name}_{uid[0]}", list(shape), dt))

    def ps(ctx, name, shape, dt=F32):
        uid[0] += 1
        t = ctx.enter_context(nc.psum_tensor(f"{name}_{uid[0]}", [128, 512] if dt == F32 else [128, 1024], dt))
        S.psum.add(t.name)
        return t

    def mm(out, lhsT, rhs, start=True, stop=True, sub=None):
        S.op('pe', lambda e: e.matmul(out, lhsT=lhsT, rhs=rhs, start=start, stop=stop), [lhsT, rhs], [out], sub)

    def tr(out, in_, ident, sub=None):
        S.op('pe', lambda e: e.transpose(out, in_, ident), [in_, ident], [out], sub)

    def act(out, in_, func, bias=None, scale=None, accum=None, sub=None, eng='act'):
        kw = {}
        if bias is not None:
            kw['bias'] = bias
        if scale is not None:
            kw['scale'] = scale
        if accum is not None:
            kw['accum_out'] = accum
        ins = [in_] + [x for x in (bias, scale) if hasattr(x, 'tensor')]
        outs = [out] + ([accum] if accum is not None else [])
        S.op('act', lambda e: e.activation(out=out, in_=in_, func=func, **kw), ins, outs, sub)

    def tt(out, in0, in1, op, sub=None, eng='dve'):
        S.op(eng, lambda e: e.tensor_tensor(out=out, in0=in0, in1=in1, op=op), [in0, in1], [out], sub)

    def tsc(out, in0, s1, s2, op0, op1=None, sub=None, eng='dve'):
        ins = [in0] + [x for x in (s1, s2) if hasattr(x, 'tensor')]
        if op1 is None:
            S.op(eng, lambda e: e.tensor_scalar(out=out, in0=in0, scalar1=s1, scalar2=None, op0=op0), ins, [out], sub)
        else:
            S.op(eng, lambda e: e.tensor_scalar(out=out, in0=in0, scalar1=s1, scalar2=s2, op0=op0, op1=op1), ins, [out], sub)

    def stt(out, in0, scalar, in1, op0, op1, sub=None):
        ins = [in0, in1] + ([scalar] if hasattr(scalar, 'tensor') else [])
        S.op('dve', lambda e: e.scalar_tensor_tensor(out=out, in0=in0, scalar=scalar, in1=in1, op0=op0, op1=op1), ins, [out], sub)

    def cp(out, in_, sub=None, eng='dve'):
        if eng == 'act':
            act(out, in_, AF.Copy, sub=sub)
        else:
            S.op(eng, lambda e: e.tensor_copy(out=out, in_=in_), [in_], [out], sub)

    def recip(out, in_, sub=None):
        S.op('dve', lambda e: e.reciprocal(out=out, in_=in_), [in_], [out], sub)

    def memset(ap, val, eng='dve', sub=None):
        S.op(eng, lambda e: e.memset(ap, val), [], [ap], sub)

    def scan(out, d0, d1, sub=None):
        S.op('dve', lambda e: e.tensor_tensor_scan(out=out, data0=d0, data1=d1, initial=0.0, op0=ALU.mult, op1=ALU.add),
             [d0, d1], [out], sub)

    per = ExitStack()
    cst = sb(per, "cst", [128, 6 * 128 + 64 + 4 + 256], F32)
    S.load(cst[:], cst_in[:, :])
    U_f = cst[:, 0:128]
    Lo_f = cst[:, 128:256]
    Ms_f = cst[:, 256:384]
    Ml_f = cst[:, 384:512]
    ID_f = cst[:, 512:640]
    ON_f = cst[:, 640:768]
    RM = cst[:, 832:836]
    BDf = cst[:, 836:964]
    BDb = cst[:, 964:1092]
    M32 = cst[:, 768:832]
    id_bf = sb(per, "id_bf", [128, 128], BF16)
    on_bf = sb(per, "on_bf", [128, 128], BF16)
    cp(id_bf[:], ID_f)
    cp(on_bf[:], ON_f)
    modT = sb(per, "modT", [128, DEPTH, 48, 2], F32)
    A1 = sb(per, "A1", [128, DEPTH, 8, 2], F32)
    A2 = sb(per, "A2", [128, DEPTH, 8, 2], F32)
    g1s = sb(per, "g1s", [128, DEPTH, 8], F32)
    g2s = sb(per, "g2s", [128, DEPTH, 8], F32)
    gfs = sb(per, "gfs", [128, 8], F32)
    S.load(g1s[:], g1T.rearrange("l p k -> p l k"))
    S.load(g2s[:], g2T.rearrange("l p k -> p l k"))
    S.load(gfs[:], gfT[:, :])
    att_g = sb(per, "att_g", [128, DEPTH], F32)
    hg_g = sb(per, "hg_g", [128, DEPTH], F32)
    S.load(att_g[:], att_gT[:, :])
    S.load(hg_g[:], hg_gT[:, :])
    lbx = sb(per, "lbx", [128, 2, DEPTH, 8], F32)
    S.load(lbx[:], hg_lbT[:, :, :, :])
    lb1 = sb(per, "lb1", [128, 2, 8], F32)
    oml1 = sb(per, "oml1", [128, 2, 8], F32)
    neglam = sb(per, "neglam", [128, DEPTH], F32)
    ssm_g = sb(per, "ssm_g", [128, DEPTH, 16], F32)
    S.load(ssm_g[:], ssm_gT.rearrange("l p k -> p l k"))
    cbT = sb(per, "cbT", [128, DEPTH, 24], F32)
    S.load(cbT[:], conv_bT.rearrange("l p k -> p l k"))

    with ExitStack() as ph:
        sc = sb(ph, "p0_sc", [128, 8, 2], F32)
        S.load(sc[:], c2_in.rearrange("(k p) s -> p k s", p=128))
        sg = sb(ph, "p0_sg", [128, 8, 2], F32)
        act(sg[:], sc[:], AF.Sigmoid)
        tt(sc[:], sc[:], sg[:], ALU.mult)
        badd = sb(ph, "p0_b", [128, DEPTH, 48], F32)
        S.load(badd[:], b_adaT.rearrange("l p k -> p l k"))
        wst = [sb(ph, f"p0_w{i}", [128, 8, 512], F32) for i in range(2)]
        pm = ps(ph, "p0_pm", [128, 96], F32)
        it = 0
        for l in range(DEPTH):
            for cg in range(12):
                w = wst[it % 2]
                it += 1
                S.load(w[:], w_ada[l, :, cg * 512:(cg + 1) * 512].rearrange("(k p) c -> p k c", p=128))
                for j in range(4):
                    ch = cg * 4 + j
                    for k in range(8):
                        mm(pm[:, ch * 2:ch * 2 + 2], w[:, k, j * 128:(j + 1) * 128], sc[:, k, :], start=(k == 0), stop=(k == 7))
            tt(modT[:, l, :, :], pm[:, 0:96].rearrange("p (c s) -> p c s", s=2), _bc_last(badd[:, l, :], 2), ALU.add)
            tsc(A1[:, l, :, :], modT[:, l, 8:16, :], 1.0, None, ALU.add)
            tt(A1[:, l, :, :], A1[:, l, :, :], _bc_last(g1s[:, l, :], 2), ALU.mult)
            tsc(A2[:, l, :, :], modT[:, l, 32:40, :], 1.0, None, ALU.add)
            tt(A2[:, l, :, :], A2[:, l, :, :], _bc_last(g2s[:, l, :], 2), ALU.mult)
        lamt = sb(ph, "p0_lam", [128, DEPTH, 4, 64], F32)
        S.load(lamt[:].rearrange("p a b c -> p (a b c)"), bass.AP(att_lam.tensor, 0, [[0, 128], [1, DEPTH * 256]]))
        pr = sb(ph, "p0_pr", [128, 64], F32)
        sm = sb(ph, "p0_sm", [128, 4], F32)
        for l in range(DEPTH):
            for j in range(2):
                tt(pr[:], lamt[:, l, 2 * j, :], lamt[:, l, 2 * j + 1, :], ALU.mult)
                act(pr[:], pr[:], AF.Copy, accum=sm[:, 2 * l + j:2 * l + j + 1])
        act(sm[:], sm[:], AF.Exp)
        for l in range(DEPTH):
            lam_init = 0.8 - 0.6 * math.exp(-0.3 * l)
            tt(neglam[:, l:l + 1], sm[:, 2 * l + 1:2 * l + 2], sm[:, 2 * l:2 * l + 1], ALU.subtract)
            tsc(neglam[:, l:l + 1], neglam[:, l:l + 1], -lam_init, None, ALU.add)
        tt(lb1[:], lbx[:, :, 1, :], lbx[:, :, 0, :], ALU.subtract)
        act(lb1[:], lb1[:], AF.Sigmoid)
        tsc(oml1[:], lb1[:], -1.0, 1.0, ALU.mult, ALU.add)
        S.barrier()

    def norm_block(ph_tiles, xb, bs, A_of_k, B_of_k, out_of_k):
        sq, pst, rs, tmp = ph_tiles
        for k in range(8):
            act(sq[:, k, :bs], xb[:, k, :bs], AF.Square)
        for k in range(8):
            mm(pst[:, :bs], on_bf[:], sq[:, k, :bs], start=(k == 0), stop=(k == 7))
        tsc(rs[:, :bs], pst[:, :bs], 1.0 / D, EPS, ALU.mult, ALU.add)
        act(rs[:, :bs], rs[:, :bs], AF.Sqrt)
        recip(rs[:, :bs], rs[:, :bs])
        for k in range(8):
            tt(tmp[:, k, :bs], xb[:, k, :bs], rs[:, :bs], ALU.mult)
            b = B_of_k(k)
            if b is None:
                act(out_of_k(k), tmp[:, k, :bs], AF.Identity, scale=A_of_k(k))
            else:
                act(out_of_k(k), tmp[:, k, :bs], AF.Identity, scale=A_of_k(k), bias=b)

    def strm(b0):
        return 0 if b0 < LC else 1

    import os
    SSD_DBG = int(os.environ.get('SSD_DBG', '0'))

    def dbgstop(n):
        if SSD_DBG == n:
            raise _Stop()

    def layer_body(l):
        last = (l == DEPTH - 1)
        xsrc = xT_in if l == 0 else XT
        with ExitStack() as ph:
            hT = sb(ph, "hT", [128, 8, T], BF16)
            with ExitStack() as p1:
                xb2 = [sb(p1, f"p1_x{i}", [128, 8, 512], F32) for i in range(2)]
                sq = sb(p1, "p1_sq", [128, 8, 512], BF16)
                rs = sb(p1, "p1_rs", [128, 512], F32)
                tmp = sb(p1, "p1_tmp", [128, 8, 512], F32)
                pst = ps(p1, "p1_ps", [128, 512], F32)
                for bi, (b0, bs) in enumerate(BLOCKS):
                    xb = xb2[bi % 2]
                    S.load(xb[:, :, :bs], xsrc[:, b0:b0 + bs].rearrange("(k p) t -> p k t", p=128))
                    s = strm(b0)
                    norm_block((sq, pst, rs, tmp), xb, bs,
                               lambda k: A1[:, l, k, s:s + 1], lambda k: modT[:, l, k, s:s + 1],
                               lambda k: hT[:, k, b0:b0 + bs])
                S.barrier()
            with ExitStack() as p2:
                cs = sb(p2, "p2_cos", [128, T], F32)
                sn = sb(p2, "p2_sin", [128, T], F32)
                S.load(cs[:], cosT[:, :])
                S.load(sn[:], sinT[:, :])
                wf = [sb(p2, f"p2_wf{i}", [128, 8, 512], F32) for i in range(2)]
                wb = [sb(p2, f"p2_wb{i}", [128, 8, 512], BF16) for i in range(3)]
                stg = [sb(p2, f"p2_st{i}", [128, 512], F32) for i in range(4)]
                stb = [sb(p2, f"p2_sb{i}", [128, 512], BF16) for i in range(4)]
                r1 = sb(p2, "p2_r1", [128, 512], F32)
                r2 = sb(p2, "p2_r2", [128, 512], F32)
                pp = [ps(p2, f"p2_p{i}", [128, 512], F32) for i in range(6)]
                cnt = {'w': 0, 'p': 0, 's': 0, 'c': 0}

                def load_w(src, c0, n):
                    i = cnt['w']
                    cnt['w'] += 1
                    f = wf[i % 2]
                    b = wb[i % 3]
                    S.load(f[:, :, :n], src[:, c0:c0 + n].rearrange("(k p) c -> p k c", p=128))
                    for k in range(8):
                        if (k + i) % 2 == 0:
                            cp(b[:, k, :n], f[:, k, :n], eng='dve')
                        else:
                            cp(b[:, k, :n], f[:, k, :n], eng='act')
                    return b

                def newp():
                    p = pp[cnt['p'] % 6]
                    cnt['p'] += 1
                    return p

                def proj_F(wt, j, b0, bs):
                    p = newp()
                    for k in range(8):
                        mm(p[:, :bs], wt[:, k, j * 128:(j + 1) * 128], hT[:, k, b0:b0 + bs], start=(k == 0), stop=(k == 7))
                    return p

                def fgroup(src, c0, ncols, evac):
                    for g0 in range(0, ncols, 512):
                        n = min(512, ncols - g0)
                        wt = load_w(src, c0 + g0, n)
                        for j in range(n // 128):
                            for (b0, bs) in BLOCKS:
                                p = proj_F(wt, j, b0, bs)
                                evac(p, g0 + j * 128, b0, bs)

                def ev_simple(dst, func, dt):
                    def f(p, gc, b0, bs):
                        i = cnt['s']
                        cnt['s'] += 1
                        st = (stb if dt == BF16 else stg)[i % 4]
                        if func is None:
                            if i % 2 == 0:
                                cp(st[:, :bs], p[:, :bs], eng='dve')
                            else:
                                cp(st[:, :bs], p[:, :bs], eng='act')
                        else:
                            act(st[:, :bs], p[:, :bs], func)
                        S.store(dst[gc:gc + 128, b0:b0 + bs], st[:, :bs])
                    return f

                def ev_silu(dst):
                    def f(p, gc, b0, bs):
                        i = cnt['s']
                        cnt['s'] += 1
                        st = stg[i % 4]
                        act(st[:, :bs], p[:, :bs], AF.Sigmoid)
                        tt(st[:, :bs], p[:, :bs], st[:, :bs], ALU.mult)
                        S.store(dst[gc:gc + 128, b0:b0 + bs], st[:, :bs])
                    return f

                def rope_group(c_orig, c_rot, dst):
                    for g0 in range(0, 1024, 512):
                        wo = load_w(w_in[l], c_orig + g0, 512)
                        wr = load_w(w_rot[l], c_rot + g0, 512)
                        for j in range(4):
                            for (b0, bs) in BLOCKS:
                                p1_ = proj_F(wo, j, b0, bs)
                                p2_ = proj_F(wr, j, b0, bs)
                                i = cnt['s']
                                cnt['s'] += 1
                                st = stb[i % 4]
                                tt(r1[:, :bs], p1_[:, :bs], cs[:, b0:b0 + bs], ALU.mult)
                                tt(r2[:, :bs], p2_[:, :bs], sn[:, b0:b0 + bs], ALU.mult)
                                tt(st[:, :bs], r1[:, :bs], r2[:, :bs], ALU.add)
                                gc = g0 + j * 128
                                S.store(dst[gc:gc + 128, b0:b0 + bs], st[:, :bs])

                def tgroup(c0, ncols, dst, dcol0, func, dt):
                    for g0 in range(0, ncols, 512):
                        n = min(512, ncols - g0)
                        wt = load_w(w_in[l], c0 + g0, n)
                        for tix in range(NT):
                            p = newp()
                            for k in range(8):
                                mm(p[:, :n], hT[:, k, tix * 128:(tix + 1) * 128], wt[:, k, :n], start=(k == 0), stop=(k == 7))
                            i = cnt['s']
                            cnt['s'] += 1
                            st = (stb if dt == BF16 else stg)[i % 4]
                            if func == 'silu':
                                act(st[:, :n], p[:, :n], AF.Sigmoid)
                                tt(st[:, :n], p[:, :n], st[:, :n], ALU.mult)
                            elif i % 2 == 0:
                                cp(st[:, :n], p[:, :n], eng='dve')
                            else:
                                cp(st[:, :n], p[:, :n], eng='act')
                            S.store(dst[tix * 128:(tix + 1) * 128, dcol0 + g0:dcol0 + g0 + n], st[:, :n])

                rope_group(O_AK, 0, KT)
                rope_group(O_AQ, 1024, QT)
                tgroup(O_AV, 1024, AV, 0, None, BF16)
                fgroup(w_in[l], O_HFF, 1024, ev_simple(HFF, None, F32))
                fgroup(w_in[l], O_HFB, 1024, ev_simple(HFB, None, F32))
                tgroup(O_HI, 1024, HI, 0, None, BF16)
                fgroup(w_in[l], O_XBC, 3072, ev_simple(XBC, None, BF16))
                tgroup(O_DTF, 64, DTR, 0, None, F32)
                fgroup(w_in[l], O_HQ, 1024, ev_silu(HQ))
                fgroup(w_in[l], O_HGATE, 1024, ev_silu(HGATE))
                tgroup(O_Z, 2048, ZZ, 0, 'silu', F32)
                fgroup(w_in[l], O_GATES, 3072, ev_simple(GATES, AF.Sigmoid, F32))
                S.barrier()
        if stop_after == ('p2', l):
            raise _Stop()

        with ExitStack() as ph:
          if not os.environ.get('SKIP_AG'):
            kT2 = [sb(ph, f"a_kT{i}", [128, T], BF16) for i in range(2)]
            qT2 = [sb(ph, f"a_qT{i}", [128, T], BF16) for i in range(2)]
            vt2 = [sb(ph, f"a_v{i}", [128, NT, 128], BF16) for i in range(2)]
            qm2 = [[sb(ph, f"a_qm{i}_{c}", [128, T], BF16) for c in range(2)] for i in range(2)]
            pb = [sb(ph, f"a_p{i}", [128, 512], BF16) for i in range(6)]
            psm = [sb(ph, f"a_psm{i}", [128, 512], BF16) for i in range(4)]
            r0 = sb(ph, "a_r0", [128, 512], F32)
            r1 = sb(ph, "a_r1", [128, 512], F32)
            t0 = sb(ph, "a_t0", [128, 512], F32)
            t1 = sb(ph, "a_t1", [128, 512], F32)
            sq = sb(ph, "a_sq", [128, 512], BF16)
            ob = [sb(ph, f"a_ob{i}", [128, 512], BF16) for i in range(2)]
            gsc = sb(ph, "a_g", [128, 1], F32)
            lam_init = 0.8 - 0.6 * math.exp(-0.3 * l)
            tsc(gsc[:], att_g[:, l:l + 1], 1.0 - lam_init, None, ALU.mult)
            pS = [ps(ph, f"a_S{i}", [128, 512], F32) for i in range(3)]
            pO = [ps(ph, f"a_O{i}", [128, 512], F32) for i in range(2)]
            pL = [ps(ph, f"a_L{i}", [128, 512], F32) for i in range(2)]
            pX = ps(ph, "a_X", [128, 512], F32)
            for h in range(8):
                kTt, qTt, vt = kT2[h % 2], qT2[h % 2], vt2[h % 2]
                S.load(kTt[:], KT[h * 128:(h + 1) * 128, :])
                S.load(qTt[:], QT[h * 128:(h + 1) * 128, :])
                S.load(vt[:], AV[:, h * 128:(h + 1) * 128].rearrange("(n p) c -> p n c", p=128))
                qm = qm2[h % 2]
                tsc(qm[0][:], qTt[:], U_f[:, 63:64], None, ALU.mult)
                tsc(qm[1][:], qTt[:], Lo_f[:, 64:65], None, ALU.mult)
                items = []
                for bi, (b0, bs) in enumerate(BLOCKS):
                    if last and b0 < LC:
                        continue
                    nk = 2 if b0 < LC else NT
                    for kt in range(nk):
                        for c in range(2):
                            items.append((bi, b0, bs, kt, c, nk))

                def emit_S(j):
                    (bi_, b0_, bs_, kt_i, c_, nk_) = items[j]
                    mm(pS[j % 3][:, :bs_], kTt[:, kt_i * 128:(kt_i + 1) * 128], qm[c_][:, b0_:b0_ + bs_])

                LOOK = 2
                for j in range(min(LOOK, len(items))):
                    emit_S(j)
                for j, (bi, b0, bs, kt, c, nk) in enumerate(items):
                    if j + LOOK < len(items):
                        emit_S(j + LOOK)
                    P = pb[j % 6]
                    act(P[:, :bs], pS[j % 3][:, :bs], AF.Exp, scale=0.125)
                    mm(pO[c][:, :bs], vt[:, kt, :], P[:, :bs], start=(kt == 0), stop=(kt == nk - 1))
                    if kt % 2 == 1:
                        sm_ = psm[((kt // 2) * 2 + c) % 4]
                        tt(sm_[:, :bs], pb[(j - 2) % 6][:, :bs], P[:, :bs], ALU.add)
                        mm(pL[c][:, :bs], on_bf[:], sm_[:, :bs], start=(kt == 1), stop=(kt == nk - 1))
                    if not (kt == nk - 1 and c == 1):
                        continue
                    recip(r0[:, :bs], pL[0][:, :bs])
                    recip(r1[:, :bs], pL[1][:, :bs])
                    tt(t0[:, :bs], pO[0][:, :bs], r0[:, :bs], ALU.mult)
                    tt(t1[:, :bs], pO[1][:, :bs], r1[:, :bs], ALU.mult)
                    stt(t0[:, :bs], t1[:, :bs], neglam[:, l:l + 1], t0[:, :bs], ALU.mult, ALU.add)
                    act(sq[:, :bs], t0[:, :bs], AF.Square)
                    mm(pX[:, :bs], on_bf[:], sq[:, :bs])
                    tsc(r0[:, :bs], pX[:, :bs], 1.0 / 128, EPS, ALU.mult, ALU.add)
                    act(r0[:, :bs], r0[:, :bs], AF.Sqrt)
                    recip(r0[:, :bs], r0[:, :bs])
                    tt(t0[:, :bs], t0[:, :bs], r0[:, :bs], ALU.mult)
                    o = ob[bi % 2]
                    act(o[:, :bs], t0[:, :bs], AF.Identity, scale=gsc[:, 0:1])
                    S.store(BR[h * 128:(h + 1) * 128, b0:b0 + bs], o[:, :bs])
            S.barrier()
        if stop_after == ('att', l):
            raise _Stop()

        with ExitStack() as ph:
          if not os.environ.get('SKIP_AG'):
            fA = sb(ph, "g_f", [128, T], F32)
            bB = sb(ph, "g_b", [128, T], F32)
            kK = sb(ph, "g_k", [128, T], F32)
            tM = sb(ph, "g_t", [128, T], F32)
            hq = sb(ph, "g_hq", [128, T], F32)
            gt = [sb(ph, f"g_gate{i}", [128, 512], F32) for i in range(2)]
            qt = sb(ph, "g_qt", [128, T], BF16)
            kt_ = sb(ph, "g_kt", [128, T], BF16)
            kh = sb(ph, "g_kh", [128, T], BF16)
            m0 = sb(ph, "g_m0", [128, T], BF16)
            osum = sb(ph, "g_os", [128, T], F32)
            v128 = sb(ph, "g_v128", [128, NT, 128], BF16)
            ebend = sb(ph, "g_eb", [128, NCH], F32)
            S32 = sb(ph, "g_S32", [128, 128], F32)
            Sbf = [sb(ph, f"g_Sbf{i}", [128, 128], BF16) for i in range(4)]
            khTm = [[sb(ph, f"g_khm{i}_{j}", [128, 128], BF16) for j in range(4)] for i in range(2)]
            kh128 = [sb(ph, f"g_kh128_{i}", [128, 128], BF16) for i in range(2)]
            At = [sb(ph, f"g_At{i}", [128, 128], BF16) for i in range(2)]
            sq = sb(ph, "g_sq", [128, 512], BF16)
            rs = sb(ph, "g_rs", [128, 512], F32)
            on_ = sb(ph, "g_on", [128, 512], F32)
            ob = [sb(ph, f"g_ob{i}", [128, 512], BF16) for i in range(2)]
            lbz = sb(ph, "g_lbz", [128, 2], F32)
            pT = [ps(ph, f"g_pT{i}", [32, 128], BF16) for i in range(2)]
            pSc = [ps(ph, f"g_pS{i}", [32, 32], F32) for i in range(2)]
            pD = [ps(ph, f"g_pD{i}", [128, 128], F32) for i in range(2)]
            pOa = ps(ph, "g_pOa", [128, 512], F32)
            pOb = ps(ph, "g_pOb", [128, 512], F32)
            S.load(m0[:], m0_in[:, :])
            memset(lbz[:, 0:1], 0.0)
            memset(lbz[:, 1:2], 1.0)
            for h in range(8):
                S.load(hq[:], HQ[h * 128:(h + 1) * 128, :])
                S.load(v128[:], HI[:, h * 128:(h + 1) * 128].rearrange("(n p) c -> p n c", p=128))
                for d in range(2):
                    S.load(fA[:], (HFF if d == 0 else HFB)[h * 128:(h + 1) * 128, :])
                    if l == 0:
                        lb_ap, oml_ap = lbz[:, 0:1], lbz[:, 1:2]
                    else:
                        lb_ap, oml_ap = lb1[:, d, h:h + 1], oml1[:, d, h:h + 1]
                    act(fA[:], fA[:], AF.Sigmoid)
                    tsc(fA[:], fA[:], oml_ap, lb_ap, ALU.mult, ALU.add)
                    tsc(kK[:], fA[:], -1.0, 1.0, ALU.mult, ALU.add)
                    act(fA[:], fA[:], AF.Ln)
                    if d == 0:
                        scan(bB[:], m0[:], fA[:])
                        bend = _strided(bB[:], 31, 32, NCH)
                    else:
                        scan(_rev(bB[:]), m0[:], _rev(fA[:]))
                        bend = _strided(bB[:], 0, 32, NCH)
                    act(tM[:], bB[:], AF.Exp)
                    tt(qt[:], hq[:], tM[:], ALU.mult)
                    act(tM[:], bB[:], AF.Exp, scale=-1.0)
                    tt(kt_[:], kK[:], tM[:], ALU.mult)
                    tt(tM[:].rearrange("p (c j) -> p c j", j=32), _bc_last(bend, 32),
                       bB[:].rearrange("p (c j) -> p c j", j=32), ALU.subtract)
                    act(tM[:], tM[:], AF.Exp)
                    tt(kh[:], kK[:], tM[:], ALU.mult)
                    act(ebend[:], bend, AF.Exp)
                    memset(S32[:], 0.0)
                    memset(Sbf[0][:], 0.0)
                    order = list(range(8)) + list(range(8, NCH)) if d == 0 else list(range(7, -1, -1)) + list(range(NCH - 1, 7, -1))
                    tiles = [0, 1] + list(range(2, NT)) if d == 0 else [1, 0] + list(range(NT - 1, 1, -1))
                    BD = BDf if d == 0 else BDb
                    LAG = 2
                    pend = []

                    def blk_of(c):
                        if c < 8:
                            return 0, 256
                        return BLOCKS[1 + (c - 8) // 16]

                    def emit_tile(ti):
                        tix = tiles[ti]
                        t0_ = tix * 128
                        ts = ti % 2
                        tr(pT[ts][:, 0:128], kh[:, t0_:t0_ + 128], id_bf[:])
                        cp(kh128[ts][:], pT[ts][:, 0:128], eng='act')
                        for j in range(4):
                            if j < 2:
                                tsc(khTm[ts][j][:], kh128[ts][:], RM[:, j:j + 1], 1.0, ALU.mult, ALU.mult, eng='pool')
                            elif j == 2:
                                tsc(khTm[ts][j][:], kh128[ts][:], RM[:, j:j + 1], None, ALU.mult)
                            else:
                                act(khTm[ts][j][:], kh128[ts][:], AF.Identity, scale=RM[:, j:j + 1])
                        mm(pSc[ts][:, 0:128], kt_[:, t0_:t0_ + 128], qt[:, t0_:t0_ + 128])
                        tt(At[ts][:], pSc[ts][:, 0:128], BD, ALU.mult)
                        bstart_, bsz_ = blk_of(tix * 4)
                        oc_ = t0_ - bstart_
                        mm(pOa[:, oc_:oc_ + 128], v128[:, tix, :], At[ts][:])
                        last_tile = (t0_ + 128 == bstart_ + bsz_) if d == 0 else (t0_ == bstart_)
                        if last_tile:
                            if d == 0:
                                cp(osum[:, bstart_:bstart_ + bsz_], pOa[:, :bsz_], eng='act')
                            else:
                                tt(osum[:, bstart_:bstart_ + bsz_], osum[:, bstart_:bstart_ + bsz_], pOa[:, :bsz_], ALU.add)

                    def emit_inter(item):
                        (i_, c0_, bstart_, bsz_, lastb_) = item
                        oc_ = c0_ - bstart_
                        mm(pOb[:, oc_:oc_ + 32], Sbf[i_ % 4][:], qt[:, c0_:c0_ + 32])
                        if lastb_:
                            tt(osum[:, bstart_:bstart_ + bsz_], osum[:, bstart_:bstart_ + bsz_], pOb[:, :bsz_], ALU.add)

                    emit_tile(0)
                    for idx, c in enumerate(order):
                        c0 = c * 32
                        sl = idx % 2
                        ti = idx // 4
                        tix = tiles[ti]
                        assert tix == c // 4
                        if idx % 4 == 1 and ti + 1 < len(tiles):
                            emit_tile(ti + 1)
                        bstart, bsz = blk_of(c)
                        last_in_blk = (c0 + 32 == bstart + bsz) if d == 0 else (c0 == bstart)
                        mm(pD[sl][:, 0:128], khTm[ti % 2][c % 4][:], v128[:, tix, :])
                        pend.append((idx, c0, bstart, bsz, last_in_blk))
                        if len(pend) > LAG:
                            emit_inter(pend.pop(0))
                        stt(S32[:], S32[:], ebend[:, c:c + 1], pD[sl][:, 0:128], ALU.mult, ALU.add)
                        cp(Sbf[(idx + 1) % 4][:], S32[:], eng='act')
                    while pend:
                        emit_inter(pend.pop(0))
                gcol = hg_g[:, l:l + 1]
                for bi, (b0, bs) in enumerate(BLOCKS):
                    if last and b0 < LC:
                        continue
                    act(sq[:, :bs], osum[:, b0:b0 + bs], AF.Square)
                    px = pOa if bi % 2 == 0 else pOb
                    mm(px[:, :bs], on_bf[:], sq[:, :bs])
                    tsc(rs[:, :bs], px[:, :bs], 1.0 / 128, EPS, ALU.mult, ALU.add)
                    act(rs[:, :bs], rs[:, :bs], AF.Sqrt)
                    recip(rs[:, :bs], rs[:, :bs])
                    tt(on_[:, :bs], osum[:, b0:b0 + bs], rs[:, :bs], ALU.mult)
                    g__ = gt[bi % 2]
                    S.load(g__[:, :bs], HGATE[h * 128:(h + 1) * 128, b0:b0 + bs])
                    tt(on_[:, :bs], on_[:, :bs], g__[:, :bs], ALU.mult)
                    o = ob[bi % 2]
                    act(o[:, :bs], on_[:, :bs], AF.Identity, scale=gcol)
                    S.store(BR[1024 + h * 128:1024 + (h + 1) * 128, b0:b0 + bs], o[:, :bs])
            S.barrier()
        if stop_after == ('gla', l):
            raise _Stop()

        PADT = T + 8
        OFFC, OFFL = 2, 2 + 256 + 4

        def poff(b0):
            return (OFFC + b0) if b0 < LC else (OFFL + (b0 - LC))

        with ExitStack() as ph:
            up = [sb(ph, f"c_u{i}", [128, PADT], BF16) for i in range(3)]
            cw = sb(ph, "c_w", [128, 24, 5], F32)
            S.load(cw[:], conv_wT[l])
            dw = sb(ph, "c_dw", [128, 24, 5, 128], BF16)
            brow_f = sb(ph, "c_brf", [1, 3072], F32)
            brow = sb(ph, "c_br", [1, 3072], BF16)
            S.load(brow_f[:], bass.AP(conv_b.tensor, l * 3072, [[0, 1], [1, 3072]]))
            cp(brow[:], brow_f[:])
            onerow = on_bf[0:1, :]
            stT = [sb(ph, f"c_sT{i}", [128, 512], BF16) for i in range(3)]
            stF = [sb(ph, f"c_sF{i}", [128, 512], BF16) for i in range(3)]
            sgm = [sb(ph, f"c_sg{i}", [128, 512], F32) for i in range(4)]
            pc = [ps(ph, f"c_p{i}", [128, 512], F32) for i in range(4)]
            for i in range(3):
                memset(up[i][:], 0.0)
            for cc in range(24):
                for k in range(5):
                    tsc(dw[:, cc, k, :], id_bf[:], cw[:, cc, k:k + 1], None, ALU.mult)
            n_p = 0
            n_s = 0
            for cc in range(24):
                u = up[cc % 3]
                S.load(u[:, OFFC:OFFC + LC], XBC[cc * 128:(cc + 1) * 128, 0:LC])
                S.load(u[:, OFFL:OFFL + L], XBC[cc * 128:(cc + 1) * 128, LC:T])
                if cc < 20:
                    for tix in range(NT):
                        t0_ = tix * 128
                        p = pc[n_p % 4]
                        n_p += 1
                        po = poff(t0_)
                        for k in range(5):
                            mm(p[:, 0:128], u[:, po + k - 2:po + k - 2 + 128], dw[:, cc, k, :], start=(k == 0), stop=False)
                        mm(p[:, 0:128], onerow, brow[0:1, cc * 128:(cc + 1) * 128], start=False, stop=True)
                        sg_ = sgm[n_s % 4]
                        st = stT[n_s % 3]
                        n_s += 1
                        act(sg_[:, 0:128], p[:, 0:128], AF.Sigmoid)
                        if cc < 16:
                            tt(sg_[:, 0:128], p[:, 0:128], sg_[:, 0:128], ALU.mult)
                            S.store(XS[t0_:t0_ + 128, cc * 128:(cc + 1) * 128], sg_[:, 0:128])
                            continue
                        tt(st[:, 0:128], p[:, 0:128], sg_[:, 0:128], ALU.mult)
                        if cc < 16:
                            S.store(XS[t0_:t0_ + 128, cc * 128:(cc + 1) * 128], st[:, 0:128])
                        else:
                            S.store(BTOK[t0_:t0_ + 128, (cc - 16) * 128:(cc - 15) * 128], st[:, 0:128])
                if cc >= 16:
                    dst = BTT if cc < 20 else CTT
                    r0_ = (cc - 16) * 128 if cc < 20 else (cc - 20) * 128
                    for (b0, bs) in BLOCKS:
                        p = pc[n_p % 4]
                        n_p += 1
                        po = poff(b0)
                        for k in range(5):
                            mm(p[:, :bs], dw[:, cc, k, :], u[:, po + k - 2:po + k - 2 + bs], start=(k == 0), stop=(k == 4))
                        sg_ = sgm[n_s % 4]
                        st = stF[n_s % 3]
                        n_s += 1
                        act(sg_[:, :bs], p[:, :bs], AF.Sigmoid, bias=cbT[:, l, cc:cc + 1])
                        stt(st[:, :bs], p[:, :bs], cbT[:, l, cc:cc + 1], sg_[:, :bs], ALU.add, ALU.mult)
                        S.store(dst[r0_:r0_ + 128, b0:b0 + bs], st[:, :bs])
            S.barrier()
        if stop_after == ('conv', l):
            raise _Stop()

        with ExitStack() as ph:
            ST32 = sb(ph, "s_ST", [128, 2048], F32)
            STb = sb(ph, "s_STb", [128, 2048], BF16)
            dtb = sb(ph, "s_dtb", [128, 64], F32)
            S.load(dtb[:], bass.AP(dt_bias.tensor, l * 64, [[0, 128], [1, 64]]))
            Ab = sb(ph, "s_A", [128, 64], F32)
            S.load(Ab[:], bass.AP(a_log.tensor, l * 64, [[0, 128], [1, 64]]))
            act(Ab[:], Ab[:], AF.Exp)
            tsc(Ab[:], Ab[:], -1.0, None, ALU.mult)
            dsk = sb(ph, "s_dsk", [128, 32], F32)
            S.load(dsk[:], bass.AP(ssm_d.tensor, l * 32, [[0, 128], [1, 32]]))
            xs_ = [sb(ph, f"s_xs{i}", [128, 2048], F32) for i in range(2)]
            btk = [sb(ph, f"s_bt{i}", [128, 512], BF16) for i in range(2)]
            bT_ = [sb(ph, f"s_bT{i}", [128, 4, 128], BF16) for i in range(2)]
            cT_ = [sb(ph, f"s_cT{i}", [128, 4, 128], BF16) for i in range(2)]
            dtr = [sb(ph, f"s_dt{i}", [128, 64], F32) for i in range(2)]
            dtv = [sb(ph, f"s_dtv{i}", [128, 32], F32) for i in range(2)]
            aav = [sb(ph, f"s_a{i}", [128, 32], F32) for i in range(2)]
            eacv = [sb(ph, f"s_eac{i}", [128, 32], F32) for i in range(2)]
            wendv = [sb(ph, f"s_wend{i}", [128, 32], F32) for i in range(2)]
            dendv = [sb(ph, f"s_dend{i}", [128, 32], F32) for i in range(2)]
            xdtv = [sb(ph, f"s_xdt{i}", [128, 2048], BF16) for i in range(2)]
            xdwv = [sb(ph, f"s_xdw{i}", [128, 2048], BF16) for i in range(2)]
            Am = [sb(ph, f"s_Am{i}", [128, 8, 128], F32) for i in range(2)]
            LT = [sb(ph, f"s_LT{i}", [128, 8, 128], F32) for i in range(2)]
            MT = [sb(ph, f"s_MT{i}", [128, 8, 128], BF16) for i in range(2)]
            CBm = [sb(ph, f"s_CB{i}", [128, 128], F32) for i in range(2)]
            ytmp = sb(ph, "s_yt", [128, 512], F32)
            yo = [sb(ph, f"s_yo{i}", [128, 2048], F32) for i in range(2)]
            yfl = [sb(ph, f"s_yf{i}", [128, 2048], F32) for i in range(2)]
            zt = [sb(ph, f"s_z{i}", [128, 2048], F32) for i in range(2)]
            stmp = sb(ph, "s_st", [128, 512], F32)
            junk = sb(ph, "s_junk", [128, 2048], BF16)
            ssq = sb(ph, "s_ssq", [128, 1], F32)
            ynb = sb(ph, "s_ynb", [128, 2048], F32)
            obt = [sb(ph, f"s_ob{i}", [128, 4, 128], BF16) for i in range(2)]
            p_ac = ps(ph, "s_pac", [128, 64], F32)
            p_cb = ps(ph, "s_pcb", [128, 128], F32)
            p_df = [ps(ph, f"s_pdf{i}", [128, 512], F32) for i in range(2)]
            p_y = ps(ph, "s_py", [128, 512], F32)
            p_yi = ps(ph, "s_pyi", [128, 512], F32)
            p_st = ps(ph, "s_pst", [128, 512], F32)
            p_tr = ps(ph, "s_ptr", [128, 4, 128], F32)
            nit = 0
            for d in range(2):
                if d == 1:
                    S.barrier()
                memset(ST32[:], 0.0)
                memset(STb[:], 0.0)
                order = [0, 1] + list(range(2, NT)) if d == 0 else [1, 0] + list(range(NT - 1, 1, -1))
                Ucum = U_f if d == 0 else Lo_f
                Mlhs = Ms_f if d == 0 else Ml_f
                Mcb = U_f if d == 0 else Lo_f
                def aside(tix, sl):
                    t0_ = tix * 128
                    xs = xs_[sl]
                    S.load(xs[:], XS[t0_:t0_ + 128, :])
                    S.load(btk[sl][:], BTOK[t0_:t0_ + 128, :])
                    S.load(bT_[sl][:], BTT[:, t0_:t0_ + 128].rearrange("(g p) t -> p g t", p=128))
                    S.load(cT_[sl][:], CTT[:, t0_:t0_ + 128].rearrange("(g p) t -> p g t", p=128))
                    S.load(dtr[sl][:], DTR[t0_:t0_ + 128, :])
                    if d == 1:
                        S.load(yfl[sl][:], YF[t0_:t0_ + 128, :])
                        S.load(zt[sl][:], ZZ[t0_:t0_ + 128, :])
                    dt_, aa_, eac_, wend_, dend_ = dtv[sl], aav[sl], eacv[sl], wendv[sl], dendv[sl]
                    tt(dt_[:], dtr[sl][:, d * 32:(d + 1) * 32], dtb[:, d * 32:(d + 1) * 32], ALU.add)
                    act(dt_[:], dt_[:], AF.Exp)
                    act(dt_[:], dt_[:], AF.Ln, bias=1.0)
                    tt(aa_[:], dt_[:], Ab[:, d * 32:(d + 1) * 32], ALU.mult)
                    mm(p_ac[:, 0:32], Ucum, aa_[:])
                    mm(p_ac[:, 32:64], ON_f, aa_[:])
                    act(eac_[:], p_ac[:, 0:32], AF.Exp)
                    act(dend_[:], p_ac[:, 32:64], AF.Exp)
                    cp(wend_[:], p_ac[:, 0:32], eng='dve')
                    tt(wend_[:], p_ac[:, 32:64], wend_[:], ALU.subtract)
                    act(wend_[:], wend_[:], AF.Exp)
                    xs3 = xs[:].rearrange("p (h q) -> p h q", q=64)
                    tt(xdtv[sl][:].rearrange("p (h q) -> p h q", q=64), xs3, _bc_last(dt_[:], 64), ALU.mult, eng='pool')
                    tt(wend_[:], wend_[:], dt_[:], ALU.mult)
                    tt(xdwv[sl][:].rearrange("p (h q) -> p h q", q=64), xs3, _bc_last(wend_[:], 64), ALU.mult)

                aside(order[0], nit % 2)
                for oi, tix in enumerate(order):
                    t0_ = tix * 128
                    sl = nit % 2
                    nit += 1
                    if oi + 1 < len(order):
                        aside(order[oi + 1], nit % 2)
                    xs = xs_[sl]
                    xs3 = xs[:].rearrange("p (h q) -> p h q", q=64)
                    aa_, eac, dend, xdt, xdw = aav[sl], eacv[sl], dendv[sl], xdtv[sl], xdwv[sl]
                    yout = yo[sl]
                    for g in range(4):
                        gs = (nit * 4 + g) % 2
                        mm(p_cb[:, 0:128], bT_[sl][:, g, :], cT_[sl][:, g, :])
                        tt(CBm[gs][:], p_cb[:, 0:128], Mcb, ALU.mult)
                        tt(Am[gs][:], _bc_mid(Ucum, 8), _bc_last(aa_[:, g * 8:(g + 1) * 8], 128), ALU.mult, eng='pool')
                        for half in range(2):
                            mm(p_df[half][:, 0:512], Mlhs, Am[gs][:, half * 4:(half + 1) * 4, :].rearrange("p h t -> p (h t)"))
                        act(LT[gs][:, 0:4, :], p_df[0][:].rearrange("p (h t) -> p h t", t=128), AF.Exp)
                        act(LT[gs][:, 4:8, :], p_df[1][:].rearrange("p (h t) -> p h t", t=128), AF.Exp)
                        tt(MT[gs][:], LT[gs][:], _bc_mid(CBm[gs][:], 8), ALU.mult)
                        dbgstop(3)
                        for hh in range(8):
                            hg = g * 8 + hh
                            mm(p_y[:, hh * 64:(hh + 1) * 64], MT[gs][:, hh, :], xdt[:, hg * 64:(hg + 1) * 64])
                        mm(p_yi[:], cT_[sl][:, g, :], STb[:, g * 512:(g + 1) * 512])
                        tt(ytmp[:].rearrange("p (h q) -> p h q", q=64), p_yi[:].rearrange("p (h q) -> p h q", q=64),
                           _bc_last(eac[:, g * 8:(g + 1) * 8], 64), ALU.mult)
                        tt(yout[:, g * 512:(g + 1) * 512], ytmp[:], p_y[:], ALU.add)
                        dbgstop(4)
                        mm(p_st[:], btk[sl][:, g * 128:(g + 1) * 128], xdw[:, g * 512:(g + 1) * 512])
                        tt(stmp[:].rearrange("p (h q) -> p h q", q=64),
                           ST32[:, g * 512:(g + 1) * 512].rearrange("p (h q) -> p h q", q=64),
                           _bc_last(dend[:, g * 8:(g + 1) * 8], 64), ALU.mult)
                        tt(ST32[:, g * 512:(g + 1) * 512], stmp[:], p_st[:], ALU.add)
                        cp(STb[:, g * 512:(g + 1) * 512], ST32[:, g * 512:(g + 1) * 512], eng='act')
                        dbgstop(5)
                    if d == 0:
                        S.store(YF[t0_:t0_ + 128, :], yout[:])
                        dbgstop(6)
                        if tix == 33:
                            dbgstop(7)
                    else:
                        if last and tix < 2:
                            continue
                        tt(yout[:], yout[:], yfl[sl][:], ALU.add)
                        tt(yfl[sl][:].rearrange("p (h q) -> p h q", q=64), xs3, _bc_last(dsk[:], 64), ALU.mult)
                        tt(yout[:], yout[:], yfl[sl][:], ALU.add)
                        tt(yout[:], yout[:], zt[sl][:], ALU.mult)
                        act(junk[:], yout[:], AF.Square, accum=ssq[:])
                        tsc(ssq[:], ssq[:], 1.0 / 2048, EPS, ALU.mult, ALU.add)
                        act(ssq[:], ssq[:], AF.Sqrt)
                        recip(ssq[:], ssq[:])
                        tsc(ynb[:], yout[:], ssq[:, 0:1], None, ALU.mult)
                        for q4 in range(4):
                            for j in range(4):
                                tr(p_tr[:, j * 128:(j + 1) * 128], ynb[:, (q4 * 4 + j) * 128:(q4 * 4 + j + 1) * 128], ID_f)
                            o = obt[q4 % 2]
                            for j in range(4):
                                act(o[:, j, :], p_tr[:, j * 128:(j + 1) * 128], AF.Identity, scale=ssm_g[:, l, q4 * 4 + j:q4 * 4 + j + 1])
                            S.store(BR[2048 + q4 * 512:2048 + (q4 + 1) * 512, t0_:t0_ + 128].rearrange("(j p) t -> p j t", p=128), o[:])
                S.barrier()
        if stop_after == ('ssd', l):
            raise _Stop()

        with ExitStack() as ph:
            wall = sb(ph, "m_w", [128, 40, 1024], BF16)
            with ExitStack() as pw:
                wstg = [sb(pw, f"m_ws{i}", [128, 4, 1024], F32) for i in range(2)]
                srcs = [(w_b_att[l], 8), (w_b_hg[l], 8), (w_b_ssm[l], 16), (w_out[l], 8)]
                kc0 = 0
                i = 0
                for (src, nk_) in srcs:
                    for k4 in range(0, nk_, 4):
                        f = wstg[i % 2]
                        S.load(f[:], src[k4 * 128:(k4 + 4) * 128, :].rearrange("(k p) c -> p k c", p=128))
                        for kk_ in range(4):
                            cp(wall[:, kc0 + k4 + kk_, :], f[:, kk_, :], eng=('dve' if (kk_ + i) % 2 == 0 else 'act'),
                               sub={wall.name: kc0 + k4 + kk_})
                        i += 1
                    kc0 += nk_
                S.barrier()
            MB = 256
            brb = [sb(ph, f"m_br{i}", [128, 32, MB], BF16) for i in range(2)]
            gb = [sb(ph, f"m_g{i}", [128, 24, MB], F32) for i in range(2)]
            xb = [sb(ph, f"m_x{i}", [128, 8, MB], F32) for i in range(2)]
            macc = sb(ph, "m_acc", [128, MB], F32)
            mtmp = sb(ph, "m_tmp", [128, MB], F32)
            mT = sb(ph, "m_mT", [128, 8, MB], BF16)
            pm = [ps(ph, f"m_p{i}", [128, MB], F32) for i in range(4)]
            npm = 0
            bi = 0
            for b0 in range(0, T, MB):
                if last and b0 < LC:
                    continue
                bs = MB
                s = strm(b0)
                br = brb[bi % 2]
                g_ = gb[bi % 2]
                x_ = xb[bi % 2]
                bi += 1
                S.load(br[:], BR[:, b0:b0 + bs].rearrange("(k p) t -> p k t", p=128))
                S.load(g_[:], GATES[:, b0:b0 + bs].rearrange("(k p) t -> p k t", p=128))
                S.load(x_[:], xsrc[:, b0:b0 + bs].rearrange("(k p) t -> p k t", p=128))
                for oc in range(8):
                    for bri, (k0, nk_) in enumerate(((0, 8), (8, 8), (16, 16))):
                        p = pm[npm % 4]
                        npm += 1
                        for k in range(nk_):
                            mm(p[:, :MB], wall[:, k0 + k, oc * 128:(oc + 1) * 128], br[:, k0 + k, :], start=(k == 0), stop=(k == nk_ - 1))
                        if bri == 0:
                            tt(macc[:], p[:, :MB], g_[:, oc, :], ALU.mult)
                        else:
                            tt(mtmp[:], p[:, :MB], g_[:, bri * 8 + oc, :], ALU.mult)
                            if bri == 1:
                                tt(macc[:], macc[:], mtmp[:], ALU.add)
                            else:
                                tt(mT[:, oc, :], macc[:], mtmp[:], ALU.add)
                for oc in range(8):
                    p = pm[npm % 4]
                    npm += 1
                    for k in range(8):
                        mm(p[:, :MB], wall[:, 32 + k, oc * 128:(oc + 1) * 128], mT[:, k, :], start=(k == 0), stop=(k == 7))
                    stt(x_[:, oc, :], p[:, :MB], modT[:, l, 16 + oc, s:s + 1], x_[:, oc, :], ALU.mult, ALU.add)
                S.store(XT[:, b0:b0 + bs].rearrange("(k p) t -> p k t", p=128), x_[:])
            S.barrier()
        if stop_after == ('merge', l):
            raise _Stop()

        with ExitStack() as ph:
            w1b = sb(ph, "f_w1", [128, 8, FFH], BF16)
            w3b = sb(ph, "f_w3", [128, 8, FFH], BF16)
            w2b = sb(ph, "f_w2", [128, 22, 1024], BF16)
            with ExitStack() as pw:
                wstg = [sb(pw, f"f_ws{i}", [128, FFH], F32) for i in range(2)]
                i = 0
                for (src, dstw) in ((ffn_w1[l], w1b), (ffn_w3[l], w3b)):
                    for k in range(8):
                        f = wstg[i % 2]
                        S.load(f[:], src[k * 128:(k + 1) * 128, :])
                        cp(dstw[:, k, :], f[:], eng=('dve' if i % 2 == 0 else 'act'))
                        i += 1
                for k in range(22):
                    f = wstg[i % 2]
                    S.load(f[:, 0:1024], ffn_w2[l][k * 128:(k + 1) * 128, :])
                    cp(w2b[:, k, :], f[:, 0:1024], eng=('dve' if i % 2 == 0 else 'act'))
                    i += 1
                S.barrier()
            FB = 256
            xb = [sb(ph, f"f_x{i}", [128, 8, FB], F32) for i in range(2)]
            sq = sb(ph, "f_sq", [128, 8, FB], BF16)
            rs = sb(ph, "f_rs", [128, FB], F32)
            tmp = sb(ph, "f_tmp", [128, 8, FB], F32)
            h2 = sb(ph, "f_h2", [128, 8, FB], BF16)
            uT = sb(ph, "f_u", [128, 22, FB], BF16)
            sg_ = [sb(ph, f"f_sg{i}", [128, FB], F32) for i in range(2)]
            s1 = [sb(ph, f"f_s1{i}", [128, FB], F32) for i in range(2)]
            yo_ = [sb(ph, f"f_yo{i}", [128, 8, FB], F32) for i in range(1)]
            pst = ps(ph, "f_pst", [128, FB], F32)
            pf = [ps(ph, f"f_p{i}", [128, FB], F32) for i in range(6)]
            npf = 0
            nblk = 0
            for b0 in range(0, T, FB):
                if last and b0 < LC:
                    continue
                bs = FB
                s = strm(b0)
                x_ = xb[nblk % 2]
                nblk += 1
                S.load(x_[:], XT[:, b0:b0 + bs].rearrange("(k p) t -> p k t", p=128))
                norm_block((sq, pst, rs, tmp), x_, bs,
                           lambda k: A2[:, l, k, s:s + 1], lambda k: modT[:, l, 24 + k, s:s + 1],
                           lambda k: h2[:, k, :])
                for hc in range(22):
                    pa = pf[npf % 6]
                    pb_ = pf[(npf + 1) % 6]
                    npf += 2
                    for k in range(8):
                        mm(pa[:, :FB], w1b[:, k, hc * 128:(hc + 1) * 128], h2[:, k, :], start=(k == 0), stop=(k == 7))
                    for k in range(8):
                        mm(pb_[:, :FB], w3b[:, k, hc * 128:(hc + 1) * 128], h2[:, k, :], start=(k == 0), stop=(k == 7))
                    sg = sg_[hc % 2]
                    s1_ = s1[hc % 2]
                    act(sg[:], pa[:, :FB], AF.Sigmoid)
                    tt(s1_[:], pa[:, :FB], sg[:], ALU.mult)
                    tt(uT[:, hc, :], pb_[:, :FB], s1_[:], ALU.mult)
                for oc in range(8):
                    p = pf[npf % 6]
                    npf += 1
                    for k in range(22):
                        mm(p[:, :FB], w2b[:, k, oc * 128:(oc + 1) * 128], uT[:, k, :], start=(k == 0), stop=(k == 21))
                    stt(x_[:, oc, :], p[:, :FB], modT[:, l, 40 + oc, s:s + 1], x_[:, oc, :], ALU.mult, ALU.add)
                if not last:
                    S.store(XT[:, b0:b0 + bs].rearrange("(k p) t -> p k t", p=128), x_[:])
                else:
                    yo = yo_[0]
                    norm_block((sq, pst, rs, tmp), x_, bs,
                               lambda k: gfs[:, k:k + 1], lambda k: None,
                               lambda k: yo[:, k, :])
                    S.store(yT[:, b0 - LC:b0 - LC + bs].rearrange("(k p) t -> p k t", p=128), yo[:])
            S.barrier()


    stopped = False
    try:
        for l in range(DEPTH):
            layer_body(l)
    except _Stop:
        stopped = True
    S.barrier()
    if not stopped:
        per.close()
        es.close()
    return nc, S


_CACHE = {}


def _consts():
    j = np.arange(128)[:, None]
    t = np.arange(128)[None, :]
    c = np.zeros((128, 6 * 128 + 64 + 4 + 256), np.float32)
    c[:, 0:128] = (j <= t)
    c[:, 128:256] = (j >= t)
    c[:, 256:384] = (j > t)
    c[:, 384:512] = (j < t)
    c[:, 512:640] = np.eye(128)
    c[:, 640:768] = 1.0
    s = np.arange(32)[:, None]
    tt_ = np.arange(32)[None, :]
    c[0:32, 768:800] = (s <= tt_)
    c[0:32, 800:832] = (s >= tt_)
    for q in range(4):
        c[32 * q:32 * q + 32, 832 + q] = 1.0
    same = (j // 32) == (t // 32)
    c[:, 836:964] = same & (j <= t)
    c[:, 964:1092] = same & (j >= t)
    return c


def _m0():
    m0 = np.ones(T, np.float32)
    m0[::32] = 0.0
    return np.ascontiguousarray(np.broadcast_to(m0[None, :], (128, T))).astype(ml_dtypes.bfloat16)


def _rope_tables():
    cos = np.ones((128, T), np.float64)
    sin = np.zeros((128, T), np.float64)
    tl = np.arange(L)
    row = (tl // 64).astype(np.float64)
    col = (tl % 64).astype(np.float64)
    for f in range(128):
        r = f % 32
        half_sel = (f % 64) // 32
        fi = r % 16
        freq = np.float32(10000.0) ** (-np.float32(fi) / np.float32(16))
        pos = row if half_sel == 0 else col
        ang = (pos.astype(np.float32) * np.float32(freq)).astype(np.float64)
        cos[f, LC:] = np.cos(ang)
        sgn = -1.0 if r < 16 else 1.0
        sin[f, LC:] = sgn * np.sin(ang)
    return cos.astype(np.float32), sin.astype(np.float32)


def _rot_perm():
    p = np.arange(1024)
    r = p % 32
    return np.where(r < 16, p + 16, p - 16)


def _prep_shared(inp):
    f = np.float32
    A = lambda a: np.ascontiguousarray(a, dtype=f)
    perm = _rot_perm()
    w_in = inp['w_in']
    w_rot = np.concatenate([w_in[:, :, O_AK:O_AK + 1024][:, :, perm], w_in[:, :, O_AQ:O_AQ + 1024][:, :, perm]], axis=2)
    cosT, sinT = _rope_tables()
    col8 = lambda g: A(g.reshape(8, 128).T)
    sh = {
        'w_ada': A(inp['w_ada']),
        'b_adaT': A(inp['b_ada'].reshape(DEPTH, 48, 128).transpose(0, 2, 1)),
        'g1T': A(inp['norm1_g'].reshape(DEPTH, 8, 128).transpose(0, 2, 1)),
        'g2T': A(inp['norm2_g'].reshape(DEPTH, 8, 128).transpose(0, 2, 1)),
        'gfT': col8(inp['final_g']),
        'w_in': A(w_in),
        'w_rot': A(w_rot),
        'cosT': cosT, 'sinT': sinT,
        'att_lam': A(inp['att_lambda'].reshape(DEPTH, 256)),
        'att_gT': A(inp['att_norm_g'].T),
        'hg_lbT': A(inp['hg_lb_logits'].reshape(2, DEPTH, 8, 128).transpose(3, 0, 1, 2)),
        'hg_gT': A(inp['hg_norm_g'].T),
        'conv_wT': A(inp['ssm_conv_w'].reshape(DEPTH, 5, 24, 128).transpose(0, 3, 2, 1)),
        'conv_bT': A(inp['ssm_conv_b'].reshape(DEPTH, 24, 128).transpose(0, 2, 1)),
        'conv_b': A(inp['ssm_conv_b']),
        'dt_bias': A(inp['ssm_dt_bias'].reshape(DEPTH, 64)),
        'a_log': A(inp['ssm_a_log'].reshape(DEPTH, 64)),
        'ssm_d': A(inp['ssm_d']),
        'ssm_gT': A(inp['ssm_norm_g'].reshape(DEPTH, 16, 128).transpose(0, 2, 1)),
        'w_b_att': A(inp['w_branch_att']), 'w_b_hg': A(inp['w_branch_hg']), 'w_b_ssm': A(inp['w_branch_ssm']),
        'w_out': A(inp['w_out']),
        'ffn_w1': A(inp['ffn_w1']), 'ffn_w3': A(inp['ffn_w3']), 'ffn_w2': A(inp['ffn_w2']),
        'cst': _consts(),
        'm0': _m0(),
    }
    return sh


def kernel(**inputs):
    inp = {k: np.asarray(v) for k, v in inputs.items()}
    if 'nc' not in _CACHE:
        _CACHE['nc'] = build_program()[0]
    nc = _CACHE['nc']
    sh = _prep_shared(inp)
    in_maps = []
    for b in range(8):
        m = dict(sh)
        m['xT'] = np.ascontiguousarray(np.concatenate([inp['ctx'][b], inp['x'][b]], axis=0).T, dtype=np.float32)
        m['c2'] = np.ascontiguousarray(np.stack([inp['c_ctx'], inp['c'][b]], axis=1), dtype=np.float32)
        in_maps.append(m)
    res = run_bass_kernel_spmd(nc, in_maps, core_ids=list(range(8)))
    out = np.stack([np.ascontiguousarray(res.results[b]['yT'].T) for b in range(8)], axis=0)
    return out.astype(np.float32)
```

```python
import math
from contextlib import ExitStack
import numpy as np
import ml_dtypes
import concourse.bass as bass
import concourse.mybir as mybir
from concourse.bass_utils import run_bass_kernel_spmd

F32 = mybir.dt.float32
BF16 = mybir.dt.bfloat16
AF = mybir.ActivationFunctionType
ALU = mybir.AluOpType

D = 1024
LC = 256
L = 4096
T = LC + L
NT = T // 128
NCH = T // 32
DEPTH = 2
EPS = 1e-6
NIN = 16448
FFH = 2816
BLOCKS = [(0, 256)] + [(256 + 512 * i, 512) for i in range(8)]
O_AK, O_AV, O_HFF, O_HFB, O_HI, O_XBC, O_DTF, O_DTB, O_AQ, O_HQ, O_HGATE, O_Z, O_GATES = (
    0, 1024, 2048, 3072, 4096, 5120, 8192, 8224, 8256, 9280, 10304, 11328, 13376)


class _Stop(Exception):
    pass


class Sched:
    def __init__(self, nc, es):
        self.nc = nc
        self.eng = {'pe': nc.tensor, 'act': nc.scalar, 'dve': nc.vector, 'pool': nc.gpsimd, 'sp': nc.sync}
        self.semh = {}
        self.cnt = {}
        for k in ['pe', 'act', 'dve', 'pool']:
            self.semh[k] = es.enter_context(nc.semaphore('s_' + k))
            self.cnt[k] = 0
        self.dpool = {'sp': [], 'pool': []}
        for q, n in (('sp', 24), ('pool', 16)):
            for i in range(n):
                nm = f'd_{q}{i}'
                self.semh[nm] = es.enter_context(nc.semaphore(nm))
                self.cnt[nm] = 0
                self.dpool[q].append(nm)
        self.drr = {'sp': 0, 'pool': 0}
        self.seen = {k: {} for k in self.eng}
        self.res = {}
        self.nops = 0
        self.psum = set()

    def _keys(self, ap, sub):
        nm = ap.tensor.name
        if nm in self.psum:
            return [(nm, None)]
        if sub is not None and nm in sub:
            v = sub[nm]
            if isinstance(v, (list, tuple)):
                return [(nm, x) for x in v]
            return [(nm, v)]
        return [(nm, None)]

    def _deps(self, rkeys, wkeys, eng=None):
        deps = []
        for k in rkeys:
            st = self.res.get(k)
            if st is not None and st[0] is not None:
                deps.append(st[0])
        for k in wkeys:
            st = self.res.get(k)
            if st is not None:
                if st[0] is not None and st[0][0] != eng:
                    deps.append(st[0])
                deps.extend(e for e in st[1] if e[0] != eng)
        return deps

    def _wait(self, eng, deps):
        need = {}
        for (k, v) in deps:
            if eng == 'pe' and k == 'pe':
                continue
            if self.seen[eng].get(k, 0) >= v:
                continue
            if need.get(k, 0) < v:
                need[k] = v
        for k, v in need.items():
            self.eng[eng].wait_ge(self.semh[k], v)
            self.seen[eng][k] = v

    def _record(self, ev, rkeys, wkeys):
        for k in rkeys:
            st = self.res.setdefault(k, [None, []])
            st[1] = [e for e in st[1] if e[0] != ev[0]] + [ev]
        for k in wkeys:
            self.res[k] = [ev, []]

    def op(self, eng, fn, ins, outs, sub=None):
        rkeys = [k for a in ins if a is not None and hasattr(a, 'tensor') for k in self._keys(a, sub)]
        wkeys = [k for a in outs for k in self._keys(a, sub)]
        wkeys += [k for k in rkeys if k[0] in self.psum and k not in wkeys]
        self._wait(eng, self._deps(rkeys, wkeys, eng))
        inst = fn(self.eng[eng])
        self.cnt[eng] += 1
        inst.then_inc(self.semh[eng], 1)
        self._record((eng, self.cnt[eng]), rkeys, wkeys)
        self.nops += 1

    def dma(self, q, out, in_, sub=None, track_out=True, track_in=True):
        rkeys = self._keys(in_, sub) if track_in else []
        wkeys = self._keys(out, sub) if track_out else []
        pool = self.dpool[q]
        nm = pool[self.drr[q] % len(pool)]
        self.drr[q] += 1
        deps = self._deps(rkeys, wkeys)
        if self.cnt[nm] > 0:
            deps.append((nm, self.cnt[nm]))
        self._wait(q, deps)
        self.eng[q].dma_start(out=out, in_=in_).then_inc(self.semh[nm], 16)
        self.cnt[nm] += 16
        self._record((nm, self.cnt[nm]), rkeys, wkeys)
        self.nops += 1

    def load(self, out, in_, sub=None):
        self.dma('sp', out, in_, sub=sub, track_in=False)

    def store(self, out, in_, sub=None):
        self.dma('pool', out, in_, sub=sub, track_out=False)

    def barrier(self):
        allev = [(k, v) for k, v in self.cnt.items() if v > 0]
        for e in self.eng:
            self._wait(e, allev)
        self.res = {}


def _bc_mid(a, n):
    ap = [list(x) for x in a.ap]
    return bass.AP(a.tensor, a.offset, [ap[0], [0, n]] + ap[1:])


def _bc_last(a, n):
    ap = [list(x) for x in a.ap]
    return bass.AP(a.tensor, a.offset, ap + [[0, n]])


def _rev(a):
    ap = [list(x) for x in a.ap]
    assert len(ap) == 2 and ap[1][0] == 1
    return bass.AP(a.tensor, a.offset + ap[1][1] - 1, [ap[0], [-1, ap[1][1]]])


def _strided(a, start, step, n):
    ap = [list(x) for x in a.ap]
    return bass.AP(a.tensor, a.offset + start, [ap[0], [step, n]])


def _pbcast(dram_ap_1d_offset_tensor, offset, n):
    return bass.AP(dram_ap_1d_offset_tensor, offset, [[0, 128], [1, n]])


def build_program(stop_after=None, dump=None):
    nc = bass.Bass("TRN2", target_bir_lowering=False)
    es = ExitStack()
    S = Sched(nc, es)

    def din(name, shape, dt=F32):
        return nc.dram_tensor(name, list(shape), dt, kind="ExternalInput").ap()

    def dscr(name, shape, dt):
        if dump is not None and name in dump:
            return nc.dram_tensor(name, list(shape), dt, kind="ExternalOutput").ap()
        return nc.dram_tensor(name, list(shape), dt).ap()

    xT_in = din("xT", [D, T])
    c2_in = din("c2", [D, 2])
    w_ada = din("w_ada", [DEPTH, D, 6 * D])
    b_adaT = din("b_adaT", [DEPTH, 128, 48])
    g1T = din("g1T", [DEPTH, 128, 8])
    g2T = din("g2T", [DEPTH, 128, 8])
    gfT = din("gfT", [128, 8])
    w_in = din("w_in", [DEPTH, D, NIN])
    w_rot = din("w_rot", [DEPTH, D, 2048])
    cosT = din("cosT", [128, T])
    sinT = din("sinT", [128, T])
    att_lam = din("att_lam", [DEPTH, 256])
    att_gT = din("att_gT", [128, DEPTH])
    hg_lbT = din("hg_lbT", [128, 2, DEPTH, 8])
    hg_gT = din("hg_gT", [128, DEPTH])
    conv_wT = din("conv_wT", [DEPTH, 128, 24, 5])
    conv_bT = din("conv_bT", [DEPTH, 128, 24])
    conv_b = din("conv_b", [DEPTH, 3072])
    dt_bias = din("dt_bias", [DEPTH, 64])
    a_log = din("a_log", [DEPTH, 64])
    ssm_d = din("ssm_d", [DEPTH, 32])
    ssm_gT = din("ssm_gT", [DEPTH, 128, 16])
    w_b_att = din("w_b_att", [DEPTH, 1024, D])
    w_b_hg = din("w_b_hg", [DEPTH, 1024, D])
    w_b_ssm = din("w_b_ssm", [DEPTH, 2048, D])
    w_out = din("w_out", [DEPTH, D, D])
    ffn_w1 = din("ffn_w1", [DEPTH, D, FFH])
    ffn_w3 = din("ffn_w3", [DEPTH, D, FFH])
    ffn_w2 = din("ffn_w2", [DEPTH, FFH, D])
    cst_in = din("cst", [128, 6 * 128 + 64 + 4 + 256])
    m0_in = din("m0", [128, T], BF16)
    yT = nc.dram_tensor("yT", [D, L], F32, kind="ExternalOutput").ap()

    XT = dscr("XT", [D, T], F32)
    QT = dscr("QT", [1024, T], BF16)
    KT = dscr("KT", [1024, T], BF16)
    AV = dscr("AV", [T, 1024], BF16)
    HFF = dscr("HFF", [1024, T], F32)
    HFB = dscr("HFB", [1024, T], F32)
    HQ = dscr("HQ", [1024, T], F32)
    HI = dscr("HI", [T, 1024], BF16)
    HGATE = dscr("HGATE", [1024, T], F32)
    XBC = dscr("XBC", [3072, T], BF16)
    DTR = dscr("DTR", [T, 64], F32)
    ZZ = dscr("ZZ", [T, 2048], F32)
    GATES = dscr("GATES", [3072, T], F32)
    XS = dscr("XS", [T, 2048], F32)
    BTOK = dscr("BTOK", [T, 512], BF16)
    BTT = dscr("BTT", [512, T], BF16)
    CTT = dscr("CTT", [512, T], BF16)
    YF = dscr("YF", [T, 2048], F32)
    BR = dscr("BR", [4096, T], BF16)

    uid = [0]

    def sb(ctx, name, shape, dt):
        uid[0] += 1
        return ctx.enter_context(nc.sbuf_tensor(f"{name}_{uid[0]}", list(shape), dt))

    def ps(ctx, name, shape, dt=F32):
        uid[0] += 1
        t = ctx.enter_context(nc.psum_tensor(f"{name}_{uid[0]}", [128, 512] if dt == F32 else [128, 1024], dt))
        S.psum.add(t.name)
        return t

    def mm(out, lhsT, rhs, start=True, stop=True, sub=None):
        S.op('pe', lambda e: e.matmul(out, lhsT=lhsT, rhs=rhs, start=start, stop=stop), [lhsT, rhs], [out], sub)

    def tr(out, in_, ident, sub=None):
        S.op('pe', lambda e: e.transpose(out, in_, ident), [in_, ident], [out], sub)

    def act(out, in_, func, bias=None, scale=None, accum=None, sub=None, eng='act'):
        kw = {}
        if bias is not None:
            kw['bias'] = bias
        if scale is not None:
            kw['scale'] = scale
        if accum is not None:
            kw['accum_out'] = accum
        ins = [in_] + [x for x in (bias, scale) if hasattr(x, 'tensor')]
        outs = [out] + ([accum] if accum is not None else [])
        S.op('act', lambda e: e.activation(out=out, in_=in_, func=func, **kw), ins, outs, sub)

    def tt(out, in0, in1, op, sub=None, eng='dve'):
        S.op(eng, lambda e: e.tensor_tensor(out=out, in0=in0, in1=in1, op=op), [in0, in1], [out], sub)

    def tsc(out, in0, s1, s2, op0, op1=None, sub=None, eng='dve'):
        ins = [in0] + [x for x in (s1, s2) if hasattr(x, 'tensor')]
        if op1 is None:
            S.op(eng, lambda e: e.tensor_scalar(out=out, in0=in0, scalar1=s1, scalar2=None, op0=op0), ins, [out], sub)
        else:
            S.op(eng, lambda e: e.tensor_scalar(out=out, in0=in0, scalar1=s1, scalar2=s2, op0=op0, op1=op1), ins, [out], sub)

    def stt(out, in0, scalar, in1, op0, op1, sub=None):
        ins = [in0, in1] + ([scalar] if hasattr(scalar, 'tensor') else [])
        S.op('dve', lambda e: e.scalar_tensor_tensor(out=out, in0=in0, scalar=scalar, in1=in1, op0=op0, op1=op1), ins, [out], sub)

    def cp(out, in_, sub=None, eng='dve'):
        if eng == 'act':
            act(out, in_, AF.Copy, sub=sub)
        else:
            S.op(eng, lambda e: e.tensor_copy(out=out, in_=in_), [in_], [out], sub)

    def recip(out, in_, sub=None):
        S.op('dve', lambda e: e.reciprocal(out=out, in_=in_), [in_], [out], sub)

    def memset(ap, val, eng='dve', sub=None):
        S.op(eng, lambda e: e.memset(ap, val), [], [ap], sub)

    def scan(out, d0, d1, sub=None):
        S.op('dve', lambda e: e.tensor_tensor_scan(out=out, data0=d0, data1=d1, initial=0.0, op0=ALU.mult, op1=ALU.add),
             [d0, d1], [out], sub)

    per = ExitStack()
    cst = sb(per, "cst", [128, 6 * 128 + 64 + 4 + 256], F32)
    S.load(cst[:], cst_in[:, :])
    U_f = cst[:, 0:128]
    Lo_f = cst[:, 128:256]
    Ms_f = cst[:, 256:384]
    Ml_f = cst[:, 384:512]
    ID_f = cst[:, 512:640]
    ON_f = cst[:, 640:768]
    RM = cst[:, 832:836]
    BDf = cst[:, 836:964]
    BDb = cst[:, 964:1092]
    M32 = cst[:, 768:832]
    id_bf = sb(per, "id_bf", [128, 128], BF16)
    on_bf = sb(per, "on_bf", [128, 128], BF16)
    cp(id_bf[:], ID_f)
    cp(on_bf[:], ON_f)
    modT = sb(per, "modT", [128, DEPTH, 48, 2], F32)
    A1 = sb(per, "A1", [128, DEPTH, 8, 2], F32)
    A2 = sb(per, "A2", [128, DEPTH, 8, 2], F32)
    g1s = sb(per, "g1s", [128, DEPTH, 8], F32)
    g2s = sb(per, "g2s", [128, DEPTH, 8], F32)
    gfs = sb(per, "gfs", [128, 8], F32)
    S.load(g1s[:], g1T.rearrange("l p k -> p l k"))
    S.load(g2s[:], g2T.rearrange("l p k -> p l k"))
    S.load(gfs[:], gfT[:, :])
    att_g = sb(per, "att_g", [128, DEPTH], F32)
    hg_g = sb(per, "hg_g", [128, DEPTH], F32)
    S.load(att_g[:], att_gT[:, :])
    S.load(hg_g[:], hg_gT[:, :])
    lbx = sb(per, "lbx", [128, 2, DEPTH, 8], F32)
    S.load(lbx[:], hg_lbT[:, :, :, :])
    lb1 = sb(per, "lb1", [128, 2, 8], F32)
    oml1 = sb(per, "oml1", [128, 2, 8], F32)
    neglam = sb(per, "neglam", [128, DEPTH], F32)
    ssm_g = sb(per, "ssm_g", [128, DEPTH, 16], F32)
    S.load(ssm_g[:], ssm_gT.rearrange("l p k -> p l k"))
    cbT = sb(per, "cbT", [128, DEPTH, 24], F32)
    S.load(cbT[:], conv_bT.rearrange("l p k -> p l k"))

    with ExitStack() as ph:
        sc = sb(ph, "p0_sc", [128, 8, 2], F32)
        S.load(sc[:], c2_in.rearrange("(k p) s -> p k s", p=128))
        sg = sb(ph, "p0_sg", [128, 8, 2], F32)
        act(sg[:], sc[:], AF.Sigmoid)
        tt(sc[:], sc[:], sg[:], ALU.mult)
        badd = sb(ph, "p0_b", [128, DEPTH, 48], F32)
        S.load(badd[:], b_adaT.rearrange("l p k -> p l k"))
        wst = [sb(ph, f"p0_w{i}", [128, 8, 512], F32) for i in range(2)]
        pm = ps(ph, "p0_pm", [128, 96], F32)
        it = 0
        for l in range(DEPTH):
            for cg in range(12):
                w = wst[it % 2]
                it += 1
                S.load(w[:], w_ada[l, :, cg * 512:(cg + 1) * 512].rearrange("(k p) c -> p k c", p=128))
                for j in range(4):
                    ch = cg * 4 + j
                    for k in range(8):
                        mm(pm[:, ch * 2:ch * 2 + 2], w[:, k, j * 128:(j + 1) * 128], sc[:, k, :], start=(k == 0), stop=(k == 7))
            tt(modT[:, l, :, :], pm[:, 0:96].rearrange("p (c s) -> p c s", s=2), _bc_last(badd[:, l, :], 2), ALU.add)
            tsc(A1[:, l, :, :], modT[:, l, 8:16, :], 1.0, None, ALU.add)
            tt(A1[:, l, :, :], A1[:, l, :, :], _bc_last(g1s[:, l, :], 2), ALU.mult)
            tsc(A2[:, l, :, :], modT[:, l, 32:40, :], 1.0, None, ALU.add)
            tt(A2[:, l, :, :], A2[:, l, :, :], _bc_last(g2s[:, l, :], 2), ALU.mult)
        lamt = sb(ph, "p0_lam", [128, DEPTH, 4, 64], F32)
        S.load(lamt[:].rearrange("p a b c -> p (a b c)"), bass.AP(att_lam.tensor, 0, [[0, 128], [1, DEPTH * 256]]))
        pr = sb(ph, "p0_pr", [128, 64], F32)
        sm = sb(ph, "p0_sm", [128, 4], F32)
        for l in range(DEPTH):
            for j in range(2):
                tt(pr[:], lamt[:, l, 2 * j, :], lamt[:, l, 2 * j + 1, :], ALU.mult)
                act(pr[:], pr[:], AF.Copy, accum=sm[:, 2 * l + j:2 * l + j + 1])
        act(sm[:], sm[:], AF.Exp)
        for l in range(DEPTH):
            lam_init = 0.8 - 0.6 * math.exp(-0.3 * l)
            tt(neglam[:, l:l + 1], sm[:, 2 * l + 1:2 * l + 2], sm[:, 2 * l:2 * l + 1], ALU.subtract)
            tsc(neglam[:, l:l + 1], neglam[:, l:l + 1], -lam_init, None, ALU.add)
        tt(lb1[:], lbx[:, :, 1, :], lbx[:, :, 0, :], ALU.subtract)
        act(lb1[:], lb1[:], AF.Sigmoid)
        tsc(oml1[:], lb1[:], -1.0, 1.0, ALU.mult, ALU.add)
        S.barrier()

    def norm_block(ph_tiles, xb, bs, A_of_k, B_of_k, out_of_k):
        sq, pst, rs, tmp = ph_tiles
        for k in range(8):
            act(sq[:, k, :bs], xb[:, k, :bs], AF.Square)
        for k in range(8):
            mm(pst[:, :bs], on_bf[:], sq[:, k, :bs], start=(k == 0), stop=(k == 7))
        tsc(rs[:, :bs], pst[:, :bs], 1.0 / D, EPS, ALU.mult, ALU.add)
        act(rs[:, :bs], rs[:, :bs], AF.Sqrt)
        recip(rs[:, :bs], rs[:, :bs])
        for k in range(8):
            tt(tmp[:, k, :bs], xb[:, k, :bs], rs[:, :bs], ALU.mult)
            b = B_of_k(k)
            if b is None:
                act(out_of_k(k), tmp[:, k, :bs], AF.Identity, scale=A_of_k(k))
            else:
                act(out_of_k(k), tmp[:, k, :bs], AF.Identity, scale=A_of_k(k), bias=b)

    def strm(b0):
        return 0 if b0 < LC else 1

    import os
    SSD_DBG = int(os.environ.get('SSD_DBG', '0'))

    def dbgstop(n):
        if SSD_DBG == n:
            raise _Stop()

    def layer_body(l):
        last = (l == DEPTH - 1)
        xsrc = xT_in if l == 0 else XT
        with ExitStack() as ph:
            hT = sb(ph, "hT", [128, 8, T], BF16)
            with ExitStack() as p1:
                xb2 = [sb(p1, f"p1_x{i}", [128, 8, 512], F32) for i in range(2)]
                sq = sb(p1, "p1_sq", [128, 8, 512], BF16)
                rs = sb(p1, "p1_rs", [128, 512], F32)
                tmp = sb(p1, "p1_tmp", [128, 8, 512], F32)
                pst = ps(p1, "p1_ps", [128, 512], F32)
                for bi, (b0, bs) in enumerate(BLOCKS):
                    xb = xb2[bi % 2]
                    S.load(xb[:, :, :bs], xsrc[:, b0:b0 + bs].rearrange("(k p) t -> p k t", p=128))
                    s = strm(b0)
                    norm_block((sq, pst, rs, tmp), xb, bs,
                               lambda k: A1[:, l, k, s:s + 1], lambda k: modT[:, l, k, s:s + 1],
                               lambda k: hT[:, k, b0:b0 + bs])
                S.barrier()
            with ExitStack() as p2:
                cs = sb(p2, "p2_cos", [128, T], F32)
                sn = sb(p2, "p2_sin", [128, T], F32)
                S.load(cs[:], cosT[:, :])
                S.load(sn[:], sinT[:, :])
                wf = [sb(p2, f"p2_wf{i}", [128, 8, 512], F32) for i in range(2)]
                wb = [sb(p2, f"p2_wb{i}", [128, 8, 512], BF16) for i in range(3)]
                stg = [sb(p2, f"p2_st{i}", [128, 512], F32) for i in range(4)]
                stb = [sb(p2, f"p2_sb{i}", [128, 512], BF16) for i in range(4)]
                r1 = sb(p2, "p2_r1", [128, 512], F32)
                r2 = sb(p2, "p2_r2", [128, 512], F32)
                pp = [ps(p2, f"p2_p{i}", [128, 512], F32) for i in range(6)]
                cnt = {'w': 0, 'p': 0, 's': 0, 'c': 0}

                def load_w(src, c0, n):
                    i = cnt['w']
                    cnt['w'] += 1
                    f = wf[i % 2]
                    b = wb[i % 3]
                    S.load(f[:, :, :n], src[:, c0:c0 + n].rearrange("(k p) c -> p k c", p=128))
                    for k in range(8):
                        if (k + i) % 2 == 0:
                            cp(b[:, k, :n], f[:, k, :n], eng='dve')
                        else:
                            cp(b[:, k, :n], f[:, k, :n], eng='act')
                    return b

                def newp():
                    p = pp[cnt['p'] % 6]
                    cnt['p'] += 1
                    return p

                def proj_F(wt, j, b0, bs):
                    p = newp()
                    for k in range(8):
                        mm(p[:, :bs], wt[:, k, j * 128:(j + 1) * 128], hT[:, k, b0:b0 + bs], start=(k == 0), stop=(k == 7))
                    return p

                def fgroup(src, c0, ncols, evac):
                    for g0 in range(0, ncols, 512):
                        n = min(512, ncols - g0)
                        wt = load_w(src, c0 + g0, n)
                        for j in range(n // 128):
                            for (b0, bs) in BLOCKS:
                                p = proj_F(wt, j, b0, bs)
                                evac(p, g0 + j * 128, b0, bs)

                def ev_simple(dst, func, dt):
                    def f(p, gc, b0, bs):
                        i = cnt['s']
                        cnt['s'] += 1
                        st = (stb if dt == BF16 else stg)[i % 4]
                        if func is None:
                            if i % 2 == 0:
                                cp(st[:, :bs], p[:, :bs], eng='dve')
                            else:
                                cp(st[:, :bs], p[:, :bs], eng='act')
                        else:
                            act(st[:, :bs], p[:, :bs], func)
                        S.store(dst[gc:gc + 128, b0:b0 + bs], st[:, :bs])
                    return f

                def ev_silu(dst):
                    def f(p, gc, b0, bs):
                        i = cnt['s']
                        cnt['s'] += 1
                        st = stg[i % 4]
                        act(st[:, :bs], p[:, :bs], AF.Sigmoid)
                        tt(st[:, :bs], p[:, :bs], st[:, :bs], ALU.mult)
                        S.store(dst[gc:gc + 128, b0:b0 + bs], st[:, :bs])
                    return f

                def rope_group(c_orig, c_rot, dst):
                    for g0 in range(0, 1024, 512):
                        wo = load_w(w_in[l], c_orig + g0, 512)
                        wr = load_w(w_rot[l], c_rot + g0, 512)
                        for j in range(4):
                            for (b0, bs) in BLOCKS:
                                p1_ = proj_F(wo, j, b0, bs)
                                p2_ = proj_F(wr, j, b0, bs)
                                i = cnt['s']
                                cnt['s'] += 1
                                st = stb[i % 4]
                                tt(r1[:, :bs], p1_[:, :bs], cs[:, b0:b0 + bs], ALU.mult)
                                tt(r2[:, :bs], p2_[:, :bs], sn[:, b0:b0 + bs], ALU.mult)
                                tt(st[:, :bs], r1[:, :bs], r2[:, :bs], ALU.add)
                                gc = g0 + j * 128
                                S.store(dst[gc:gc + 128, b0:b0 + bs], st[:, :bs])

                def tgroup(c0, ncols, dst, dcol0, func, dt):
                    for g0 in range(0, ncols, 512):
                        n = min(512, ncols - g0)
                        wt = load_w(w_in[l], c0 + g0, n)
                        for tix in range(NT):
                            p = newp()
                            for k in range(8):
                                mm(p[:, :n], hT[:, k, tix * 128:(tix + 1) * 128], wt[:, k, :n], start=(k == 0), stop=(k == 7))
                            i = cnt['s']
                            cnt['s'] += 1
                            st = (stb if dt == BF16 else stg)[i % 4]
                            if func == 'silu':
                                act(st[:, :n], p[:, :n], AF.Sigmoid)
                                tt(st[:, :n], p[:, :n], st[:, :n], ALU.mult)
                            elif i % 2 == 0:
                                cp(st[:, :n], p[:, :n], eng='dve')
                            else:
                                cp(st[:, :n], p[:, :n], eng='act')
                            S.store(dst[tix * 128:(tix + 1) * 128, dcol0 + g0:dcol0 + g0 + n], st[:, :n])

                rope_group(O_AK, 0, KT)
                rope_group(O_AQ, 1024, QT)
                tgroup(O_AV, 1024, AV, 0, None, BF16)
                fgroup(w_in[l], O_HFF, 1024, ev_simple(HFF, None, F32))
                fgroup(w_in[l], O_HFB, 1024, ev_simple(HFB, None, F32))
                tgroup(O_HI, 1024, HI, 0, None, BF16)
                fgroup(w_in[l], O_XBC, 3072, ev_simple(XBC, None, BF16))
                tgroup(O_DTF, 64, DTR, 0, None, F32)
                fgroup(w_in[l], O_HQ, 1024, ev_silu(HQ))
                fgroup(w_in[l], O_HGATE, 1024, ev_silu(HGATE))
                tgroup(O_Z, 2048, ZZ, 0, 'silu', F32)
                fgroup(w_in[l], O_GATES, 3072, ev_simple(GATES, AF.Sigmoid, F32))
                S.barrier()
        if stop_after == ('p2', l):
            raise _Stop()

        with ExitStack() as ph:
          if not os.environ.get('SKIP_AG'):
            kT2 = [sb(ph, f"a_kT{i}", [128, T], BF16) for i in range(2)]
            qT2 = [sb(ph, f"a_qT{i}", [128, T], BF16) for i in range(2)]
            vt2 = [sb(ph, f"a_v{i}", [128, NT, 128], BF16) for i in range(2)]
            qm2 = [[sb(ph, f"a_qm{i}_{c}", [128, T], BF16) for c in range(2)] for i in range(2)]
            pb = [sb(ph, f"a_p{i}", [128, 512], BF16) for i in range(6)]
            psm = [sb(ph, f"a_psm{i}", [128, 512], BF16) for i in range(4)]
            r0 = sb(ph, "a_r0", [128, 512], F32)
            r1 = sb(ph, "a_r1", [128, 512], F32)
            t0 = sb(ph, "a_t0", [128, 512], F32)
            t1 = sb(ph, "a_t1", [128, 512], F32)
            sq = sb(ph, "a_sq", [128, 512], BF16)
            ob = [sb(ph, f"a_ob{i}", [128, 512], BF16) for i in range(2)]
            gsc = sb(ph, "a_g", [128, 1], F32)
            lam_init = 0.8 - 0.6 * math.exp(-0.3 * l)
            tsc(gsc[:], att_g[:, l:l + 1], 1.0 - lam_init, None, ALU.mult)
            pS = [ps(ph, f"a_S{i}", [128, 512], F32) for i in range(3)]
            pO = [ps(ph, f"a_O{i}", [128, 512], F32) for i in range(2)]
            pL = [ps(ph, f"a_L{i}", [128, 512], F32) for i in range(2)]
            pX = ps(ph, "a_X", [128, 512], F32)
            for h in range(8):
                kTt, qTt, vt = kT2[h % 2], qT2[h % 2], vt2[h % 2]
                S.load(kTt[:], KT[h * 128:(h + 1) * 128, :])
                S.load(qTt[:], QT[h * 128:(h + 1) * 128, :])
                S.load(vt[:], AV[:, h * 128:(h + 1) * 128].rearrange("(n p) c -> p n c", p=128))
                qm = qm2[h % 2]
                tsc(qm[0][:], qTt[:], U_f[:, 63:64], None, ALU.mult)
                tsc(qm[1][:], qTt[:], Lo_f[:, 64:65], None, ALU.mult)
                items = []
                for bi, (b0, bs) in enumerate(BLOCKS):
                    if last and b0 < LC:
                        continue
                    nk = 2 if b0 < LC else NT
                    for kt in range(nk):
                        for c in range(2):
                            items.append((bi, b0, bs, kt, c, nk))

                def emit_S(j):
                    (bi_, b0_, bs_, kt_i, c_, nk_) = items[j]
                    mm(pS[j % 3][:, :bs_], kTt[:, kt_i * 128:(kt_i + 1) * 128], qm[c_][:, b0_:b0_ + bs_])

                LOOK = 2
                for j in range(min(LOOK, len(items))):
                    emit_S(j)
                for j, (bi, b0, bs, kt, c, nk) in enumerate(items):
                    if j + LOOK < len(items):
                        emit_S(j + LOOK)
                    P = pb[j % 6]
                    act(P[:, :bs], pS[j % 3][:, :bs], AF.Exp, scale=0.125)
                    mm(pO[c][:, :bs], vt[:, kt, :], P[:, :bs], start=(kt == 0), stop=(kt == nk - 1))
                    if kt % 2 == 1:
                        sm_ = psm[((kt // 2) * 2 + c) % 4]
                        tt(sm_[:, :bs], pb[(j - 2) % 6][:, :bs], P[:, :bs], ALU.add)
                        mm(pL[c][:, :bs], on_bf[:], sm_[:, :bs], start=(kt == 1), stop=(kt == nk - 1))
                    if not (kt == nk - 1 and c == 1):
                        continue
                    recip(r0[:, :bs], pL[0][:, :bs])
                    recip(r1[:, :bs], pL[1][:, :bs])
                    tt(t0[:, :bs], pO[0][:, :bs], r0[:, :bs], ALU.mult)
                    tt(t1[:, :bs], pO[1][:, :bs], r1[:, :bs], ALU.mult)
                    stt(t0[:, :bs], t1[:, :bs], neglam[:, l:l + 1], t0[:, :bs], ALU.mult, ALU.add)
                    act(sq[:, :bs], t0[:, :bs], AF.Square)
                    mm(pX[:, :bs], on_bf[:], sq[:, :bs])
                    tsc(r0[:, :bs], pX[:, :bs], 1.0 / 128, EPS, ALU.mult, ALU.add)
                    act(r0[:, :bs], r0[:, :bs], AF.Sqrt)
                    recip(r0[:, :bs], r0[:, :bs])
                    tt(t0[:, :bs], t0[:, :bs], r0[:, :bs], ALU.mult)
                    o = ob[bi % 2]
                    act(o[:, :bs], t0[:, :bs], AF.Identity, scale=gsc[:, 0:1])
                    S.store(BR[h * 128:(h + 1) * 128, b0:b0 + bs], o[:, :bs])
            S.barrier()
        if stop_after == ('att', l):
            raise _Stop()

        with ExitStack() as ph:
          if not os.environ.get('SKIP_AG'):
            fA = sb(ph, "g_f", [128, T], F32)
            bB = sb(ph, "g_b", [128, T], F32)
            kK = sb(ph, "g_k", [128, T], F32)
            tM = sb(ph, "g_t", [128, T], F32)
            hq = sb(ph, "g_hq", [128, T], F32)
            gt = [sb(ph, f"g_gate{i}", [128, 512], F32) for i in range(2)]
            qt = sb(ph, "g_qt", [128, T], BF16)
            kt_ = sb(ph, "g_kt", [128, T], BF16)
            kh = sb(ph, "g_kh", [128, T], BF16)
            m0 = sb(ph, "g_m0", [128, T], BF16)
            osum = sb(ph, "g_os", [128, T], F32)
            v128 = sb(ph, "g_v128", [128, NT, 128], BF16)
            ebend = sb(ph, "g_eb", [128, NCH], F32)
            S32 = sb(ph, "g_S32", [128, 128], F32)
            Sbf = [sb(ph, f"g_Sbf{i}", [128, 128], BF16) for i in range(4)]
            khTm = [[sb(ph, f"g_khm{i}_{j}", [128, 128], BF16) for j in range(4)] for i in range(2)]
            kh128 = [sb(ph, f"g_kh128_{i}", [128, 128], BF16) for i in range(2)]
            At = [sb(ph, f"g_At{i}", [128, 128], BF16) for i in range(2)]
            sq = sb(ph, "g_sq", [128, 512], BF16)
            rs = sb(ph, "g_rs", [128, 512], F32)
            on_ = sb(ph, "g_on", [128, 512], F32)
            ob = [sb(ph, f"g_ob{i}", [128, 512], BF16) for i in range(2)]
            lbz = sb(ph, "g_lbz", [128, 2], F32)
            pT = [ps(ph, "g_pT0", [32, 128], BF16)]
            pSc = [ps(ph, "g_pS0", [32, 32], F32)]
            pD = [ps(ph, f"g_pD{i}", [128, 128], F32) for i in range(4)]
            pOa = ps(ph, "g_pOa", [128, 512], F32)
            pOb = ps(ph, "g_pOb", [128, 512], F32)
            S.load(m0[:], m0_in[:, :])
            memset(lbz[:, 0:1], 0.0)
            memset(lbz[:, 1:2], 1.0)
            for h in range(8):
                S.load(hq[:], HQ[h * 128:(h + 1) * 128, :])
                S.load(v128[:], HI[:, h * 128:(h + 1) * 128].rearrange("(n p) c -> p n c", p=128))
                for d in range(2):
                    S.load(fA[:], (HFF if d == 0 else HFB)[h * 128:(h + 1) * 128, :])
                    if l == 0:
                        lb_ap, oml_ap = lbz[:, 0:1], lbz[:, 1:2]
                    else:
                        lb_ap, oml_ap = lb1[:, d, h:h + 1], oml1[:, d, h:h + 1]
                    act(fA[:], fA[:], AF.Sigmoid)
                    tsc(fA[:], fA[:], oml_ap, lb_ap, ALU.mult, ALU.add)
                    tsc(kK[:], fA[:], -1.0, 1.0, ALU.mult, ALU.add)
                    act(fA[:], fA[:], AF.Ln)
                    if d == 0:
                        scan(bB[:], m0[:], fA[:])
                        bend = _strided(bB[:], 31, 32, NCH)
                    else:
                        scan(_rev(bB[:]), m0[:], _rev(fA[:]))
                        bend = _strided(bB[:], 0, 32, NCH)
                    act(tM[:], bB[:], AF.Exp)
                    tt(qt[:], hq[:], tM[:], ALU.mult)
                    act(tM[:], bB[:], AF.Exp, scale=-1.0)
                    tt(kt_[:], kK[:], tM[:], ALU.mult)
                    tt(tM[:].rearrange("p (c j) -> p c j", j=32), _bc_last(bend, 32),
                       bB[:].rearrange("p (c j) -> p c j", j=32), ALU.subtract)
                    act(tM[:], tM[:], AF.Exp)
                    tt(kh[:], kK[:], tM[:], ALU.mult)
                    act(ebend[:], bend, AF.Exp)
                    memset(S32[:], 0.0)
                    memset(Sbf[0][:], 0.0)
                    order = list(range(8)) + list(range(8, NCH)) if d == 0 else list(range(7, -1, -1)) + list(range(NCH - 1, 7, -1))
                    tiles = [0, 1] + list(range(2, NT)) if d == 0 else [1, 0] + list(range(NT - 1, 1, -1))
                    BD = BDf if d == 0 else BDb
                    LAG = 2
                    pend = []

                    def blk_of(c):
                        if c < 8:
                            return 0, 256
                        return BLOCKS[1 + (c - 8) // 16]

                    def emit_tile(ti):
                        tix = tiles[ti]
                        t0_ = tix * 128
                        ts = ti % 2
                        tr(pT[0][:, 0:128], kh[:, t0_:t0_ + 128], id_bf[:])
                        cp(kh128[ts][:], pT[0][:, 0:128], eng='act')
                        for j in range(4):
                            if j < 2:
                                tsc(khTm[ts][j][:], kh128[ts][:], RM[:, j:j + 1], 1.0, ALU.mult, ALU.mult, eng='pool')
                            elif j == 2:
                                tsc(khTm[ts][j][:], kh128[ts][:], RM[:, j:j + 1], None, ALU.mult)
                            else:
                                act(khTm[ts][j][:], kh128[ts][:], AF.Identity, scale=RM[:, j:j + 1])
                        mm(pSc[0][:, 0:128], kt_[:, t0_:t0_ + 128], qt[:, t0_:t0_ + 128])
                        tt(At[ts][:], pSc[0][:, 0:128], BD, ALU.mult)
                        bstart_, bsz_ = blk_of(tix * 4)
                        oc_ = t0_ - bstart_
                        mm(pOa[:, oc_:oc_ + 128], v128[:, tix, :], At[ts][:])
                        last_tile = (t0_ + 128 == bstart_ + bsz_) if d == 0 else (t0_ == bstart_)
                        if last_tile:
                            if d == 0:
                                cp(osum[:, bstart_:bstart_ + bsz_], pOa[:, :bsz_], eng='act')
                            else:
                                tt(osum[:, bstart_:bstart_ + bsz_], osum[:, bstart_:bstart_ + bsz_], pOa[:, :bsz_], ALU.add)

                    def emit_inter(item):
                        (i_, c0_, bstart_, bsz_, lastb_) = item
                        oc_ = c0_ - bstart_
                        mm(pOb[:, oc_:oc_ + 32], Sbf[i_ % 4][:], qt[:, c0_:c0_ + 32])
                        if lastb_:
                            tt(osum[:, bstart_:bstart_ + bsz_], osum[:, bstart_:bstart_ + bsz_], pOb[:, :bsz_], ALU.add)

                    emit_tile(0)
                    for idx, c in enumerate(order):
                        c0 = c * 32
                        sl = idx % 4
                        ti = idx // 4
                        tix = tiles[ti]
                        assert tix == c // 4
                        if idx % 4 == 1 and ti + 1 < len(tiles):
                            emit_tile(ti + 1)
                        bstart, bsz = blk_of(c)
                        last_in_blk = (c0 + 32 == bstart + bsz) if d == 0 else (c0 == bstart)
                        mm(pD[sl][:, 0:128], khTm[ti % 2][c % 4][:], v128[:, tix, :])
                        pend.append((idx, c0, bstart, bsz, last_in_blk))
                        if len(pend) > LAG:
                            emit_inter(pend.pop(0))
                        stt(S32[:], S32[:], ebend[:, c:c + 1], pD[sl][:, 0:128], ALU.mult, ALU.add)
                        cp(Sbf[(idx + 1) % 4][:], S32[:], eng='act')
                    while pend:
                        emit_inter(pend.pop(0))
                gcol = hg_g[:, l:l + 1]
                for bi, (b0, bs) in enumerate(BLOCKS):
                    if last and b0 < LC:
                        continue
                    act(sq[:, :bs], osum[:, b0:b0 + bs], AF.Square)
                    px = pOa if bi % 2 == 0 else pOb
                    mm(px[:, :bs], on_bf[:], sq[:, :bs])
                    tsc(rs[:, :bs], px[:, :bs], 1.0 / 128, EPS, ALU.mult, ALU.add)
                    act(rs[:, :bs], rs[:, :bs], AF.Sqrt)
                    recip(rs[:, :bs], rs[:, :bs])
                    tt(on_[:, :bs], osum[:, b0:b0 + bs], rs[:, :bs], ALU.mult)
                    g__ = gt[bi % 2]
                    S.load(g__[:, :bs], HGATE[h * 128:(h + 1) * 128, b0:b0 + bs])
                    tt(on_[:, :bs], on_[:, :bs], g__[:, :bs], ALU.mult)
                    o = ob[bi % 2]
                    act(o[:, :bs], on_[:, :bs], AF.Identity, scale=gcol)
                    S.store(BR[1024 + h * 128:1024 + (h + 1) * 128, b0:b0 + bs], o[:, :bs])
            S.barrier()
        if stop_after == ('gla', l):
            raise _Stop()

        PADT = T + 8
        OFFC, OFFL = 2, 2 + 256 + 4

        def poff(b0):
            return (OFFC + b0) if b0 < LC else (OFFL + (b0 - LC))

        with ExitStack() as ph:
            up = [sb(ph, f"c_u{i}", [128, PADT], BF16) for i in range(3)]
            cw = sb(ph, "c_w", [128, 24, 5], F32)
            S.load(cw[:], conv_wT[l])
            dw = sb(ph, "c_dw", [128, 24, 5, 128], BF16)
            brow_f = sb(ph, "c_brf", [1, 3072], F32)
            brow = sb(ph, "c_br", [1, 3072], BF16)
            S.load(brow_f[:], bass.AP(conv_b.tensor, l * 3072, [[0, 1], [1, 3072]]))
            cp(brow[:], brow_f[:])
            onerow = on_bf[0:1, :]
            stT = [sb(ph, f"c_sT{i}", [128, 512], BF16) for i in range(3)]
            stF = [sb(ph, f"c_sF{i}", [128, 512], BF16) for i in range(3)]
            sgm = [sb(ph, f"c_sg{i}", [128, 512], F32) for i in range(4)]
            pc = [ps(ph, f"c_p{i}", [128, 512], F32) for i in range(4)]
            for i in range(3):
                memset(up[i][:], 0.0)
            for cc in range(24):
                for k in range(5):
                    tsc(dw[:, cc, k, :], id_bf[:], cw[:, cc, k:k + 1], None, ALU.mult)
            n_p = 0
            n_s = 0
            for cc in range(24):
                u = up[cc % 3]
                S.load(u[:, OFFC:OFFC + LC], XBC[cc * 128:(cc + 1) * 128, 0:LC])
                S.load(u[:, OFFL:OFFL + L], XBC[cc * 128:(cc + 1) * 128, LC:T])
                if cc < 20:
                    for tix in range(NT):
                        t0_ = tix * 128
                        p = pc[n_p % 4]
                        n_p += 1
                        po = poff(t0_)
                        for k in range(5):
                            mm(p[:, 0:128], u[:, po + k - 2:po + k - 2 + 128], dw[:, cc, k, :], start=(k == 0), stop=False)
                        mm(p[:, 0:128], onerow, brow[0:1, cc * 128:(cc + 1) * 128], start=False, stop=True)
                        sg_ = sgm[n_s % 4]
                        st = stT[n_s % 3]
                        n_s += 1
                        act(sg_[:, 0:128], p[:, 0:128], AF.Sigmoid)
                        if cc < 16:
                            tt(sg_[:, 0:128], p[:, 0:128], sg_[:, 0:128], ALU.mult)
                            S.store(XS[t0_:t0_ + 128, cc * 128:(cc + 1) * 128], sg_[:, 0:128])
                            continue
                        tt(st[:, 0:128], p[:, 0:128], sg_[:, 0:128], ALU.mult)
                        if cc < 16:
                            S.store(XS[t0_:t0_ + 128, cc * 128:(cc + 1) * 128], st[:, 0:128])
                        else:
                            S.store(BTOK[t0_:t0_ + 128, (cc - 16) * 128:(cc - 15) * 128], st[:, 0:128])
                if cc >= 16:
                    dst = BTT if cc < 20 else CTT
                    r0_ = (cc - 16) * 128 if cc < 20 else (cc - 20) * 128
                    for (b0, bs) in BLOCKS:
                        p = pc[n_p % 4]
                        n_p += 1
                        po = poff(b0)
                        for k in range(5):
                            mm(p[:, :bs], dw[:, cc, k, :], u[:, po + k - 2:po + k - 2 + bs], start=(k == 0), stop=(k == 4))
                        sg_ = sgm[n_s % 4]
                        st = stF[n_s % 3]
                        n_s += 1
                        act(sg_[:, :bs], p[:, :bs], AF.Sigmoid, bias=cbT[:, l, cc:cc + 1])
                        stt(st[:, :bs], p[:, :bs], cbT[:, l, cc:cc + 1], sg_[:, :bs], ALU.add, ALU.mult)
                        S.store(dst[r0_:r0_ + 128, b0:b0 + bs], st[:, :bs])
            S.barrier()
        if stop_after == ('conv', l):
            raise _Stop()

        with ExitStack() as ph:
            ST32 = sb(ph, "s_ST", [128, 2048], F32)
            STb = sb(ph, "s_STb", [128, 2048], BF16)
            dtb = sb(ph, "s_dtb", [128, 64], F32)
            S.load(dtb[:], bass.AP(dt_bias.tensor, l * 64, [[0, 128], [1, 64]]))
            Ab = sb(ph, "s_A", [128, 64], F32)
            S.load(Ab[:], bass.AP(a_log.tensor, l * 64, [[0, 128], [1, 64]]))
            act(Ab[:], Ab[:], AF.Exp)
            tsc(Ab[:], Ab[:], -1.0, None, ALU.mult)
            dsk = sb(ph, "s_dsk", [128, 32], F32)
            S.load(dsk[:], bass.AP(ssm_d.tensor, l * 32, [[0, 128], [1, 32]]))
            xs_ = [sb(ph, f"s_xs{i}", [128, 2048], F32) for i in range(2)]
            btk = [sb(ph, f"s_bt{i}", [128, 512], BF16) for i in range(2)]
            bT_ = [sb(ph, f"s_bT{i}", [128, 4, 128], BF16) for i in range(2)]
            cT_ = [sb(ph, f"s_cT{i}", [128, 4, 128], BF16) for i in range(2)]
            dtr = [sb(ph, f"s_dt{i}", [128, 64], F32) for i in range(2)]
            dtv = [sb(ph, f"s_dtv{i}", [128, 32], F32) for i in range(2)]
            aav = [sb(ph, f"s_a{i}", [128, 32], F32) for i in range(2)]
            eacv = [sb(ph, f"s_eac{i}", [128, 32], F32) for i in range(2)]
            wendv = [sb(ph, f"s_wend{i}", [128, 32], F32) for i in range(2)]
            dendv = [sb(ph, f"s_dend{i}", [128, 32], F32) for i in range(2)]
            xdtv = [sb(ph, f"s_xdt{i}", [128, 2048], BF16) for i in range(2)]
            xdwv = [sb(ph, f"s_xdw{i}", [128, 2048], BF16) for i in range(2)]
            Am = [sb(ph, f"s_Am{i}", [128, 8, 128], F32) for i in range(2)]
            LT = [sb(ph, f"s_LT{i}", [128, 8, 128], F32) for i in range(2)]
            MT = [sb(ph, f"s_MT{i}", [128, 8, 128], BF16) for i in range(2)]
            CBm = [sb(ph, f"s_CB{i}", [128, 128], F32) for i in range(2)]
            ytmp = sb(ph, "s_yt", [128, 512], F32)
            yo = [sb(ph, f"s_yo{i}", [128, 2048], F32) for i in range(2)]
            yfl = [sb(ph, f"s_yf{i}", [128, 2048], F32) for i in range(2)]
            zt = [sb(ph, f"s_z{i}", [128, 2048], F32) for i in range(2)]
            stmp = sb(ph, "s_st", [128, 512], F32)
            junk = sb(ph, "s_junk", [128, 2048], BF16)
            ssq = sb(ph, "s_ssq", [128, 1], F32)
            ynb = sb(ph, "s_ynb", [128, 2048], F32)
            obt = [sb(ph, f"s_ob{i}", [128, 4, 128], BF16) for i in range(2)]
            p_ac = ps(ph, "s_pac", [128, 64], F32)
            p_cb = ps(ph, "s_pcb", [128, 128], F32)
            p_df = [ps(ph, f"s_pdf{i}", [128, 512], F32) for i in range(2)]
            p_y = ps(ph, "s_py", [128, 512], F32)
            p_yi = ps(ph, "s_pyi", [128, 512], F32)
            p_st = ps(ph, "s_pst", [128, 512], F32)
            p_tr = ps(ph, "s_ptr", [128, 4, 128], F32)
            nit = 0
            for d in range(2):
                if d == 1:
                    S.barrier()
                memset(ST32[:], 0.0)
                memset(STb[:], 0.0)
                order = [0, 1] + list(range(2, NT)) if d == 0 else [1, 0] + list(range(NT - 1, 1, -1))
                Ucum = U_f if d == 0 else Lo_f
                Mlhs = Ms_f if d == 0 else Ml_f
                Mcb = U_f if d == 0 else Lo_f
                def aside(tix, sl):
                    t0_ = tix * 128
                    xs = xs_[sl]
                    S.load(xs[:], XS[t0_:t0_ + 128, :])
                    S.load(btk[sl][:], BTOK[t0_:t0_ + 128, :])
                    S.load(bT_[sl][:], BTT[:, t0_:t0_ + 128].rearrange("(g p) t -> p g t", p=128))
                    S.load(cT_[sl][:], CTT[:, t0_:t0_ + 128].rearrange("(g p) t -> p g t", p=128))
                    S.load(dtr[sl][:], DTR[t0_:t0_ + 128, :])
                    if d == 1:
                        S.load(yfl[sl][:], YF[t0_:t0_ + 128, :])
                        S.load(zt[sl][:], ZZ[t0_:t0_ + 128, :])
                    dt_, aa_, eac_, wend_, dend_ = dtv[sl], aav[sl], eacv[sl], wendv[sl], dendv[sl]
                    tt(dt_[:], dtr[sl][:, d * 32:(d + 1) * 32], dtb[:, d * 32:(d + 1) * 32], ALU.add)
                    act(dt_[:], dt_[:], AF.Exp)
                    act(dt_[:], dt_[:], AF.Ln, bias=1.0)
                    tt(aa_[:], dt_[:], Ab[:, d * 32:(d + 1) * 32], ALU.mult)
                    mm(p_ac[:, 0:32], Ucum, aa_[:])
                    mm(p_ac[:, 32:64], ON_f, aa_[:])
                    act(eac_[:], p_ac[:, 0:32], AF.Exp)
                    act(dend_[:], p_ac[:, 32:64], AF.Exp)
                    cp(wend_[:], p_ac[:, 0:32], eng='dve')
                    tt(wend_[:], p_ac[:, 32:64], wend_[:], ALU.subtract)
                    act(wend_[:], wend_[:], AF.Exp)
                    xs3 = xs[:].rearrange("p (h q) -> p h q", q=64)
                    tt(xdtv[sl][:].rearrange("p (h q) -> p h q", q=64), xs3, _bc_last(dt_[:], 64), ALU.mult, eng='pool')
                    tt(wend_[:], wend_[:], dt_[:], ALU.mult)
                    tt(xdwv[sl][:].rearrange("p (h q) -> p h q", q=64), xs3, _bc_last(wend_[:], 64), ALU.mult)

                aside(order[0], nit % 2)
                for oi, tix in enumerate(order):
                    t0_ = tix * 128
                    sl = nit % 2
                    nit += 1
                    if oi + 1 < len(order):
                        aside(order[oi + 1], nit % 2)
                    xs = xs_[sl]
                    xs3 = xs[:].rearrange("p (h q) -> p h q", q=64)
                    aa_, eac, dend, xdt, xdw = aav[sl], eacv[sl], dendv[sl], xdtv[sl], xdwv[sl]
                    yout = yo[sl]
                    for g in range(4):
                        gs = (nit * 4 + g) % 2
                        mm(p_cb[:, 0:128], bT_[sl][:, g, :], cT_[sl][:, g, :])
                        tt(CBm[gs][:], p_cb[:, 0:128], Mcb, ALU.mult)
                        tt(Am[gs][:], _bc_mid(Ucum, 8), _bc_last(aa_[:, g * 8:(g + 1) * 8], 128), ALU.mult, eng='pool')
                        for half in range(2):
                            mm(p_df[half][:, 0:512], Mlhs, Am[gs][:, half * 4:(half + 1) * 4, :].rearrange("p h t -> p (h t)"))
                        act(LT[gs][:, 0:4, :], p_df[0][:].rearrange("p (h t) -> p h t", t=128), AF.Exp)
                        act(LT[gs][:, 4:8, :], p_df[1][:].rearrange("p (h t) -> p h t", t=128), AF.Exp)
                        tt(MT[gs][:], LT[gs][:], _bc_mid(CBm[gs][:], 8), ALU.mult)
                        dbgstop(3)
                        for hh in range(8):
                            hg = g * 8 + hh
                            mm(p_y[:, hh * 64:(hh + 1) * 64], MT[gs][:, hh, :], xdt[:, hg * 64:(hg + 1) * 64])
                        mm(p_yi[:], cT_[sl][:, g, :], STb[:, g * 512:(g + 1) * 512])
                        tt(ytmp[:].rearrange("p (h q) -> p h q", q=64), p_yi[:].rearrange("p (h q) -> p h q", q=64),
                           _bc_last(eac[:, g * 8:(g + 1) * 8], 64), ALU.mult)
                        tt(yout[:, g * 512:(g + 1) * 512], ytmp[:], p_y[:], ALU.add)
                        dbgstop(4)
                        mm(p_st[:], btk[sl][:, g * 128:(g + 1) * 128], xdw[:, g * 512:(g + 1) * 512])
                        tt(stmp[:].rearrange("p (h q) -> p h q", q=64),
                           ST32[:, g * 512:(g + 1) * 512].rearrange("p (h q) -> p h q", q=64),
                           _bc_last(dend[:, g * 8:(g + 1) * 8], 64), ALU.mult)
                        tt(ST32[:, g * 512:(g + 1) * 512], stmp[:], p_st[:], ALU.add)
                        cp(STb[:, g * 512:(g + 1) * 512], ST32[:, g * 512:(g + 1) * 512], eng='act')
                        dbgstop(5)
                    if d == 0:
                        S.store(YF[t0_:t0_ + 128, :], yout[:])
                        dbgstop(6)
                        if tix == 33:
                            dbgstop(7)
                    else:
                        if last and tix < 2:
                            continue
                        tt(yout[:], yout[:], yfl[sl][:], ALU.add)
                        tt(yfl[sl][:].rearrange("p (h q) -> p h q", q=64), xs3, _bc_last(dsk[:], 64), ALU.mult)
                        tt(yout[:], yout[:], yfl[sl][:], ALU.add)
                        tt(yout[:], yout[:], zt[sl][:], ALU.mult)
                        act(junk[:], yout[:], AF.Square, accum=ssq[:])
                        tsc(ssq[:], ssq[:], 1.0 / 2048, EPS, ALU.mult, ALU.add)
                        act(ssq[:], ssq[:], AF.Sqrt)
                        recip(ssq[:], ssq[:])
                        tsc(ynb[:], yout[:], ssq[:, 0:1], None, ALU.mult)
                        for q4 in range(4):
                            for j in range(4):
                                tr(p_tr[:, j * 128:(j + 1) * 128], ynb[:, (q4 * 4 + j) * 128:(q4 * 4 + j + 1) * 128], ID_f)
                            o = obt[q4 % 2]
                            for j in range(4):
                                act(o[:, j, :], p_tr[:, j * 128:(j + 1) * 128], AF.Identity, scale=ssm_g[:, l, q4 * 4 + j:q4 * 4 + j + 1])
                            S.store(BR[2048 + q4 * 512:2048 + (q4 + 1) * 512, t0_:t0_ + 128].rearrange("(j p) t -> p j t", p=128), o[:])
                S.barrier()
        if stop_after == ('ssd', l):
            raise _Stop()

        with ExitStack() as ph:
            wall = sb(ph, "m_w", [128, 40, 1024], BF16)
            with ExitStack() as pw:
                wstg = [sb(pw, f"m_ws{i}", [128, 4, 1024], F32) for i in range(2)]
                srcs = [(w_b_att[l], 8), (w_b_hg[l], 8), (w_b_ssm[l], 16), (w_out[l], 8)]
                kc0 = 0
                i = 0
                for (src, nk_) in srcs:
                    for k4 in range(0, nk_, 4):
                        f = wstg[i % 2]
                        S.load(f[:], src[k4 * 128:(k4 + 4) * 128, :].rearrange("(k p) c -> p k c", p=128))
                        for kk_ in range(4):
                            cp(wall[:, kc0 + k4 + kk_, :], f[:, kk_, :], eng=('dve' if (kk_ + i) % 2 == 0 else 'act'),
                               sub={wall.name: kc0 + k4 + kk_})
                        i += 1
                    kc0 += nk_
                S.barrier()
            MB = 256
            brb = [sb(ph, f"m_br{i}", [128, 32, MB], BF16) for i in range(2)]
            gb = [sb(ph, f"m_g{i}", [128, 24, MB], F32) for i in range(2)]
            xb = [sb(ph, f"m_x{i}", [128, 8, MB], F32) for i in range(2)]
            macc = sb(ph, "m_acc", [128, MB], F32)
            mtmp = sb(ph, "m_tmp", [128, MB], F32)
            mT = sb(ph, "m_mT", [128, 8, MB], BF16)
            pm = [ps(ph, f"m_p{i}", [128, MB], F32) for i in range(4)]
            npm = 0
            bi = 0
            for b0 in range(0, T, MB):
                if last and b0 < LC:
                    continue
                bs = MB
                s = strm(b0)
                br = brb[bi % 2]
                g_ = gb[bi % 2]
                x_ = xb[bi % 2]
                bi += 1
                S.load(br[:], BR[:, b0:b0 + bs].rearrange("(k p) t -> p k t", p=128))
                S.load(g_[:], GATES[:, b0:b0 + bs].rearrange("(k p) t -> p k t", p=128))
                S.load(x_[:], xsrc[:, b0:b0 + bs].rearrange("(k p) t -> p k t", p=128))
                for oc in range(8):
                    for bri, (k0, nk_) in enumerate(((0, 8), (8, 8), (16, 16))):
                        p = pm[npm % 4]
                        npm += 1
                        for k in range(nk_):
                            mm(p[:, :MB], wall[:, k0 + k, oc * 128:(oc + 1) * 128], br[:, k0 + k, :], start=(k == 0), stop=(k == nk_ - 1))
                        if bri == 0:
                            tt(macc[:], p[:, :MB], g_[:, oc, :], ALU.mult)
                        else:
                            tt(mtmp[:], p[:, :MB], g_[:, bri * 8 + oc, :], ALU.mult)
                            if bri == 1:
                                tt(macc[:], macc[:], mtmp[:], ALU.add)
                            else:
                                tt(mT[:, oc, :], macc[:], mtmp[:], ALU.add)
                for oc in range(8):
                    p = pm[npm % 4]
                    npm += 1
                    for k in range(8):
                        mm(p[:, :MB], wall[:, 32 + k, oc * 128:(oc + 1) * 128], mT[:, k, :], start=(k == 0), stop=(k == 7))
                    stt(x_[:, oc, :], p[:, :MB], modT[:, l, 16 + oc, s:s + 1], x_[:, oc, :], ALU.mult, ALU.add)
                S.store(XT[:, b0:b0 + bs].rearrange("(k p) t -> p k t", p=128), x_[:])
            S.barrier()
        if stop_after == ('merge', l):
            raise _Stop()

        with ExitStack() as ph:
            w1b = sb(ph, "f_w1", [128, 8, FFH], BF16)
            w3b = sb(ph, "f_w3", [128, 8, FFH], BF16)
            w2b = sb(ph, "f_w2", [128, 22, 1024], BF16)
            with ExitStack() as pw:
                wstg = [sb(pw, f"f_ws{i}", [128, FFH], F32) for i in range(2)]
                i = 0
                for (src, dstw) in ((ffn_w1[l], w1b), (ffn_w3[l], w3b)):
                    for k in range(8):
                        f = wstg[i % 2]
                        S.load(f[:], src[k * 128:(k + 1) * 128, :])
                        cp(dstw[:, k, :], f[:], eng=('dve' if i % 2 == 0 else 'act'))
                        i += 1
                for k in range(22):
                    f = wstg[i % 2]
                    S.load(f[:, 0:1024], ffn_w2[l][k * 128:(k + 1) * 128, :])
                    cp(w2b[:, k, :], f[:, 0:1024], eng=('dve' if i % 2 == 0 else 'act'))
                    i += 1
                S.barrier()
            FB = 256
            xb = [sb(ph, f"f_x{i}", [128, 8, FB], F32) for i in range(2)]
            sq = sb(ph, "f_sq", [128, 8, FB], BF16)
            rs = sb(ph, "f_rs", [128, FB], F32)
            tmp = sb(ph, "f_tmp", [128, 8, FB], F32)
            h2 = sb(ph, "f_h2", [128, 8, FB], BF16)
            uT = sb(ph, "f_u", [128, 22, FB], BF16)
            sg_ = [sb(ph, f"f_sg{i}", [128, FB], F32) for i in range(2)]
            s1 = [sb(ph, f"f_s1{i}", [128, FB], F32) for i in range(2)]
            yo_ = [sb(ph, f"f_yo{i}", [128, 8, FB], F32) for i in range(1)]
            pst = ps(ph, "f_pst", [128, FB], F32)
            pf = [ps(ph, f"f_p{i}", [128, FB], F32) for i in range(6)]
            npf = 0
            nblk = 0
            for b0 in range(0, T, FB):
                if last and b0 < LC:
                    continue
                bs = FB
                s = strm(b0)
                x_ = xb[nblk % 2]
                nblk += 1
                S.load(x_[:], XT[:, b0:b0 + bs].rearrange("(k p) t -> p k t", p=128))
                norm_block((sq, pst, rs, tmp), x_, bs,
                           lambda k: A2[:, l, k, s:s + 1], lambda k: modT[:, l, 24 + k, s:s + 1],
                           lambda k: h2[:, k, :])
                for hc in range(22):
                    pa = pf[npf % 6]
                    pb_ = pf[(npf + 1) % 6]
                    npf += 2
                    for k in range(8):
                        mm(pa[:, :FB], w1b[:, k, hc * 128:(hc + 1) * 128], h2[:, k, :], start=(k == 0), stop=(k == 7))
                    for k in range(8):
                        mm(pb_[:, :FB], w3b[:, k, hc * 128:(hc + 1) * 128], h2[:, k, :], start=(k == 0), stop=(k == 7))
                    sg = sg_[hc % 2]
                    s1_ = s1[hc % 2]
                    act(sg[:], pa[:, :FB], AF.Sigmoid)
                    tt(s1_[:], pa[:, :FB], sg[:], ALU.mult)
                    tt(uT[:, hc, :], pb_[:, :FB], s1_[:], ALU.mult)
                for oc in range(8):
                    p = pf[npf % 6]
                    npf += 1
                    for k in range(22):
                        mm(p[:, :FB], w2b[:, k, oc * 128:(oc + 1) * 128], uT[:, k, :], start=(k == 0), stop=(k == 21))
                    stt(x_[:, oc, :], p[:, :FB], modT[:, l, 40 + oc, s:s + 1], x_[:, oc, :], ALU.mult, ALU.add)
                if not last:
                    S.store(XT[:, b0:b0 + bs].rearrange("(k p) t -> p k t", p=128), x_[:])
                else:
                    yo = yo_[0]
                    norm_block((sq, pst, rs, tmp), x_, bs,
                               lambda k: gfs[:, k:k + 1], lambda k: None,
                               lambda k: yo[:, k, :])
                    S.store(yT[:, b0 - LC:b0 - LC + bs].rearrange("(k p) t -> p k t", p=128), yo[:])
            S.barrier()


    stopped = False
    try:
        for l in range(DEPTH):
            layer_body(l)
    except _Stop:
        stopped = True
    S.barrier()
    if not stopped:
        per.close()
        es.close()
    return nc, S


_CACHE = {}


def _consts():
    j = np.arange(128)[:, None]
    t = np.arange(128)[None, :]
    c = np.zeros((128, 6 * 128 + 64 + 4 + 256), np.float32)
    c[:, 0:128] = (j <= t)
    c[:, 128:256] = (j >= t)
    c[:, 256:384] = (j > t)
    c[:, 384:512] = (j < t)
    c[:, 512:640] = np.eye(128)
    c[:, 640:768] = 1.0
    s = np.arange(32)[:, None]
    tt_ = np.arange(32)[None, :]
    c[0:32, 768:800] = (s <= tt_)
    c[0:32, 800:832] = (s >= tt_)
    for q in range(4):
        c[32 * q:32 * q + 32, 832 + q] = 1.0
    same = (j // 32) == (t // 32)
    c[:, 836:964] = same & (j <= t)
    c[:, 964:1092] = same & (j >= t)
    return c


def _m0():
    m0 = np.ones(T, np.float32)
    m0[::32] = 0.0
    return np.ascontiguousarray(np.broadcast_to(m0[None, :], (128, T))).astype(ml_dtypes.bfloat16)


def _rope_tables():
    cos = np.ones((128, T), np.float64)
    sin = np.zeros((128, T), np.float64)
    tl = np.arange(L)
    row = (tl // 64).astype(np.float64)
    col = (tl % 64).astype(np.float64)
    for f in range(128):
        r = f % 32
        half_sel = (f % 64) // 32
        fi = r % 16
        freq = np.float32(10000.0) ** (-np.float32(fi) / np.float32(16))
        pos = row if half_sel == 0 else col
        ang = (pos.astype(np.float32) * np.float32(freq)).astype(np.float64)
        cos[f, LC:] = np.cos(ang)
        sgn = -1.0 if r < 16 else 1.0
        sin[f, LC:] = sgn * np.sin(ang)
    return cos.astype(np.float32), sin.astype(np.float32)


def _rot_perm():
    p = np.arange(1024)
    r = p % 32
    return np.where(r < 16, p + 16, p - 16)


def _prep_shared(inp):
    f = np.float32
    A = lambda a: np.ascontiguousarray(a, dtype=f)
    perm = _rot_perm()
    w_in = inp['w_in']
    w_rot = np.concatenate([w_in[:, :, O_AK:O_AK + 1024][:, :, perm], w_in[:, :, O_AQ:O_AQ + 1024][:, :, perm]], axis=2)
    cosT, sinT = _rope_tables()
    col8 = lambda g: A(g.reshape(8, 128).T)
    sh = {
        'w_ada': A(inp['w_ada']),
        'b_adaT': A(inp['b_ada'].reshape(DEPTH, 48, 128).transpose(0, 2, 1)),
        'g1T': A(inp['norm1_g'].reshape(DEPTH, 8, 128).transpose(0, 2, 1)),
        'g2T': A(inp['norm2_g'].reshape(DEPTH, 8, 128).transpose(0, 2, 1)),
        'gfT': col8(inp['final_g']),
        'w_in': A(w_in),
        'w_rot': A(w_rot),
        'cosT': cosT, 'sinT': sinT,
        'att_lam': A(inp['att_lambda'].reshape(DEPTH, 256)),
        'att_gT': A(inp['att_norm_g'].T),
        'hg_lbT': A(inp['hg_lb_logits'].reshape(2, DEPTH, 8, 128).transpose(3, 0, 1, 2)),
        'hg_gT': A(inp['hg_norm_g'].T),
        'conv_wT': A(inp['ssm_conv_w'].reshape(DEPTH, 5, 24, 128).transpose(0, 3, 2, 1)),
        'conv_bT': A(inp['ssm_conv_b'].reshape(DEPTH, 24, 128).transpose(0, 2, 1)),
        'conv_b': A(inp['ssm_conv_b']),
        'dt_bias': A(inp['ssm_dt_bias'].reshape(DEPTH, 64)),
        'a_log': A(inp['ssm_a_log'].reshape(DEPTH, 64)),
        'ssm_d': A(inp['ssm_d']),
        'ssm_gT': A(inp['ssm_norm_g'].reshape(DEPTH, 16, 128).transpose(0, 2, 1)),
        'w_b_att': A(inp['w_branch_att']), 'w_b_hg': A(inp['w_branch_hg']), 'w_b_ssm': A(inp['w_branch_ssm']),
        'w_out': A(inp['w_out']),
        'ffn_w1': A(inp['ffn_w1']), 'ffn_w3': A(inp['ffn_w3']), 'ffn_w2': A(inp['ffn_w2']),
        'cst': _consts(),
        'm0': _m0(),
    }
    return sh


def kernel(**inputs):
    inp = {k: np.asarray(v) for k, v in inputs.items()}
    if 'nc' not in _CACHE:
        _CACHE['nc'] = build_program()[0]
    nc = _CACHE['nc']
    sh = _prep_shared(inp)
    in_maps = []
    for b in range(8):
        m = dict(sh)
        m['xT'] = np.ascontiguousarray(np.concatenate([inp['ctx'][b], inp['x'][b]], axis=0).T, dtype=np.float32)
        m['c2'] = np.ascontiguousarray(np.stack([inp['c_ctx'], inp['c'][b]], axis=1), dtype=np.float32)
        in_maps.append(m)
    res = run_bass_kernel_spmd(nc, in_maps, core_ids=list(range(8)))
    out = np.stack([np.ascontiguousarray(res.results[b]['yT'].T) for b in range(8)], axis=0)
    return out.astype(np.float32)
```

```python
import math
from contextlib import ExitStack
import numpy as np
import ml_dtypes
import concourse.bass as bass
import concourse.mybir as mybir
from concourse.bass_utils import run_bass_kernel_spmd

F32 = mybir.dt.float32
BF16 = mybir.dt.bfloat16
AF = mybir.ActivationFunctionType
ALU = mybir.AluOpType

D = 1024
LC = 256
L = 4096
T = LC + L
NT = T // 128
NCH = T // 32
DEPTH = 2
EPS = 1e-6
NIN = 16448
FFH = 2816
BLOCKS = [(0, 256)] + [(256 + 512 * i, 512) for i in range(8)]
O_AK, O_AV, O_HFF, O_HFB, O_HI, O_XBC, O_DTF, O_DTB, O_AQ, O_HQ, O_HGATE, O_Z, O_GATES = (
    0, 1024, 2048, 3072, 4096, 5120, 8192, 8224, 8256, 9280, 10304, 11328, 13376)


class _Stop(Exception):
    pass


class Sched:
    def __init__(self, nc, es):
        self.nc = nc
        self.eng = {'pe': nc.tensor, 'act': nc.scalar, 'dve': nc.vector, 'pool': nc.gpsimd, 'sp': nc.sync}
        self.semh = {}
        self.cnt = {}
        for k in ['pe', 'act', 'dve', 'pool']:
            self.semh[k] = es.enter_context(nc.semaphore('s_' + k))
            self.cnt[k] = 0
        self.dpool = {'sp': [], 'pool': []}
        for q, n in (('sp', 24), ('pool', 16)):
            for i in range(n):
                nm = f'd_{q}{i}'
                self.semh[nm] = es.enter_context(nc.semaphore(nm))
                self.cnt[nm] = 0
                self.dpool[q].append(nm)
        self.drr = {'sp': 0, 'pool': 0}
        self.seen = {k: {} for k in self.eng}
        self.res = {}
        self.nops = 0
        self.psum = set()

    def _keys(self, ap, sub):
        nm = ap.tensor.name
        if nm in self.psum:
            return [(nm, None)]
        if sub is not None and nm in sub:
            v = sub[nm]
            if isinstance(v, (list, tuple)):
                return [(nm, x) for x in v]
            return [(nm, v)]
        return [(nm, None)]

    def _deps(self, rkeys, wkeys, eng=None):
        deps = []
        for k in rkeys:
            st = self.res.get(k)
            if st is not None and st[0] is not None:
                deps.append(st[0])
        for k in wkeys:
            st = self.res.get(k)
            if st is not None:
                if st[0] is not None and st[0][0] != eng:
                    deps.append(st[0])
                deps.extend(e for e in st[1] if e[0] != eng)
        return deps

    def _wait(self, eng, deps):
        need = {}
        for (k, v) in deps:
            if eng == 'pe' and k == 'pe':
                continue
            if self.seen[eng].get(k, 0) >= v:
                continue
            if need.get(k, 0) < v:
                need[k] = v
        for k, v in need.items():
            self.eng[eng].wait_ge(self.semh[k], v)
            self.seen[eng][k] = v

    def _record(self, ev, rkeys, wkeys):
        for k in rkeys:
            st = self.res.setdefault(k, [None, []])
            st[1] = [e for e in st[1] if e[0] != ev[0]] + [ev]
        for k in wkeys:
            self.res[k] = [ev, []]

    def op(self, eng, fn, ins, outs, sub=None, stream_self=False):
        rkeys = [k for a in ins if a is not None and hasattr(a, 'tensor') for k in self._keys(a, sub)]
        wkeys = [k for a in outs for k in self._keys(a, sub)]
        wkeys += [k for k in rkeys if k[0] in self.psum and k not in wkeys]
        deps = self._deps(rkeys, wkeys, eng)
        if stream_self:
            deps = [d_ for d_ in deps if d_[0] != eng]
        self._wait(eng, deps)
        inst = fn(self.eng[eng])
        self.cnt[eng] += 1
        inst.then_inc(self.semh[eng], 1)
        self._record((eng, self.cnt[eng]), rkeys, wkeys)
        self.nops += 1

    def dma(self, q, out, in_, sub=None, track_out=True, track_in=True):
        rkeys = self._keys(in_, sub) if track_in else []
        wkeys = self._keys(out, sub) if track_out else []
        pool = self.dpool[q]
        nm = pool[self.drr[q] % len(pool)]
        self.drr[q] += 1
        deps = self._deps(rkeys, wkeys)
        if self.cnt[nm] > 0:
            deps.append((nm, self.cnt[nm]))
        self._wait(q, deps)
        self.eng[q].dma_start(out=out, in_=in_).then_inc(self.semh[nm], 16)
        self.cnt[nm] += 16
        self._record((nm, self.cnt[nm]), rkeys, wkeys)
        self.nops += 1

    def load(self, out, in_, sub=None):
        self.dma('sp', out, in_, sub=sub, track_in=False)

    def store(self, out, in_, sub=None):
        self.dma('pool', out, in_, sub=sub, track_out=False)

    def barrier(self):
        allev = [(k, v) for k, v in self.cnt.items() if v > 0]
        for e in self.eng:
            self._wait(e, allev)
        self.res = {}


def _bc_mid(a, n):
    ap = [list(x) for x in a.ap]
    return bass.AP(a.tensor, a.offset, [ap[0], [0, n]] + ap[1:])


def _bc_last(a, n):
    ap = [list(x) for x in a.ap]
    return bass.AP(a.tensor, a.offset, ap + [[0, n]])


def _rev(a):
    ap = [list(x) for x in a.ap]
    assert len(ap) == 2 and ap[1][0] == 1
    return bass.AP(a.tensor, a.offset + ap[1][1] - 1, [ap[0], [-1, ap[1][1]]])


def _strided(a, start, step, n):
    ap = [list(x) for x in a.ap]
    return bass.AP(a.tensor, a.offset + start, [ap[0], [step, n]])


def _pbcast(dram_ap_1d_offset_tensor, offset, n):
    return bass.AP(dram_ap_1d_offset_tensor, offset, [[0, 128], [1, n]])


def build_program(stop_after=None, dump=None):
    nc = bass.Bass("TRN2", target_bir_lowering=False)
    es = ExitStack()
    S = Sched(nc, es)

    def din(name, shape, dt=F32):
        return nc.dram_tensor(name, list(shape), dt, kind="ExternalInput").ap()

    def dscr(name, shape, dt):
        if dump is not None and name in dump:
            return nc.dram_tensor(name, list(shape), dt, kind="ExternalOutput").ap()
        return nc.dram_tensor(name, list(shape), dt).ap()

    xT_in = din("xT", [D, T])
    c2_in = din("c2", [D, 2])
    w_ada = din("w_ada", [DEPTH, D, 6 * D])
    b_adaT = din("b_adaT", [DEPTH, 128, 48])
    g1T = din("g1T", [DEPTH, 128, 8])
    g2T = din("g2T", [DEPTH, 128, 8])
    gfT = din("gfT", [128, 8])
    w_in = din("w_in", [DEPTH, D, NIN])
    w_rot = din("w_rot", [DEPTH, D, 2048])
    cosT = din("cosT", [128, T])
    sinT = din("sinT", [128, T])
    att_lam = din("att_lam", [DEPTH, 256])
    att_gT = din("att_gT", [128, DEPTH])
    hg_lbT = din("hg_lbT", [128, 2, DEPTH, 8])
    hg_gT = din("hg_gT", [128, DEPTH])
    conv_wT = din("conv_wT", [DEPTH, 128, 24, 5])
    conv_bT = din("conv_bT", [DEPTH, 128, 24])
    conv_b = din("conv_b", [DEPTH, 3072])
    dt_bias = din("dt_bias", [DEPTH, 64])
    a_log = din("a_log", [DEPTH, 64])
    ssm_d = din("ssm_d", [DEPTH, 32])
    ssm_gT = din("ssm_gT", [DEPTH, 128, 16])
    w_b_att = din("w_b_att", [DEPTH, 1024, D])
    w_b_hg = din("w_b_hg", [DEPTH, 1024, D])
    w_b_ssm = din("w_b_ssm", [DEPTH, 2048, D])
    w_out = din("w_out", [DEPTH, D, D])
    ffn_w1 = din("ffn_w1", [DEPTH, D, FFH])
    ffn_w3 = din("ffn_w3", [DEPTH, D, FFH])
    ffn_w2 = din("ffn_w2", [DEPTH, FFH, D])
    cst_in = din("cst", [128, 6 * 128 + 64 + 4 + 256])
    m0_in = din("m0", [128, T], BF16)
    yT = nc.dram_tensor("yT", [D, L], F32, kind="ExternalOutput").ap()

    XT = dscr("XT", [D, T], F32)
    QT = dscr("QT", [1024, T], BF16)
    KT = dscr("KT", [1024, T], BF16)
    AV = dscr("AV", [T, 1024], BF16)
    HFF = dscr("HFF", [1024, T], F32)
    HFB = dscr("HFB", [1024, T], F32)
    HQ = dscr("HQ", [1024, T], F32)
    HI = dscr("HI", [T, 1024], BF16)
    HGATE = dscr("HGATE", [1024, T], F32)
    XBC = dscr("XBC", [3072, T], BF16)
    DTR = dscr("DTR", [T, 64], F32)
    ZZ = dscr("ZZ", [T, 2048], F32)
    GATES = dscr("GATES", [3072, T], F32)
    XS = dscr("XS", [T, 2048], F32)
    BTOK = dscr("BTOK", [T, 512], BF16)
    BTT = dscr("BTT", [512, T], BF16)
    CTT = dscr("CTT", [512, T], BF16)
    YF = dscr("YF", [T, 2048], F32)
    BR = dscr("BR", [4096, T], BF16)

    uid = [0]

    def sb(ctx, name, shape, dt):
        uid[0] += 1
        return ctx.enter_context(nc.sbuf_tensor(f"{name}_{uid[0]}", list(shape), dt))

    def ps(ctx, name, shape, dt=F32):
        uid[0] += 1
        t = ctx.enter_context(nc.psum_tensor(f"{name}_{uid[0]}", [128, 512] if dt == F32 else [128, 1024], dt))
        S.psum.add(t.name)
        return t

    def mm(out, lhsT, rhs, start=True, stop=True, sub=None):
        S.op('pe', lambda e: e.matmul(out, lhsT=lhsT, rhs=rhs, start=start, stop=stop), [lhsT, rhs], [out], sub)

    def tr(out, in_, ident, sub=None):
        S.op('pe', lambda e: e.transpose(out, in_, ident), [in_, ident], [out], sub)

    def act(out, in_, func, bias=None, scale=None, accum=None, sub=None, eng='act'):
        kw = {}
        if bias is not None:
            kw['bias'] = bias
        if scale is not None:
            kw['scale'] = scale
        if accum is not None:
            kw['accum_out'] = accum
        ins = [in_] + [x for x in (bias, scale) if hasattr(x, 'tensor')]
        outs = [out] + ([accum] if accum is not None else [])
        S.op('act', lambda e: e.activation(out=out, in_=in_, func=func, **kw), ins, outs, sub)

    def tt(out, in0, in1, op, sub=None, eng='dve'):
        S.op(eng, lambda e: e.tensor_tensor(out=out, in0=in0, in1=in1, op=op), [in0, in1], [out], sub)

    def tsc(out, in0, s1, s2, op0, op1=None, sub=None, eng='dve'):
        ins = [in0] + [x for x in (s1, s2) if hasattr(x, 'tensor')]
        if op1 is None:
            S.op(eng, lambda e: e.tensor_scalar(out=out, in0=in0, scalar1=s1, scalar2=None, op0=op0), ins, [out], sub)
        else:
            S.op(eng, lambda e: e.tensor_scalar(out=out, in0=in0, scalar1=s1, scalar2=s2, op0=op0, op1=op1), ins, [out], sub)

    def stt(out, in0, scalar, in1, op0, op1, sub=None, stream_self=False):
        ins = [in0, in1] + ([scalar] if hasattr(scalar, 'tensor') else [])
        S.op('dve', lambda e: e.scalar_tensor_tensor(out=out, in0=in0, scalar=scalar, in1=in1, op0=op0, op1=op1), ins, [out], sub,
             stream_self=stream_self)

    def cp(out, in_, sub=None, eng='dve'):
        if eng == 'act':
            act(out, in_, AF.Copy, sub=sub)
        else:
            S.op(eng, lambda e: e.tensor_copy(out=out, in_=in_), [in_], [out], sub)

    def recip(out, in_, sub=None):
        S.op('dve', lambda e: e.reciprocal(out=out, in_=in_), [in_], [out], sub)

    def memset(ap, val, eng='dve', sub=None):
        S.op(eng, lambda e: e.memset(ap, val), [], [ap], sub)

    def scan(out, d0, d1, sub=None):
        S.op('dve', lambda e: e.tensor_tensor_scan(out=out, data0=d0, data1=d1, initial=0.0, op0=ALU.mult, op1=ALU.add),
             [d0, d1], [out], sub)

    per = ExitStack()
    cst = sb(per, "cst", [128, 6 * 128 + 64 + 4 + 256], F32)
    S.load(cst[:], cst_in[:, :])
    U_f = cst[:, 0:128]
    Lo_f = cst[:, 128:256]
    Ms_f = cst[:, 256:384]
    Ml_f = cst[:, 384:512]
    ID_f = cst[:, 512:640]
    ON_f = cst[:, 640:768]
    RM = cst[:, 832:836]
    BDf = cst[:, 836:964]
    BDb = cst[:, 964:1092]
    M32 = cst[:, 768:832]
    id_bf = sb(per, "id_bf", [128, 128], BF16)
    on_bf = sb(per, "on_bf", [128, 128], BF16)
    cp(id_bf[:], ID_f)
    cp(on_bf[:], ON_f)
    modT = sb(per, "modT", [128, DEPTH, 48, 2], F32)
    A1 = sb(per, "A1", [128, DEPTH, 8, 2], F32)
    A2 = sb(per, "A2", [128, DEPTH, 8, 2], F32)
    g1s = sb(per, "g1s", [128, DEPTH, 8], F32)
    g2s = sb(per, "g2s", [128, DEPTH, 8], F32)
    gfs = sb(per, "gfs", [128, 8], F32)
    S.load(g1s[:], g1T.rearrange("l p k -> p l k"))
    S.load(g2s[:], g2T.rearrange("l p k -> p l k"))
    S.load(gfs[:], gfT[:, :])
    att_g = sb(per, "att_g", [128, DEPTH], F32)
    hg_g = sb(per, "hg_g", [128, DEPTH], F32)
    S.load(att_g[:], att_gT[:, :])
    S.load(hg_g[:], hg_gT[:, :])
    lbx = sb(per, "lbx", [128, 2, DEPTH, 8], F32)
    S.load(lbx[:], hg_lbT[:, :, :, :])
    lb1 = sb(per, "lb1", [128, 2, 8], F32)
    oml1 = sb(per, "oml1", [128, 2, 8], F32)
    neglam = sb(per, "neglam", [128, DEPTH], F32)
    ssm_g = sb(per, "ssm_g", [128, DEPTH, 16], F32)
    S.load(ssm_g[:], ssm_gT.rearrange("l p k -> p l k"))
    cbT = sb(per, "cbT", [128, DEPTH, 24], F32)
    S.load(cbT[:], conv_bT.rearrange("l p k -> p l k"))

    with ExitStack() as ph:
        sc = sb(ph, "p0_sc", [128, 8, 2], F32)
        S.load(sc[:], c2_in.rearrange("(k p) s -> p k s", p=128))
        sg = sb(ph, "p0_sg", [128, 8, 2], F32)
        act(sg[:], sc[:], AF.Sigmoid)
        tt(sc[:], sc[:], sg[:], ALU.mult)
        badd = sb(ph, "p0_b", [128, DEPTH, 48], F32)
        S.load(badd[:], b_adaT.rearrange("l p k -> p l k"))
        wst = [sb(ph, f"p0_w{i}", [128, 8, 512], F32) for i in range(2)]
        pm = ps(ph, "p0_pm", [128, 96], F32)
        it = 0
        for l in range(DEPTH):
            for cg in range(12):
                w = wst[it % 2]
                it += 1
                S.load(w[:], w_ada[l, :, cg * 512:(cg + 1) * 512].rearrange("(k p) c -> p k c", p=128))
                for j in range(4):
                    ch = cg * 4 + j
                    for k in range(8):
                        mm(pm[:, ch * 2:ch * 2 + 2], w[:, k, j * 128:(j + 1) * 128], sc[:, k, :], start=(k == 0), stop=(k == 7))
            tt(modT[:, l, :, :], pm[:, 0:96].rearrange("p (c s) -> p c s", s=2), _bc_last(badd[:, l, :], 2), ALU.add)
            tsc(A1[:, l, :, :], modT[:, l, 8:16, :], 1.0, None, ALU.add)
            tt(A1[:, l, :, :], A1[:, l, :, :], _bc_last(g1s[:, l, :], 2), ALU.mult)
            tsc(A2[:, l, :, :], modT[:, l, 32:40, :], 1.0, None, ALU.add)
            tt(A2[:, l, :, :], A2[:, l, :, :], _bc_last(g2s[:, l, :], 2), ALU.mult)
        lamt = sb(ph, "p0_lam", [128, DEPTH, 4, 64], F32)
        S.load(lamt[:].rearrange("p a b c -> p (a b c)"), bass.AP(att_lam.tensor, 0, [[0, 128], [1, DEPTH * 256]]))
        pr = sb(ph, "p0_pr", [128, 64], F32)
        sm = sb(ph, "p0_sm", [128, 4], F32)
        for l in range(DEPTH):
            for j in range(2):
                tt(pr[:], lamt[:, l, 2 * j, :], lamt[:, l, 2 * j + 1, :], ALU.mult)
                act(pr[:], pr[:], AF.Copy, accum=sm[:, 2 * l + j:2 * l + j + 1])
        act(sm[:], sm[:], AF.Exp)
        for l in range(DEPTH):
            lam_init = 0.8 - 0.6 * math.exp(-0.3 * l)
            tt(neglam[:, l:l + 1], sm[:, 2 * l + 1:2 * l + 2], sm[:, 2 * l:2 * l + 1], ALU.subtract)
            tsc(neglam[:, l:l + 1], neglam[:, l:l + 1], -lam_init, None, ALU.add)
        tt(lb1[:], lbx[:, :, 1, :], lbx[:, :, 0, :], ALU.subtract)
        act(lb1[:], lb1[:], AF.Sigmoid)
        tsc(oml1[:], lb1[:], -1.0, 1.0, ALU.mult, ALU.add)
        S.barrier()

    def norm_block(ph_tiles, xb, bs, A_of_k, B_of_k, out_of_k):
        sq, pst, rs, tmp = ph_tiles
        for k in range(8):
            act(sq[:, k, :bs], xb[:, k, :bs], AF.Square)
        for k in range(8):
            mm(pst[:, :bs], on_bf[:], sq[:, k, :bs], start=(k == 0), stop=(k == 7))
        tsc(rs[:, :bs], pst[:, :bs], 1.0 / D, EPS, ALU.mult, ALU.add)
        act(rs[:, :bs], rs[:, :bs], AF.Sqrt)
        recip(rs[:, :bs], rs[:, :bs])
        for k in range(8):
            tt(tmp[:, k, :bs], xb[:, k, :bs], rs[:, :bs], ALU.mult)
            b = B_of_k(k)
            if b is None:
                act(out_of_k(k), tmp[:, k, :bs], AF.Identity, scale=A_of_k(k))
            else:
                act(out_of_k(k), tmp[:, k, :bs], AF.Identity, scale=A_of_k(k), bias=b)

    def strm(b0):
        return 0 if b0 < LC else 1

    import os
    SSD_DBG = int(os.environ.get('SSD_DBG', '0'))

    def dbgstop(n):
        if SSD_DBG == n:
            raise _Stop()

    def layer_body(l):
        last = (l == DEPTH - 1)
        xsrc = xT_in if l == 0 else XT
        with ExitStack() as ph:
            hT = sb(ph, "hT", [128, 8, T], BF16)
            with ExitStack() as p1:
                xb2 = [sb(p1, f"p1_x{i}", [128, 8, 512], F32) for i in range(2)]
                sq = sb(p1, "p1_sq", [128, 8, 512], BF16)
                rs = sb(p1, "p1_rs", [128, 512], F32)
                tmp = sb(p1, "p1_tmp", [128, 8, 512], F32)
                pst = ps(p1, "p1_ps", [128, 512], F32)
                for bi, (b0, bs) in enumerate(BLOCKS):
                    xb = xb2[bi % 2]
                    S.load(xb[:, :, :bs], xsrc[:, b0:b0 + bs].rearrange("(k p) t -> p k t", p=128))
                    s = strm(b0)
                    norm_block((sq, pst, rs, tmp), xb, bs,
                               lambda k: A1[:, l, k, s:s + 1], lambda k: modT[:, l, k, s:s + 1],
                               lambda k: hT[:, k, b0:b0 + bs])
                S.barrier()
            with ExitStack() as p2:
                cs = sb(p2, "p2_cos", [128, T], F32)
                sn = sb(p2, "p2_sin", [128, T], F32)
                S.load(cs[:], cosT[:, :])
                S.load(sn[:], sinT[:, :])
                wf = [sb(p2, f"p2_wf{i}", [128, 8, 512], F32) for i in range(2)]
                wb = [sb(p2, f"p2_wb{i}", [128, 8, 512], BF16) for i in range(3)]
                stg = [sb(p2, f"p2_st{i}", [128, 512], F32) for i in range(4)]
                stb = [sb(p2, f"p2_sb{i}", [128, 512], BF16) for i in range(4)]
                r1 = sb(p2, "p2_r1", [128, 512], F32)
                r2 = sb(p2, "p2_r2", [128, 512], F32)
                pp = [ps(p2, f"p2_p{i}", [128, 512], F32) for i in range(6)]
                cnt = {'w': 0, 'p': 0, 's': 0, 'c': 0}

                def load_w(src, c0, n):
                    i = cnt['w']
                    cnt['w'] += 1
                    f = wf[i % 2]
                    b = wb[i % 3]
                    S.load(f[:, :, :n], src[:, c0:c0 + n].rearrange("(k p) c -> p k c", p=128))
                    for k in range(8):
                        if (k + i) % 2 == 0:
                            cp(b[:, k, :n], f[:, k, :n], eng='dve')
                        else:
                            cp(b[:, k, :n], f[:, k, :n], eng='act')
                    return b

                def newp():
                    p = pp[cnt['p'] % 6]
                    cnt['p'] += 1
                    return p

                def proj_F(wt, j, b0, bs):
                    p = newp()
                    for k in range(8):
                        mm(p[:, :bs], wt[:, k, j * 128:(j + 1) * 128], hT[:, k, b0:b0 + bs], start=(k == 0), stop=(k == 7))
                    return p

                def fgroup(src, c0, ncols, evac):
                    for g0 in range(0, ncols, 512):
                        n = min(512, ncols - g0)
                        wt = load_w(src, c0 + g0, n)
                        for j in range(n // 128):
                            for (b0, bs) in BLOCKS:
                                p = proj_F(wt, j, b0, bs)
                                evac(p, g0 + j * 128, b0, bs)

                def ev_simple(dst, func, dt):
                    def f(p, gc, b0, bs):
                        i = cnt['s']
                        cnt['s'] += 1
                        st = (stb if dt == BF16 else stg)[i % 4]
                        if func is None:
                            if i % 2 == 0:
                                cp(st[:, :bs], p[:, :bs], eng='dve')
                            else:
                                cp(st[:, :bs], p[:, :bs], eng='act')
                        else:
                            act(st[:, :bs], p[:, :bs], func)
                        S.store(dst[gc:gc + 128, b0:b0 + bs], st[:, :bs])
                    return f

                def ev_silu(dst):
                    def f(p, gc, b0, bs):
                        i = cnt['s']
                        cnt['s'] += 1
                        st = stg[i % 4]
                        act(st[:, :bs], p[:, :bs], AF.Sigmoid)
                        tt(st[:, :bs], p[:, :bs], st[:, :bs], ALU.mult)
                        S.store(dst[gc:gc + 128, b0:b0 + bs], st[:, :bs])
                    return f

                def rope_group(c_orig, c_rot, dst):
                    for g0 in range(0, 1024, 512):
                        wo = load_w(w_in[l], c_orig + g0, 512)
                        wr = load_w(w_rot[l], c_rot + g0, 512)
                        for j in range(4):
                            for (b0, bs) in BLOCKS:
                                p1_ = proj_F(wo, j, b0, bs)
                                p2_ = proj_F(wr, j, b0, bs)
                                i = cnt['s']
                                cnt['s'] += 1
                                st = stb[i % 4]
                                tt(r1[:, :bs], p1_[:, :bs], cs[:, b0:b0 + bs], ALU.mult)
                                tt(r2[:, :bs], p2_[:, :bs], sn[:, b0:b0 + bs], ALU.mult)
                                tt(st[:, :bs], r1[:, :bs], r2[:, :bs], ALU.add)
                                gc = g0 + j * 128
                                S.store(dst[gc:gc + 128, b0:b0 + bs], st[:, :bs])

                def tgroup(c0, ncols, dst, dcol0, func, dt):
                    for g0 in range(0, ncols, 512):
                        n = min(512, ncols - g0)
                        wt = load_w(w_in[l], c0 + g0, n)
                        for tix in range(NT):
                            p = newp()
                            for k in range(8):
                                mm(p[:, :n], hT[:, k, tix * 128:(tix + 1) * 128], wt[:, k, :n], start=(k == 0), stop=(k == 7))
                            i = cnt['s']
                            cnt['s'] += 1
                            st = (stb if dt == BF16 else stg)[i % 4]
                            if func == 'silu':
                                act(st[:, :n], p[:, :n], AF.Sigmoid)
                                tt(st[:, :n], p[:, :n], st[:, :n], ALU.mult)
                            elif i % 2 == 0:
                                cp(st[:, :n], p[:, :n], eng='dve')
                            else:
                                cp(st[:, :n], p[:, :n], eng='act')
                            S.store(dst[tix * 128:(tix + 1) * 128, dcol0 + g0:dcol0 + g0 + n], st[:, :n])

                rope_group(O_AK, 0, KT)
                rope_group(O_AQ, 1024, QT)
                tgroup(O_AV, 1024, AV, 0, None, BF16)
                fgroup(w_in[l], O_HFF, 1024, ev_simple(HFF, None, F32))
                fgroup(w_in[l], O_HFB, 1024, ev_simple(HFB, None, F32))
                tgroup(O_HI, 1024, HI, 0, None, BF16)
                fgroup(w_in[l], O_XBC, 3072, ev_simple(XBC, None, BF16))
                tgroup(O_DTF, 64, DTR, 0, None, F32)
                fgroup(w_in[l], O_HQ, 1024, ev_silu(HQ))
                fgroup(w_in[l], O_HGATE, 1024, ev_silu(HGATE))
                tgroup(O_Z, 2048, ZZ, 0, 'silu', F32)
                fgroup(w_in[l], O_GATES, 3072, ev_simple(GATES, AF.Sigmoid, F32))
                S.barrier()
        if stop_after == ('p2', l):
            raise _Stop()

        with ExitStack() as ph:
          if not os.environ.get('SKIP_AG'):
            kT2 = [sb(ph, f"a_kT{i}", [128, T], BF16) for i in range(2)]
            qT2 = [sb(ph, f"a_qT{i}", [128, T], BF16) for i in range(2)]
            vt2 = [sb(ph, f"a_v{i}", [128, NT, 128], BF16) for i in range(2)]
            qm2 = [[sb(ph, f"a_qm{i}_{c}", [128, T], BF16) for c in range(2)] for i in range(2)]
            pb = [sb(ph, f"a_p{i}", [128, 512], BF16) for i in range(6)]
            psm = [sb(ph, f"a_psm{i}", [128, 512], BF16) for i in range(4)]
            r0 = sb(ph, "a_r0", [128, 512], F32)
            r1 = sb(ph, "a_r1", [128, 512], F32)
            t0 = sb(ph, "a_t0", [128, 512], F32)
            t1 = sb(ph, "a_t1", [128, 512], F32)
            sq = sb(ph, "a_sq", [128, 512], BF16)
            ob = [sb(ph, f"a_ob{i}", [128, 512], BF16) for i in range(2)]
            gsc = sb(ph, "a_g", [128, 1], F32)
            lam_init = 0.8 - 0.6 * math.exp(-0.3 * l)
            tsc(gsc[:], att_g[:, l:l + 1], 1.0 - lam_init, None, ALU.mult)
            pS = [ps(ph, f"a_S{i}", [128, 512], F32) for i in range(3)]
            pO = [ps(ph, f"a_O{i}", [128, 512], F32) for i in range(2)]
            pL = [ps(ph, f"a_L{i}", [128, 512], F32) for i in range(2)]
            pX = ps(ph, "a_X", [128, 512], F32)
            for h in range(8):
                kTt, qTt, vt = kT2[h % 2], qT2[h % 2], vt2[h % 2]
                S.load(kTt[:], KT[h * 128:(h + 1) * 128, :])
                S.load(qTt[:], QT[h * 128:(h + 1) * 128, :])
                S.load(vt[:], AV[:, h * 128:(h + 1) * 128].rearrange("(n p) c -> p n c", p=128))
                qm = qm2[h % 2]
                tsc(qm[0][:], qTt[:], U_f[:, 63:64], None, ALU.mult)
                tsc(qm[1][:], qTt[:], Lo_f[:, 64:65], None, ALU.mult)
                items = []
                for bi, (b0, bs) in enumerate(BLOCKS):
                    if last and b0 < LC:
                        continue
                    nk = 2 if b0 < LC else NT
                    for kt in range(nk):
                        for c in range(2):
                            items.append((bi, b0, bs, kt, c, nk))

                def emit_S(j):
                    (bi_, b0_, bs_, kt_i, c_, nk_) = items[j]
                    mm(pS[j % 3][:, :bs_], kTt[:, kt_i * 128:(kt_i + 1) * 128], qm[c_][:, b0_:b0_ + bs_])

                LOOK = 2
                for j in range(min(LOOK, len(items))):
                    emit_S(j)
                for j, (bi, b0, bs, kt, c, nk) in enumerate(items):
                    if j + LOOK < len(items):
                        emit_S(j + LOOK)
                    P = pb[j % 6]
                    act(P[:, :bs], pS[j % 3][:, :bs], AF.Exp, scale=0.125)
                    mm(pO[c][:, :bs], vt[:, kt, :], P[:, :bs], start=(kt == 0), stop=(kt == nk - 1))
                    if kt % 2 == 1:
                        sm_ = psm[((kt // 2) * 2 + c) % 4]
                        tt(sm_[:, :bs], pb[(j - 2) % 6][:, :bs], P[:, :bs], ALU.add)
                        mm(pL[c][:, :bs], on_bf[:], sm_[:, :bs], start=(kt == 1), stop=(kt == nk - 1))
                    if not (kt == nk - 1 and c == 1):
                        continue
                    recip(r0[:, :bs], pL[0][:, :bs])
                    recip(r1[:, :bs], pL[1][:, :bs])
                    tt(t0[:, :bs], pO[0][:, :bs], r0[:, :bs], ALU.mult)
                    tt(t1[:, :bs], pO[1][:, :bs], r1[:, :bs], ALU.mult)
                    stt(t0[:, :bs], t1[:, :bs], neglam[:, l:l + 1], t0[:, :bs], ALU.mult, ALU.add)
                    act(sq[:, :bs], t0[:, :bs], AF.Square)
                    mm(pX[:, :bs], on_bf[:], sq[:, :bs])
                    tsc(r0[:, :bs], pX[:, :bs], 1.0 / 128, EPS, ALU.mult, ALU.add)
                    act(r0[:, :bs], r0[:, :bs], AF.Sqrt)
                    recip(r0[:, :bs], r0[:, :bs])
                    tt(t0[:, :bs], t0[:, :bs], r0[:, :bs], ALU.mult)
                    o = ob[bi % 2]
                    act(o[:, :bs], t0[:, :bs], AF.Identity, scale=gsc[:, 0:1])
                    S.store(BR[h * 128:(h + 1) * 128, b0:b0 + bs], o[:, :bs])
            S.barrier()
        if stop_after == ('att', l):
            raise _Stop()

        with ExitStack() as ph:
          if not os.environ.get('SKIP_AG'):
            fA = sb(ph, "g_f", [128, T], F32)
            bB = sb(ph, "g_b", [128, T], F32)
            kK = sb(ph, "g_k", [128, T], F32)
            tM = sb(ph, "g_t", [128, T], F32)
            hq = sb(ph, "g_hq", [128, T], F32)
            gt = [sb(ph, f"g_gate{i}", [128, 512], F32) for i in range(2)]
            qt = sb(ph, "g_qt", [128, T], BF16)
            kt_ = sb(ph, "g_kt", [128, T], BF16)
            kh = sb(ph, "g_kh", [128, T], BF16)
            m0 = sb(ph, "g_m0", [128, T], BF16)
            osum = sb(ph, "g_os", [128, T], F32)
            v128 = sb(ph, "g_v128", [128, NT, 128], BF16)
            ebend = sb(ph, "g_eb", [128, NCH], F32)
            S32 = sb(ph, "g_S32", [128, 128], F32)
            Sbf = [sb(ph, f"g_Sbf{i}", [128, 128], BF16) for i in range(4)]
            khTm = [[sb(ph, f"g_khm{i}_{j}", [128, 128], BF16) for j in range(4)] for i in range(2)]
            kh128 = [sb(ph, f"g_kh128_{i}", [128, 128], BF16) for i in range(2)]
            At = [sb(ph, f"g_At{i}", [128, 128], BF16) for i in range(2)]
            sq = sb(ph, "g_sq", [128, 512], BF16)
            rs = sb(ph, "g_rs", [128, 512], F32)
            on_ = sb(ph, "g_on", [128, 512], F32)
            ob = [sb(ph, f"g_ob{i}", [128, 512], BF16) for i in range(2)]
            lbz = sb(ph, "g_lbz", [128, 2], F32)
            pT = [ps(ph, "g_pT0", [32, 128], BF16)]
            pSc = [ps(ph, "g_pS0", [32, 32], F32)]
            pD = [ps(ph, f"g_pD{i}", [128, 128], F32) for i in range(4)]
            pOa = ps(ph, "g_pOa", [128, 512], F32)
            pOb = ps(ph, "g_pOb", [128, 512], F32)
            S.load(m0[:], m0_in[:, :])
            memset(lbz[:, 0:1], 0.0)
            memset(lbz[:, 1:2], 1.0)
            for h in range(8):
                S.load(hq[:], HQ[h * 128:(h + 1) * 128, :])
                S.load(v128[:], HI[:, h * 128:(h + 1) * 128].rearrange("(n p) c -> p n c", p=128))
                for d in range(2):
                    S.load(fA[:], (HFF if d == 0 else HFB)[h * 128:(h + 1) * 128, :])
                    if l == 0:
                        lb_ap, oml_ap = lbz[:, 0:1], lbz[:, 1:2]
                    else:
                        lb_ap, oml_ap = lb1[:, d, h:h + 1], oml1[:, d, h:h + 1]
                    act(fA[:], fA[:], AF.Sigmoid)
                    tsc(fA[:], fA[:], oml_ap, lb_ap, ALU.mult, ALU.add)
                    tsc(kK[:], fA[:], -1.0, 1.0, ALU.mult, ALU.add)
                    act(fA[:], fA[:], AF.Ln)
                    if d == 0:
                        scan(bB[:], m0[:], fA[:])
                        bend = _strided(bB[:], 31, 32, NCH)
                    else:
                        scan(_rev(bB[:]), m0[:], _rev(fA[:]))
                        bend = _strided(bB[:], 0, 32, NCH)
                    act(tM[:], bB[:], AF.Exp)
                    tt(qt[:], hq[:], tM[:], ALU.mult)
                    act(tM[:], bB[:], AF.Exp, scale=-1.0)
                    tt(kt_[:], kK[:], tM[:], ALU.mult)
                    tt(tM[:].rearrange("p (c j) -> p c j", j=32), _bc_last(bend, 32),
                       bB[:].rearrange("p (c j) -> p c j", j=32), ALU.subtract)
                    act(tM[:], tM[:], AF.Exp)
                    tt(kh[:], kK[:], tM[:], ALU.mult)
                    act(ebend[:], bend, AF.Exp)
                    memset(S32[:], 0.0)
                    memset(Sbf[0][:], 0.0)
                    order = list(range(8)) + list(range(8, NCH)) if d == 0 else list(range(7, -1, -1)) + list(range(NCH - 1, 7, -1))
                    tiles = [0, 1] + list(range(2, NT)) if d == 0 else [1, 0] + list(range(NT - 1, 1, -1))
                    BD = BDf if d == 0 else BDb
                    LAG = 2
                    pend = []

                    def blk_of(c):
                        if c < 8:
                            return 0, 256
                        return BLOCKS[1 + (c - 8) // 16]

                    def emit_tile(ti):
                        tix = tiles[ti]
                        t0_ = tix * 128
                        ts = ti % 2
                        tr(pT[0][:, 0:128], kh[:, t0_:t0_ + 128], id_bf[:])
                        cp(kh128[ts][:], pT[0][:, 0:128], eng='act')
                        for j in range(4):
                            if j < 2:
                                tsc(khTm[ts][j][:], kh128[ts][:], RM[:, j:j + 1], 1.0, ALU.mult, ALU.mult, eng='pool')
                            elif j == 2:
                                tsc(khTm[ts][j][:], kh128[ts][:], RM[:, j:j + 1], None, ALU.mult)
                            else:
                                act(khTm[ts][j][:], kh128[ts][:], AF.Identity, scale=RM[:, j:j + 1])
                        mm(pSc[0][:, 0:128], kt_[:, t0_:t0_ + 128], qt[:, t0_:t0_ + 128])
                        tt(At[ts][:], pSc[0][:, 0:128], BD, ALU.mult)
                        bstart_, bsz_ = blk_of(tix * 4)
                        oc_ = t0_ - bstart_
                        mm(pOa[:, oc_:oc_ + 128], v128[:, tix, :], At[ts][:])
                        last_tile = (t0_ + 128 == bstart_ + bsz_) if d == 0 else (t0_ == bstart_)
                        if last_tile:
                            if d == 0:
                                cp(osum[:, bstart_:bstart_ + bsz_], pOa[:, :bsz_], eng='act')
                            else:
                                tt(osum[:, bstart_:bstart_ + bsz_], osum[:, bstart_:bstart_ + bsz_], pOa[:, :bsz_], ALU.add)

                    def emit_inter(item):
                        (i_, c0_, bstart_, bsz_, lastb_) = item
                        oc_ = c0_ - bstart_
                        mm(pOb[:, oc_:oc_ + 32], Sbf[i_ % 4][:], qt[:, c0_:c0_ + 32])
                        if lastb_:
                            tt(osum[:, bstart_:bstart_ + bsz_], osum[:, bstart_:bstart_ + bsz_], pOb[:, :bsz_], ALU.add)

                    emit_tile(0)
                    for idx, c in enumerate(order):
                        c0 = c * 32
                        sl = idx % 4
                        ti = idx // 4
                        tix = tiles[ti]
                        assert tix == c // 4
                        if idx % 4 == 1 and ti + 1 < len(tiles):
                            emit_tile(ti + 1)
                        bstart, bsz = blk_of(c)
                        last_in_blk = (c0 + 32 == bstart + bsz) if d == 0 else (c0 == bstart)
                        mm(pD[sl][:, 0:128], khTm[ti % 2][c % 4][:], v128[:, tix, :])
                        pend.append((idx, c0, bstart, bsz, last_in_blk))
                        if len(pend) > LAG:
                            emit_inter(pend.pop(0))
                        stt(S32[:], S32[:], ebend[:, c:c + 1], pD[sl][:, 0:128], ALU.mult, ALU.add, stream_self=True)
                        cp(Sbf[(idx + 1) % 4][:], S32[:], eng='act')
                    while pend:
                        emit_inter(pend.pop(0))
                gcol = hg_g[:, l:l + 1]
                for bi, (b0, bs) in enumerate(BLOCKS):
                    if last and b0 < LC:
                        continue
                    act(sq[:, :bs], osum[:, b0:b0 + bs], AF.Square)
                    px = pOa if bi % 2 == 0 else pOb
                    mm(px[:, :bs], on_bf[:], sq[:, :bs])
                    tsc(rs[:, :bs], px[:, :bs], 1.0 / 128, EPS, ALU.mult, ALU.add)
                    act(rs[:, :bs], rs[:, :bs], AF.Sqrt)
                    recip(rs[:, :bs], rs[:, :bs])
                    tt(on_[:, :bs], osum[:, b0:b0 + bs], rs[:, :bs], ALU.mult)
                    g__ = gt[bi % 2]
                    S.load(g__[:, :bs], HGATE[h * 128:(h + 1) * 128, b0:b0 + bs])
                    tt(on_[:, :bs], on_[:, :bs], g__[:, :bs], ALU.mult)
                    o = ob[bi % 2]
                    act(o[:, :bs], on_[:, :bs], AF.Identity, scale=gcol)
                    S.store(BR[1024 + h * 128:1024 + (h + 1) * 128, b0:b0 + bs], o[:, :bs])
            S.barrier()
        if stop_after == ('gla', l):
            raise _Stop()

        PADT = T + 8
        OFFC, OFFL = 2, 2 + 256 + 4

        def poff(b0):
            return (OFFC + b0) if b0 < LC else (OFFL + (b0 - LC))

        with ExitStack() as ph:
            up = [sb(ph, f"c_u{i}", [128, PADT], BF16) for i in range(3)]
            cw = sb(ph, "c_w", [128, 24, 5], F32)
            S.load(cw[:], conv_wT[l])
            dw = sb(ph, "c_dw", [128, 24, 5, 128], BF16)
            brow_f = sb(ph, "c_brf", [1, 3072], F32)
            brow = sb(ph, "c_br", [1, 3072], BF16)
            S.load(brow_f[:], bass.AP(conv_b.tensor, l * 3072, [[0, 1], [1, 3072]]))
            cp(brow[:], brow_f[:])
            onerow = on_bf[0:1, :]
            stT = [sb(ph, f"c_sT{i}", [128, 512], BF16) for i in range(3)]
            stF = [sb(ph, f"c_sF{i}", [128, 512], BF16) for i in range(3)]
            sgm = [sb(ph, f"c_sg{i}", [128, 512], F32) for i in range(4)]
            pc = [ps(ph, f"c_p{i}", [128, 512], F32) for i in range(4)]
            for i in range(3):
                memset(up[i][:], 0.0)
            for cc in range(24):
                for k in range(5):
                    tsc(dw[:, cc, k, :], id_bf[:], cw[:, cc, k:k + 1], None, ALU.mult)
            n_p = 0
            n_s = 0
            for cc in range(24):
                u = up[cc % 3]
                S.load(u[:, OFFC:OFFC + LC], XBC[cc * 128:(cc + 1) * 128, 0:LC])
                S.load(u[:, OFFL:OFFL + L], XBC[cc * 128:(cc + 1) * 128, LC:T])
                if cc < 20:
                    for tix in range(NT):
                        t0_ = tix * 128
                        p = pc[n_p % 4]
                        n_p += 1
                        po = poff(t0_)
                        for k in range(5):
                            mm(p[:, 0:128], u[:, po + k - 2:po + k - 2 + 128], dw[:, cc, k, :], start=(k == 0), stop=False)
                        mm(p[:, 0:128], onerow, brow[0:1, cc * 128:(cc + 1) * 128], start=False, stop=True)
                        sg_ = sgm[n_s % 4]
                        st = stT[n_s % 3]
                        n_s += 1
                        act(sg_[:, 0:128], p[:, 0:128], AF.Sigmoid)
                        if cc < 16:
                            tt(sg_[:, 0:128], p[:, 0:128], sg_[:, 0:128], ALU.mult)
                            S.store(XS[t0_:t0_ + 128, cc * 128:(cc + 1) * 128], sg_[:, 0:128])
                            continue
                        tt(st[:, 0:128], p[:, 0:128], sg_[:, 0:128], ALU.mult)
                        if cc < 16:
                            S.store(XS[t0_:t0_ + 128, cc * 128:(cc + 1) * 128], st[:, 0:128])
                        else:
                            S.store(BTOK[t0_:t0_ + 128, (cc - 16) * 128:(cc - 15) * 128], st[:, 0:128])
                if cc >= 16:
                    dst = BTT if cc < 20 else CTT
                    r0_ = (cc - 16) * 128 if cc < 20 else (cc - 20) * 128
                    for (b0, bs) in BLOCKS:
                        p = pc[n_p % 4]
                        n_p += 1
                        po = poff(b0)
                        for k in range(5):
                            mm(p[:, :bs], dw[:, cc, k, :], u[:, po + k - 2:po + k - 2 + bs], start=(k == 0), stop=(k == 4))
                        sg_ = sgm[n_s % 4]
                        st = stF[n_s % 3]
                        n_s += 1
                        act(sg_[:, :bs], p[:, :bs], AF.Sigmoid, bias=cbT[:, l, cc:cc + 1])
                        stt(st[:, :bs], p[:, :bs], cbT[:, l, cc:cc + 1], sg_[:, :bs], ALU.add, ALU.mult)
                        S.store(dst[r0_:r0_ + 128, b0:b0 + bs], st[:, :bs])
            S.barrier()
        if stop_after == ('conv', l):
            raise _Stop()

        with ExitStack() as ph:
            ST32 = sb(ph, "s_ST", [128, 2048], F32)
            STb = sb(ph, "s_STb", [128, 2048], BF16)
            dtb = sb(ph, "s_dtb", [128, 64], F32)
            S.load(dtb[:], bass.AP(dt_bias.tensor, l * 64, [[0, 128], [1, 64]]))
            Ab = sb(ph, "s_A", [128, 64], F32)
            S.load(Ab[:], bass.AP(a_log.tensor, l * 64, [[0, 128], [1, 64]]))
            act(Ab[:], Ab[:], AF.Exp)
            tsc(Ab[:], Ab[:], -1.0, None, ALU.mult)
            dsk = sb(ph, "s_dsk", [128, 32], F32)
            S.load(dsk[:], bass.AP(ssm_d.tensor, l * 32, [[0, 128], [1, 32]]))
            xs_ = [sb(ph, f"s_xs{i}", [128, 2048], F32) for i in range(2)]
            btk = [sb(ph, f"s_bt{i}", [128, 512], BF16) for i in range(2)]
            bT_ = [sb(ph, f"s_bT{i}", [128, 4, 128], BF16) for i in range(2)]
            cT_ = [sb(ph, f"s_cT{i}", [128, 4, 128], BF16) for i in range(2)]
            dtr = [sb(ph, f"s_dt{i}", [128, 64], F32) for i in range(2)]
            dtv = [sb(ph, f"s_dtv{i}", [128, 32], F32) for i in range(2)]
            aav = [sb(ph, f"s_a{i}", [128, 32], F32) for i in range(2)]
            eacv = [sb(ph, f"s_eac{i}", [128, 32], F32) for i in range(2)]
            wendv = [sb(ph, f"s_wend{i}", [128, 32], F32) for i in range(2)]
            dendv = [sb(ph, f"s_dend{i}", [128, 32], F32) for i in range(2)]
            xdtv = [sb(ph, f"s_xdt{i}", [128, 2048], BF16) for i in range(2)]
            xdwv = [sb(ph, f"s_xdw{i}", [128, 2048], BF16) for i in range(2)]
            Am = [sb(ph, f"s_Am{i}", [128, 8, 128], F32) for i in range(2)]
            LT = [sb(ph, f"s_LT{i}", [128, 8, 128], F32) for i in range(2)]
            MT = [sb(ph, f"s_MT{i}", [128, 8, 128], BF16) for i in range(2)]
            CBm = [sb(ph, f"s_CB{i}", [128, 128], F32) for i in range(2)]
            ytmp = sb(ph, "s_yt", [128, 512], F32)
            yo = [sb(ph, f"s_yo{i}", [128, 2048], F32) for i in range(2)]
            yfl = [sb(ph, f"s_yf{i}", [128, 2048], F32) for i in range(2)]
            zt = [sb(ph, f"s_z{i}", [128, 2048], F32) for i in range(2)]
            stmp = sb(ph, "s_st", [128, 512], F32)
            junk = sb(ph, "s_junk", [128, 2048], BF16)
            ssq = sb(ph, "s_ssq", [128, 1], F32)
            ynb = sb(ph, "s_ynb", [128, 2048], F32)
            obt = [sb(ph, f"s_ob{i}", [128, 4, 128], BF16) for i in range(2)]
            p_ac = ps(ph, "s_pac", [128, 64], F32)
            p_cb = ps(ph, "s_pcb", [128, 128], F32)
            p_df = [ps(ph, f"s_pdf{i}", [128, 512], F32) for i in range(2)]
            p_y = ps(ph, "s_py", [128, 512], F32)
            p_yi = ps(ph, "s_pyi", [128, 512], F32)
            p_st = ps(ph, "s_pst", [128, 512], F32)
            p_tr = ps(ph, "s_ptr", [128, 4, 128], F32)
            nit = 0
            for d in range(2):
                if d == 1:
                    S.barrier()
                memset(ST32[:], 0.0)
                memset(STb[:], 0.0)
                order = [0, 1] + list(range(2, NT)) if d == 0 else [1, 0] + list(range(NT - 1, 1, -1))
                Ucum = U_f if d == 0 else Lo_f
                Mlhs = Ms_f if d == 0 else Ml_f
                Mcb = U_f if d == 0 else Lo_f
                def aside(tix, sl):
                    t0_ = tix * 128
                    xs = xs_[sl]
                    S.load(xs[:], XS[t0_:t0_ + 128, :])
                    S.load(btk[sl][:], BTOK[t0_:t0_ + 128, :])
                    S.load(bT_[sl][:], BTT[:, t0_:t0_ + 128].rearrange("(g p) t -> p g t", p=128))
                    S.load(cT_[sl][:], CTT[:, t0_:t0_ + 128].rearrange("(g p) t -> p g t", p=128))
                    S.load(dtr[sl][:], DTR[t0_:t0_ + 128, :])
                    if d == 1:
                        S.load(yfl[sl][:], YF[t0_:t0_ + 128, :])
                        S.load(zt[sl][:], ZZ[t0_:t0_ + 128, :])
                    dt_, aa_, eac_, wend_, dend_ = dtv[sl], aav[sl], eacv[sl], wendv[sl], dendv[sl]
                    tt(dt_[:], dtr[sl][:, d * 32:(d + 1) * 32], dtb[:, d * 32:(d + 1) * 32], ALU.add)
                    act(dt_[:], dt_[:], AF.Exp)
                    act(dt_[:], dt_[:], AF.Ln, bias=1.0)
                    tt(aa_[:], dt_[:], Ab[:, d * 32:(d + 1) * 32], ALU.mult)
                    mm(p_ac[:, 0:32], Ucum, aa_[:])
                    mm(p_ac[:, 32:64], ON_f, aa_[:])
                    act(eac_[:], p_ac[:, 0:32], AF.Exp)
                    act(dend_[:], p_ac[:, 32:64], AF.Exp)
                    cp(wend_[:], p_ac[:, 0:32], eng='dve')
                    tt(wend_[:], p_ac[:, 32:64], wend_[:], ALU.subtract)
                    act(wend_[:], wend_[:], AF.Exp)
                    xs3 = xs[:].rearrange("p (h q) -> p h q", q=64)
                    tt(xdtv[sl][:].rearrange("p (h q) -> p h q", q=64), xs3, _bc_last(dt_[:], 64), ALU.mult, eng='pool')
                    tt(wend_[:], wend_[:], dt_[:], ALU.mult)
                    tt(xdwv[sl][:].rearrange("p (h q) -> p h q", q=64), xs3, _bc_last(wend_[:], 64), ALU.mult)

                aside(order[0], nit % 2)
                for oi, tix in enumerate(order):
                    t0_ = tix * 128
                    sl = nit % 2
                    nit += 1
                    if oi + 1 < len(order):
                        aside(order[oi + 1], nit % 2)
                    xs = xs_[sl]
                    xs3 = xs[:].rearrange("p (h q) -> p h q", q=64)
                    aa_, eac, dend, xdt, xdw = aav[sl], eacv[sl], dendv[sl], xdtv[sl], xdwv[sl]
                    yout = yo[sl]
                    for g in range(4):
                        gs = (nit * 4 + g) % 2
                        mm(p_cb[:, 0:128], bT_[sl][:, g, :], cT_[sl][:, g, :])
                        tt(CBm[gs][:], p_cb[:, 0:128], Mcb, ALU.mult)
                        tt(Am[gs][:], _bc_mid(Ucum, 8), _bc_last(aa_[:, g * 8:(g + 1) * 8], 128), ALU.mult, eng='pool')
                        for half in range(2):
                            mm(p_df[half][:, 0:512], Mlhs, Am[gs][:, half * 4:(half + 1) * 4, :].rearrange("p h t -> p (h t)"))
                        act(LT[gs][:, 0:4, :], p_df[0][:].rearrange("p (h t) -> p h t", t=128), AF.Exp)
                        act(LT[gs][:, 4:8, :], p_df[1][:].rearrange("p (h t) -> p h t", t=128), AF.Exp)
                        tt(MT[gs][:], LT[gs][:], _bc_mid(CBm[gs][:], 8), ALU.mult)
                        dbgstop(3)
                        for hh in range(8):
                            hg = g * 8 + hh
                            mm(p_y[:, hh * 64:(hh + 1) * 64], MT[gs][:, hh, :], xdt[:, hg * 64:(hg + 1) * 64])
                        mm(p_yi[:], cT_[sl][:, g, :], STb[:, g * 512:(g + 1) * 512])
                        tt(ytmp[:].rearrange("p (h q) -> p h q", q=64), p_yi[:].rearrange("p (h q) -> p h q", q=64),
                           _bc_last(eac[:, g * 8:(g + 1) * 8], 64), ALU.mult)
                        tt(yout[:, g * 512:(g + 1) * 512], ytmp[:], p_y[:], ALU.add)
                        dbgstop(4)
                        mm(p_st[:], btk[sl][:, g * 128:(g + 1) * 128], xdw[:, g * 512:(g + 1) * 512])
                        tt(stmp[:].rearrange("p (h q) -> p h q", q=64),
                           ST32[:, g * 512:(g + 1) * 512].rearrange("p (h q) -> p h q", q=64),
                           _bc_last(dend[:, g * 8:(g + 1) * 8], 64), ALU.mult)
                        tt(ST32[:, g * 512:(g + 1) * 512], stmp[:], p_st[:], ALU.add)
                        cp(STb[:, g * 512:(g + 1) * 512], ST32[:, g * 512:(g + 1) * 512], eng='act')
                        dbgstop(5)
                    if d == 0:
                        S.store(YF[t0_:t0_ + 128, :], yout[:])
                        dbgstop(6)
                        if tix == 33:
                            dbgstop(7)
                    else:
                        if last and tix < 2:
                            continue
                        tt(yout[:], yout[:], yfl[sl][:], ALU.add)
                        tt(yfl[sl][:].rearrange("p (h q) -> p h q", q=64), xs3, _bc_last(dsk[:], 64), ALU.mult)
                        tt(yout[:], yout[:], yfl[sl][:], ALU.add)
                        tt(yout[:], yout[:], zt[sl][:], ALU.mult)
                        act(junk[:], yout[:], AF.Square, accum=ssq[:])
                        tsc(ssq[:], ssq[:], 1.0 / 2048, EPS, ALU.mult, ALU.add)
                        act(ssq[:], ssq[:], AF.Sqrt)
                        recip(ssq[:], ssq[:])
                        tsc(ynb[:], yout[:], ssq[:, 0:1], None, ALU.mult)
                        for q4 in range(4):
                            for j in range(4):
                                tr(p_tr[:, j * 128:(j + 1) * 128], ynb[:, (q4 * 4 + j) * 128:(q4 * 4 + j + 1) * 128], ID_f)
                            o = obt[q4 % 2]
                            for j in range(4):
                                act(o[:, j, :], p_tr[:, j * 128:(j + 1) * 128], AF.Identity, scale=ssm_g[:, l, q4 * 4 + j:q4 * 4 + j + 1])
                            S.store(BR[2048 + q4 * 512:2048 + (q4 + 1) * 512, t0_:t0_ + 128].rearrange("(j p) t -> p j t", p=128), o[:])
                S.barrier()
        if stop_after == ('ssd', l):
            raise _Stop()

        with ExitStack() as ph:
            wall = sb(ph, "m_w", [128, 40, 1024], BF16)
            with ExitStack() as pw:
                wstg = [sb(pw, f"m_ws{i}", [128, 4, 1024], F32) for i in range(2)]
                srcs = [(w_b_att[l], 8), (w_b_hg[l], 8), (w_b_ssm[l], 16), (w_out[l], 8)]
                kc0 = 0
                i = 0
                for (src, nk_) in srcs:
                    for k4 in range(0, nk_, 4):
                        f = wstg[i % 2]
                        S.load(f[:], src[k4 * 128:(k4 + 4) * 128, :].rearrange("(k p) c -> p k c", p=128))
                        for kk_ in range(4):
                            cp(wall[:, kc0 + k4 + kk_, :], f[:, kk_, :], eng=('dve' if (kk_ + i) % 2 == 0 else 'act'),
                               sub={wall.name: kc0 + k4 + kk_})
                        i += 1
                    kc0 += nk_
                S.barrier()
            MB = 256
            brb = [sb(ph, f"m_br{i}", [128, 32, MB], BF16) for i in range(2)]
            gb = [sb(ph, f"m_g{i}", [128, 24, MB], F32) for i in range(2)]
            xb = [sb(ph, f"m_x{i}", [128, 8, MB], F32) for i in range(2)]
            macc = sb(ph, "m_acc", [128, MB], F32)
            mtmp = sb(ph, "m_tmp", [128, MB], F32)
            mT = sb(ph, "m_mT", [128, 8, MB], BF16)
            pm = [ps(ph, f"m_p{i}", [128, MB], F32) for i in range(4)]
            npm = 0
            bi = 0
            for b0 in range(0, T, MB):
                if last and b0 < LC:
                    continue
                bs = MB
                s = strm(b0)
                br = brb[bi % 2]
                g_ = gb[bi % 2]
                x_ = xb[bi % 2]
                bi += 1
                S.load(br[:], BR[:, b0:b0 + bs].rearrange("(k p) t -> p k t", p=128))
                S.load(g_[:], GATES[:, b0:b0 + bs].rearrange("(k p) t -> p k t", p=128))
                S.load(x_[:], xsrc[:, b0:b0 + bs].rearrange("(k p) t -> p k t", p=128))
                for oc in range(8):
                    for bri, (k0, nk_) in enumerate(((0, 8), (8, 8), (16, 16))):
                        p = pm[npm % 4]
                        npm += 1
                        for k in range(nk_):
                            mm(p[:, :MB], wall[:, k0 + k, oc * 128:(oc + 1) * 128], br[:, k0 + k, :], start=(k == 0), stop=(k == nk_ - 1))
                        if bri == 0:
                            tt(macc[:], p[:, :MB], g_[:, oc, :], ALU.mult)
                        else:
                            tt(mtmp[:], p[:, :MB], g_[:, bri * 8 + oc, :], ALU.mult)
                            if bri == 1:
                                tt(macc[:], macc[:], mtmp[:], ALU.add)
                            else:
                                tt(mT[:, oc, :], macc[:], mtmp[:], ALU.add)
                for oc in range(8):
                    p = pm[npm % 4]
                    npm += 1
                    for k in range(8):
                        mm(p[:, :MB], wall[:, 32 + k, oc * 128:(oc + 1) * 128], mT[:, k, :], start=(k == 0), stop=(k == 7))
                    stt(x_[:, oc, :], p[:, :MB], modT[:, l, 16 + oc, s:s + 1], x_[:, oc, :], ALU.mult, ALU.add)
                S.store(XT[:, b0:b0 + bs].rearrange("(k p) t -> p k t", p=128), x_[:])
            S.barrier()
        if stop_after == ('merge', l):
            raise _Stop()

        with ExitStack() as ph:
            w1b = sb(ph, "f_w1", [128, 8, FFH], BF16)
            w3b = sb(ph, "f_w3", [128, 8, FFH], BF16)
            w2b = sb(ph, "f_w2", [128, 22, 1024], BF16)
            with ExitStack() as pw:
                wstg = [sb(pw, f"f_ws{i}", [128, FFH], F32) for i in range(2)]
                i = 0
                for (src, dstw) in ((ffn_w1[l], w1b), (ffn_w3[l], w3b)):
                    for k in range(8):
                        f = wstg[i % 2]
                        S.load(f[:], src[k * 128:(k + 1) * 128, :])
                        cp(dstw[:, k, :], f[:], eng=('dve' if i % 2 == 0 else 'act'))
                        i += 1
                for k in range(22):
                    f = wstg[i % 2]
                    S.load(f[:, 0:1024], ffn_w2[l][k * 128:(k + 1) * 128, :])
                    cp(w2b[:, k, :], f[:, 0:1024], eng=('dve' if i % 2 == 0 else 'act'))
                    i += 1
                S.barrier()
            FB = 256
            xb = [sb(ph, f"f_x{i}", [128, 8, FB], F32) for i in range(2)]
            sq = sb(ph, "f_sq", [128, 8, FB], BF16)
            rs = sb(ph, "f_rs", [128, FB], F32)
            tmp = sb(ph, "f_tmp", [128, 8, FB], F32)
            h2 = sb(ph, "f_h2", [128, 8, FB], BF16)
            uT = sb(ph, "f_u", [128, 22, FB], BF16)
            sg_ = [sb(ph, f"f_sg{i}", [128, FB], F32) for i in range(2)]
            s1 = [sb(ph, f"f_s1{i}", [128, FB], F32) for i in range(2)]
            yo_ = [sb(ph, f"f_yo{i}", [128, 8, FB], F32) for i in range(1)]
            pst = ps(ph, "f_pst", [128, FB], F32)
            pf = [ps(ph, f"f_p{i}", [128, FB], F32) for i in range(6)]
            npf = 0
            nblk = 0
            for b0 in range(0, T, FB):
                if last and b0 < LC:
                    continue
                bs = FB
                s = strm(b0)
                x_ = xb[nblk % 2]
                nblk += 1
                S.load(x_[:], XT[:, b0:b0 + bs].rearrange("(k p) t -> p k t", p=128))
                norm_block((sq, pst, rs, tmp), x_, bs,
                           lambda k: A2[:, l, k, s:s + 1], lambda k: modT[:, l, 24 + k, s:s + 1],
                           lambda k: h2[:, k, :])
                for hc in range(22):
                    pa = pf[npf % 6]
                    pb_ = pf[(npf + 1) % 6]
                    npf += 2
                    for k in range(8):
                        mm(pa[:, :FB], w1b[:, k, hc * 128:(hc + 1) * 128], h2[:, k, :], start=(k == 0), stop=(k == 7))
                    for k in range(8):
                        mm(pb_[:, :FB], w3b[:, k, hc * 128:(hc + 1) * 128], h2[:, k, :], start=(k == 0), stop=(k == 7))
                    sg = sg_[hc % 2]
                    s1_ = s1[hc % 2]
                    act(sg[:], pa[:, :FB], AF.Sigmoid)
                    tt(s1_[:], pa[:, :FB], sg[:], ALU.mult)
                    tt(uT[:, hc, :], pb_[:, :FB], s1_[:], ALU.mult)
                for oc in range(8):
                    p = pf[npf % 6]
                    npf += 1
                    for k in range(22):
                        mm(p[:, :FB], w2b[:, k, oc * 128:(oc + 1) * 128], uT[:, k, :], start=(k == 0), stop=(k == 21))
                    stt(x_[:, oc, :], p[:, :FB], modT[:, l, 40 + oc, s:s + 1], x_[:, oc, :], ALU.mult, ALU.add)
                if not last:
                    S.store(XT[:, b0:b0 + bs].rearrange("(k p) t -> p k t", p=128), x_[:])
                else:
                    yo = yo_[0]
                    norm_block((sq, pst, rs, tmp), x_, bs,
                               lambda k: gfs[:, k:k + 1], lambda k: None,
                               lambda k: yo[:, k, :])
                    S.store(yT[:, b0 - LC:b0 - LC + bs].rearrange("(k p) t -> p k t", p=128), yo[:])
            S.barrier()


    stopped = False
    try:
        for l in range(DEPTH):
            layer_body(l)
    except _Stop:
        stopped = True
    S.barrier()
    if not stopped:
        per.close()
        es.close()
    return nc, S


_CACHE = {}


def _consts():
    j = np.arange(128)[:, None]
    t = np.arange(128)[None, :]
    c = np.zeros((128, 6 * 128 + 64 + 4 + 256), np.float32)
    c[:, 0:128] = (j <= t)
    c[:, 128:256] = (j >= t)
    c[:, 256:384] = (j > t)
    c[:, 384:512] = (j < t)
    c[:, 512:640] = np.eye(128)
    c[:, 640:768] = 1.0
    s = np.arange(32)[:, None]
    tt_ = np.arange(32)[None, :]
    c[0:32, 768:800] = (s <= tt_)
    c[0:32, 800:832] = (s >= tt_)
    for q in range(4):
        c[32 * q:32 * q + 32, 832 + q] = 1.0
    same = (j // 32) == (t // 32)
    c[:, 836:964] = same & (j <= t)
    c[:, 964:1092] = same & (j >= t)
    return c


def _m0():
    m0 = np.ones(T, np.float32)
    m0[::32] = 0.0
    return np.ascontiguousarray(np.broadcast_to(m0[None, :], (128, T))).astype(ml_dtypes.bfloat16)


def _rope_tables():
    cos = np.ones((128, T), np.float64)
    sin = np.zeros((128, T), np.float64)
    tl = np.arange(L)
    row = (tl // 64).astype(np.float64)
    col = (tl % 64).astype(np.float64)
    for f in range(128):
        r = f % 32
        half_sel = (f % 64) // 32
        fi = r % 16
        freq = np.float32(10000.0) ** (-np.float32(fi) / np.float32(16))
        pos = row if half_sel == 0 else col
        ang = (pos.astype(np.float32) * np.float32(freq)).astype(np.float64)
        cos[f, LC:] = np.cos(ang)
        sgn = -1.0 if r < 16 else 1.0
        sin[f, LC:] = sgn * np.sin(ang)
    return cos.astype(np.float32), sin.astype(np.float32)


def _rot_perm():
    p = np.arange(1024)
    r = p % 32
    return np.where(r < 16, p + 16, p - 16)


def _prep_shared(inp):
    f = np.float32
    A = lambda a: np.ascontiguousarray(a, dtype=f)
    perm = _rot_perm()
    w_in = inp['w_in']
    w_rot = np.concatenate([w_in[:, :, O_AK:O_AK + 1024][:, :, perm], w_in[:, :, O_AQ:O_AQ + 1024][:, :, perm]], axis=2)
    cosT, sinT = _rope_tables()
    col8 = lambda g: A(g.reshape(8, 128).T)
    sh = {
        'w_ada': A(inp['w_ada']),
        'b_adaT': A(inp['b_ada'].reshape(DEPTH, 48, 128).transpose(0, 2, 1)),
        'g1T': A(inp['norm1_g'].reshape(DEPTH, 8, 128).transpose(0, 2, 1)),
        'g2T': A(inp['norm2_g'].reshape(DEPTH, 8, 128).transpose(0, 2, 1)),
        'gfT': col8(inp['final_g']),
        'w_in': A(w_in),
        'w_rot': A(w_rot),
        'cosT': cosT, 'sinT': sinT,
        'att_lam': A(inp['att_lambda'].reshape(DEPTH, 256)),
        'att_gT': A(inp['att_norm_g'].T),
        'hg_lbT': A(inp['hg_lb_logits'].reshape(2, DEPTH, 8, 128).transpose(3, 0, 1, 2)),
        'hg_gT': A(inp['hg_norm_g'].T),
        'conv_wT': A(inp['ssm_conv_w'].reshape(DEPTH, 5, 24, 128).transpose(0, 3, 2, 1)),
        'conv_bT': A(inp['ssm_conv_b'].reshape(DEPTH, 24, 128).transpose(0, 2, 1)),
        'conv_b': A(inp['ssm_conv_b']),
        'dt_bias': A(inp['ssm_dt_bias'].reshape(DEPTH, 64)),
        'a_log': A(inp['ssm_a_log'].reshape(DEPTH, 64)),
        'ssm_d': A(inp['ssm_d']),
        'ssm_gT': A(inp['ssm_norm_g'].reshape(DEPTH, 16, 128).transpose(0, 2, 1)),
        'w_b_att': A(inp['w_branch_att']), 'w_b_hg': A(inp['w_branch_hg']), 'w_b_ssm': A(inp['w_branch_ssm']),
        'w_out': A(inp['w_out']),
        'ffn_w1': A(inp['ffn_w1']), 'ffn_w3': A(inp['ffn_w3']), 'ffn_w2': A(inp['ffn_w2']),
        'cst': _consts(),
        'm0': _m0(),
    }
    return sh


def kernel(**inputs):
    inp = {k: np.asarray(v) for k, v in inputs.items()}
    if 'nc' not in _CACHE:
        _CACHE['nc'] = build_program()[0]
    nc = _CACHE['nc']
    sh = _prep_shared(inp)
    in_maps = []
    for b in range(8):
        m = dict(sh)
        m['xT'] = np.ascontiguousarray(np.concatenate([inp['ctx'][b], inp['x'][b]], axis=0).T, dtype=np.float32)
        m['c2'] = np.ascontiguousarray(np.stack([inp['c_ctx'], inp['c'][b]], axis=1), dtype=np.float32)
        in_maps.append(m)
    res = run_bass_kernel_spmd(nc, in_maps, core_ids=list(range(8)))
    out = np.stack([np.ascontiguousarray(res.results[b]['yT'].T) for b in range(8)], axis=0)
    return out.astype(np.float32)
```

```python
import math
from contextlib import ExitStack
import numpy as np
import ml_dtypes
import concourse.bass as bass
import concourse.mybir as mybir
from concourse.bass_utils import run_bass_kernel_spmd

F32 = mybir.dt.float32
BF16 = mybir.dt.bfloat16
AF = mybir.ActivationFunctionType
ALU = mybir.AluOpType

D = 1024
LC = 256
L = 4096
T = LC + L
NT = T // 128
NCH = T // 32
DEPTH = 2
EPS = 1e-6
NIN = 16448
FFH = 2816
BLOCKS = [(0, 256)] + [(256 + 512 * i, 512) for i in range(8)]
O_AK, O_AV, O_HFF, O_HFB, O_HI, O_XBC, O_DTF, O_DTB, O_AQ, O_HQ, O_HGATE, O_Z, O_GATES = (
    0, 1024, 2048, 3072, 4096, 5120, 8192, 8224, 8256, 9280, 10304, 11328, 13376)


class _Stop(Exception):
    pass


class Sched:
    def __init__(self, nc, es):
        self.nc = nc
        self.eng = {'pe': nc.tensor, 'act': nc.scalar, 'dve': nc.vector, 'pool': nc.gpsimd, 'sp': nc.sync}
        self.semh = {}
        self.cnt = {}
        for k in ['pe', 'act', 'dve', 'pool']:
            self.semh[k] = es.enter_context(nc.semaphore('s_' + k))
            self.cnt[k] = 0
        self.dpool = {'sp': [], 'pool': []}
        for q, n in (('sp', 24), ('pool', 16)):
            for i in range(n):
                nm = f'd_{q}{i}'
                self.semh[nm] = es.enter_context(nc.semaphore(nm))
                self.cnt[nm] = 0
                self.dpool[q].append(nm)
        self.drr = {'sp': 0, 'pool': 0}
        self.seen = {k: {} for k in self.eng}
        self.res = {}
        self.nops = 0
        self.psum = set()

    def _keys(self, ap, sub):
        nm = ap.tensor.name
        if nm in self.psum:
            return [(nm, None)]
        if sub is not None and nm in sub:
            v = sub[nm]
            if isinstance(v, (list, tuple)):
                return [(nm, x) for x in v]
            return [(nm, v)]
        return [(nm, None)]

    def _deps(self, rkeys, wkeys, eng=None):
        deps = []
        for k in rkeys:
            st = self.res.get(k)
            if st is not None and st[0] is not None:
                deps.append(st[0])
        for k in wkeys:
            st = self.res.get(k)
            if st is not None:
                if st[0] is not None and st[0][0] != eng:
                    deps.append(st[0])
                deps.extend(e for e in st[1] if e[0] != eng)
        return deps

    def _wait(self, eng, deps):
        need = {}
        for (k, v) in deps:
            if eng == 'pe' and k == 'pe':
                continue
            if self.seen[eng].get(k, 0) >= v:
                continue
            if need.get(k, 0) < v:
                need[k] = v
        for k, v in need.items():
            self.eng[eng].wait_ge(self.semh[k], v)
            self.seen[eng][k] = v

    def _record(self, ev, rkeys, wkeys):
        for k in rkeys:
            st = self.res.setdefault(k, [None, []])
            st[1] = [e for e in st[1] if e[0] != ev[0]] + [ev]
        for k in wkeys:
            self.res[k] = [ev, []]

    def op(self, eng, fn, ins, outs, sub=None, stream_self=False):
        rkeys = [k for a in ins if a is not None and hasattr(a, 'tensor') for k in self._keys(a, sub)]
        wkeys = [k for a in outs for k in self._keys(a, sub)]
        wkeys += [k for k in rkeys if k[0] in self.psum and k not in wkeys]
        deps = self._deps(rkeys, wkeys, eng)
        if stream_self:
            deps = [d_ for d_ in deps if d_[0] != eng]
        self._wait(eng, deps)
        inst = fn(self.eng[eng])
        self.cnt[eng] += 1
        inst.then_inc(self.semh[eng], 1)
        self._record((eng, self.cnt[eng]), rkeys, wkeys)
        self.nops += 1

    def dma(self, q, out, in_, sub=None, track_out=True, track_in=True):
        rkeys = self._keys(in_, sub) if track_in else []
        wkeys = self._keys(out, sub) if track_out else []
        pool = self.dpool[q]
        nm = pool[self.drr[q] % len(pool)]
        self.drr[q] += 1
        deps = self._deps(rkeys, wkeys)
        if self.cnt[nm] > 0:
            deps.append((nm, self.cnt[nm]))
        self._wait(q, deps)
        self.eng[q].dma_start(out=out, in_=in_).then_inc(self.semh[nm], 16)
        self.cnt[nm] += 16
        self._record((nm, self.cnt[nm]), rkeys, wkeys)
        self.nops += 1

    def load(self, out, in_, sub=None):
        self.dma('sp', out, in_, sub=sub, track_in=False)

    def store(self, out, in_, sub=None):
        self.dma('pool', out, in_, sub=sub, track_out=False)

    def barrier(self):
        allev = [(k, v) for k, v in self.cnt.items() if v > 0]
        for e in self.eng:
            self._wait(e, allev)
        self.res = {}


def _bc_mid(a, n):
    ap = [list(x) for x in a.ap]
    return bass.AP(a.tensor, a.offset, [ap[0], [0, n]] + ap[1:])


def _bc_last(a, n):
    ap = [list(x) for x in a.ap]
    return bass.AP(a.tensor, a.offset, ap + [[0, n]])


def _rev(a):
    ap = [list(x) for x in a.ap]
    assert len(ap) == 2 and ap[1][0] == 1
    return bass.AP(a.tensor, a.offset + ap[1][1] - 1, [ap[0], [-1, ap[1][1]]])


def _strided(a, start, step, n):
    ap = [list(x) for x in a.ap]
    return bass.AP(a.tensor, a.offset + start, [ap[0], [step, n]])


def _pbcast(dram_ap_1d_offset_tensor, offset, n):
    return bass.AP(dram_ap_1d_offset_tensor, offset, [[0, 128], [1, n]])


def build_program(stop_after=None, dump=None):
    nc = bass.Bass("TRN2", target_bir_lowering=False)
    es = ExitStack()
    S = Sched(nc, es)

    def din(name, shape, dt=F32):
        return nc.dram_tensor(name, list(shape), dt, kind="ExternalInput").ap()

    def dscr(name, shape, dt):
        if dump is not None and name in dump:
            return nc.dram_tensor(name, list(shape), dt, kind="ExternalOutput").ap()
        return nc.dram_tensor(name, list(shape), dt).ap()

    xT_in = din("xT", [D, T])
    c2_in = din("c2", [D, 2])
    w_ada = din("w_ada", [DEPTH, D, 6 * D])
    b_adaT = din("b_adaT", [DEPTH, 128, 48])
    g1T = din("g1T", [DEPTH, 128, 8])
    g2T = din("g2T", [DEPTH, 128, 8])
    gfT = din("gfT", [128, 8])
    w_in = din("w_in", [DEPTH, D, NIN])
    w_rot = din("w_rot", [DEPTH, D, 2048])
    cosT = din("cosT", [128, T])
    sinT = din("sinT", [128, T])
    att_lam = din("att_lam", [DEPTH, 256])
    att_gT = din("att_gT", [128, DEPTH])
    hg_lbT = din("hg_lbT", [128, 2, DEPTH, 8])
    hg_gT = din("hg_gT", [128, DEPTH])
    conv_wT = din("conv_wT", [DEPTH, 128, 24, 5])
    conv_bT = din("conv_bT", [DEPTH, 128, 24])
    conv_b = din("conv_b", [DEPTH, 3072])
    dt_bias = din("dt_bias", [DEPTH, 64])
    a_log = din("a_log", [DEPTH, 64])
    ssm_d = din("ssm_d", [DEPTH, 32])
    ssm_gT = din("ssm_gT", [DEPTH, 128, 16])
    w_b_att = din("w_b_att", [DEPTH, 1024, D])
    w_b_hg = din("w_b_hg", [DEPTH, 1024, D])
    w_b_ssm = din("w_b_ssm", [DEPTH, 2048, D])
    w_out = din("w_out", [DEPTH, D, D])
    ffn_w1 = din("ffn_w1", [DEPTH, D, FFH])
    ffn_w3 = din("ffn_w3", [DEPTH, D, FFH])
    ffn_w2 = din("ffn_w2", [DEPTH, FFH, D])
    cst_in = din("cst", [128, 6 * 128 + 64 + 4 + 256])
    m0_in = din("m0", [128, T], BF16)
    yT = nc.dram_tensor("yT", [D, L], F32, kind="ExternalOutput").ap()

    XT = dscr("XT", [D, T], F32)
    QT = dscr("QT", [1024, T], BF16)
    KT = dscr("KT", [1024, T], BF16)
    AV = dscr("AV", [T, 1024], BF16)
    HFF = dscr("HFF", [1024, T], F32)
    HFB = dscr("HFB", [1024, T], F32)
    HQ = dscr("HQ", [1024, T], F32)
    HI = dscr("HI", [T, 1024], BF16)
    HGATE = dscr("HGATE", [1024, T], F32)
    XBC = dscr("XBC", [3072, T], BF16)
    DTR = dscr("DTR", [T, 64], F32)
    ZZ = dscr("ZZ", [T, 2048], F32)
    GATES = dscr("GATES", [3072, T], F32)
    XS = dscr("XS", [T, 2048], F32)
    BTOK = dscr("BTOK", [T, 512], BF16)
    BTT = dscr("BTT", [512, T], BF16)
    CTT = dscr("CTT", [512, T], BF16)
    YF = dscr("YF", [T, 2048], F32)
    BR = dscr("BR", [4096, T], BF16)

    uid = [0]

    def sb(ctx, name, shape, dt):
        uid[0] += 1
        return ctx.enter_context(nc.sbuf_tensor(f"{name}_{uid[0]}", list(shape), dt))

    def ps(ctx, name, shape, dt=F32):
        uid[0] += 1
        t = ctx.enter_context(nc.psum_tensor(f"{name}_{uid[0]}", [128, 512] if dt == F32 else [128, 1024], dt))
        S.psum.add(t.name)
        return t

    def mm(out, lhsT, rhs, start=True, stop=True, sub=None):
        S.op('pe', lambda e: e.matmul(out, lhsT=lhsT, rhs=rhs, start=start, stop=stop), [lhsT, rhs], [out], sub)

    def tr(out, in_, ident, sub=None):
        S.op('pe', lambda e: e.transpose(out, in_, ident), [in_, ident], [out], sub)

    def act(out, in_, func, bias=None, scale=None, accum=None, sub=None, eng='act'):
        kw = {}
        if bias is not None:
            kw['bias'] = bias
        if scale is not None:
            kw['scale'] = scale
        if accum is not None:
            kw['accum_out'] = accum
        ins = [in_] + [x for x in (bias, scale) if hasattr(x, 'tensor')]
        outs = [out] + ([accum] if accum is not None else [])
        S.op('act', lambda e: e.activation(out=out, in_=in_, func=func, **kw), ins, outs, sub)

    def tt(out, in0, in1, op, sub=None, eng='dve'):
        S.op(eng, lambda e: e.tensor_tensor(out=out, in0=in0, in1=in1, op=op), [in0, in1], [out], sub)

    def tsc(out, in0, s1, s2, op0, op1=None, sub=None, eng='dve'):
        ins = [in0] + [x for x in (s1, s2) if hasattr(x, 'tensor')]
        if op1 is None:
            S.op(eng, lambda e: e.tensor_scalar(out=out, in0=in0, scalar1=s1, scalar2=None, op0=op0), ins, [out], sub)
        else:
            S.op(eng, lambda e: e.tensor_scalar(out=out, in0=in0, scalar1=s1, scalar2=s2, op0=op0, op1=op1), ins, [out], sub)

    def stt(out, in0, scalar, in1, op0, op1, sub=None, stream_self=False):
        ins = [in0, in1] + ([scalar] if hasattr(scalar, 'tensor') else [])
        S.op('dve', lambda e: e.scalar_tensor_tensor(out=out, in0=in0, scalar=scalar, in1=in1, op0=op0, op1=op1), ins, [out], sub,
             stream_self=stream_self)

    def cp(out, in_, sub=None, eng='dve'):
        if eng == 'act':
            act(out, in_, AF.Copy, sub=sub)
        else:
            S.op(eng, lambda e: e.tensor_copy(out=out, in_=in_), [in_], [out], sub)

    def recip(out, in_, sub=None):
        S.op('dve', lambda e: e.reciprocal(out=out, in_=in_), [in_], [out], sub)

    def memset(ap, val, eng='dve', sub=None):
        S.op(eng, lambda e: e.memset(ap, val), [], [ap], sub)

    def scan(out, d0, d1, sub=None):
        S.op('dve', lambda e: e.tensor_tensor_scan(out=out, data0=d0, data1=d1, initial=0.0, op0=ALU.mult, op1=ALU.add),
             [d0, d1], [out], sub)

    per = ExitStack()
    cst = sb(per, "cst", [128, 6 * 128 + 64 + 4 + 256], F32)
    S.load(cst[:], cst_in[:, :])
    U_f = cst[:, 0:128]
    Lo_f = cst[:, 128:256]
    Ms_f = cst[:, 256:384]
    Ml_f = cst[:, 384:512]
    ID_f = cst[:, 512:640]
    ON_f = cst[:, 640:768]
    RM = cst[:, 832:836]
    BDf = cst[:, 836:964]
    BDb = cst[:, 964:1092]
    M32 = cst[:, 768:832]
    id_bf = sb(per, "id_bf", [128, 128], BF16)
    on_bf = sb(per, "on_bf", [128, 128], BF16)
    cp(id_bf[:], ID_f)
    cp(on_bf[:], ON_f)
    modT = sb(per, "modT", [128, DEPTH, 48, 2], F32)
    A1 = sb(per, "A1", [128, DEPTH, 8, 2], F32)
    A2 = sb(per, "A2", [128, DEPTH, 8, 2], F32)
    g1s = sb(per, "g1s", [128, DEPTH, 8], F32)
    g2s = sb(per, "g2s", [128, DEPTH, 8], F32)
    gfs = sb(per, "gfs", [128, 8], F32)
    S.load(g1s[:], g1T.rearrange("l p k -> p l k"))
    S.load(g2s[:], g2T.rearrange("l p k -> p l k"))
    S.load(gfs[:], gfT[:, :])
    att_g = sb(per, "att_g", [128, DEPTH], F32)
    hg_g = sb(per, "hg_g", [128, DEPTH], F32)
    S.load(att_g[:], att_gT[:, :])
    S.load(hg_g[:], hg_gT[:, :])
    lbx = sb(per, "lbx", [128, 2, DEPTH, 8], F32)
    S.load(lbx[:], hg_lbT[:, :, :, :])
    lb1 = sb(per, "lb1", [128, 2, 8], F32)
    oml1 = sb(per, "oml1", [128, 2, 8], F32)
    neglam = sb(per, "neglam", [128, DEPTH], F32)
    ssm_g = sb(per, "ssm_g", [128, DEPTH, 16], F32)
    S.load(ssm_g[:], ssm_gT.rearrange("l p k -> p l k"))
    cbT = sb(per, "cbT", [128, DEPTH, 24], F32)
    S.load(cbT[:], conv_bT.rearrange("l p k -> p l k"))

    with ExitStack() as ph:
        sc = sb(ph, "p0_sc", [128, 8, 2], F32)
        S.load(sc[:], c2_in.rearrange("(k p) s -> p k s", p=128))
        sg = sb(ph, "p0_sg", [128, 8, 2], F32)
        act(sg[:], sc[:], AF.Sigmoid)
        tt(sc[:], sc[:], sg[:], ALU.mult)
        badd = sb(ph, "p0_b", [128, DEPTH, 48], F32)
        S.load(badd[:], b_adaT.rearrange("l p k -> p l k"))
        wst = [sb(ph, f"p0_w{i}", [128, 8, 512], F32) for i in range(2)]
        pm = ps(ph, "p0_pm", [128, 96], F32)
        it = 0
        for l in range(DEPTH):
            for cg in range(12):
                w = wst[it % 2]
                it += 1
                S.load(w[:], w_ada[l, :, cg * 512:(cg + 1) * 512].rearrange("(k p) c -> p k c", p=128))
                for j in range(4):
                    ch = cg * 4 + j
                    for k in range(8):
                        mm(pm[:, ch * 2:ch * 2 + 2], w[:, k, j * 128:(j + 1) * 128], sc[:, k, :], start=(k == 0), stop=(k == 7))
            tt(modT[:, l, :, :], pm[:, 0:96].rearrange("p (c s) -> p c s", s=2), _bc_last(badd[:, l, :], 2), ALU.add)
            tsc(A1[:, l, :, :], modT[:, l, 8:16, :], 1.0, None, ALU.add)
            tt(A1[:, l, :, :], A1[:, l, :, :], _bc_last(g1s[:, l, :], 2), ALU.mult)
            tsc(A2[:, l, :, :], modT[:, l, 32:40, :], 1.0, None, ALU.add)
            tt(A2[:, l, :, :], A2[:, l, :, :], _bc_last(g2s[:, l, :], 2), ALU.mult)
        lamt = sb(ph, "p0_lam", [128, DEPTH, 4, 64], F32)
        S.load(lamt[:].rearrange("p a b c -> p (a b c)"), bass.AP(att_lam.tensor, 0, [[0, 128], [1, DEPTH * 256]]))
        pr = sb(ph, "p0_pr", [128, 64], F32)
        sm = sb(ph, "p0_sm", [128, 4], F32)
        for l in range(DEPTH):
            for j in range(2):
                tt(pr[:], lamt[:, l, 2 * j, :], lamt[:, l, 2 * j + 1, :], ALU.mult)
                act(pr[:], pr[:], AF.Copy, accum=sm[:, 2 * l + j:2 * l + j + 1])
        act(sm[:], sm[:], AF.Exp)
        for l in range(DEPTH):
            lam_init = 0.8 - 0.6 * math.exp(-0.3 * l)
            tt(neglam[:, l:l + 1], sm[:, 2 * l + 1:2 * l + 2], sm[:, 2 * l:2 * l + 1], ALU.subtract)
            tsc(neglam[:, l:l + 1], neglam[:, l:l + 1], -lam_init, None, ALU.add)
        tt(lb1[:], lbx[:, :, 1, :], lbx[:, :, 0, :], ALU.subtract)
        act(lb1[:], lb1[:], AF.Sigmoid)
        tsc(oml1[:], lb1[:], -1.0, 1.0, ALU.mult, ALU.add)
        S.barrier()

    def norm_block(ph_tiles, xb, bs, A_of_k, B_of_k, out_of_k):
        sq, pst, rs, tmp = ph_tiles
        for k in range(8):
            act(sq[:, k, :bs], xb[:, k, :bs], AF.Square)
        for k in range(8):
            mm(pst[:, :bs], on_bf[:], sq[:, k, :bs], start=(k == 0), stop=(k == 7))
        tsc(rs[:, :bs], pst[:, :bs], 1.0 / D, EPS, ALU.mult, ALU.add)
        act(rs[:, :bs], rs[:, :bs], AF.Sqrt)
        recip(rs[:, :bs], rs[:, :bs])
        for k in range(8):
            tt(tmp[:, k, :bs], xb[:, k, :bs], rs[:, :bs], ALU.mult)
            b = B_of_k(k)
            if b is None:
                act(out_of_k(k), tmp[:, k, :bs], AF.Identity, scale=A_of_k(k))
            else:
                act(out_of_k(k), tmp[:, k, :bs], AF.Identity, scale=A_of_k(k), bias=b)

    def strm(b0):
        return 0 if b0 < LC else 1

    import os
    SSD_DBG = int(os.environ.get('SSD_DBG', '0'))

    def dbgstop(n):
        if SSD_DBG == n:
            raise _Stop()

    def layer_body(l):
        last = (l == DEPTH - 1)
        xsrc = xT_in if l == 0 else XT
        with ExitStack() as ph:
            hT = sb(ph, "hT", [128, 8, T], BF16)
            with ExitStack() as p1:
                xb2 = [sb(p1, f"p1_x{i}", [128, 8, 512], F32) for i in range(2)]
                sq = sb(p1, "p1_sq", [128, 8, 512], BF16)
                rs = sb(p1, "p1_rs", [128, 512], F32)
                tmp = sb(p1, "p1_tmp", [128, 8, 512], F32)
                pst = ps(p1, "p1_ps", [128, 512], F32)
                for bi, (b0, bs) in enumerate(BLOCKS):
                    xb = xb2[bi % 2]
                    S.load(xb[:, :, :bs], xsrc[:, b0:b0 + bs].rearrange("(k p) t -> p k t", p=128))
                    s = strm(b0)
                    norm_block((sq, pst, rs, tmp), xb, bs,
                               lambda k: A1[:, l, k, s:s + 1], lambda k: modT[:, l, k, s:s + 1],
                               lambda k: hT[:, k, b0:b0 + bs])
                S.barrier()
            with ExitStack() as p2:
                cs = sb(p2, "p2_cos", [128, T], F32)
                sn = sb(p2, "p2_sin", [128, T], F32)
                S.load(cs[:], cosT[:, :])
                S.load(sn[:], sinT[:, :])
                wf = [sb(p2, f"p2_wf{i}", [128, 8, 512], F32) for i in range(2)]
                wb = [sb(p2, f"p2_wb{i}", [128, 8, 512], BF16) for i in range(3)]
                stg = [sb(p2, f"p2_st{i}", [128, 512], F32) for i in range(4)]
                stb = [sb(p2, f"p2_sb{i}", [128, 512], BF16) for i in range(4)]
                r1 = sb(p2, "p2_r1", [128, 512], F32)
                r2 = sb(p2, "p2_r2", [128, 512], F32)
                pp = [ps(p2, f"p2_p{i}", [128, 512], F32) for i in range(6)]
                cnt = {'w': 0, 'p': 0, 's': 0, 'c': 0}

                def load_w(src, c0, n):
                    i = cnt['w']
                    cnt['w'] += 1
                    f = wf[i % 2]
                    b = wb[i % 3]
                    S.load(f[:, :, :n], src[:, c0:c0 + n].rearrange("(k p) c -> p k c", p=128))
                    for k in range(8):
                        if (k + i) % 2 == 0:
                            cp(b[:, k, :n], f[:, k, :n], eng='dve')
                        else:
                            cp(b[:, k, :n], f[:, k, :n], eng='act')
                    return b

                def newp():
                    p = pp[cnt['p'] % 6]
                    cnt['p'] += 1
                    return p

                def proj_F(wt, j, b0, bs):
                    p = newp()
                    for k in range(8):
                        mm(p[:, :bs], wt[:, k, j * 128:(j + 1) * 128], hT[:, k, b0:b0 + bs], start=(k == 0), stop=(k == 7))
                    return p

                def fgroup(src, c0, ncols, evac):
                    for g0 in range(0, ncols, 512):
                        n = min(512, ncols - g0)
                        wt = load_w(src, c0 + g0, n)
                        for j in range(n // 128):
                            for (b0, bs) in BLOCKS:
                                p = proj_F(wt, j, b0, bs)
                                evac(p, g0 + j * 128, b0, bs)

                def ev_simple(dst, func, dt):
                    def f(p, gc, b0, bs):
                        i = cnt['s']
                        cnt['s'] += 1
                        st = (stb if dt == BF16 else stg)[i % 4]
                        if func is None:
                            if i % 2 == 0:
                                cp(st[:, :bs], p[:, :bs], eng='dve')
                            else:
                                cp(st[:, :bs], p[:, :bs], eng='act')
                        else:
                            act(st[:, :bs], p[:, :bs], func)
                        S.store(dst[gc:gc + 128, b0:b0 + bs], st[:, :bs])
                    return f

                def ev_silu(dst):
                    def f(p, gc, b0, bs):
                        i = cnt['s']
                        cnt['s'] += 1
                        st = stg[i % 4]
                        act(st[:, :bs], p[:, :bs], AF.Sigmoid)
                        tt(st[:, :bs], p[:, :bs], st[:, :bs], ALU.mult)
                        S.store(dst[gc:gc + 128, b0:b0 + bs], st[:, :bs])
                    return f

                def rope_group(c_orig, c_rot, dst):
                    for g0 in range(0, 1024, 512):
                        wo = load_w(w_in[l], c_orig + g0, 512)
                        wr = load_w(w_rot[l], c_rot + g0, 512)
                        for j in range(4):
                            for (b0, bs) in BLOCKS:
                                p1_ = proj_F(wo, j, b0, bs)
                                p2_ = proj_F(wr, j, b0, bs)
                                i = cnt['s']
                                cnt['s'] += 1
                                st = stb[i % 4]
                                tt(r1[:, :bs], p1_[:, :bs], cs[:, b0:b0 + bs], ALU.mult)
                                tt(r2[:, :bs], p2_[:, :bs], sn[:, b0:b0 + bs], ALU.mult)
                                tt(st[:, :bs], r1[:, :bs], r2[:, :bs], ALU.add)
                                gc = g0 + j * 128
                                S.store(dst[gc:gc + 128, b0:b0 + bs], st[:, :bs])

                def tgroup(c0, ncols, dst, dcol0, func, dt):
                    for g0 in range(0, ncols, 512):
                        n = min(512, ncols - g0)
                        wt = load_w(w_in[l], c0 + g0, n)
                        for tix in range(NT):
                            p = newp()
                            for k in range(8):
                                mm(p[:, :n], hT[:, k, tix * 128:(tix + 1) * 128], wt[:, k, :n], start=(k == 0), stop=(k == 7))
                            i = cnt['s']
                            cnt['s'] += 1
                            st = (stb if dt == BF16 else stg)[i % 4]
                            if func == 'silu':
                                act(st[:, :n], p[:, :n], AF.Sigmoid)
                                tt(st[:, :n], p[:, :n], st[:, :n], ALU.mult)
                            elif i % 2 == 0:
                                cp(st[:, :n], p[:, :n], eng='dve')
                            else:
                                cp(st[:, :n], p[:, :n], eng='act')
                            S.store(dst[tix * 128:(tix + 1) * 128, dcol0 + g0:dcol0 + g0 + n], st[:, :n])

                rope_group(O_AK, 0, KT)
                rope_group(O_AQ, 1024, QT)
                tgroup(O_AV, 1024, AV, 0, None, BF16)
                fgroup(w_in[l], O_HFF, 1024, ev_simple(HFF, None, F32))
                fgroup(w_in[l], O_HFB, 1024, ev_simple(HFB, None, F32))
                tgroup(O_HI, 1024, HI, 0, None, BF16)
                fgroup(w_in[l], O_XBC, 3072, ev_simple(XBC, None, BF16))
                tgroup(O_DTF, 64, DTR, 0, None, F32)
                fgroup(w_in[l], O_HQ, 1024, ev_silu(HQ))
                fgroup(w_in[l], O_HGATE, 1024, ev_silu(HGATE))
                tgroup(O_Z, 2048, ZZ, 0, 'silu', F32)
                fgroup(w_in[l], O_GATES, 3072, ev_simple(GATES, AF.Sigmoid, F32))
                S.barrier()
        if stop_after == ('p2', l):
            raise _Stop()

        with ExitStack() as ph:
          if not os.environ.get('SKIP_AG'):
            kT2 = [sb(ph, f"a_kT{i}", [128, T], BF16) for i in range(2)]
            qT2 = [sb(ph, f"a_qT{i}", [128, T], BF16) for i in range(2)]
            vt2 = [sb(ph, f"a_v{i}", [128, NT, 128], BF16) for i in range(2)]
            qm2 = [[sb(ph, f"a_qm{i}_{c}", [128, T], BF16) for c in range(2)] for i in range(2)]
            pb = [sb(ph, f"a_p{i}", [128, 512], BF16) for i in range(6)]
            psm = [sb(ph, f"a_psm{i}", [128, 512], BF16) for i in range(4)]
            r0 = sb(ph, "a_r0", [128, 512], F32)
            r1 = sb(ph, "a_r1", [128, 512], F32)
            t0 = sb(ph, "a_t0", [128, 512], F32)
            t1 = sb(ph, "a_t1", [128, 512], F32)
            sq = sb(ph, "a_sq", [128, 512], BF16)
            ob = [sb(ph, f"a_ob{i}", [128, 512], BF16) for i in range(2)]
            gsc = sb(ph, "a_g", [128, 1], F32)
            lam_init = 0.8 - 0.6 * math.exp(-0.3 * l)
            tsc(gsc[:], att_g[:, l:l + 1], 1.0 - lam_init, None, ALU.mult)
            pS = [ps(ph, f"a_S{i}", [128, 512], F32) for i in range(3)]
            pO = [ps(ph, f"a_O{i}", [128, 512], F32) for i in range(2)]
            pL = [ps(ph, f"a_L{i}", [128, 512], F32) for i in range(2)]
            pX = ps(ph, "a_X", [128, 512], F32)
            for h in range(8):
                kTt, qTt, vt = kT2[h % 2], qT2[h % 2], vt2[h % 2]
                S.load(kTt[:], KT[h * 128:(h + 1) * 128, :])
                S.load(qTt[:], QT[h * 128:(h + 1) * 128, :])
                S.load(vt[:], AV[:, h * 128:(h + 1) * 128].rearrange("(n p) c -> p n c", p=128))
                qm = qm2[h % 2]
                tsc(qm[0][:], qTt[:], U_f[:, 63:64], None, ALU.mult)
                tsc(qm[1][:], qTt[:], Lo_f[:, 64:65], None, ALU.mult)
                items = []
                for bi, (b0, bs) in enumerate(BLOCKS):
                    if last and b0 < LC:
                        continue
                    nk = 2 if b0 < LC else NT
                    for kt in range(nk):
                        for c in range(2):
                            items.append((bi, b0, bs, kt, c, nk))

                def emit_S(j):
                    (bi_, b0_, bs_, kt_i, c_, nk_) = items[j]
                    mm(pS[j % 3][:, :bs_], kTt[:, kt_i * 128:(kt_i + 1) * 128], qm[c_][:, b0_:b0_ + bs_])

                LOOK = 2
                for j in range(min(LOOK, len(items))):
                    emit_S(j)
                for j, (bi, b0, bs, kt, c, nk) in enumerate(items):
                    if j + LOOK < len(items):
                        emit_S(j + LOOK)
                    P = pb[j % 6]
                    act(P[:, :bs], pS[j % 3][:, :bs], AF.Exp, scale=0.125)
                    mm(pO[c][:, :bs], vt[:, kt, :], P[:, :bs], start=(kt == 0), stop=(kt == nk - 1))
                    if kt % 2 == 1:
                        sm_ = psm[((kt // 2) * 2 + c) % 4]
                        tt(sm_[:, :bs], pb[(j - 2) % 6][:, :bs], P[:, :bs], ALU.add)
                        mm(pL[c][:, :bs], on_bf[:], sm_[:, :bs], start=(kt == 1), stop=(kt == nk - 1))
                    if not (kt == nk - 1 and c == 1):
                        continue
                    recip(r0[:, :bs], pL[0][:, :bs])
                    recip(r1[:, :bs], pL[1][:, :bs])
                    tt(t0[:, :bs], pO[0][:, :bs], r0[:, :bs], ALU.mult)
                    tt(t1[:, :bs], pO[1][:, :bs], r1[:, :bs], ALU.mult)
                    stt(t0[:, :bs], t1[:, :bs], neglam[:, l:l + 1], t0[:, :bs], ALU.mult, ALU.add)
                    act(sq[:, :bs], t0[:, :bs], AF.Square)
                    mm(pX[:, :bs], on_bf[:], sq[:, :bs])
                    tsc(r0[:, :bs], pX[:, :bs], 1.0 / 128, EPS, ALU.mult, ALU.add)
                    act(r0[:, :bs], r0[:, :bs], AF.Sqrt)
                    recip(r0[:, :bs], r0[:, :bs])
                    tt(t0[:, :bs], t0[:, :bs], r0[:, :bs], ALU.mult)
                    o = ob[bi % 2]
                    act(o[:, :bs], t0[:, :bs], AF.Identity, scale=gsc[:, 0:1])
                    S.store(BR[h * 128:(h + 1) * 128, b0:b0 + bs], o[:, :bs])
            S.barrier()
        if stop_after == ('att', l):
            raise _Stop()

        with ExitStack() as ph:
          if not os.environ.get('SKIP_AG'):
            fA = sb(ph, "g_f", [128, T], F32)
            bB = sb(ph, "g_b", [128, T], F32)
            kK = sb(ph, "g_k", [128, T], F32)
            tM = sb(ph, "g_t", [128, T], F32)
            hq = sb(ph, "g_hq", [128, T], F32)
            gt = [sb(ph, f"g_gate{i}", [128, 512], F32) for i in range(2)]
            qt = sb(ph, "g_qt", [128, T], BF16)
            kt_ = sb(ph, "g_kt", [128, T], BF16)
            kh = sb(ph, "g_kh", [128, T], BF16)
            m0 = sb(ph, "g_m0", [128, T], BF16)
            osum = sb(ph, "g_os", [128, T], F32)
            v128 = sb(ph, "g_v128", [128, NT, 128], BF16)
            ebend = sb(ph, "g_eb", [128, NCH], F32)
            S32 = [sb(ph, f"g_S32_{i}", [128, 128], F32) for i in range(2)]
            Sbf = [sb(ph, f"g_Sbf{i}", [128, 128], BF16) for i in range(4)]
            khTm = [[sb(ph, f"g_khm{i}_{j}", [128, 128], BF16) for j in range(4)] for i in range(2)]
            kh128 = [sb(ph, f"g_kh128_{i}", [128, 128], BF16) for i in range(2)]
            At = [sb(ph, f"g_At{i}", [128, 128], BF16) for i in range(2)]
            sq = sb(ph, "g_sq", [128, 512], BF16)
            rs = sb(ph, "g_rs", [128, 512], F32)
            on_ = sb(ph, "g_on", [128, 512], F32)
            ob = [sb(ph, f"g_ob{i}", [128, 512], BF16) for i in range(2)]
            lbz = sb(ph, "g_lbz", [128, 2], F32)
            pT = [ps(ph, "g_pT0", [32, 128], BF16)]
            pSc = [ps(ph, "g_pS0", [32, 32], F32)]
            pD = [ps(ph, f"g_pD{i}", [128, 128], F32) for i in range(4)]
            pOa = ps(ph, "g_pOa", [128, 512], F32)
            pOb = ps(ph, "g_pOb", [128, 512], F32)
            S.load(m0[:], m0_in[:, :])
            memset(lbz[:, 0:1], 0.0)
            memset(lbz[:, 1:2], 1.0)
            for h in range(8):
                S.load(hq[:], HQ[h * 128:(h + 1) * 128, :])
                S.load(v128[:], HI[:, h * 128:(h + 1) * 128].rearrange("(n p) c -> p n c", p=128))
                for d in range(2):
                    S.load(fA[:], (HFF if d == 0 else HFB)[h * 128:(h + 1) * 128, :])
                    if l == 0:
                        lb_ap, oml_ap = lbz[:, 0:1], lbz[:, 1:2]
                    else:
                        lb_ap, oml_ap = lb1[:, d, h:h + 1], oml1[:, d, h:h + 1]
                    act(fA[:], fA[:], AF.Sigmoid)
                    tsc(fA[:], fA[:], oml_ap, lb_ap, ALU.mult, ALU.add)
                    tsc(kK[:], fA[:], -1.0, 1.0, ALU.mult, ALU.add)
                    act(fA[:], fA[:], AF.Ln)
                    if d == 0:
                        scan(bB[:], m0[:], fA[:])
                        bend = _strided(bB[:], 31, 32, NCH)
                    else:
                        scan(_rev(bB[:]), m0[:], _rev(fA[:]))
                        bend = _strided(bB[:], 0, 32, NCH)
                    act(tM[:], bB[:], AF.Exp)
                    tt(qt[:], hq[:], tM[:], ALU.mult)
                    act(tM[:], bB[:], AF.Exp, scale=-1.0)
                    tt(kt_[:], kK[:], tM[:], ALU.mult)
                    tt(tM[:].rearrange("p (c j) -> p c j", j=32), _bc_last(bend, 32),
                       bB[:].rearrange("p (c j) -> p c j", j=32), ALU.subtract)
                    act(tM[:], tM[:], AF.Exp)
                    tt(kh[:], kK[:], tM[:], ALU.mult)
                    act(ebend[:], bend, AF.Exp)
                    memset(S32[0][:], 0.0)
                    memset(Sbf[0][:], 0.0)
                    order = list(range(8)) + list(range(8, NCH)) if d == 0 else list(range(7, -1, -1)) + list(range(NCH - 1, 7, -1))
                    tiles = [0, 1] + list(range(2, NT)) if d == 0 else [1, 0] + list(range(NT - 1, 1, -1))
                    BD = BDf if d == 0 else BDb
                    LAG = 2
                    pend = []

                    def blk_of(c):
                        if c < 8:
                            return 0, 256
                        return BLOCKS[1 + (c - 8) // 16]

                    def emit_tile(ti):
                        tix = tiles[ti]
                        t0_ = tix * 128
                        ts = ti % 2
                        tr(pT[0][:, 0:128], kh[:, t0_:t0_ + 128], id_bf[:])
                        cp(kh128[ts][:], pT[0][:, 0:128], eng='act')
                        for j in range(4):
                            if j < 2:
                                tsc(khTm[ts][j][:], kh128[ts][:], RM[:, j:j + 1], 1.0, ALU.mult, ALU.mult, eng='pool')
                            elif j == 2:
                                tsc(khTm[ts][j][:], kh128[ts][:], RM[:, j:j + 1], None, ALU.mult)
                            else:
                                act(khTm[ts][j][:], kh128[ts][:], AF.Identity, scale=RM[:, j:j + 1])
                        mm(pSc[0][:, 0:128], kt_[:, t0_:t0_ + 128], qt[:, t0_:t0_ + 128])
                        tt(At[ts][:], pSc[0][:, 0:128], BD, ALU.mult)
                        bstart_, bsz_ = blk_of(tix * 4)
                        oc_ = t0_ - bstart_
                        mm(pOa[:, oc_:oc_ + 128], v128[:, tix, :], At[ts][:])
                        last_tile = (t0_ + 128 == bstart_ + bsz_) if d == 0 else (t0_ == bstart_)
                        if last_tile:
                            if d == 0:
                                cp(osum[:, bstart_:bstart_ + bsz_], pOa[:, :bsz_], eng='act')
                            else:
                                tt(osum[:, bstart_:bstart_ + bsz_], osum[:, bstart_:bstart_ + bsz_], pOa[:, :bsz_], ALU.add)

                    def emit_inter(item):
                        (i_, c0_, bstart_, bsz_, lastb_) = item
                        oc_ = c0_ - bstart_
                        mm(pOb[:, oc_:oc_ + 32], Sbf[i_ % 4][:], qt[:, c0_:c0_ + 32])
                        if lastb_:
                            tt(osum[:, bstart_:bstart_ + bsz_], osum[:, bstart_:bstart_ + bsz_], pOb[:, :bsz_], ALU.add)

                    emit_tile(0)
                    for idx, c in enumerate(order):
                        c0 = c * 32
                        sl = idx % 4
                        ti = idx // 4
                        tix = tiles[ti]
                        assert tix == c // 4
                        if idx % 4 == 1 and ti + 1 < len(tiles):
                            emit_tile(ti + 1)
                        bstart, bsz = blk_of(c)
                        last_in_blk = (c0 + 32 == bstart + bsz) if d == 0 else (c0 == bstart)
                        mm(pD[sl][:, 0:128], khTm[ti % 2][c % 4][:], v128[:, tix, :])
                        pend.append((idx, c0, bstart, bsz, last_in_blk))
                        if len(pend) > LAG:
                            emit_inter(pend.pop(0))
                        stt(S32[(idx + 1) % 2][:], S32[idx % 2][:], ebend[:, c:c + 1], pD[sl][:, 0:128], ALU.mult, ALU.add, stream_self=True)
                        cp(Sbf[(idx + 1) % 4][:], S32[(idx + 1) % 2][:], eng='act')
                    while pend:
                        emit_inter(pend.pop(0))
                gcol = hg_g[:, l:l + 1]
                for bi, (b0, bs) in enumerate(BLOCKS):
                    if last and b0 < LC:
                        continue
                    act(sq[:, :bs], osum[:, b0:b0 + bs], AF.Square)
                    px = pOa if bi % 2 == 0 else pOb
                    mm(px[:, :bs], on_bf[:], sq[:, :bs])
                    tsc(rs[:, :bs], px[:, :bs], 1.0 / 128, EPS, ALU.mult, ALU.add)
                    act(rs[:, :bs], rs[:, :bs], AF.Sqrt)
                    recip(rs[:, :bs], rs[:, :bs])
                    tt(on_[:, :bs], osum[:, b0:b0 + bs], rs[:, :bs], ALU.mult)
                    g__ = gt[bi % 2]
                    S.load(g__[:, :bs], HGATE[h * 128:(h + 1) * 128, b0:b0 + bs])
                    tt(on_[:, :bs], on_[:, :bs], g__[:, :bs], ALU.mult)
                    o = ob[bi % 2]
                    act(o[:, :bs], on_[:, :bs], AF.Identity, scale=gcol)
                    S.store(BR[1024 + h * 128:1024 + (h + 1) * 128, b0:b0 + bs], o[:, :bs])
            S.barrier()
        if stop_after == ('gla', l):
            raise _Stop()

        PADT = T + 8
        OFFC, OFFL = 2, 2 + 256 + 4

        def poff(b0):
            return (OFFC + b0) if b0 < LC else (OFFL + (b0 - LC))

        with ExitStack() as ph:
            up = [sb(ph, f"c_u{i}", [128, PADT], BF16) for i in range(3)]
            cw = sb(ph, "c_w", [128, 24, 5], F32)
            S.load(cw[:], conv_wT[l])
            dw = sb(ph, "c_dw", [128, 24, 5, 128], BF16)
            brow_f = sb(ph, "c_brf", [1, 3072], F32)
            brow = sb(ph, "c_br", [1, 3072], BF16)
            S.load(brow_f[:], bass.AP(conv_b.tensor, l * 3072, [[0, 1], [1, 3072]]))
            cp(brow[:], brow_f[:])
            onerow = on_bf[0:1, :]
            stT = [sb(ph, f"c_sT{i}", [128, 512], BF16) for i in range(3)]
            stF = [sb(ph, f"c_sF{i}", [128, 512], BF16) for i in range(3)]
            sgm = [sb(ph, f"c_sg{i}", [128, 512], F32) for i in range(4)]
            pc = [ps(ph, f"c_p{i}", [128, 512], F32) for i in range(4)]
            for i in range(3):
                memset(up[i][:], 0.0)
            for cc in range(24):
                for k in range(5):
                    tsc(dw[:, cc, k, :], id_bf[:], cw[:, cc, k:k + 1], None, ALU.mult)
            n_p = 0
            n_s = 0
            for cc in range(24):
                u = up[cc % 3]
                S.load(u[:, OFFC:OFFC + LC], XBC[cc * 128:(cc + 1) * 128, 0:LC])
                S.load(u[:, OFFL:OFFL + L], XBC[cc * 128:(cc + 1) * 128, LC:T])
                if cc < 20:
                    for tix in range(NT):
                        t0_ = tix * 128
                        p = pc[n_p % 4]
                        n_p += 1
                        po = poff(t0_)
                        for k in range(5):
                            mm(p[:, 0:128], u[:, po + k - 2:po + k - 2 + 128], dw[:, cc, k, :], start=(k == 0), stop=False)
                        mm(p[:, 0:128], onerow, brow[0:1, cc * 128:(cc + 1) * 128], start=False, stop=True)
                        sg_ = sgm[n_s % 4]
                        st = stT[n_s % 3]
                        n_s += 1
                        act(sg_[:, 0:128], p[:, 0:128], AF.Sigmoid)
                        if cc < 16:
                            tt(sg_[:, 0:128], p[:, 0:128], sg_[:, 0:128], ALU.mult)
                            S.store(XS[t0_:t0_ + 128, cc * 128:(cc + 1) * 128], sg_[:, 0:128])
                            continue
                        tt(st[:, 0:128], p[:, 0:128], sg_[:, 0:128], ALU.mult)
                        if cc < 16:
                            S.store(XS[t0_:t0_ + 128, cc * 128:(cc + 1) * 128], st[:, 0:128])
                        else:
                            S.store(BTOK[t0_:t0_ + 128, (cc - 16) * 128:(cc - 15) * 128], st[:, 0:128])
                if cc >= 16:
                    dst = BTT if cc < 20 else CTT
                    r0_ = (cc - 16) * 128 if cc < 20 else (cc - 20) * 128
                    for (b0, bs) in BLOCKS:
                        p = pc[n_p % 4]
                        n_p += 1
                        po = poff(b0)
                        for k in range(5):
                            mm(p[:, :bs], dw[:, cc, k, :], u[:, po + k - 2:po + k - 2 + bs], start=(k == 0), stop=(k == 4))
                        sg_ = sgm[n_s % 4]
                        st = stF[n_s % 3]
                        n_s += 1
                        act(sg_[:, :bs], p[:, :bs], AF.Sigmoid, bias=cbT[:, l, cc:cc + 1])
                        stt(st[:, :bs], p[:, :bs], cbT[:, l, cc:cc + 1], sg_[:, :bs], ALU.add, ALU.mult)
                        S.store(dst[r0_:r0_ + 128, b0:b0 + bs], st[:, :bs])
            S.barrier()
        if stop_after == ('conv', l):
            raise _Stop()

        with ExitStack() as ph:
            ST32 = sb(ph, "s_ST", [128, 2048], F32)
            STb = sb(ph, "s_STb", [128, 2048], BF16)
            dtb = sb(ph, "s_dtb", [128, 64], F32)
            S.load(dtb[:], bass.AP(dt_bias.tensor, l * 64, [[0, 128], [1, 64]]))
            Ab = sb(ph, "s_A", [128, 64], F32)
            S.load(Ab[:], bass.AP(a_log.tensor, l * 64, [[0, 128], [1, 64]]))
            act(Ab[:], Ab[:], AF.Exp)
            tsc(Ab[:], Ab[:], -1.0, None, ALU.mult)
            dsk = sb(ph, "s_dsk", [128, 32], F32)
            S.load(dsk[:], bass.AP(ssm_d.tensor, l * 32, [[0, 128], [1, 32]]))
            xs_ = [sb(ph, f"s_xs{i}", [128, 2048], F32) for i in range(2)]
            btk = [sb(ph, f"s_bt{i}", [128, 512], BF16) for i in range(2)]
            bT_ = [sb(ph, f"s_bT{i}", [128, 4, 128], BF16) for i in range(2)]
            cT_ = [sb(ph, f"s_cT{i}", [128, 4, 128], BF16) for i in range(2)]
            dtr = [sb(ph, f"s_dt{i}", [128, 64], F32) for i in range(2)]
            dtv = [sb(ph, f"s_dtv{i}", [128, 32], F32) for i in range(2)]
            aav = [sb(ph, f"s_a{i}", [128, 32], F32) for i in range(2)]
            eacv = [sb(ph, f"s_eac{i}", [128, 32], F32) for i in range(2)]
            wendv = [sb(ph, f"s_wend{i}", [128, 32], F32) for i in range(2)]
            dendv = [sb(ph, f"s_dend{i}", [128, 32], F32) for i in range(2)]
            xdtv = [sb(ph, f"s_xdt{i}", [128, 2048], BF16) for i in range(2)]
            xdwv = [sb(ph, f"s_xdw{i}", [128, 2048], BF16) for i in range(2)]
            Am = [sb(ph, f"s_Am{i}", [128, 8, 128], F32) for i in range(2)]
            LT = [sb(ph, f"s_LT{i}", [128, 8, 128], F32) for i in range(2)]
            MT = [sb(ph, f"s_MT{i}", [128, 8, 128], BF16) for i in range(2)]
            CBm = [sb(ph, f"s_CB{i}", [128, 128], F32) for i in range(2)]
            ytmp = sb(ph, "s_yt", [128, 512], F32)
            yo = [sb(ph, f"s_yo{i}", [128, 2048], F32) for i in range(2)]
            yfl = [sb(ph, f"s_yf{i}", [128, 2048], F32) for i in range(2)]
            zt = [sb(ph, f"s_z{i}", [128, 2048], F32) for i in range(2)]
            stmp = sb(ph, "s_st", [128, 512], F32)
            junk = sb(ph, "s_junk", [128, 2048], BF16)
            ssq = sb(ph, "s_ssq", [128, 1], F32)
            ynb = sb(ph, "s_ynb", [128, 2048], F32)
            obt = [sb(ph, f"s_ob{i}", [128, 4, 128], BF16) for i in range(2)]
            p_ac = ps(ph, "s_pac", [128, 64], F32)
            p_cb = ps(ph, "s_pcb", [128, 128], F32)
            p_df = [ps(ph, f"s_pdf{i}", [128, 512], F32) for i in range(2)]
            p_y = ps(ph, "s_py", [128, 512], F32)
            p_yi = ps(ph, "s_pyi", [128, 512], F32)
            p_st = ps(ph, "s_pst", [128, 512], F32)
            p_tr = ps(ph, "s_ptr", [128, 4, 128], F32)
            nit = 0
            for d in range(2):
                if d == 1:
                    S.barrier()
                memset(ST32[:], 0.0)
                memset(STb[:], 0.0)
                order = [0, 1] + list(range(2, NT)) if d == 0 else [1, 0] + list(range(NT - 1, 1, -1))
                Ucum = U_f if d == 0 else Lo_f
                Mlhs = Ms_f if d == 0 else Ml_f
                Mcb = U_f if d == 0 else Lo_f
                def aside(tix, sl):
                    t0_ = tix * 128
                    xs = xs_[sl]
                    S.load(xs[:], XS[t0_:t0_ + 128, :])
                    S.load(btk[sl][:], BTOK[t0_:t0_ + 128, :])
                    S.load(bT_[sl][:], BTT[:, t0_:t0_ + 128].rearrange("(g p) t -> p g t", p=128))
                    S.load(cT_[sl][:], CTT[:, t0_:t0_ + 128].rearrange("(g p) t -> p g t", p=128))
                    S.load(dtr[sl][:], DTR[t0_:t0_ + 128, :])
                    if d == 1:
                        S.load(yfl[sl][:], YF[t0_:t0_ + 128, :])
                        S.load(zt[sl][:], ZZ[t0_:t0_ + 128, :])
                    dt_, aa_, eac_, wend_, dend_ = dtv[sl], aav[sl], eacv[sl], wendv[sl], dendv[sl]
                    tt(dt_[:], dtr[sl][:, d * 32:(d + 1) * 32], dtb[:, d * 32:(d + 1) * 32], ALU.add)
                    act(dt_[:], dt_[:], AF.Exp)
                    act(dt_[:], dt_[:], AF.Ln, bias=1.0)
                    tt(aa_[:], dt_[:], Ab[:, d * 32:(d + 1) * 32], ALU.mult)
                    mm(p_ac[:, 0:32], Ucum, aa_[:])
                    mm(p_ac[:, 32:64], ON_f, aa_[:])
                    act(eac_[:], p_ac[:, 0:32], AF.Exp)
                    act(dend_[:], p_ac[:, 32:64], AF.Exp)
                    cp(wend_[:], p_ac[:, 0:32], eng='dve')
                    tt(wend_[:], p_ac[:, 32:64], wend_[:], ALU.subtract)
                    act(wend_[:], wend_[:], AF.Exp)
                    xs3 = xs[:].rearrange("p (h q) -> p h q", q=64)
                    tt(xdtv[sl][:].rearrange("p (h q) -> p h q", q=64), xs3, _bc_last(dt_[:], 64), ALU.mult, eng='pool')
                    tt(wend_[:], wend_[:], dt_[:], ALU.mult)
                    tt(xdwv[sl][:].rearrange("p (h q) -> p h q", q=64), xs3, _bc_last(wend_[:], 64), ALU.mult)

                aside(order[0], nit % 2)
                for oi, tix in enumerate(order):
                    t0_ = tix * 128
                    sl = nit % 2
                    nit += 1
                    if oi + 1 < len(order):
                        aside(order[oi + 1], nit % 2)
                    xs = xs_[sl]
                    xs3 = xs[:].rearrange("p (h q) -> p h q", q=64)
                    aa_, eac, dend, xdt, xdw = aav[sl], eacv[sl], dendv[sl], xdtv[sl], xdwv[sl]
                    yout = yo[sl]
                    for g in range(4):
                        gs = (nit * 4 + g) % 2
                        mm(p_cb[:, 0:128], bT_[sl][:, g, :], cT_[sl][:, g, :])
                        tt(CBm[gs][:], p_cb[:, 0:128], Mcb, ALU.mult)
                        tt(Am[gs][:], _bc_mid(Ucum, 8), _bc_last(aa_[:, g * 8:(g + 1) * 8], 128), ALU.mult, eng='pool')
                        for half in range(2):
                            mm(p_df[half][:, 0:512], Mlhs, Am[gs][:, half * 4:(half + 1) * 4, :].rearrange("p h t -> p (h t)"))
                        act(LT[gs][:, 0:4, :], p_df[0][:].rearrange("p (h t) -> p h t", t=128), AF.Exp)
                        act(LT[gs][:, 4:8, :], p_df[1][:].rearrange("p (h t) -> p h t", t=128), AF.Exp)
                        tt(MT[gs][:], LT[gs][:], _bc_mid(CBm[gs][:], 8), ALU.mult)
                        dbgstop(3)
                        for hh in range(8):
                            hg = g * 8 + hh
                            mm(p_y[:, hh * 64:(hh + 1) * 64], MT[gs][:, hh, :], xdt[:, hg * 64:(hg + 1) * 64])
                        mm(p_yi[:], cT_[sl][:, g, :], STb[:, g * 512:(g + 1) * 512])
                        tt(ytmp[:].rearrange("p (h q) -> p h q", q=64), p_yi[:].rearrange("p (h q) -> p h q", q=64),
                           _bc_last(eac[:, g * 8:(g + 1) * 8], 64), ALU.mult)
                        tt(yout[:, g * 512:(g + 1) * 512], ytmp[:], p_y[:], ALU.add)
                        dbgstop(4)
                        mm(p_st[:], btk[sl][:, g * 128:(g + 1) * 128], xdw[:, g * 512:(g + 1) * 512])
                        tt(stmp[:].rearrange("p (h q) -> p h q", q=64),
                           ST32[:, g * 512:(g + 1) * 512].rearrange("p (h q) -> p h q", q=64),
                           _bc_last(dend[:, g * 8:(g + 1) * 8], 64), ALU.mult)
                        tt(ST32[:, g * 512:(g + 1) * 512], stmp[:], p_st[:], ALU.add)
                        cp(STb[:, g * 512:(g + 1) * 512], ST32[:, g * 512:(g + 1) * 512], eng='act')
                        dbgstop(5)
                    if d == 0:
                        S.store(YF[t0_:t0_ + 128, :], yout[:])
                        dbgstop(6)
                        if tix == 33:
                            dbgstop(7)
                    else:
                        if last and tix < 2:
                            continue
                        tt(yout[:], yout[:], yfl[sl][:], ALU.add)
                        tt(yfl[sl][:].rearrange("p (h q) -> p h q", q=64), xs3, _bc_last(dsk[:], 64), ALU.mult)
                        tt(yout[:], yout[:], yfl[sl][:], ALU.add)
                        tt(yout[:], yout[:], zt[sl][:], ALU.mult)
                        act(junk[:], yout[:], AF.Square, accum=ssq[:])
                        tsc(ssq[:], ssq[:], 1.0 / 2048, EPS, ALU.mult, ALU.add)
                        act(ssq[:], ssq[:], AF.Sqrt)
                        recip(ssq[:], ssq[:])
                        tsc(ynb[:], yout[:], ssq[:, 0:1], None, ALU.mult)
                        for q4 in range(4):
                            for j in range(4):
                                tr(p_tr[:, j * 128:(j + 1) * 128], ynb[:, (q4 * 4 + j) * 128:(q4 * 4 + j + 1) * 128], ID_f)
                            o = obt[q4 % 2]
                            for j in range(4):
                                act(o[:, j, :], p_tr[:, j * 128:(j + 1) * 128], AF.Identity, scale=ssm_g[:, l, q4 * 4 + j:q4 * 4 + j + 1])
                            S.store(BR[2048 + q4 * 512:2048 + (q4 + 1) * 512, t0_:t0_ + 128].rearrange("(j p) t -> p j t", p=128), o[:])
                S.barrier()
        if stop_after == ('ssd', l):
            raise _Stop()

        with ExitStack() as ph:
            wall = sb(ph, "m_w", [128, 40, 1024], BF16)
            with ExitStack() as pw:
                wstg = [sb(pw, f"m_ws{i}", [128, 4, 1024], F32) for i in range(2)]
                srcs = [(w_b_att[l], 8), (w_b_hg[l], 8), (w_b_ssm[l], 16), (w_out[l], 8)]
                kc0 = 0
                i = 0
                for (src, nk_) in srcs:
                    for k4 in range(0, nk_, 4):
                        f = wstg[i % 2]
                        S.load(f[:], src[k4 * 128:(k4 + 4) * 128, :].rearrange("(k p) c -> p k c", p=128))
                        for kk_ in range(4):
                            cp(wall[:, kc0 + k4 + kk_, :], f[:, kk_, :], eng=('dve' if (kk_ + i) % 2 == 0 else 'act'),
                               sub={wall.name: kc0 + k4 + kk_})
                        i += 1
                    kc0 += nk_
                S.barrier()
            MB = 256
            brb = [sb(ph, f"m_br{i}", [128, 32, MB], BF16) for i in range(2)]
            gb = [sb(ph, f"m_g{i}", [128, 24, MB], F32) for i in range(2)]
            xb = [sb(ph, f"m_x{i}", [128, 8, MB], F32) for i in range(2)]
            macc = sb(ph, "m_acc", [128, MB], F32)
            mtmp = sb(ph, "m_tmp", [128, MB], F32)
            mT = sb(ph, "m_mT", [128, 8, MB], BF16)
            pm = [ps(ph, f"m_p{i}", [128, MB], F32) for i in range(4)]
            npm = 0
            bi = 0
            for b0 in range(0, T, MB):
                if last and b0 < LC:
                    continue
                bs = MB
                s = strm(b0)
                br = brb[bi % 2]
                g_ = gb[bi % 2]
                x_ = xb[bi % 2]
                bi += 1
                S.load(br[:], BR[:, b0:b0 + bs].rearrange("(k p) t -> p k t", p=128))
                S.load(g_[:], GATES[:, b0:b0 + bs].rearrange("(k p) t -> p k t", p=128))
                S.load(x_[:], xsrc[:, b0:b0 + bs].rearrange("(k p) t -> p k t", p=128))
                for oc in range(8):
                    for bri, (k0, nk_) in enumerate(((0, 8), (8, 8), (16, 16))):
                        p = pm[npm % 4]
                        npm += 1
                        for k in range(nk_):
                            mm(p[:, :MB], wall[:, k0 + k, oc * 128:(oc + 1) * 128], br[:, k0 + k, :], start=(k == 0), stop=(k == nk_ - 1))
                        if bri == 0:
                            tt(macc[:], p[:, :MB], g_[:, oc, :], ALU.mult)
                        else:
                            tt(mtmp[:], p[:, :MB], g_[:, bri * 8 + oc, :], ALU.mult)
                            if bri == 1:
                                tt(macc[:], macc[:], mtmp[:], ALU.add)
                            else:
                                tt(mT[:, oc, :], macc[:], mtmp[:], ALU.add)
                for oc in range(8):
                    p = pm[npm % 4]
                    npm += 1
                    for k in range(8):
                        mm(p[:, :MB], wall[:, 32 + k, oc * 128:(oc + 1) * 128], mT[:, k, :], start=(k == 0), stop=(k == 7))
                    stt(x_[:, oc, :], p[:, :MB], modT[:, l, 16 + oc, s:s + 1], x_[:, oc, :], ALU.mult, ALU.add)
                S.store(XT[:, b0:b0 + bs].rearrange("(k p) t -> p k t", p=128), x_[:])
            S.barrier()
        if stop_after == ('merge', l):
            raise _Stop()

        with ExitStack() as ph:
            w1b = sb(ph, "f_w1", [128, 8, FFH], BF16)
            w3b = sb(ph, "f_w3", [128, 8, FFH], BF16)
            w2b = sb(ph, "f_w2", [128, 22, 1024], BF16)
            with ExitStack() as pw:
                wstg = [sb(pw, f"f_ws{i}", [128, FFH], F32) for i in range(2)]
                i = 0
                for (src, dstw) in ((ffn_w1[l], w1b), (ffn_w3[l], w3b)):
                    for k in range(8):
                        f = wstg[i % 2]
                        S.load(f[:], src[k * 128:(k + 1) * 128, :])
                        cp(dstw[:, k, :], f[:], eng=('dve' if i % 2 == 0 else 'act'))
                        i += 1
                for k in range(22):
                    f = wstg[i % 2]
                    S.load(f[:, 0:1024], ffn_w2[l][k * 128:(k + 1) * 128, :])
                    cp(w2b[:, k, :], f[:, 0:1024], eng=('dve' if i % 2 == 0 else 'act'))
                    i += 1
                S.barrier()
            FB = 256
            xb = [sb(ph, f"f_x{i}", [128, 8, FB], F32) for i in range(2)]
            sq = sb(ph, "f_sq", [128, 8, FB], BF16)
            rs = sb(ph, "f_rs", [128, FB], F32)
            tmp = sb(ph, "f_tmp", [128, 8, FB], F32)
            h2 = sb(ph, "f_h2", [128, 8, FB], BF16)
            uT = sb(ph, "f_u", [128, 22, FB], BF16)
            sg_ = [sb(ph, f"f_sg{i}", [128, FB], F32) for i in range(2)]
            s1 = [sb(ph, f"f_s1{i}", [128, FB], F32) for i in range(2)]
            yo_ = [sb(ph, f"f_yo{i}", [128, 8, FB], F32) for i in range(1)]
            pst = ps(ph, "f_pst", [128, FB], F32)
            pf = [ps(ph, f"f_p{i}", [128, FB], F32) for i in range(6)]
            npf = 0
            nblk = 0
            for b0 in range(0, T, FB):
                if last and b0 < LC:
                    continue
                bs = FB
                s = strm(b0)
                x_ = xb[nblk % 2]
                nblk += 1
                S.load(x_[:], XT[:, b0:b0 + bs].rearrange("(k p) t -> p k t", p=128))
                norm_block((sq, pst, rs, tmp), x_, bs,
                           lambda k: A2[:, l, k, s:s + 1], lambda k: modT[:, l, 24 + k, s:s + 1],
                           lambda k: h2[:, k, :])
                for hc in range(22):
                    pa = pf[npf % 6]
                    pb_ = pf[(npf + 1) % 6]
                    npf += 2
                    for k in range(8):
                        mm(pa[:, :FB], w1b[:, k, hc * 128:(hc + 1) * 128], h2[:, k, :], start=(k == 0), stop=(k == 7))
                    for k in range(8):
                        mm(pb_[:, :FB], w3b[:, k, hc * 128:(hc + 1) * 128], h2[:, k, :], start=(k == 0), stop=(k == 7))
                    sg = sg_[hc % 2]
                    s1_ = s1[hc % 2]
                    act(sg[:], pa[:, :FB], AF.Sigmoid)
                    tt(s1_[:], pa[:, :FB], sg[:], ALU.mult)
                    tt(uT[:, hc, :], pb_[:, :FB], s1_[:], ALU.mult)
                for oc in range(8):
                    p = pf[npf % 6]
                    npf += 1
                    for k in range(22):
                        mm(p[:, :FB], w2b[:, k, oc * 128:(oc + 1) * 128], uT[:, k, :], start=(k == 0), stop=(k == 21))
                    stt(x_[:, oc, :], p[:, :FB], modT[:, l, 40 + oc, s:s + 1], x_[:, oc, :], ALU.mult, ALU.add)
                if not last:
                    S.store(XT[:, b0:b0 + bs].rearrange("(k p) t -> p k t", p=128), x_[:])
                else:
                    yo = yo_[0]
                    norm_block((sq, pst, rs, tmp), x_, bs,
                               lambda k: gfs[:, k:k + 1], lambda k: None,
                               lambda k: yo[:, k, :])
                    S.store(yT[:, b0 - LC:b0 - LC + bs].rearrange("(k p) t -> p k t", p=128), yo[:])
            S.barrier()


    stopped = False
    try:
        for l in range(DEPTH):
            layer_body(l)
    except _Stop:
        stopped = True
    S.barrier()
    if not stopped:
        per.close()
        es.close()
    return nc, S


_CACHE = {}


def _consts():
    j = np.arange(128)[:, None]
    t = np.arange(128)[None, :]
    c = np.zeros((128, 6 * 128 + 64 + 4 + 256), np.float32)
    c[:, 0:128] = (j <= t)
    c[:, 128:256] = (j >= t)
    c[:, 256:384] = (j > t)
    c[:, 384:512] = (j < t)
    c[:, 512:640] = np.eye(128)
    c[:, 640:768] = 1.0
    s = np.arange(32)[:, None]
    tt_ = np.arange(32)[None, :]
    c[0:32, 768:800] = (s <= tt_)
    c[0:32, 800:832] = (s >= tt_)
    for q in range(4):
        c[32 * q:32 * q + 32, 832 + q] = 1.0
    same = (j // 32) == (t // 32)
    c[:, 836:964] = same & (j <= t)
    c[:, 964:1092] = same & (j >= t)
    return c


def _m0():
    m0 = np.ones(T, np.float32)
    m0[::32] = 0.0
    return np.ascontiguousarray(np.broadcast_to(m0[None, :], (128, T))).astype(ml_dtypes.bfloat16)


def _rope_tables():
    cos = np.ones((128, T), np.float64)
    sin = np.zeros((128, T), np.float64)
    tl = np.arange(L)
    row = (tl // 64).astype(np.float64)
    col = (tl % 64).astype(np.float64)
    for f in range(128):
        r = f % 32
        half_sel = (f % 64) // 32
        fi = r % 16
        freq = np.float32(10000.0) ** (-np.float32(fi) / np.float32(16))
        pos = row if half_sel == 0 else col
        ang = (pos.astype(np.float32) * np.float32(freq)).astype(np.float64)
        cos[f, LC:] = np.cos(ang)
        sgn = -1.0 if r < 16 else 1.0
        sin[f, LC:] = sgn * np.sin(ang)
    return cos.astype(np.float32), sin.astype(np.float32)


def _rot_perm():
    p = np.arange(1024)
    r = p % 32
    return np.where(r < 16, p + 16, p - 16)


def _prep_shared(inp):
    f = np.float32
    A = lambda a: np.ascontiguousarray(a, dtype=f)
    perm = _rot_perm()
    w_in = inp['w_in']
    w_rot = np.concatenate([w_in[:, :, O_AK:O_AK + 1024][:, :, perm], w_in[:, :, O_AQ:O_AQ + 1024][:, :, perm]], axis=2)
    cosT, sinT = _rope_tables()
    col8 = lambda g: A(g.reshape(8, 128).T)
    sh = {
        'w_ada': A(inp['w_ada']),
        'b_adaT': A(inp['b_ada'].reshape(DEPTH, 48, 128).transpose(0, 2, 1)),
        'g1T': A(inp['norm1_g'].reshape(DEPTH, 8, 128).transpose(0, 2, 1)),
        'g2T': A(inp['norm2_g'].reshape(DEPTH, 8, 128).transpose(0, 2, 1)),
        'gfT': col8(inp['final_g']),
        'w_in': A(w_in),
        'w_rot': A(w_rot),
        'cosT': cosT, 'sinT': sinT,
        'att_lam': A(inp['att_lambda'].reshape(DEPTH, 256)),
        'att_gT': A(inp['att_norm_g'].T),
        'hg_lbT': A(inp['hg_lb_logits'].reshape(2, DEPTH, 8, 128).transpose(3, 0, 1, 2)),
        'hg_gT': A(inp['hg_norm_g'].T),
        'conv_wT': A(inp['ssm_conv_w'].reshape(DEPTH, 5, 24, 128).transpose(0, 3, 2, 1)),
        'conv_bT': A(inp['ssm_conv_b'].reshape(DEPTH, 24, 128).transpose(0, 2, 1)),
        'conv_b': A(inp['ssm_conv_b']),
        'dt_bias': A(inp['ssm_dt_bias'].reshape(DEPTH, 64)),
        'a_log': A(inp['ssm_a_log'].reshape(DEPTH, 64)),
        'ssm_d': A(inp['ssm_d']),
        'ssm_gT': A(inp['ssm_norm_g'].reshape(DEPTH, 16, 128).transpose(0, 2, 1)),
        'w_b_att': A(inp['w_branch_att']), 'w_b_hg': A(inp['w_branch_hg']), 'w_b_ssm': A(inp['w_branch_ssm']),
        'w_out': A(inp['w_out']),
        'ffn_w1': A(inp['ffn_w1']), 'ffn_w3': A(inp['ffn_w3']), 'ffn_w2': A(inp['ffn_w2']),
        'cst': _consts(),
        'm0': _m0(),
    }
    return sh


def kernel(**inputs):
    inp = {k: np.asarray(v) for k, v in inputs.items()}
    if 'nc' not in _CACHE:
        _CACHE['nc'] = build_program()[0]
    nc = _CACHE['nc']
    sh = _prep_shared(inp)
    in_maps = []
    for b in range(8):
        m = dict(sh)
        m['xT'] = np.ascontiguousarray(np.concatenate([inp['ctx'][b], inp['x'][b]], axis=0).T, dtype=np.float32)
        m['c2'] = np.ascontiguousarray(np.stack([inp['c_ctx'], inp['c'][b]], axis=1), dtype=np.float32)
        in_maps.append(m)
    res = run_bass_kernel_spmd(nc, in_maps, core_ids=list(range(8)))
    out = np.stack([np.ascontiguousarray(res.results[b]['yT'].T) for b in range(8)], axis=0)
    return out.astype(np.float32)
```

```python
import math
from contextlib import ExitStack
import numpy as np
import ml_dtypes
import concourse.bass as bass
import concourse.mybir as mybir
from concourse.bass_utils import run_bass_kernel_spmd

F32 = mybir.dt.float32
BF16 = mybir.dt.bfloat16
AF = mybir.ActivationFunctionType
ALU = mybir.AluOpType

D = 1024
LC = 256
L = 4096
T = LC + L
NT = T // 128
NCH = T // 32
DEPTH = 2
EPS = 1e-6
NIN = 16448
FFH = 2816
BLOCKS = [(0, 256)] + [(256 + 512 * i, 512) for i in range(8)]
O_AK, O_AV, O_HFF, O_HFB, O_HI, O_XBC, O_DTF, O_DTB, O_AQ, O_HQ, O_HGATE, O_Z, O_GATES = (
    0, 1024, 2048, 3072, 4096, 5120, 8192, 8224, 8256, 9280, 10304, 11328, 13376)


class _Stop(Exception):
    pass


class Sched:
    def __init__(self, nc, es):
        self.nc = nc
        self.eng = {'pe': nc.tensor, 'act': nc.scalar, 'dve': nc.vector, 'pool': nc.gpsimd, 'sp': nc.sync}
        self.semh = {}
        self.cnt = {}
        for k in ['pe', 'act', 'dve', 'pool']:
            self.semh[k] = es.enter_context(nc.semaphore('s_' + k))
            self.cnt[k] = 0
        self.dpool = {'sp': [], 'pool': []}
        for q, n in (('sp', 24), ('pool', 16)):
            for i in range(n):
                nm = f'd_{q}{i}'
                self.semh[nm] = es.enter_context(nc.semaphore(nm))
                self.cnt[nm] = 0
                self.dpool[q].append(nm)
        self.drr = {'sp': 0, 'pool': 0}
        self.seen = {k: {} for k in self.eng}
        self.res = {}
        self.nops = 0
        self.psum = set()

    def _keys(self, ap, sub):
        nm = ap.tensor.name
        if nm in self.psum:
            return [(nm, None)]
        if sub is not None and nm in sub:
            v = sub[nm]
            if isinstance(v, (list, tuple)):
                return [(nm, x) for x in v]
            return [(nm, v)]
        return [(nm, None)]

    def _deps(self, rkeys, wkeys, eng=None):
        deps = []
        for k in rkeys:
            st = self.res.get(k)
            if st is not None and st[0] is not None:
                deps.append(st[0])
        for k in wkeys:
            st = self.res.get(k)
            if st is not None:
                if st[0] is not None and st[0][0] != eng:
                    deps.append(st[0])
                deps.extend(e for e in st[1] if e[0] != eng)
        return deps

    def _wait(self, eng, deps):
        need = {}
        for (k, v) in deps:
            if eng == 'pe' and k == 'pe':
                continue
            if self.seen[eng].get(k, 0) >= v:
                continue
            if need.get(k, 0) < v:
                need[k] = v
        for k, v in need.items():
            self.eng[eng].wait_ge(self.semh[k], v)
            self.seen[eng][k] = v

    def _record(self, ev, rkeys, wkeys):
        for k in rkeys:
            st = self.res.setdefault(k, [None, []])
            st[1] = [e for e in st[1] if e[0] != ev[0]] + [ev]
        for k in wkeys:
            self.res[k] = [ev, []]

    def op(self, eng, fn, ins, outs, sub=None, stream_self=False):
        rkeys = [k for a in ins if a is not None and hasattr(a, 'tensor') for k in self._keys(a, sub)]
        wkeys = [k for a in outs for k in self._keys(a, sub)]
        wkeys += [k for k in rkeys if k[0] in self.psum and k not in wkeys]
        deps = self._deps(rkeys, wkeys, eng)
        if stream_self:
            deps = [d_ for d_ in deps if d_[0] != eng]
        self._wait(eng, deps)
        inst = fn(self.eng[eng])
        self.cnt[eng] += 1
        inst.then_inc(self.semh[eng], 1)
        self._record((eng, self.cnt[eng]), rkeys, wkeys)
        self.nops += 1

    def dma(self, q, out, in_, sub=None, track_out=True, track_in=True):
        rkeys = self._keys(in_, sub) if track_in else []
        wkeys = self._keys(out, sub) if track_out else []
        pool = self.dpool[q]
        nm = pool[self.drr[q] % len(pool)]
        self.drr[q] += 1
        deps = self._deps(rkeys, wkeys)
        if self.cnt[nm] > 0:
            deps.append((nm, self.cnt[nm]))
        self._wait(q, deps)
        self.eng[q].dma_start(out=out, in_=in_).then_inc(self.semh[nm], 16)
        self.cnt[nm] += 16
        self._record((nm, self.cnt[nm]), rkeys, wkeys)
        self.nops += 1

    def load(self, out, in_, sub=None):
        self.dma('sp', out, in_, sub=sub, track_in=False)

    def store(self, out, in_, sub=None):
        self.dma('pool', out, in_, sub=sub, track_out=False)

    def barrier(self):
        allev = [(k, v) for k, v in self.cnt.items() if v > 0]
        for e in self.eng:
            self._wait(e, allev)
        self.res = {}


def _bc_mid(a, n):
    ap = [list(x) for x in a.ap]
    return bass.AP(a.tensor, a.offset, [ap[0], [0, n]] + ap[1:])


def _bc_last(a, n):
    ap = [list(x) for x in a.ap]
    return bass.AP(a.tensor, a.offset, ap + [[0, n]])


def _rev(a):
    ap = [list(x) for x in a.ap]
    assert len(ap) == 2 and ap[1][0] == 1
    return bass.AP(a.tensor, a.offset + ap[1][1] - 1, [ap[0], [-1, ap[1][1]]])


def _strided(a, start, step, n):
    ap = [list(x) for x in a.ap]
    return bass.AP(a.tensor, a.offset + start, [ap[0], [step, n]])


def _pbcast(dram_ap_1d_offset_tensor, offset, n):
    return bass.AP(dram_ap_1d_offset_tensor, offset, [[0, 128], [1, n]])


def build_program(stop_after=None, dump=None):
    nc = bass.Bass("TRN2", target_bir_lowering=False)
    es = ExitStack()
    S = Sched(nc, es)

    def din(name, shape, dt=F32):
        return nc.dram_tensor(name, list(shape), dt, kind="ExternalInput").ap()

    def dscr(name, shape, dt):
        if dump is not None and name in dump:
            return nc.dram_tensor(name, list(shape), dt, kind="ExternalOutput").ap()
        return nc.dram_tensor(name, list(shape), dt).ap()

    xT_in = din("xT", [D, T])
    c2_in = din("c2", [D, 2])
    w_ada = din("w_ada", [DEPTH, D, 6 * D])
    b_adaT = din("b_adaT", [DEPTH, 128, 48])
    g1T = din("g1T", [DEPTH, 128, 8])
    g2T = din("g2T", [DEPTH, 128, 8])
    gfT = din("gfT", [128, 8])
    w_in = din("w_in", [DEPTH, D, NIN])
    w_rot = din("w_rot", [DEPTH, D, 2048])
    cosT = din("cosT", [128, T])
    sinT = din("sinT", [128, T])
    att_lam = din("att_lam", [DEPTH, 256])
    att_gT = din("att_gT", [128, DEPTH])
    hg_lbT = din("hg_lbT", [128, 2, DEPTH, 8])
    hg_gT = din("hg_gT", [128, DEPTH])
    conv_wT = din("conv_wT", [DEPTH, 128, 24, 5])
    conv_bT = din("conv_bT", [DEPTH, 128, 24])
    conv_b = din("conv_b", [DEPTH, 3072])
    dt_bias = din("dt_bias", [DEPTH, 64])
    a_log = din("a_log", [DEPTH, 64])
    ssm_d = din("ssm_d", [DEPTH, 32])
    ssm_gT = din("ssm_gT", [DEPTH, 128, 16])
    w_b_att = din("w_b_att", [DEPTH, 1024, D])
    w_b_hg = din("w_b_hg", [DEPTH, 1024, D])
    w_b_ssm = din("w_b_ssm", [DEPTH, 2048, D])
    w_out = din("w_out", [DEPTH, D, D])
    ffn_w1 = din("ffn_w1", [DEPTH, D, FFH])
    ffn_w3 = din("ffn_w3", [DEPTH, D, FFH])
    ffn_w2 = din("ffn_w2", [DEPTH, FFH, D])
    cst_in = din("cst", [128, 6 * 128 + 64 + 4 + 256])
    m0_in = din("m0", [128, T], BF16)
    yT = nc.dram_tensor("yT", [D, L], F32, kind="ExternalOutput").ap()

    XT = dscr("XT", [D, T], F32)
    QT = dscr("QT", [1024, T], BF16)
    KT = dscr("KT", [1024, T], BF16)
    AV = dscr("AV", [T, 1024], BF16)
    HFF = dscr("HFF", [1024, T], F32)
    HFB = dscr("HFB", [1024, T], F32)
    HQ = dscr("HQ", [1024, T], F32)
    HI = dscr("HI", [T, 1024], BF16)
    HGATE = dscr("HGATE", [1024, T], F32)
    XBC = dscr("XBC", [3072, T], BF16)
    DTR = dscr("DTR", [T, 64], F32)
    ZZ = dscr("ZZ", [T, 2048], F32)
    GATES = dscr("GATES", [3072, T], F32)
    XS = dscr("XS", [T, 2048], F32)
    BTOK = dscr("BTOK", [T, 512], BF16)
    BTT = dscr("BTT", [512, T], BF16)
    CTT = dscr("CTT", [512, T], BF16)
    YF = dscr("YF", [T, 2048], F32)
    BR = dscr("BR", [4096, T], BF16)

    uid = [0]

    def sb(ctx, name, shape, dt):
        uid[0] += 1
        return ctx.enter_context(nc.sbuf_tensor(f"{name}_{uid[0]}", list(shape), dt))

    def ps(ctx, name, shape, dt=F32):
        uid[0] += 1
        t = ctx.enter_context(nc.psum_tensor(f"{name}_{uid[0]}", [128, 512] if dt == F32 else [128, 1024], dt))
        S.psum.add(t.name)
        return t

    def mm(out, lhsT, rhs, start=True, stop=True, sub=None):
        S.op('pe', lambda e: e.matmul(out, lhsT=lhsT, rhs=rhs, start=start, stop=stop), [lhsT, rhs], [out], sub)

    def tr(out, in_, ident, sub=None):
        S.op('pe', lambda e: e.transpose(out, in_, ident), [in_, ident], [out], sub)

    def act(out, in_, func, bias=None, scale=None, accum=None, sub=None, eng='act'):
        kw = {}
        if bias is not None:
            kw['bias'] = bias
        if scale is not None:
            kw['scale'] = scale
        if accum is not None:
            kw['accum_out'] = accum
        ins = [in_] + [x for x in (bias, scale) if hasattr(x, 'tensor')]
        outs = [out] + ([accum] if accum is not None else [])
        S.op('act', lambda e: e.activation(out=out, in_=in_, func=func, **kw), ins, outs, sub)

    def tt(out, in0, in1, op, sub=None, eng='dve'):
        S.op(eng, lambda e: e.tensor_tensor(out=out, in0=in0, in1=in1, op=op), [in0, in1], [out], sub)

    def tsc(out, in0, s1, s2, op0, op1=None, sub=None, eng='dve'):
        ins = [in0] + [x for x in (s1, s2) if hasattr(x, 'tensor')]
        if op1 is None:
            S.op(eng, lambda e: e.tensor_scalar(out=out, in0=in0, scalar1=s1, scalar2=None, op0=op0), ins, [out], sub)
        else:
            S.op(eng, lambda e: e.tensor_scalar(out=out, in0=in0, scalar1=s1, scalar2=s2, op0=op0, op1=op1), ins, [out], sub)

    def stt(out, in0, scalar, in1, op0, op1, sub=None, stream_self=False):
        ins = [in0, in1] + ([scalar] if hasattr(scalar, 'tensor') else [])
        S.op('dve', lambda e: e.scalar_tensor_tensor(out=out, in0=in0, scalar=scalar, in1=in1, op0=op0, op1=op1), ins, [out], sub,
             stream_self=stream_self)

    def cp(out, in_, sub=None, eng='dve'):
        if eng == 'act':
            act(out, in_, AF.Copy, sub=sub)
        else:
            S.op(eng, lambda e: e.tensor_copy(out=out, in_=in_), [in_], [out], sub)

    def recip(out, in_, sub=None):
        S.op('dve', lambda e: e.reciprocal(out=out, in_=in_), [in_], [out], sub)

    def memset(ap, val, eng='dve', sub=None):
        S.op(eng, lambda e: e.memset(ap, val), [], [ap], sub)

    def scan(out, d0, d1, sub=None):
        S.op('dve', lambda e: e.tensor_tensor_scan(out=out, data0=d0, data1=d1, initial=0.0, op0=ALU.mult, op1=ALU.add),
             [d0, d1], [out], sub)

    per = ExitStack()
    cst = sb(per, "cst", [128, 6 * 128 + 64 + 4 + 256], F32)
    S.load(cst[:], cst_in[:, :])
    U_f = cst[:, 0:128]
    Lo_f = cst[:, 128:256]
    Ms_f = cst[:, 256:384]
    Ml_f = cst[:, 384:512]
    ID_f = cst[:, 512:640]
    ON_f = cst[:, 640:768]
    RM = cst[:, 832:836]
    BDf = cst[:, 836:964]
    BDb = cst[:, 964:1092]
    M32 = cst[:, 768:832]
    id_bf = sb(per, "id_bf", [128, 128], BF16)
    on_bf = sb(per, "on_bf", [128, 128], BF16)
    cp(id_bf[:], ID_f)
    cp(on_bf[:], ON_f)
    modT = sb(per, "modT", [128, DEPTH, 48, 2], F32)
    A1 = sb(per, "A1", [128, DEPTH, 8, 2], F32)
    A2 = sb(per, "A2", [128, DEPTH, 8, 2], F32)
    g1s = sb(per, "g1s", [128, DEPTH, 8], F32)
    g2s = sb(per, "g2s", [128, DEPTH, 8], F32)
    gfs = sb(per, "gfs", [128, 8], F32)
    S.load(g1s[:], g1T.rearrange("l p k -> p l k"))
    S.load(g2s[:], g2T.rearrange("l p k -> p l k"))
    S.load(gfs[:], gfT[:, :])
    att_g = sb(per, "att_g", [128, DEPTH], F32)
    hg_g = sb(per, "hg_g", [128, DEPTH], F32)
    S.load(att_g[:], att_gT[:, :])
    S.load(hg_g[:], hg_gT[:, :])
    lbx = sb(per, "lbx", [128, 2, DEPTH, 8], F32)
    S.load(lbx[:], hg_lbT[:, :, :, :])
    lb1 = sb(per, "lb1", [128, 2, 8], F32)
    oml1 = sb(per, "oml1", [128, 2, 8], F32)
    neglam = sb(per, "neglam", [128, DEPTH], F32)
    ssm_g = sb(per, "ssm_g", [128, DEPTH, 16], F32)
    S.load(ssm_g[:], ssm_gT.rearrange("l p k -> p l k"))
    cbT = sb(per, "cbT", [128, DEPTH, 24], F32)
    S.load(cbT[:], conv_bT.rearrange("l p k -> p l k"))

    with ExitStack() as ph:
        sc = sb(ph, "p0_sc", [128, 8, 2], F32)
        S.load(sc[:], c2_in.rearrange("(k p) s -> p k s", p=128))
        sg = sb(ph, "p0_sg", [128, 8, 2], F32)
        act(sg[:], sc[:], AF.Sigmoid)
        tt(sc[:], sc[:], sg[:], ALU.mult)
        badd = sb(ph, "p0_b", [128, DEPTH, 48], F32)
        S.load(badd[:], b_adaT.rearrange("l p k -> p l k"))
        wst = [sb(ph, f"p0_w{i}", [128, 8, 512], F32) for i in range(2)]
        pm = ps(ph, "p0_pm", [128, 96], F32)
        it = 0
        for l in range(DEPTH):
            for cg in range(12):
                w = wst[it % 2]
                it += 1
                S.load(w[:], w_ada[l, :, cg * 512:(cg + 1) * 512].rearrange("(k p) c -> p k c", p=128))
                for j in range(4):
                    ch = cg * 4 + j
                    for k in range(8):
                        mm(pm[:, ch * 2:ch * 2 + 2], w[:, k, j * 128:(j + 1) * 128], sc[:, k, :], start=(k == 0), stop=(k == 7))
            tt(modT[:, l, :, :], pm[:, 0:96].rearrange("p (c s) -> p c s", s=2), _bc_last(badd[:, l, :], 2), ALU.add)
            tsc(A1[:, l, :, :], modT[:, l, 8:16, :], 1.0, None, ALU.add)
            tt(A1[:, l, :, :], A1[:, l, :, :], _bc_last(g1s[:, l, :], 2), ALU.mult)
            tsc(A2[:, l, :, :], modT[:, l, 32:40, :], 1.0, None, ALU.add)
            tt(A2[:, l, :, :], A2[:, l, :, :], _bc_last(g2s[:, l, :], 2), ALU.mult)
        lamt = sb(ph, "p0_lam", [128, DEPTH, 4, 64], F32)
        S.load(lamt[:].rearrange("p a b c -> p (a b c)"), bass.AP(att_lam.tensor, 0, [[0, 128], [1, DEPTH * 256]]))
        pr = sb(ph, "p0_pr", [128, 64], F32)
        sm = sb(ph, "p0_sm", [128, 4], F32)
        for l in range(DEPTH):
            for j in range(2):
                tt(pr[:], lamt[:, l, 2 * j, :], lamt[:, l, 2 * j + 1, :], ALU.mult)
                act(pr[:], pr[:], AF.Copy, accum=sm[:, 2 * l + j:2 * l + j + 1])
        act(sm[:], sm[:], AF.Exp)
        for l in range(DEPTH):
            lam_init = 0.8 - 0.6 * math.exp(-0.3 * l)
            tt(neglam[:, l:l + 1], sm[:, 2 * l + 1:2 * l + 2], sm[:, 2 * l:2 * l + 1], ALU.subtract)
            tsc(neglam[:, l:l + 1], neglam[:, l:l + 1], -lam_init, None, ALU.add)
        tt(lb1[:], lbx[:, :, 1, :], lbx[:, :, 0, :], ALU.subtract)
        act(lb1[:], lb1[:], AF.Sigmoid)
        tsc(oml1[:], lb1[:], -1.0, 1.0, ALU.mult, ALU.add)
        S.barrier()

    def norm_block(ph_tiles, xb, bs, A_of_k, B_of_k, out_of_k):
        sq, pst, rs, tmp = ph_tiles
        for k in range(8):
            act(sq[:, k, :bs], xb[:, k, :bs], AF.Square)
        for k in range(8):
            mm(pst[:, :bs], on_bf[:], sq[:, k, :bs], start=(k == 0), stop=(k == 7))
        tsc(rs[:, :bs], pst[:, :bs], 1.0 / D, EPS, ALU.mult, ALU.add)
        act(rs[:, :bs], rs[:, :bs], AF.Sqrt)
        recip(rs[:, :bs], rs[:, :bs])
        for k in range(8):
            tt(tmp[:, k, :bs], xb[:, k, :bs], rs[:, :bs], ALU.mult)
            b = B_of_k(k)
            if b is None:
                act(out_of_k(k), tmp[:, k, :bs], AF.Identity, scale=A_of_k(k))
            else:
                act(out_of_k(k), tmp[:, k, :bs], AF.Identity, scale=A_of_k(k), bias=b)

    def strm(b0):
        return 0 if b0 < LC else 1

    import os
    SSD_DBG = int(os.environ.get('SSD_DBG', '0'))

    def dbgstop(n):
        if SSD_DBG == n:
            raise _Stop()

    def layer_body(l):
        last = (l == DEPTH - 1)
        xsrc = xT_in if l == 0 else XT
        with ExitStack() as ph:
            hT = sb(ph, "hT", [128, 8, T], BF16)
            with ExitStack() as p1:
                xb2 = [sb(p1, f"p1_x{i}", [128, 8, 512], F32) for i in range(2)]
                sq = sb(p1, "p1_sq", [128, 8, 512], BF16)
                rs = sb(p1, "p1_rs", [128, 512], F32)
                tmp = sb(p1, "p1_tmp", [128, 8, 512], F32)
                pst = ps(p1, "p1_ps", [128, 512], F32)
                for bi, (b0, bs) in enumerate(BLOCKS):
                    xb = xb2[bi % 2]
                    S.load(xb[:, :, :bs], xsrc[:, b0:b0 + bs].rearrange("(k p) t -> p k t", p=128))
                    s = strm(b0)
                    norm_block((sq, pst, rs, tmp), xb, bs,
                               lambda k: A1[:, l, k, s:s + 1], lambda k: modT[:, l, k, s:s + 1],
                               lambda k: hT[:, k, b0:b0 + bs])
                S.barrier()
            with ExitStack() as p2:
                cs = sb(p2, "p2_cos", [128, T], F32)
                sn = sb(p2, "p2_sin", [128, T], F32)
                S.load(cs[:], cosT[:, :])
                S.load(sn[:], sinT[:, :])
                wf = [sb(p2, f"p2_wf{i}", [128, 8, 512], F32) for i in range(2)]
                wb = [sb(p2, f"p2_wb{i}", [128, 8, 512], BF16) for i in range(3)]
                stg = [sb(p2, f"p2_st{i}", [128, 512], F32) for i in range(4)]
                stb = [sb(p2, f"p2_sb{i}", [128, 512], BF16) for i in range(4)]
                r1 = sb(p2, "p2_r1", [128, 512], F32)
                r2 = sb(p2, "p2_r2", [128, 512], F32)
                pp = [ps(p2, f"p2_p{i}", [128, 512], F32) for i in range(6)]
                cnt = {'w': 0, 'p': 0, 's': 0, 'c': 0}

                def load_w(src, c0, n):
                    i = cnt['w']
                    cnt['w'] += 1
                    f = wf[i % 2]
                    b = wb[i % 3]
                    S.load(f[:, :, :n], src[:, c0:c0 + n].rearrange("(k p) c -> p k c", p=128))
                    for k in range(8):
                        if (k + i) % 2 == 0:
                            cp(b[:, k, :n], f[:, k, :n], eng='dve')
                        else:
                            cp(b[:, k, :n], f[:, k, :n], eng='act')
                    return b

                def newp():
                    p = pp[cnt['p'] % 6]
                    cnt['p'] += 1
                    return p

                def proj_F(wt, j, b0, bs):
                    p = newp()
                    for k in range(8):
                        mm(p[:, :bs], wt[:, k, j * 128:(j + 1) * 128], hT[:, k, b0:b0 + bs], start=(k == 0), stop=(k == 7))
                    return p

                def fgroup(src, c0, ncols, evac):
                    for g0 in range(0, ncols, 512):
                        n = min(512, ncols - g0)
                        wt = load_w(src, c0 + g0, n)
                        for j in range(n // 128):
                            for (b0, bs) in BLOCKS:
                                p = proj_F(wt, j, b0, bs)
                                evac(p, g0 + j * 128, b0, bs)

                def ev_simple(dst, func, dt):
                    def f(p, gc, b0, bs):
                        i = cnt['s']
                        cnt['s'] += 1
                        st = (stb if dt == BF16 else stg)[i % 4]
                        if func is None:
                            if i % 2 == 0:
                                cp(st[:, :bs], p[:, :bs], eng='dve')
                            else:
                                cp(st[:, :bs], p[:, :bs], eng='act')
                        else:
                            act(st[:, :bs], p[:, :bs], func)
                        S.store(dst[gc:gc + 128, b0:b0 + bs], st[:, :bs])
                    return f

                def ev_silu(dst):
                    def f(p, gc, b0, bs):
                        i = cnt['s']
                        cnt['s'] += 1
                        st = stg[i % 4]
                        act(st[:, :bs], p[:, :bs], AF.Sigmoid)
                        tt(st[:, :bs], p[:, :bs], st[:, :bs], ALU.mult)
                        S.store(dst[gc:gc + 128, b0:b0 + bs], st[:, :bs])
                    return f

                def rope_group(c_orig, c_rot, dst):
                    for g0 in range(0, 1024, 512):
                        wo = load_w(w_in[l], c_orig + g0, 512)
                        wr = load_w(w_rot[l], c_rot + g0, 512)
                        for j in range(4):
                            for (b0, bs) in BLOCKS:
                                p1_ = proj_F(wo, j, b0, bs)
                                p2_ = proj_F(wr, j, b0, bs)
                                i = cnt['s']
                                cnt['s'] += 1
                                st = stb[i % 4]
                                tt(r1[:, :bs], p1_[:, :bs], cs[:, b0:b0 + bs], ALU.mult)
                                tt(r2[:, :bs], p2_[:, :bs], sn[:, b0:b0 + bs], ALU.mult)
                                tt(st[:, :bs], r1[:, :bs], r2[:, :bs], ALU.add)
                                gc = g0 + j * 128
                                S.store(dst[gc:gc + 128, b0:b0 + bs], st[:, :bs])

                def tgroup(c0, ncols, dst, dcol0, func, dt):
                    for g0 in range(0, ncols, 512):
                        n = min(512, ncols - g0)
                        wt = load_w(w_in[l], c0 + g0, n)
                        for tix in range(NT):
                            p = newp()
                            for k in range(8):
                                mm(p[:, :n], hT[:, k, tix * 128:(tix + 1) * 128], wt[:, k, :n], start=(k == 0), stop=(k == 7))
                            i = cnt['s']
                            cnt['s'] += 1
                            st = (stb if dt == BF16 else stg)[i % 4]
                            if func == 'silu':
                                act(st[:, :n], p[:, :n], AF.Sigmoid)
                                tt(st[:, :n], p[:, :n], st[:, :n], ALU.mult)
                            elif i % 2 == 0:
                                cp(st[:, :n], p[:, :n], eng='dve')
                            else:
                                cp(st[:, :n], p[:, :n], eng='act')
                            S.store(dst[tix * 128:(tix + 1) * 128, dcol0 + g0:dcol0 + g0 + n], st[:, :n])

                rope_group(O_AK, 0, KT)
                rope_group(O_AQ, 1024, QT)
                tgroup(O_AV, 1024, AV, 0, None, BF16)
                fgroup(w_in[l], O_HFF, 1024, ev_simple(HFF, None, F32))
                fgroup(w_in[l], O_HFB, 1024, ev_simple(HFB, None, F32))
                tgroup(O_HI, 1024, HI, 0, None, BF16)
                fgroup(w_in[l], O_XBC, 3072, ev_simple(XBC, None, BF16))
                tgroup(O_DTF, 64, DTR, 0, None, F32)
                fgroup(w_in[l], O_HQ, 1024, ev_silu(HQ))
                fgroup(w_in[l], O_HGATE, 1024, ev_silu(HGATE))
                tgroup(O_Z, 2048, ZZ, 0, 'silu', F32)
                fgroup(w_in[l], O_GATES, 3072, ev_simple(GATES, AF.Sigmoid, F32))
                S.barrier()
        if stop_after == ('p2', l):
            raise _Stop()

        with ExitStack() as ph:
          if not os.environ.get('SKIP_AG'):
            kT2 = [sb(ph, f"a_kT{i}", [128, T], BF16) for i in range(2)]
            qT2 = [sb(ph, f"a_qT{i}", [128, T], BF16) for i in range(2)]
            vt2 = [sb(ph, f"a_v{i}", [128, NT, 128], BF16) for i in range(2)]
            qm2 = [[sb(ph, f"a_qm{i}_{c}", [128, T], BF16) for c in range(2)] for i in range(2)]
            pb = [sb(ph, f"a_p{i}", [128, 512], BF16) for i in range(6)]
            psm = [sb(ph, f"a_psm{i}", [128, 512], BF16) for i in range(4)]
            r0 = sb(ph, "a_r0", [128, 512], F32)
            r1 = sb(ph, "a_r1", [128, 512], F32)
            t0 = sb(ph, "a_t0", [128, 512], F32)
            t1 = sb(ph, "a_t1", [128, 512], F32)
            sq = sb(ph, "a_sq", [128, 512], BF16)
            ob = [sb(ph, f"a_ob{i}", [128, 512], BF16) for i in range(2)]
            gsc = sb(ph, "a_g", [128, 1], F32)
            lam_init = 0.8 - 0.6 * math.exp(-0.3 * l)
            tsc(gsc[:], att_g[:, l:l + 1], 1.0 - lam_init, None, ALU.mult)
            pS = [ps(ph, f"a_S{i}", [128, 512], F32) for i in range(4)]
            pO = [ps(ph, f"a_O{i}", [128, 512], F32) for i in range(2)]
            pL = [ps(ph, f"a_L{i}", [128, 512], F32) for i in range(2)]
            for h in range(8):
                kTt, qTt, vt = kT2[h % 2], qT2[h % 2], vt2[h % 2]
                S.load(kTt[:], KT[h * 128:(h + 1) * 128, :])
                S.load(qTt[:], QT[h * 128:(h + 1) * 128, :])
                S.load(vt[:], AV[:, h * 128:(h + 1) * 128].rearrange("(n p) c -> p n c", p=128))
                qm = qm2[h % 2]
                tsc(qm[0][:], qTt[:], U_f[:, 63:64], None, ALU.mult)
                tsc(qm[1][:], qTt[:], Lo_f[:, 64:65], None, ALU.mult)
                items = []
                for bi, (b0, bs) in enumerate(BLOCKS):
                    if last and b0 < LC:
                        continue
                    nk = 2 if b0 < LC else NT
                    for kt in range(nk):
                        for c in range(2):
                            items.append((bi, b0, bs, kt, c, nk))

                def emit_S(j):
                    (bi_, b0_, bs_, kt_i, c_, nk_) = items[j]
                    mm(pS[j % 4][:, :bs_], kTt[:, kt_i * 128:(kt_i + 1) * 128], qm[c_][:, b0_:b0_ + bs_])

                LOOK = 3
                for j in range(min(LOOK, len(items))):
                    emit_S(j)
                for j, (bi, b0, bs, kt, c, nk) in enumerate(items):
                    if j + LOOK < len(items):
                        emit_S(j + LOOK)
                    P = pb[j % 6]
                    act(P[:, :bs], pS[j % 4][:, :bs], AF.Exp, scale=0.125)
                    mm(pO[c][:, :bs], vt[:, kt, :], P[:, :bs], start=(kt == 0), stop=(kt == nk - 1))
                    if kt % 2 == 1:
                        sm_ = psm[((kt // 2) * 2 + c) % 4]
                        tt(sm_[:, :bs], pb[(j - 2) % 6][:, :bs], P[:, :bs], ALU.add)
                        mm(pL[c][:, :bs], on_bf[:], sm_[:, :bs], start=(kt == 1), stop=(kt == nk - 1))
                    if not (kt == nk - 1 and c == 1):
                        continue
                    recip(r0[:, :bs], pL[0][:, :bs])
                    recip(r1[:, :bs], pL[1][:, :bs])
                    tt(t0[:, :bs], pO[0][:, :bs], r0[:, :bs], ALU.mult)
                    tt(t1[:, :bs], pO[1][:, :bs], r1[:, :bs], ALU.mult)
                    stt(t0[:, :bs], t1[:, :bs], neglam[:, l:l + 1], t0[:, :bs], ALU.mult, ALU.add)
                    act(sq[:, :bs], t0[:, :bs], AF.Square)
                    pX = pL[0]
                    mm(pX[:, :bs], on_bf[:], sq[:, :bs])
                    tsc(r0[:, :bs], pX[:, :bs], 1.0 / 128, EPS, ALU.mult, ALU.add)
                    act(r0[:, :bs], r0[:, :bs], AF.Sqrt)
                    recip(r0[:, :bs], r0[:, :bs])
                    tt(t0[:, :bs], t0[:, :bs], r0[:, :bs], ALU.mult)
                    o = ob[bi % 2]
                    act(o[:, :bs], t0[:, :bs], AF.Identity, scale=gsc[:, 0:1])
                    S.store(BR[h * 128:(h + 1) * 128, b0:b0 + bs], o[:, :bs])
            S.barrier()
        if stop_after == ('att', l):
            raise _Stop()

        with ExitStack() as ph:
          if not os.environ.get('SKIP_AG'):
            fA = sb(ph, "g_f", [128, T], F32)
            bB = sb(ph, "g_b", [128, T], F32)
            kK = sb(ph, "g_k", [128, T], F32)
            tM = sb(ph, "g_t", [128, T], F32)
            hq = sb(ph, "g_hq", [128, T], F32)
            gt = [sb(ph, f"g_gate{i}", [128, 512], F32) for i in range(2)]
            qt = sb(ph, "g_qt", [128, T], BF16)
            kt_ = sb(ph, "g_kt", [128, T], BF16)
            kh = sb(ph, "g_kh", [128, T], BF16)
            m0 = sb(ph, "g_m0", [128, T], BF16)
            osum = sb(ph, "g_os", [128, T], F32)
            v128 = sb(ph, "g_v128", [128, NT, 128], BF16)
            ebend = sb(ph, "g_eb", [128, NCH], F32)
            S32 = [sb(ph, f"g_S32_{i}", [128, 128], F32) for i in range(2)]
            Sbf = [sb(ph, f"g_Sbf{i}", [128, 128], BF16) for i in range(4)]
            khTm = [[sb(ph, f"g_khm{i}_{j}", [128, 128], BF16) for j in range(4)] for i in range(2)]
            kh128 = [sb(ph, f"g_kh128_{i}", [128, 128], BF16) for i in range(2)]
            At = [sb(ph, f"g_At{i}", [128, 128], BF16) for i in range(2)]
            sq = sb(ph, "g_sq", [128, 512], BF16)
            rs = sb(ph, "g_rs", [128, 512], F32)
            on_ = sb(ph, "g_on", [128, 512], F32)
            ob = [sb(ph, f"g_ob{i}", [128, 512], BF16) for i in range(2)]
            lbz = sb(ph, "g_lbz", [128, 2], F32)
            pT = [ps(ph, "g_pT0", [32, 128], BF16)]
            pSc = [ps(ph, "g_pS0", [32, 32], F32)]
            pD = [ps(ph, f"g_pD{i}", [128, 128], F32) for i in range(4)]
            pOa = ps(ph, "g_pOa", [128, 512], F32)
            pOb = ps(ph, "g_pOb", [128, 512], F32)
            S.load(m0[:], m0_in[:, :])
            memset(lbz[:, 0:1], 0.0)
            memset(lbz[:, 1:2], 1.0)
            for h in range(8):
                S.load(hq[:], HQ[h * 128:(h + 1) * 128, :])
                S.load(v128[:], HI[:, h * 128:(h + 1) * 128].rearrange("(n p) c -> p n c", p=128))
                for d in range(2):
                    S.load(fA[:], (HFF if d == 0 else HFB)[h * 128:(h + 1) * 128, :])
                    if l == 0:
                        lb_ap, oml_ap = lbz[:, 0:1], lbz[:, 1:2]
                    else:
                        lb_ap, oml_ap = lb1[:, d, h:h + 1], oml1[:, d, h:h + 1]
                    act(fA[:], fA[:], AF.Sigmoid)
                    tsc(fA[:], fA[:], oml_ap, lb_ap, ALU.mult, ALU.add)
                    tsc(kK[:], fA[:], -1.0, 1.0, ALU.mult, ALU.add)
                    act(fA[:], fA[:], AF.Ln)
                    if d == 0:
                        scan(bB[:], m0[:], fA[:])
                        bend = _strided(bB[:], 31, 32, NCH)
                    else:
                        scan(_rev(bB[:]), m0[:], _rev(fA[:]))
                        bend = _strided(bB[:], 0, 32, NCH)
                    act(tM[:], bB[:], AF.Exp)
                    tt(qt[:], hq[:], tM[:], ALU.mult)
                    act(tM[:], bB[:], AF.Exp, scale=-1.0)
                    tt(kt_[:], kK[:], tM[:], ALU.mult)
                    tt(tM[:].rearrange("p (c j) -> p c j", j=32), _bc_last(bend, 32),
                       bB[:].rearrange("p (c j) -> p c j", j=32), ALU.subtract)
                    act(tM[:], tM[:], AF.Exp)
                    tt(kh[:], kK[:], tM[:], ALU.mult)
                    act(ebend[:], bend, AF.Exp)
                    memset(S32[0][:], 0.0)
                    memset(Sbf[0][:], 0.0)
                    order = list(range(8)) + list(range(8, NCH)) if d == 0 else list(range(7, -1, -1)) + list(range(NCH - 1, 7, -1))
                    tiles = [0, 1] + list(range(2, NT)) if d == 0 else [1, 0] + list(range(NT - 1, 1, -1))
                    BD = BDf if d == 0 else BDb
                    LAG = 2
                    pend = []

                    def blk_of(c):
                        if c < 8:
                            return 0, 256
                        return BLOCKS[1 + (c - 8) // 16]

                    def emit_tile(ti):
                        tix = tiles[ti]
                        t0_ = tix * 128
                        ts = ti % 2
                        tr(pT[0][:, 0:128], kh[:, t0_:t0_ + 128], id_bf[:])
                        cp(kh128[ts][:], pT[0][:, 0:128], eng='act')
                        for j in range(4):
                            if j < 2:
                                tsc(khTm[ts][j][:], kh128[ts][:], RM[:, j:j + 1], 1.0, ALU.mult, ALU.mult, eng='pool')
                            elif j == 2:
                                tsc(khTm[ts][j][:], kh128[ts][:], RM[:, j:j + 1], None, ALU.mult)
                            else:
                                act(khTm[ts][j][:], kh128[ts][:], AF.Identity, scale=RM[:, j:j + 1])
                        mm(pSc[0][:, 0:128], kt_[:, t0_:t0_ + 128], qt[:, t0_:t0_ + 128])
                        tt(At[ts][:], pSc[0][:, 0:128], BD, ALU.mult)
                        bstart_, bsz_ = blk_of(tix * 4)
                        oc_ = t0_ - bstart_
                        mm(pOa[:, oc_:oc_ + 128], v128[:, tix, :], At[ts][:])
                        last_tile = (t0_ + 128 == bstart_ + bsz_) if d == 0 else (t0_ == bstart_)
                        if last_tile:
                            if d == 0:
                                cp(osum[:, bstart_:bstart_ + bsz_], pOa[:, :bsz_], eng='act')
                            else:
                                tt(osum[:, bstart_:bstart_ + bsz_], osum[:, bstart_:bstart_ + bsz_], pOa[:, :bsz_], ALU.add)

                    def emit_inter(item):
                        (i_, c0_, bstart_, bsz_, lastb_) = item
                        oc_ = c0_ - bstart_
                        mm(pOb[:, oc_:oc_ + 32], Sbf[i_ % 4][:], qt[:, c0_:c0_ + 32])
                        if lastb_:
                            tt(osum[:, bstart_:bstart_ + bsz_], osum[:, bstart_:bstart_ + bsz_], pOb[:, :bsz_], ALU.add)

                    emit_tile(0)
                    for idx, c in enumerate(order):
                        c0 = c * 32
                        sl = idx % 4
                        ti = idx // 4
                        tix = tiles[ti]
                        assert tix == c // 4
                        if idx % 4 == 1 and ti + 1 < len(tiles):
                            emit_tile(ti + 1)
                        bstart, bsz = blk_of(c)
                        last_in_blk = (c0 + 32 == bstart + bsz) if d == 0 else (c0 == bstart)
                        mm(pD[sl][:, 0:128], khTm[ti % 2][c % 4][:], v128[:, tix, :])
                        pend.append((idx, c0, bstart, bsz, last_in_blk))
                        if len(pend) > LAG:
                            emit_inter(pend.pop(0))
                        stt(S32[(idx + 1) % 2][:], S32[idx % 2][:], ebend[:, c:c + 1], pD[sl][:, 0:128], ALU.mult, ALU.add, stream_self=True)
                        cp(Sbf[(idx + 1) % 4][:], S32[(idx + 1) % 2][:], eng='act')
                    while pend:
                        emit_inter(pend.pop(0))
                gcol = hg_g[:, l:l + 1]
                for bi, (b0, bs) in enumerate(BLOCKS):
                    if last and b0 < LC:
                        continue
                    act(sq[:, :bs], osum[:, b0:b0 + bs], AF.Square)
                    px = pOa if bi % 2 == 0 else pOb
                    mm(px[:, :bs], on_bf[:], sq[:, :bs])
                    tsc(rs[:, :bs], px[:, :bs], 1.0 / 128, EPS, ALU.mult, ALU.add)
                    act(rs[:, :bs], rs[:, :bs], AF.Sqrt)
                    recip(rs[:, :bs], rs[:, :bs])
                    tt(on_[:, :bs], osum[:, b0:b0 + bs], rs[:, :bs], ALU.mult)
                    g__ = gt[bi % 2]
                    S.load(g__[:, :bs], HGATE[h * 128:(h + 1) * 128, b0:b0 + bs])
                    tt(on_[:, :bs], on_[:, :bs], g__[:, :bs], ALU.mult)
                    o = ob[bi % 2]
                    act(o[:, :bs], on_[:, :bs], AF.Identity, scale=gcol)
                    S.store(BR[1024 + h * 128:1024 + (h + 1) * 128, b0:b0 + bs], o[:, :bs])
            S.barrier()
        if stop_after == ('gla', l):
            raise _Stop()

        PADT = T + 8
        OFFC, OFFL = 2, 2 + 256 + 4

        def poff(b0):
            return (OFFC + b0) if b0 < LC else (OFFL + (b0 - LC))

        with ExitStack() as ph:
            up = [sb(ph, f"c_u{i}", [128, PADT], BF16) for i in range(3)]
            cw = sb(ph, "c_w", [128, 24, 5], F32)
            S.load(cw[:], conv_wT[l])
            dw = sb(ph, "c_dw", [128, 24, 5, 128], BF16)
            brow_f = sb(ph, "c_brf", [1, 3072], F32)
            brow = sb(ph, "c_br", [1, 3072], BF16)
            S.load(brow_f[:], bass.AP(conv_b.tensor, l * 3072, [[0, 1], [1, 3072]]))
            cp(brow[:], brow_f[:])
            onerow = on_bf[0:1, :]
            stT = [sb(ph, f"c_sT{i}", [128, 512], BF16) for i in range(3)]
            stF = [sb(ph, f"c_sF{i}", [128, 512], BF16) for i in range(3)]
            sgm = [sb(ph, f"c_sg{i}", [128, 512], F32) for i in range(4)]
            pc = [ps(ph, f"c_p{i}", [128, 512], F32) for i in range(4)]
            for i in range(3):
                memset(up[i][:], 0.0)
            for cc in range(24):
                for k in range(5):
                    tsc(dw[:, cc, k, :], id_bf[:], cw[:, cc, k:k + 1], None, ALU.mult)
            n_p = 0
            n_s = 0
            for cc in range(24):
                u = up[cc % 3]
                S.load(u[:, OFFC:OFFC + LC], XBC[cc * 128:(cc + 1) * 128, 0:LC])
                S.load(u[:, OFFL:OFFL + L], XBC[cc * 128:(cc + 1) * 128, LC:T])
                if cc < 20:
                    for tix in range(NT):
                        t0_ = tix * 128
                        p = pc[n_p % 4]
                        n_p += 1
                        po = poff(t0_)
                        for k in range(5):
                            mm(p[:, 0:128], u[:, po + k - 2:po + k - 2 + 128], dw[:, cc, k, :], start=(k == 0), stop=False)
                        mm(p[:, 0:128], onerow, brow[0:1, cc * 128:(cc + 1) * 128], start=False, stop=True)
                        sg_ = sgm[n_s % 4]
                        st = stT[n_s % 3]
                        n_s += 1
                        act(sg_[:, 0:128], p[:, 0:128], AF.Sigmoid)
                        if cc < 16:
                            tt(sg_[:, 0:128], p[:, 0:128], sg_[:, 0:128], ALU.mult)
                            S.store(XS[t0_:t0_ + 128, cc * 128:(cc + 1) * 128], sg_[:, 0:128])
                            continue
                        tt(st[:, 0:128], p[:, 0:128], sg_[:, 0:128], ALU.mult)
                        if cc < 16:
                            S.store(XS[t0_:t0_ + 128, cc * 128:(cc + 1) * 128], st[:, 0:128])
                        else:
                            S.store(BTOK[t0_:t0_ + 128, (cc - 16) * 128:(cc - 15) * 128], st[:, 0:128])
                if cc >= 16:
                    dst = BTT if cc < 20 else CTT
                    r0_ = (cc - 16) * 128 if cc < 20 else (cc - 20) * 128
                    for (b0, bs) in BLOCKS:
                        p = pc[n_p % 4]
                        n_p += 1
                        po = poff(b0)
                        for k in range(5):
                            mm(p[:, :bs], dw[:, cc, k, :], u[:, po + k - 2:po + k - 2 + bs], start=(k == 0), stop=(k == 4))
                        sg_ = sgm[n_s % 4]
                        st = stF[n_s % 3]
                        n_s += 1
                        act(sg_[:, :bs], p[:, :bs], AF.Sigmoid, bias=cbT[:, l, cc:cc + 1])
                        stt(st[:, :bs], p[:, :bs], cbT[:, l, cc:cc + 1], sg_[:, :bs], ALU.add, ALU.mult)
                        S.store(dst[r0_:r0_ + 128, b0:b0 + bs], st[:, :bs])
            S.barrier()
        if stop_after == ('conv', l):
            raise _Stop()

        with ExitStack() as ph:
            ST32 = sb(ph, "s_ST", [128, 2048], F32)
            STb = sb(ph, "s_STb", [128, 2048], BF16)
            dtb = sb(ph, "s_dtb", [128, 64], F32)
            S.load(dtb[:], bass.AP(dt_bias.tensor, l * 64, [[0, 128], [1, 64]]))
            Ab = sb(ph, "s_A", [128, 64], F32)
            S.load(Ab[:], bass.AP(a_log.tensor, l * 64, [[0, 128], [1, 64]]))
            act(Ab[:], Ab[:], AF.Exp)
            tsc(Ab[:], Ab[:], -1.0, None, ALU.mult)
            dsk = sb(ph, "s_dsk", [128, 32], F32)
            S.load(dsk[:], bass.AP(ssm_d.tensor, l * 32, [[0, 128], [1, 32]]))
            xs_ = [sb(ph, f"s_xs{i}", [128, 2048], F32) for i in range(2)]
            btk = [sb(ph, f"s_bt{i}", [128, 512], BF16) for i in range(2)]
            bT_ = [sb(ph, f"s_bT{i}", [128, 4, 128], BF16) for i in range(2)]
            cT_ = [sb(ph, f"s_cT{i}", [128, 4, 128], BF16) for i in range(2)]
            dtr = [sb(ph, f"s_dt{i}", [128, 64], F32) for i in range(2)]
            dtv = [sb(ph, f"s_dtv{i}", [128, 32], F32) for i in range(2)]
            aav = [sb(ph, f"s_a{i}", [128, 32], F32) for i in range(2)]
            eacv = [sb(ph, f"s_eac{i}", [128, 32], F32) for i in range(2)]
            wendv = [sb(ph, f"s_wend{i}", [128, 32], F32) for i in range(2)]
            dendv = [sb(ph, f"s_dend{i}", [128, 32], F32) for i in range(2)]
            xdtv = [sb(ph, f"s_xdt{i}", [128, 2048], BF16) for i in range(2)]
            xdwv = [sb(ph, f"s_xdw{i}", [128, 2048], BF16) for i in range(2)]
            Am = [sb(ph, f"s_Am{i}", [128, 8, 128], F32) for i in range(2)]
            LT = [sb(ph, f"s_LT{i}", [128, 8, 128], F32) for i in range(2)]
            MT = [sb(ph, f"s_MT{i}", [128, 8, 128], BF16) for i in range(2)]
            CBm = [sb(ph, f"s_CB{i}", [128, 128], F32) for i in range(2)]
            ytmp = sb(ph, "s_yt", [128, 512], F32)
            yo = [sb(ph, f"s_yo{i}", [128, 2048], F32) for i in range(2)]
            yfl = [sb(ph, f"s_yf{i}", [128, 2048], F32) for i in range(2)]
            zt = [sb(ph, f"s_z{i}", [128, 2048], F32) for i in range(2)]
            stmp = sb(ph, "s_st", [128, 512], F32)
            junk = sb(ph, "s_junk", [128, 2048], BF16)
            ssq = sb(ph, "s_ssq", [128, 1], F32)
            ynb = sb(ph, "s_ynb", [128, 2048], F32)
            obt = [sb(ph, f"s_ob{i}", [128, 4, 128], BF16) for i in range(2)]
            p_ac = ps(ph, "s_pac", [128, 64], F32)
            p_cb = ps(ph, "s_pcb", [128, 128], F32)
            p_df = [ps(ph, f"s_pdf{i}", [128, 512], F32) for i in range(2)]
            p_y = ps(ph, "s_py", [128, 512], F32)
            p_yi = ps(ph, "s_pyi", [128, 512], F32)
            p_st = ps(ph, "s_pst", [128, 512], F32)
            p_tr = ps(ph, "s_ptr", [128, 4, 128], F32)
            nit = 0
            for d in range(2):
                if d == 1:
                    S.barrier()
                memset(ST32[:], 0.0)
                memset(STb[:], 0.0)
                order = [0, 1] + list(range(2, NT)) if d == 0 else [1, 0] + list(range(NT - 1, 1, -1))
                Ucum = U_f if d == 0 else Lo_f
                Mlhs = Ms_f if d == 0 else Ml_f
                Mcb = U_f if d == 0 else Lo_f
                def aside(tix, sl):
                    t0_ = tix * 128
                    xs = xs_[sl]
                    S.load(xs[:], XS[t0_:t0_ + 128, :])
                    S.load(btk[sl][:], BTOK[t0_:t0_ + 128, :])
                    S.load(bT_[sl][:], BTT[:, t0_:t0_ + 128].rearrange("(g p) t -> p g t", p=128))
                    S.load(cT_[sl][:], CTT[:, t0_:t0_ + 128].rearrange("(g p) t -> p g t", p=128))
                    S.load(dtr[sl][:], DTR[t0_:t0_ + 128, :])
                    if d == 1:
                        S.load(yfl[sl][:], YF[t0_:t0_ + 128, :])
                        S.load(zt[sl][:], ZZ[t0_:t0_ + 128, :])
                    dt_, aa_, eac_, wend_, dend_ = dtv[sl], aav[sl], eacv[sl], wendv[sl], dendv[sl]
                    tt(dt_[:], dtr[sl][:, d * 32:(d + 1) * 32], dtb[:, d * 32:(d + 1) * 32], ALU.add)
                    act(dt_[:], dt_[:], AF.Exp)
                    act(dt_[:], dt_[:], AF.Ln, bias=1.0)
                    tt(aa_[:], dt_[:], Ab[:, d * 32:(d + 1) * 32], ALU.mult)
                    mm(p_ac[:, 0:32], Ucum, aa_[:])
                    mm(p_ac[:, 32:64], ON_f, aa_[:])
                    act(eac_[:], p_ac[:, 0:32], AF.Exp)
                    act(dend_[:], p_ac[:, 32:64], AF.Exp)
                    cp(wend_[:], p_ac[:, 0:32], eng='dve')
                    tt(wend_[:], p_ac[:, 32:64], wend_[:], ALU.subtract)
                    act(wend_[:], wend_[:], AF.Exp)
                    xs3 = xs[:].rearrange("p (h q) -> p h q", q=64)
                    tt(xdtv[sl][:].rearrange("p (h q) -> p h q", q=64), xs3, _bc_last(dt_[:], 64), ALU.mult, eng='pool')
                    tt(wend_[:], wend_[:], dt_[:], ALU.mult)
                    tt(xdwv[sl][:].rearrange("p (h q) -> p h q", q=64), xs3, _bc_last(wend_[:], 64), ALU.mult)

                aside(order[0], nit % 2)
                for oi, tix in enumerate(order):
                    t0_ = tix * 128
                    sl = nit % 2
                    nit += 1
                    if oi + 1 < len(order):
                        aside(order[oi + 1], nit % 2)
                    xs = xs_[sl]
                    xs3 = xs[:].rearrange("p (h q) -> p h q", q=64)
                    aa_, eac, dend, xdt, xdw = aav[sl], eacv[sl], dendv[sl], xdtv[sl], xdwv[sl]
                    yout = yo[sl]
                    for g in range(4):
                        gs = (nit * 4 + g) % 2
                        mm(p_cb[:, 0:128], bT_[sl][:, g, :], cT_[sl][:, g, :])
                        tt(CBm[gs][:], p_cb[:, 0:128], Mcb, ALU.mult)
                        tt(Am[gs][:], _bc_mid(Ucum, 8), _bc_last(aa_[:, g * 8:(g + 1) * 8], 128), ALU.mult, eng='pool')
                        for half in range(2):
                            mm(p_df[half][:, 0:512], Mlhs, Am[gs][:, half * 4:(half + 1) * 4, :].rearrange("p h t -> p (h t)"))
                        act(LT[gs][:, 0:4, :], p_df[0][:].rearrange("p (h t) -> p h t", t=128), AF.Exp)
                        act(LT[gs][:, 4:8, :], p_df[1][:].rearrange("p (h t) -> p h t", t=128), AF.Exp)
                        tt(MT[gs][:], LT[gs][:], _bc_mid(CBm[gs][:], 8), ALU.mult)
                        dbgstop(3)
                        for hh in range(8):
                            hg = g * 8 + hh
                            mm(p_y[:, hh * 64:(hh + 1) * 64], MT[gs][:, hh, :], xdt[:, hg * 64:(hg + 1) * 64])
                        mm(p_yi[:], cT_[sl][:, g, :], STb[:, g * 512:(g + 1) * 512])
                        tt(ytmp[:].rearrange("p (h q) -> p h q", q=64), p_yi[:].rearrange("p (h q) -> p h q", q=64),
                           _bc_last(eac[:, g * 8:(g + 1) * 8], 64), ALU.mult)
                        tt(yout[:, g * 512:(g + 1) * 512], ytmp[:], p_y[:], ALU.add)
                        dbgstop(4)
                        mm(p_st[:], btk[sl][:, g * 128:(g + 1) * 128], xdw[:, g * 512:(g + 1) * 512])
                        tt(stmp[:].rearrange("p (h q) -> p h q", q=64),
                           ST32[:, g * 512:(g + 1) * 512].rearrange("p (h q) -> p h q", q=64),
                           _bc_last(dend[:, g * 8:(g + 1) * 8], 64), ALU.mult)
                        tt(ST32[:, g * 512:(g + 1) * 512], stmp[:], p_st[:], ALU.add)
                        cp(STb[:, g * 512:(g + 1) * 512], ST32[:, g * 512:(g + 1) * 512], eng='act')
                        dbgstop(5)
                    if d == 0:
                        S.store(YF[t0_:t0_ + 128, :], yout[:])
                        dbgstop(6)
                        if tix == 33:
                            dbgstop(7)
                    else:
                        if last and tix < 2:
                            continue
                        tt(yout[:], yout[:], yfl[sl][:], ALU.add)
                        tt(yfl[sl][:].rearrange("p (h q) -> p h q", q=64), xs3, _bc_last(dsk[:], 64), ALU.mult)
                        tt(yout[:], yout[:], yfl[sl][:], ALU.add)
                        tt(yout[:], yout[:], zt[sl][:], ALU.mult)
                        act(junk[:], yout[:], AF.Square, accum=ssq[:])
                        tsc(ssq[:], ssq[:], 1.0 / 2048, EPS, ALU.mult, ALU.add)
                        act(ssq[:], ssq[:], AF.Sqrt)
                        recip(ssq[:], ssq[:])
                        tsc(ynb[:], yout[:], ssq[:, 0:1], None, ALU.mult)
                        for q4 in range(4):
                            for j in range(4):
                                tr(p_tr[:, j * 128:(j + 1) * 128], ynb[:, (q4 * 4 + j) * 128:(q4 * 4 + j + 1) * 128], ID_f)
                            o = obt[q4 % 2]
                            for j in range(4):
                                act(o[:, j, :], p_tr[:, j * 128:(j + 1) * 128], AF.Identity, scale=ssm_g[:, l, q4 * 4 + j:q4 * 4 + j + 1])
                            S.store(BR[2048 + q4 * 512:2048 + (q4 + 1) * 512, t0_:t0_ + 128].rearrange("(j p) t -> p j t", p=128), o[:])
                S.barrier()
        if stop_after == ('ssd', l):
            raise _Stop()

        with ExitStack() as ph:
            wall = sb(ph, "m_w", [128, 40, 1024], BF16)
            with ExitStack() as pw:
                wstg = [sb(pw, f"m_ws{i}", [128, 4, 1024], F32) for i in range(2)]
                srcs = [(w_b_att[l], 8), (w_b_hg[l], 8), (w_b_ssm[l], 16), (w_out[l], 8)]
                kc0 = 0
                i = 0
                for (src, nk_) in srcs:
                    for k4 in range(0, nk_, 4):
                        f = wstg[i % 2]
                        S.load(f[:], src[k4 * 128:(k4 + 4) * 128, :].rearrange("(k p) c -> p k c", p=128))
                        for kk_ in range(4):
                            cp(wall[:, kc0 + k4 + kk_, :], f[:, kk_, :], eng=('dve' if (kk_ + i) % 2 == 0 else 'act'),
                               sub={wall.name: kc0 + k4 + kk_})
                        i += 1
                    kc0 += nk_
                S.barrier()
            MB = 256
            brb = [sb(ph, f"m_br{i}", [128, 32, MB], BF16) for i in range(2)]
            gb = [sb(ph, f"m_g{i}", [128, 24, MB], F32) for i in range(2)]
            xb = [sb(ph, f"m_x{i}", [128, 8, MB], F32) for i in range(2)]
            macc = sb(ph, "m_acc", [128, MB], F32)
            mtmp = sb(ph, "m_tmp", [128, MB], F32)
            mT = sb(ph, "m_mT", [128, 8, MB], BF16)
            pm = [ps(ph, f"m_p{i}", [128, MB], F32) for i in range(4)]
            npm = 0
            bi = 0
            for b0 in range(0, T, MB):
                if last and b0 < LC:
                    continue
                bs = MB
                s = strm(b0)
                br = brb[bi % 2]
                g_ = gb[bi % 2]
                x_ = xb[bi % 2]
                bi += 1
                S.load(br[:], BR[:, b0:b0 + bs].rearrange("(k p) t -> p k t", p=128))
                S.load(g_[:], GATES[:, b0:b0 + bs].rearrange("(k p) t -> p k t", p=128))
                S.load(x_[:], xsrc[:, b0:b0 + bs].rearrange("(k p) t -> p k t", p=128))
                for oc in range(8):
                    for bri, (k0, nk_) in enumerate(((0, 8), (8, 8), (16, 16))):
                        p = pm[npm % 4]
                        npm += 1
                        for k in range(nk_):
                            mm(p[:, :MB], wall[:, k0 + k, oc * 128:(oc + 1) * 128], br[:, k0 + k, :], start=(k == 0), stop=(k == nk_ - 1))
                        if bri == 0:
                            tt(macc[:], p[:, :MB], g_[:, oc, :], ALU.mult)
                        else:
                            tt(mtmp[:], p[:, :MB], g_[:, bri * 8 + oc, :], ALU.mult)
                            if bri == 1:
                                tt(macc[:], macc[:], mtmp[:], ALU.add)
                            else:
                                tt(mT[:, oc, :], macc[:], mtmp[:], ALU.add)
                for oc in range(8):
                    p = pm[npm % 4]
                    npm += 1
                    for k in range(8):
                        mm(p[:, :MB], wall[:, 32 + k, oc * 128:(oc + 1) * 128], mT[:, k, :], start=(k == 0), stop=(k == 7))
                    stt(x_[:, oc, :], p[:, :MB], modT[:, l, 16 + oc, s:s + 1], x_[:, oc, :], ALU.mult, ALU.add)
                S.store(XT[:, b0:b0 + bs].rearrange("(k p) t -> p k t", p=128), x_[:])
            S.barrier()
        if stop_after == ('merge', l):
            raise _Stop()

        with ExitStack() as ph:
            w1b = sb(ph, "f_w1", [128, 8, FFH], BF16)
            w3b = sb(ph, "f_w3", [128, 8, FFH], BF16)
            w2b = sb(ph, "f_w2", [128, 22, 1024], BF16)
            with ExitStack() as pw:
                wstg = [sb(pw, f"f_ws{i}", [128, FFH], F32) for i in range(2)]
                i = 0
                for (src, dstw) in ((ffn_w1[l], w1b), (ffn_w3[l], w3b)):
                    for k in range(8):
                        f = wstg[i % 2]
                        S.load(f[:], src[k * 128:(k + 1) * 128, :])
                        cp(dstw[:, k, :], f[:], eng=('dve' if i % 2 == 0 else 'act'))
                        i += 1
                for k in range(22):
                    f = wstg[i % 2]
                    S.load(f[:, 0:1024], ffn_w2[l][k * 128:(k + 1) * 128, :])
                    cp(w2b[:, k, :], f[:, 0:1024], eng=('dve' if i % 2 == 0 else 'act'))
                    i += 1
                S.barrier()
            FB = 256
            xb = [sb(ph, f"f_x{i}", [128, 8, FB], F32) for i in range(2)]
            sq = sb(ph, "f_sq", [128, 8, FB], BF16)
            rs = sb(ph, "f_rs", [128, FB], F32)
            tmp = sb(ph, "f_tmp", [128, 8, FB], F32)
            h2 = sb(ph, "f_h2", [128, 8, FB], BF16)
            uT = sb(ph, "f_u", [128, 22, FB], BF16)
            sg_ = [sb(ph, f"f_sg{i}", [128, FB], F32) for i in range(2)]
            s1 = [sb(ph, f"f_s1{i}", [128, FB], F32) for i in range(2)]
            yo_ = [sb(ph, f"f_yo{i}", [128, 8, FB], F32) for i in range(1)]
            pst = ps(ph, "f_pst", [128, FB], F32)
            pf = [ps(ph, f"f_p{i}", [128, FB], F32) for i in range(6)]
            npf = 0
            nblk = 0
            for b0 in range(0, T, FB):
                if last and b0 < LC:
                    continue
                bs = FB
                s = strm(b0)
                x_ = xb[nblk % 2]
                nblk += 1
                S.load(x_[:], XT[:, b0:b0 + bs].rearrange("(k p) t -> p k t", p=128))
                norm_block((sq, pst, rs, tmp), x_, bs,
                           lambda k: A2[:, l, k, s:s + 1], lambda k: modT[:, l, 24 + k, s:s + 1],
                           lambda k: h2[:, k, :])
                for hc in range(22):
                    pa = pf[npf % 6]
                    pb_ = pf[(npf + 1) % 6]
                    npf += 2
                    for k in range(8):
                        mm(pa[:, :FB], w1b[:, k, hc * 128:(hc + 1) * 128], h2[:, k, :], start=(k == 0), stop=(k == 7))
                    for k in range(8):
                        mm(pb_[:, :FB], w3b[:, k, hc * 128:(hc + 1) * 128], h2[:, k, :], start=(k == 0), stop=(k == 7))
                    sg = sg_[hc % 2]
                    s1_ = s1[hc % 2]
                    act(sg[:], pa[:, :FB], AF.Sigmoid)
                    tt(s1_[:], pa[:, :FB], sg[:], ALU.mult)
                    tt(uT[:, hc, :], pb_[:, :FB], s1_[:], ALU.mult)
                for oc in range(8):
                    p = pf[npf % 6]
                    npf += 1
                    for k in range(22):
                        mm(p[:, :FB], w2b[:, k, oc * 128:(oc + 1) * 128], uT[:, k, :], start=(k == 0), stop=(k == 21))
                    stt(x_[:, oc, :], p[:, :FB], modT[:, l, 40 + oc, s:s + 1], x_[:, oc, :], ALU.mult, ALU.add)
                if not last:
                    S.store(XT[:, b0:b0 + bs].rearrange("(k p) t -> p k t", p=128), x_[:])
                else:
                    yo = yo_[0]
                    norm_block((sq, pst, rs, tmp), x_, bs,
                               lambda k: gfs[:, k:k + 1], lambda k: None,
                               lambda k: yo[:, k, :])
                    S.store(yT[:, b0 - LC:b0 - LC + bs].rearrange("(k p) t -> p k t", p=128), yo[:])
            S.barrier()


    stopped = False
    try:
        for l in range(DEPTH):
            layer_body(l)
    except _Stop:
        stopped = True
    S.barrier()
    if not stopped:
        per.close()
        es.close()
    return nc, S


_CACHE = {}


def _consts():
    j = np.arange(128)[:, None]
    t = np.arange(128)[None, :]
    c = np.zeros((128, 6 * 128 + 64 + 4 + 256), np.float32)
    c[:, 0:128] = (j <= t)
    c[:, 128:256] = (j >= t)
    c[:, 256:384] = (j > t)
    c[:, 384:512] = (j < t)
    c[:, 512:640] = np.eye(128)
    c[:, 640:768] = 1.0
    s = np.arange(32)[:, None]
    tt_ = np.arange(32)[None, :]
    c[0:32, 768:800] = (s <= tt_)
    c[0:32, 800:832] = (s >= tt_)
    for q in range(4):
        c[32 * q:32 * q + 32, 832 + q] = 1.0
    same = (j // 32) == (t // 32)
    c[:, 836:964] = same & (j <= t)
    c[:, 964:1092] = same & (j >= t)
    return c


def _m0():
    m0 = np.ones(T, np.float32)
    m0[::32] = 0.0
    return np.ascontiguousarray(np.broadcast_to(m0[None, :], (128, T))).astype(ml_dtypes.bfloat16)


def _rope_tables():
    cos = np.ones((128, T), np.float64)
    sin = np.zeros((128, T), np.float64)
    tl = np.arange(L)
    row = (tl // 64).astype(np.float64)
    col = (tl % 64).astype(np.float64)
    for f in range(128):
        r = f % 32
        half_sel = (f % 64) // 32
        fi = r % 16
        freq = np.float32(10000.0) ** (-np.float32(fi) / np.float32(16))
        pos = row if half_sel == 0 else col
        ang = (pos.astype(np.float32) * np.float32(freq)).astype(np.float64)
        cos[f, LC:] = np.cos(ang)
        sgn = -1.0 if r < 16 else 1.0
        sin[f, LC:] = sgn * np.sin(ang)
    return cos.astype(np.float32), sin.astype(np.float32)


def _rot_perm():
    p = np.arange(1024)
    r = p % 32
    return np.where(r < 16, p + 16, p - 16)


def _prep_shared(inp):
    f = np.float32
    A = lambda a: np.ascontiguousarray(a, dtype=f)
    perm = _rot_perm()
    w_in = inp['w_in']
    w_rot = np.concatenate([w_in[:, :, O_AK:O_AK + 1024][:, :, perm], w_in[:, :, O_AQ:O_AQ + 1024][:, :, perm]], axis=2)
    cosT, sinT = _rope_tables()
    col8 = lambda g: A(g.reshape(8, 128).T)
    sh = {
        'w_ada': A(inp['w_ada']),
        'b_adaT': A(inp['b_ada'].reshape(DEPTH, 48, 128).transpose(0, 2, 1)),
        'g1T': A(inp['norm1_g'].reshape(DEPTH, 8, 128).transpose(0, 2, 1)),
        'g2T': A(inp['norm2_g'].reshape(DEPTH, 8, 128).transpose(0, 2, 1)),
        'gfT': col8(inp['final_g']),
        'w_in': A(w_in),
        'w_rot': A(w_rot),
        'cosT': cosT, 'sinT': sinT,
        'att_lam': A(inp['att_lambda'].reshape(DEPTH, 256)),
        'att_gT': A(inp['att_norm_g'].T),
        'hg_lbT': A(inp['hg_lb_logits'].reshape(2, DEPTH, 8, 128).transpose(3, 0, 1, 2)),
        'hg_gT': A(inp['hg_norm_g'].T),
        'conv_wT': A(inp['ssm_conv_w'].reshape(DEPTH, 5, 24, 128).transpose(0, 3, 2, 1)),
        'conv_bT': A(inp['ssm_conv_b'].reshape(DEPTH, 24, 128).transpose(0, 2, 1)),
        'conv_b': A(inp['ssm_conv_b']),
        'dt_bias': A(inp['ssm_dt_bias'].reshape(DEPTH, 64)),
        'a_log': A(inp['ssm_a_log'].reshape(DEPTH, 64)),
        'ssm_d': A(inp['ssm_d']),
        'ssm_gT': A(inp['ssm_norm_g'].reshape(DEPTH, 16, 128).transpose(0, 2, 1)),
        'w_b_att': A(inp['w_branch_att']), 'w_b_hg': A(inp['w_branch_hg']), 'w_b_ssm': A(inp['w_branch_ssm']),
        'w_out': A(inp['w_out']),
        'ffn_w1': A(inp['ffn_w1']), 'ffn_w3': A(inp['ffn_w3']), 'ffn_w2': A(inp['ffn_w2']),
        'cst': _consts(),
        'm0': _m0(),
    }
    return sh


def kernel(**inputs):
    inp = {k: np.asarray(v) for k, v in inputs.items()}
    if 'nc' not in _CACHE:
        _CACHE['nc'] = build_program()[0]
    nc = _CACHE['nc']
    sh = _prep_shared(inp)
    in_maps = []
    for b in range(8):
        m = dict(sh)
        m['xT'] = np.ascontiguousarray(np.concatenate([inp['ctx'][b], inp['x'][b]], axis=0).T, dtype=np.float32)
        m['c2'] = np.ascontiguousarray(np.stack([inp['c_ctx'], inp['c'][b]], axis=1), dtype=np.float32)
        in_maps.append(m)
    res = run_bass_kernel_spmd(nc, in_maps, core_ids=list(range(8)))
    out = np.stack([np.ascontiguousarray(res.results[b]['yT'].T) for b in range(8)], axis=0)
    return out.astype(np.float32)
```

```python
import math
from contextlib import ExitStack
import numpy as np
import ml_dtypes
import concourse.bass as bass
import concourse.mybir as mybir
from concourse.bass_utils import run_bass_kernel_spmd

F32 = mybir.dt.float32
BF16 = mybir.dt.bfloat16
AF = mybir.ActivationFunctionType
ALU = mybir.AluOpType

D = 1024
LC = 256
L = 4096
T = LC + L
NT = T // 128
NCH = T // 32
DEPTH = 2
EPS = 1e-6
NIN = 16448
FFH = 2816
BLOCKS = [(0, 256)] + [(256 + 512 * i, 512) for i in range(8)]
O_AK, O_AV, O_HFF, O_HFB, O_HI, O_XBC, O_DTF, O_DTB, O_AQ, O_HQ, O_HGATE, O_Z, O_GATES = (
    0, 1024, 2048, 3072, 4096, 5120, 8192, 8224, 8256, 9280, 10304, 11328, 13376)


class _Stop(Exception):
    pass


class Sched:
    def __init__(self, nc, es):
        self.nc = nc
        self.eng = {'pe': nc.tensor, 'act': nc.scalar, 'dve': nc.vector, 'pool': nc.gpsimd, 'sp': nc.sync}
        self.semh = {}
        self.cnt = {}
        for k in ['pe', 'act', 'dve', 'pool']:
            self.semh[k] = es.enter_context(nc.semaphore('s_' + k))
            self.cnt[k] = 0
        self.dpool = {'sp': [], 'pool': []}
        for q, n in (('sp', 24), ('pool', 16)):
            for i in range(n):
                nm = f'd_{q}{i}'
                self.semh[nm] = es.enter_context(nc.semaphore(nm))
                self.cnt[nm] = 0
                self.dpool[q].append(nm)
        self.drr = {'sp': 0, 'pool': 0}
        self.seen = {k: {} for k in self.eng}
        self.res = {}
        self.nops = 0
        self.psum = set()

    def _keys(self, ap, sub):
        nm = ap.tensor.name
        if nm in self.psum:
            return [(nm, None)]
        if sub is not None and nm in sub:
            v = sub[nm]
            if isinstance(v, (list, tuple)):
                return [(nm, x) for x in v]
            return [(nm, v)]
        return [(nm, None)]

    def _deps(self, rkeys, wkeys, eng=None):
        deps = []
        for k in rkeys:
            st = self.res.get(k)
            if st is not None and st[0] is not None:
                deps.append(st[0])
        for k in wkeys:
            st = self.res.get(k)
            if st is not None:
                if st[0] is not None and st[0][0] != eng:
                    deps.append(st[0])
                deps.extend(e for e in st[1] if e[0] != eng)
        return deps

    def _wait(self, eng, deps):
        need = {}
        for (k, v) in deps:
            if eng == 'pe' and k == 'pe':
                continue
            if self.seen[eng].get(k, 0) >= v:
                continue
            if need.get(k, 0) < v:
                need[k] = v
        for k, v in need.items():
            self.eng[eng].wait_ge(self.semh[k], v)
            self.seen[eng][k] = v

    def _record(self, ev, rkeys, wkeys):
        for k in rkeys:
            st = self.res.setdefault(k, [None, []])
            st[1] = [e for e in st[1] if e[0] != ev[0]] + [ev]
        for k in wkeys:
            self.res[k] = [ev, []]

    def op(self, eng, fn, ins, outs, sub=None, stream_self=False):
        rkeys = [k for a in ins if a is not None and hasattr(a, 'tensor') for k in self._keys(a, sub)]
        wkeys = [k for a in outs for k in self._keys(a, sub)]
        wkeys += [k for k in rkeys if k[0] in self.psum and k not in wkeys]
        deps = self._deps(rkeys, wkeys, eng)
        if stream_self:
            deps = [d_ for d_ in deps if d_[0] != eng]
        self._wait(eng, deps)
        inst = fn(self.eng[eng])
        self.cnt[eng] += 1
        inst.then_inc(self.semh[eng], 1)
        self._record((eng, self.cnt[eng]), rkeys, wkeys)
        self.nops += 1

    def dma(self, q, out, in_, sub=None, track_out=True, track_in=True):
        rkeys = self._keys(in_, sub) if track_in else []
        wkeys = self._keys(out, sub) if track_out else []
        pool = self.dpool[q]
        nm = pool[self.drr[q] % len(pool)]
        self.drr[q] += 1
        deps = self._deps(rkeys, wkeys)
        if self.cnt[nm] > 0:
            deps.append((nm, self.cnt[nm]))
        self._wait(q, deps)
        self.eng[q].dma_start(out=out, in_=in_).then_inc(self.semh[nm], 16)
        self.cnt[nm] += 16
        self._record((nm, self.cnt[nm]), rkeys, wkeys)
        self.nops += 1

    def load(self, out, in_, sub=None):
        self.dma('sp', out, in_, sub=sub, track_in=False)

    def store(self, out, in_, sub=None):
        self.dma('pool', out, in_, sub=sub, track_out=False)

    def barrier(self):
        allev = [(k, v) for k, v in self.cnt.items() if v > 0]
        for e in self.eng:
            self._wait(e, allev)
        self.res = {}


def _bc_mid(a, n):
    ap = [list(x) for x in a.ap]
    return bass.AP(a.tensor, a.offset, [ap[0], [0, n]] + ap[1:])


def _bc_last(a, n):
    ap = [list(x) for x in a.ap]
    return bass.AP(a.tensor, a.offset, ap + [[0, n]])


def _rev(a):
    ap = [list(x) for x in a.ap]
    assert len(ap) == 2 and ap[1][0] == 1
    return bass.AP(a.tensor, a.offset + ap[1][1] - 1, [ap[0], [-1, ap[1][1]]])


def _strided(a, start, step, n):
    ap = [list(x) for x in a.ap]
    return bass.AP(a.tensor, a.offset + start, [ap[0], [step, n]])


def _pbcast(dram_ap_1d_offset_tensor, offset, n):
    return bass.AP(dram_ap_1d_offset_tensor, offset, [[0, 128], [1, n]])


def build_program(stop_after=None, dump=None):
    nc = bass.Bass("TRN2", target_bir_lowering=False)
    es = ExitStack()
    S = Sched(nc, es)

    def din(name, shape, dt=F32):
        return nc.dram_tensor(name, list(shape), dt, kind="ExternalInput").ap()

    def dscr(name, shape, dt):
        if dump is not None and name in dump:
            return nc.dram_tensor(name, list(shape), dt, kind="ExternalOutput").ap()
        return nc.dram_tensor(name, list(shape), dt).ap()

    xT_in = din("xT", [D, T])
    c2_in = din("c2", [D, 2])
    w_ada = din("w_ada", [DEPTH, D, 6 * D])
    b_adaT = din("b_adaT", [DEPTH, 128, 48])
    g1T = din("g1T", [DEPTH, 128, 8])
    g2T = din("g2T", [DEPTH, 128, 8])
    gfT = din("gfT", [128, 8])
    w_in = din("w_in", [DEPTH, D, NIN])
    w_rot = din("w_rot", [DEPTH, D, 2048])
    cosT = din("cosT", [128, T])
    sinT = din("sinT", [128, T])
    att_lam = din("att_lam", [DEPTH, 256])
    att_gT = din("att_gT", [128, DEPTH])
    hg_lbT = din("hg_lbT", [128, 2, DEPTH, 8])
    hg_gT = din("hg_gT", [128, DEPTH])
    conv_wT = din("conv_wT", [DEPTH, 128, 24, 5])
    conv_bT = din("conv_bT", [DEPTH, 128, 24])
    conv_b = din("conv_b", [DEPTH, 3072])
    dt_bias = din("dt_bias", [DEPTH, 64])
    a_log = din("a_log", [DEPTH, 64])
    ssm_d = din("ssm_d", [DEPTH, 32])
    ssm_gT = din("ssm_gT", [DEPTH, 128, 16])
    w_b_att = din("w_b_att", [DEPTH, 1024, D])
    w_b_hg = din("w_b_hg", [DEPTH, 1024, D])
    w_b_ssm = din("w_b_ssm", [DEPTH, 2048, D])
    w_out = din("w_out", [DEPTH, D, D])
    ffn_w1 = din("ffn_w1", [DEPTH, D, FFH])
    ffn_w3 = din("ffn_w3", [DEPTH, D, FFH])
    ffn_w2 = din("ffn_w2", [DEPTH, FFH, D])
    cst_in = din("cst", [128, 6 * 128 + 64 + 4 + 256])
    m0_in = din("m0", [128, T], BF16)
    yT = nc.dram_tensor("yT", [D, L], F32, kind="ExternalOutput").ap()

    XT = dscr("XT", [D, T], F32)
    QT = dscr("QT", [1024, T], BF16)
    KT = dscr("KT", [1024, T], BF16)
    AV = dscr("AV", [T, 1024], BF16)
    HFF = dscr("HFF", [1024, T], F32)
    HFB = dscr("HFB", [1024, T], F32)
    HQ = dscr("HQ", [1024, T], F32)
    HI = dscr("HI", [T, 1024], BF16)
    HGATE = dscr("HGATE", [1024, T], F32)
    XBC = dscr("XBC", [3072, T], BF16)
    DTR = dscr("DTR", [T, 64], F32)
    ZZ = dscr("ZZ", [T, 2048], F32)
    GATES = dscr("GATES", [3072, T], F32)
    XS = dscr("XS", [T, 2048], F32)
    BTOK = dscr("BTOK", [T, 512], BF16)
    BTT = dscr("BTT", [512, T], BF16)
    CTT = dscr("CTT", [512, T], BF16)
    YF = dscr("YF", [T, 2048], F32)
    BR = dscr("BR", [4096, T], BF16)

    uid = [0]

    def sb(ctx, name, shape, dt):
        uid[0] += 1
        return ctx.enter_context(nc.sbuf_tensor(f"{name}_{uid[0]}", list(shape), dt))

    def ps(ctx, name, shape, dt=F32):
        uid[0] += 1
        t = ctx.enter_context(nc.psum_tensor(f"{name}_{uid[0]}", [128, 512] if dt == F32 else [128, 1024], dt))
        S.psum.add(t.name)
        return t

    def mm(out, lhsT, rhs, start=True, stop=True, sub=None):
        S.op('pe', lambda e: e.matmul(out, lhsT=lhsT, rhs=rhs, start=start, stop=stop), [lhsT, rhs], [out], sub)

    def tr(out, in_, ident, sub=None):
        S.op('pe', lambda e: e.transpose(out, in_, ident), [in_, ident], [out], sub)

    def act(out, in_, func, bias=None, scale=None, accum=None, sub=None, eng='act'):
        kw = {}
        if bias is not None:
            kw['bias'] = bias
        if scale is not None:
            kw['scale'] = scale
        if accum is not None:
            kw['accum_out'] = accum
        ins = [in_] + [x for x in (bias, scale) if hasattr(x, 'tensor')]
        outs = [out] + ([accum] if accum is not None else [])
        S.op('act', lambda e: e.activation(out=out, in_=in_, func=func, **kw), ins, outs, sub)

    def tt(out, in0, in1, op, sub=None, eng='dve'):
        S.op(eng, lambda e: e.tensor_tensor(out=out, in0=in0, in1=in1, op=op), [in0, in1], [out], sub)

    def tsc(out, in0, s1, s2, op0, op1=None, sub=None, eng='dve'):
        ins = [in0] + [x for x in (s1, s2) if hasattr(x, 'tensor')]
        if op1 is None:
            S.op(eng, lambda e: e.tensor_scalar(out=out, in0=in0, scalar1=s1, scalar2=None, op0=op0), ins, [out], sub)
        else:
            S.op(eng, lambda e: e.tensor_scalar(out=out, in0=in0, scalar1=s1, scalar2=s2, op0=op0, op1=op1), ins, [out], sub)

    def stt(out, in0, scalar, in1, op0, op1, sub=None, stream_self=False):
        ins = [in0, in1] + ([scalar] if hasattr(scalar, 'tensor') else [])
        S.op('dve', lambda e: e.scalar_tensor_tensor(out=out, in0=in0, scalar=scalar, in1=in1, op0=op0, op1=op1), ins, [out], sub,
             stream_self=stream_self)

    def cp(out, in_, sub=None, eng='dve'):
        if eng == 'act':
            act(out, in_, AF.Copy, sub=sub)
        else:
            S.op(eng, lambda e: e.tensor_copy(out=out, in_=in_), [in_], [out], sub)

    def recip(out, in_, sub=None):
        S.op('dve', lambda e: e.reciprocal(out=out, in_=in_), [in_], [out], sub)

    def memset(ap, val, eng='dve', sub=None):
        S.op(eng, lambda e: e.memset(ap, val), [], [ap], sub)

    def scan(out, d0, d1, sub=None):
        S.op('dve', lambda e: e.tensor_tensor_scan(out=out, data0=d0, data1=d1, initial=0.0, op0=ALU.mult, op1=ALU.add),
             [d0, d1], [out], sub)

    per = ExitStack()
    cst = sb(per, "cst", [128, 6 * 128 + 64 + 4 + 256], F32)
    S.load(cst[:], cst_in[:, :])
    U_f = cst[:, 0:128]
    Lo_f = cst[:, 128:256]
    Ms_f = cst[:, 256:384]
    Ml_f = cst[:, 384:512]
    ID_f = cst[:, 512:640]
    ON_f = cst[:, 640:768]
    RM = cst[:, 832:836]
    BDf = cst[:, 836:964]
    BDb = cst[:, 964:1092]
    M32 = cst[:, 768:832]
    id_bf = sb(per, "id_bf", [128, 128], BF16)
    on_bf = sb(per, "on_bf", [128, 128], BF16)
    cp(id_bf[:], ID_f)
    cp(on_bf[:], ON_f)
    modT = sb(per, "modT", [128, DEPTH, 48, 2], F32)
    A1 = sb(per, "A1", [128, DEPTH, 8, 2], F32)
    A2 = sb(per, "A2", [128, DEPTH, 8, 2], F32)
    g1s = sb(per, "g1s", [128, DEPTH, 8], F32)
    g2s = sb(per, "g2s", [128, DEPTH, 8], F32)
    gfs = sb(per, "gfs", [128, 8], F32)
    S.load(g1s[:], g1T.rearrange("l p k -> p l k"))
    S.load(g2s[:], g2T.rearrange("l p k -> p l k"))
    S.load(gfs[:], gfT[:, :])
    att_g = sb(per, "att_g", [128, DEPTH], F32)
    hg_g = sb(per, "hg_g", [128, DEPTH], F32)
    S.load(att_g[:], att_gT[:, :])
    S.load(hg_g[:], hg_gT[:, :])
    lbx = sb(per, "lbx", [128, 2, DEPTH, 8], F32)
    S.load(lbx[:], hg_lbT[:, :, :, :])
    lb1 = sb(per, "lb1", [128, 2, 8], F32)
    oml1 = sb(per, "oml1", [128, 2, 8], F32)
    neglam = sb(per, "neglam", [128, DEPTH], F32)
    ssm_g = sb(per, "ssm_g", [128, DEPTH, 16], F32)
    S.load(ssm_g[:], ssm_gT.rearrange("l p k -> p l k"))
    cbT = sb(per, "cbT", [128, DEPTH, 24], F32)
    S.load(cbT[:], conv_bT.rearrange("l p k -> p l k"))

    with ExitStack() as ph:
        sc = sb(ph, "p0_sc", [128, 8, 2], F32)
        S.load(sc[:], c2_in.rearrange("(k p) s -> p k s", p=128))
        sg = sb(ph, "p0_sg", [128, 8, 2], F32)
        act(sg[:], sc[:], AF.Sigmoid)
        tt(sc[:], sc[:], sg[:], ALU.mult)
        badd = sb(ph, "p0_b", [128, DEPTH, 48], F32)
        S.load(badd[:], b_adaT.rearrange("l p k -> p l k"))
        wst = [sb(ph, f"p0_w{i}", [128, 8, 512], F32) for i in range(2)]
        pm = ps(ph, "p0_pm", [128, 96], F32)
        it = 0
        for l in range(DEPTH):
            for cg in range(12):
                w = wst[it % 2]
                it += 1
                S.load(w[:], w_ada[l, :, cg * 512:(cg + 1) * 512].rearrange("(k p) c -> p k c", p=128))
                for j in range(4):
                    ch = cg * 4 + j
                    for k in range(8):
                        mm(pm[:, ch * 2:ch * 2 + 2], w[:, k, j * 128:(j + 1) * 128], sc[:, k, :], start=(k == 0), stop=(k == 7))
            tt(modT[:, l, :, :], pm[:, 0:96].rearrange("p (c s) -> p c s", s=2), _bc_last(badd[:, l, :], 2), ALU.add)
            tsc(A1[:, l, :, :], modT[:, l, 8:16, :], 1.0, None, ALU.add)
            tt(A1[:, l, :, :], A1[:, l, :, :], _bc_last(g1s[:, l, :], 2), ALU.mult)
            tsc(A2[:, l, :, :], modT[:, l, 32:40, :], 1.0, None, ALU.add)
            tt(A2[:, l, :, :], A2[:, l, :, :], _bc_last(g2s[:, l, :], 2), ALU.mult)
        lamt = sb(ph, "p0_lam", [128, DEPTH, 4, 64], F32)
        S.load(lamt[:].rearrange("p a b c -> p (a b c)"), bass.AP(att_lam.tensor, 0, [[0, 128], [1, DEPTH * 256]]))
        pr = sb(ph, "p0_pr", [128, 64], F32)
        sm = sb(ph, "p0_sm", [128, 4], F32)
        for l in range(DEPTH):
            for j in range(2):
                tt(pr[:], lamt[:, l, 2 * j, :], lamt[:, l, 2 * j + 1, :], ALU.mult)
                act(pr[:], pr[:], AF.Copy, accum=sm[:, 2 * l + j:2 * l + j + 1])
        act(sm[:], sm[:], AF.Exp)
        for l in range(DEPTH):
            lam_init = 0.8 - 0.6 * math.exp(-0.3 * l)
            tt(neglam[:, l:l + 1], sm[:, 2 * l + 1:2 * l + 2], sm[:, 2 * l:2 * l + 1], ALU.subtract)
            tsc(neglam[:, l:l + 1], neglam[:, l:l + 1], -lam_init, None, ALU.add)
        tt(lb1[:], lbx[:, :, 1, :], lbx[:, :, 0, :], ALU.subtract)
        act(lb1[:], lb1[:], AF.Sigmoid)
        tsc(oml1[:], lb1[:], -1.0, 1.0, ALU.mult, ALU.add)
        S.barrier()

    def norm_block(ph_tiles, xb, bs, A_of_k, B_of_k, out_of_k):
        sq, pst, rs, tmp = ph_tiles
        for k in range(8):
            act(sq[:, k, :bs], xb[:, k, :bs], AF.Square)
        for k in range(8):
            mm(pst[:, :bs], on_bf[:], sq[:, k, :bs], start=(k == 0), stop=(k == 7))
        tsc(rs[:, :bs], pst[:, :bs], 1.0 / D, EPS, ALU.mult, ALU.add)
        act(rs[:, :bs], rs[:, :bs], AF.Sqrt)
        recip(rs[:, :bs], rs[:, :bs])
        for k in range(8):
            tt(tmp[:, k, :bs], xb[:, k, :bs], rs[:, :bs], ALU.mult)
            b = B_of_k(k)
            if b is None:
                act(out_of_k(k), tmp[:, k, :bs], AF.Identity, scale=A_of_k(k))
            else:
                act(out_of_k(k), tmp[:, k, :bs], AF.Identity, scale=A_of_k(k), bias=b)

    def strm(b0):
        return 0 if b0 < LC else 1

    import os
    SSD_DBG = int(os.environ.get('SSD_DBG', '0'))

    def dbgstop(n):
        if SSD_DBG == n:
            raise _Stop()

    def layer_body(l):
        last = (l == DEPTH - 1)
        xsrc = xT_in if l == 0 else XT
        with ExitStack() as ph:
            hT = sb(ph, "hT", [128, 8, T], BF16)
            with ExitStack() as p1:
                xb2 = [sb(p1, f"p1_x{i}", [128, 8, 512], F32) for i in range(2)]
                sq = sb(p1, "p1_sq", [128, 8, 512], BF16)
                rs = sb(p1, "p1_rs", [128, 512], F32)
                tmp = sb(p1, "p1_tmp", [128, 8, 512], F32)
                pst = ps(p1, "p1_ps", [128, 512], F32)
                for bi, (b0, bs) in enumerate(BLOCKS):
                    xb = xb2[bi % 2]
                    S.load(xb[:, :, :bs], xsrc[:, b0:b0 + bs].rearrange("(k p) t -> p k t", p=128))
                    s = strm(b0)
                    norm_block((sq, pst, rs, tmp), xb, bs,
                               lambda k: A1[:, l, k, s:s + 1], lambda k: modT[:, l, k, s:s + 1],
                               lambda k: hT[:, k, b0:b0 + bs])
                S.barrier()
            with ExitStack() as p2:
                cs = sb(p2, "p2_cos", [128, T], F32)
                sn = sb(p2, "p2_sin", [128, T], F32)
                S.load(cs[:], cosT[:, :])
                S.load(sn[:], sinT[:, :])
                wf = [sb(p2, f"p2_wf{i}", [128, 8, 512], F32) for i in range(2)]
                wb = [sb(p2, f"p2_wb{i}", [128, 8, 512], BF16) for i in range(3)]
                stg = [sb(p2, f"p2_st{i}", [128, 512], F32) for i in range(4)]
                stb = [sb(p2, f"p2_sb{i}", [128, 512], BF16) for i in range(4)]
                r1 = sb(p2, "p2_r1", [128, 512], F32)
                r2 = sb(p2, "p2_r2", [128, 512], F32)
                pp = [ps(p2, f"p2_p{i}", [128, 512], F32) for i in range(6)]
                cnt = {'w': 0, 'p': 0, 's': 0, 'c': 0}

                def load_w(src, c0, n):
                    i = cnt['w']
                    cnt['w'] += 1
                    f = wf[i % 2]
                    b = wb[i % 3]
                    S.load(f[:, :, :n], src[:, c0:c0 + n].rearrange("(k p) c -> p k c", p=128))
                    for k in range(8):
                        if (k + i) % 2 == 0:
                            cp(b[:, k, :n], f[:, k, :n], eng='dve')
                        else:
                            cp(b[:, k, :n], f[:, k, :n], eng='act')
                    return b

                def newp():
                    p = pp[cnt['p'] % 6]
                    cnt['p'] += 1
                    return p

                def proj_F(wt, j, b0, bs):
                    p = newp()
                    for k in range(8):
                        mm(p[:, :bs], wt[:, k, j * 128:(j + 1) * 128], hT[:, k, b0:b0 + bs], start=(k == 0), stop=(k == 7))
                    return p

                def fgroup(src, c0, ncols, evac):
                    for g0 in range(0, ncols, 512):
                        n = min(512, ncols - g0)
                        wt = load_w(src, c0 + g0, n)
                        for j in range(n // 128):
                            for (b0, bs) in BLOCKS:
                                p = proj_F(wt, j, b0, bs)
                                evac(p, g0 + j * 128, b0, bs)

                def ev_simple(dst, func, dt):
                    def f(p, gc, b0, bs):
                        i = cnt['s']
                        cnt['s'] += 1
                        st = (stb if dt == BF16 else stg)[i % 4]
                        if func is None:
                            if i % 2 == 0:
                                cp(st[:, :bs], p[:, :bs], eng='dve')
                            else:
                                cp(st[:, :bs], p[:, :bs], eng='act')
                        else:
                            act(st[:, :bs], p[:, :bs], func)
                        S.store(dst[gc:gc + 128, b0:b0 + bs], st[:, :bs])
                    return f

                def ev_silu(dst):
                    def f(p, gc, b0, bs):
                        i = cnt['s']
                        cnt['s'] += 1
                        st = stg[i % 4]
                        act(st[:, :bs], p[:, :bs], AF.Sigmoid)
                        tt(st[:, :bs], p[:, :bs], st[:, :bs], ALU.mult)
                        S.store(dst[gc:gc + 128, b0:b0 + bs], st[:, :bs])
                    return f

                def rope_group(c_orig, c_rot, dst):
                    for g0 in range(0, 1024, 512):
                        wo = load_w(w_in[l], c_orig + g0, 512)
                        wr = load_w(w_rot[l], c_rot + g0, 512)
                        for j in range(4):
                            for (b0, bs) in BLOCKS:
                                p1_ = proj_F(wo, j, b0, bs)
                                p2_ = proj_F(wr, j, b0, bs)
                                i = cnt['s']
                                cnt['s'] += 1
                                st = stb[i % 4]
                                tt(r1[:, :bs], p1_[:, :bs], cs[:, b0:b0 + bs], ALU.mult)
                                tt(r2[:, :bs], p2_[:, :bs], sn[:, b0:b0 + bs], ALU.mult)
                                tt(st[:, :bs], r1[:, :bs], r2[:, :bs], ALU.add)
                                gc = g0 + j * 128
                                S.store(dst[gc:gc + 128, b0:b0 + bs], st[:, :bs])

                def tgroup(c0, ncols, dst, dcol0, func, dt):
                    for g0 in range(0, ncols, 512):
                        n = min(512, ncols - g0)
                        wt = load_w(w_in[l], c0 + g0, n)
                        for tix in range(NT):
                            p = newp()
                            for k in range(8):
                                mm(p[:, :n], hT[:, k, tix * 128:(tix + 1) * 128], wt[:, k, :n], start=(k == 0), stop=(k == 7))
                            i = cnt['s']
                            cnt['s'] += 1
                            st = (stb if dt == BF16 else stg)[i % 4]
                            if func == 'silu':
                                act(st[:, :n], p[:, :n], AF.Sigmoid)
                                tt(st[:, :n], p[:, :n], st[:, :n], ALU.mult)
                            elif i % 2 == 0:
                                cp(st[:, :n], p[:, :n], eng='dve')
                            else:
                                cp(st[:, :n], p[:, :n], eng='act')
                            S.store(dst[tix * 128:(tix + 1) * 128, dcol0 + g0:dcol0 + g0 + n], st[:, :n])

                rope_group(O_AK, 0, KT)
                rope_group(O_AQ, 1024, QT)
                tgroup(O_AV, 1024, AV, 0, None, BF16)
                fgroup(w_in[l], O_HFF, 1024, ev_simple(HFF, None, F32))
                fgroup(w_in[l], O_HFB, 1024, ev_simple(HFB, None, F32))
                tgroup(O_HI, 1024, HI, 0, None, BF16)
                fgroup(w_in[l], O_XBC, 3072, ev_simple(XBC, None, BF16))
                tgroup(O_DTF, 64, DTR, 0, None, F32)
                fgroup(w_in[l], O_HQ, 1024, ev_silu(HQ))
                fgroup(w_in[l], O_HGATE, 1024, ev_silu(HGATE))
                tgroup(O_Z, 2048, ZZ, 0, 'silu', F32)
                fgroup(w_in[l], O_GATES, 3072, ev_simple(GATES, AF.Sigmoid, F32))
                S.barrier()
        if stop_after == ('p2', l):
            raise _Stop()

        with ExitStack() as ph:
          if not os.environ.get('SKIP_AG'):
            kT2 = [sb(ph, f"a_kT{i}", [128, T], BF16) for i in range(2)]
            qT2 = [sb(ph, f"a_qT{i}", [128, T], BF16) for i in range(2)]
            vt2 = [sb(ph, f"a_v{i}", [128, NT, 128], BF16) for i in range(2)]
            qm2 = [[sb(ph, f"a_qm{i}_{c}", [128, T], BF16) for c in range(2)] for i in range(2)]
            pb = [sb(ph, f"a_p{i}", [128, 512], BF16) for i in range(6)]
            psm = [sb(ph, f"a_psm{i}", [128, 512], BF16) for i in range(4)]
            r0 = sb(ph, "a_r0", [128, 512], F32)
            r1 = sb(ph, "a_r1", [128, 512], F32)
            t0 = sb(ph, "a_t0", [128, 512], F32)
            t1 = sb(ph, "a_t1", [128, 512], F32)
            sq = sb(ph, "a_sq", [128, 512], BF16)
            ob = [sb(ph, f"a_ob{i}", [128, 512], BF16) for i in range(2)]
            gsc = sb(ph, "a_g", [128, 1], F32)
            lam_init = 0.8 - 0.6 * math.exp(-0.3 * l)
            tsc(gsc[:], att_g[:, l:l + 1], 1.0 - lam_init, None, ALU.mult)
            pS = [ps(ph, f"a_S{i}", [128, 512], F32) for i in range(4)]
            pO = [ps(ph, f"a_O{i}", [128, 512], F32) for i in range(2)]
            pL = [ps(ph, f"a_L{i}", [128, 512], F32) for i in range(2)]
            for h in range(8):
                kTt, qTt, vt = kT2[h % 2], qT2[h % 2], vt2[h % 2]
                S.load(kTt[:], KT[h * 128:(h + 1) * 128, :])
                S.load(qTt[:], QT[h * 128:(h + 1) * 128, :])
                S.load(vt[:], AV[:, h * 128:(h + 1) * 128].rearrange("(n p) c -> p n c", p=128))
                qm = qm2[h % 2]
                tsc(qm[0][:], qTt[:], U_f[:, 63:64], None, ALU.mult)
                tsc(qm[1][:], qTt[:], Lo_f[:, 64:65], None, ALU.mult)
                items = []
                for bi, (b0, bs) in enumerate(BLOCKS):
                    if last and b0 < LC:
                        continue
                    nk = 2 if b0 < LC else NT
                    for kt in range(nk):
                        for c in range(2):
                            items.append((bi, b0, bs, kt, c, nk))

                def emit_S(j):
                    (bi_, b0_, bs_, kt_i, c_, nk_) = items[j]
                    mm(pS[j % 4][:, :bs_], kTt[:, kt_i * 128:(kt_i + 1) * 128], qm[c_][:, b0_:b0_ + bs_])

                LOOK = 3
                for j in range(min(LOOK, len(items))):
                    emit_S(j)
                for j, (bi, b0, bs, kt, c, nk) in enumerate(items):
                    if j + LOOK < len(items):
                        emit_S(j + LOOK)
                    P = pb[j % 6]
                    act(P[:, :bs], pS[j % 4][:, :bs], AF.Exp, scale=0.125)
                    mm(pO[c][:, :bs], vt[:, kt, :], P[:, :bs], start=(kt == 0), stop=(kt == nk - 1))
                    if kt % 2 == 1:
                        sm_ = psm[((kt // 2) * 2 + c) % 4]
                        tt(sm_[:, :bs], pb[(j - 2) % 6][:, :bs], P[:, :bs], ALU.add)
                        mm(pL[c][:, :bs], on_bf[:], sm_[:, :bs], start=(kt == 1), stop=(kt == nk - 1))
                    if not (kt == nk - 1 and c == 1):
                        continue
                    recip(r0[:, :bs], pL[0][:, :bs])
                    recip(r1[:, :bs], pL[1][:, :bs])
                    tt(t0[:, :bs], pO[0][:, :bs], r0[:, :bs], ALU.mult)
                    tt(t1[:, :bs], pO[1][:, :bs], r1[:, :bs], ALU.mult)
                    stt(t0[:, :bs], t1[:, :bs], neglam[:, l:l + 1], t0[:, :bs], ALU.mult, ALU.add)
                    act(sq[:, :bs], t0[:, :bs], AF.Square)
                    pX = pL[0]
                    mm(pX[:, :bs], on_bf[:], sq[:, :bs])
                    tsc(r0[:, :bs], pX[:, :bs], 1.0 / 128, EPS, ALU.mult, ALU.add)
                    act(r0[:, :bs], r0[:, :bs], AF.Sqrt)
                    recip(r0[:, :bs], r0[:, :bs])
                    tt(t0[:, :bs], t0[:, :bs], r0[:, :bs], ALU.mult)
                    o = ob[bi % 2]
                    act(o[:, :bs], t0[:, :bs], AF.Identity, scale=gsc[:, 0:1])
                    S.store(BR[h * 128:(h + 1) * 128, b0:b0 + bs], o[:, :bs])
            S.barrier()
        if stop_after == ('att', l):
            raise _Stop()

        with ExitStack() as ph:
          if not os.environ.get('SKIP_AG'):
            fA = sb(ph, "g_f", [128, T], F32)
            bB = sb(ph, "g_b", [128, T], F32)
            kK = sb(ph, "g_k", [128, T], F32)
            tM = sb(ph, "g_t", [128, T], F32)
            hq = sb(ph, "g_hq", [128, T], F32)
            gt = [sb(ph, f"g_gate{i}", [128, 512], F32) for i in range(2)]
            qtL = [sb(ph, f"g_qt{i}", [128, T], BF16) for i in range(2)]
            ktL = [sb(ph, f"g_kt{i}", [128, T], BF16) for i in range(2)]
            khL = [sb(ph, f"g_kh{i}", [128, T], BF16) for i in range(2)]
            m0 = sb(ph, "g_m0", [128, T], BF16)
            osum = sb(ph, "g_os", [128, T], F32)
            v128 = sb(ph, "g_v128", [128, NT, 128], BF16)
            ebL = [sb(ph, f"g_eb{i}", [128, NCH], F32) for i in range(2)]
            S32 = [sb(ph, f"g_S32_{i}", [128, 128], F32) for i in range(2)]
            Sbf = [sb(ph, f"g_Sbf{i}", [128, 128], BF16) for i in range(4)]
            khTm = [[sb(ph, f"g_khm{i}_{j}", [128, 128], BF16) for j in range(4)] for i in range(2)]
            kh128 = [sb(ph, f"g_kh128_{i}", [128, 128], BF16) for i in range(2)]
            At = [sb(ph, f"g_At{i}", [128, 128], BF16) for i in range(2)]
            sq = sb(ph, "g_sq", [128, 512], BF16)
            rs = sb(ph, "g_rs", [128, 512], F32)
            on_ = sb(ph, "g_on", [128, 512], F32)
            ob = [sb(ph, f"g_ob{i}", [128, 512], BF16) for i in range(2)]
            lbz = sb(ph, "g_lbz", [128, 2], F32)
            pT = [ps(ph, "g_pT0", [32, 128], BF16)]
            pSc = [ps(ph, "g_pS0", [32, 32], F32)]
            pD = [ps(ph, f"g_pD{i}", [128, 128], F32) for i in range(4)]
            pOa = ps(ph, "g_pOa", [128, 512], F32)
            pOb = ps(ph, "g_pOb", [128, 512], F32)
            S.load(m0[:], m0_in[:, :])
            memset(lbz[:, 0:1], 0.0)
            memset(lbz[:, 1:2], 1.0)
            def prep_thunks(h, d):
                qt, kt_, kh, ebend = qtL[d], ktL[d], khL[d], ebL[d]
                if l == 0:
                    lb_ap, oml_ap = lbz[:, 0:1], lbz[:, 1:2]
                else:
                    lb_ap, oml_ap = lb1[:, d, h:h + 1], oml1[:, d, h:h + 1]
                bend = _strided(bB[:], 31, 32, NCH) if d == 0 else _strided(bB[:], 0, 32, NCH)
                th = []
                if d == 0:
                    th.append(lambda: S.load(hq[:], HQ[h * 128:(h + 1) * 128, :]))
                th.append(lambda: S.load(fA[:], (HFF if d == 0 else HFB)[h * 128:(h + 1) * 128, :]))
                th.append(lambda: act(fA[:], fA[:], AF.Sigmoid))
                th.append(lambda: tsc(fA[:], fA[:], oml_ap, lb_ap, ALU.mult, ALU.add))
                th.append(lambda: tsc(kK[:], fA[:], -1.0, 1.0, ALU.mult, ALU.add))
                th.append(lambda: act(fA[:], fA[:], AF.Ln))
                if d == 0:
                    th.append(lambda: scan(bB[:], m0[:], fA[:]))
                else:
                    th.append(lambda: scan(_rev(bB[:]), m0[:], _rev(fA[:])))
                th.append(lambda: act(tM[:], bB[:], AF.Exp))
                th.append(lambda: tt(qt[:], hq[:], tM[:], ALU.mult))
                th.append(lambda: act(tM[:], bB[:], AF.Exp, scale=-1.0))
                th.append(lambda: tt(kt_[:], kK[:], tM[:], ALU.mult))
                th.append(lambda: tt(tM[:].rearrange("p (c j) -> p c j", j=32), _bc_last(bend, 32),
                                     bB[:].rearrange("p (c j) -> p c j", j=32), ALU.subtract))
                th.append(lambda: act(tM[:], tM[:], AF.Exp))
                th.append(lambda: tt(kh[:], kK[:], tM[:], ALU.mult))
                th.append(lambda: act(ebend[:], bend, AF.Exp))
                return th

            pairs = [(h_, d_) for h_ in range(8) for d_ in range(2)]
            for t_ in prep_thunks(0, 0):
                t_()
            for h in range(8):
                S.load(v128[:], HI[:, h * 128:(h + 1) * 128].rearrange("(n p) c -> p n c", p=128))
                for d in range(2):
                    qt, kt_, kh, ebend = qtL[d], ktL[d], khL[d], ebL[d]
                    pi_ = h * 2 + d
                    pending = prep_thunks(*pairs[pi_ + 1]) if pi_ + 1 < len(pairs) else []
                    memset(S32[0][:], 0.0)
                    memset(Sbf[0][:], 0.0)
                    order = list(range(8)) + list(range(8, NCH)) if d == 0 else list(range(7, -1, -1)) + list(range(NCH - 1, 7, -1))
                    tiles = [0, 1] + list(range(2, NT)) if d == 0 else [1, 0] + list(range(NT - 1, 1, -1))
                    BD = BDf if d == 0 else BDb
                    LAG = 2
                    pend = []

                    def blk_of(c):
                        if c < 8:
                            return 0, 256
                        return BLOCKS[1 + (c - 8) // 16]

                    def emit_tile(ti):
                        tix = tiles[ti]
                        t0_ = tix * 128
                        ts = ti % 2
                        tr(pT[0][:, 0:128], kh[:, t0_:t0_ + 128], id_bf[:])
                        cp(kh128[ts][:], pT[0][:, 0:128], eng='act')
                        for j in range(4):
                            if j < 2:
                                tsc(khTm[ts][j][:], kh128[ts][:], RM[:, j:j + 1], 1.0, ALU.mult, ALU.mult, eng='pool')
                            elif j == 2:
                                tsc(khTm[ts][j][:], kh128[ts][:], RM[:, j:j + 1], None, ALU.mult)
                            else:
                                act(khTm[ts][j][:], kh128[ts][:], AF.Identity, scale=RM[:, j:j + 1])
                        mm(pSc[0][:, 0:128], kt_[:, t0_:t0_ + 128], qt[:, t0_:t0_ + 128])
                        tt(At[ts][:], pSc[0][:, 0:128], BD, ALU.mult)
                        bstart_, bsz_ = blk_of(tix * 4)
                        oc_ = t0_ - bstart_
                        mm(pOa[:, oc_:oc_ + 128], v128[:, tix, :], At[ts][:])
                        last_tile = (t0_ + 128 == bstart_ + bsz_) if d == 0 else (t0_ == bstart_)
                        if last_tile:
                            if d == 0:
                                cp(osum[:, bstart_:bstart_ + bsz_], pOa[:, :bsz_], eng='act')
                            else:
                                tt(osum[:, bstart_:bstart_ + bsz_], osum[:, bstart_:bstart_ + bsz_], pOa[:, :bsz_], ALU.add)

                    def emit_inter(item):
                        (i_, c0_, bstart_, bsz_, lastb_) = item
                        oc_ = c0_ - bstart_
                        mm(pOb[:, oc_:oc_ + 32], Sbf[i_ % 4][:], qt[:, c0_:c0_ + 32])
                        if lastb_:
                            tt(osum[:, bstart_:bstart_ + bsz_], osum[:, bstart_:bstart_ + bsz_], pOb[:, :bsz_], ALU.add)

                    emit_tile(0)
                    for idx, c in enumerate(order):
                        c0 = c * 32
                        sl = idx % 4
                        ti = idx // 4
                        tix = tiles[ti]
                        assert tix == c // 4
                        if idx % 4 == 1 and ti + 1 < len(tiles):
                            emit_tile(ti + 1)
                        if pending and idx % 8 == 3:
                            pending.pop(0)()
                        bstart, bsz = blk_of(c)
                        last_in_blk = (c0 + 32 == bstart + bsz) if d == 0 else (c0 == bstart)
                        mm(pD[sl][:, 0:128], khTm[ti % 2][c % 4][:], v128[:, tix, :])
                        pend.append((idx, c0, bstart, bsz, last_in_blk))
                        if len(pend) > LAG:
                            emit_inter(pend.pop(0))
                        stt(S32[(idx + 1) % 2][:], S32[idx % 2][:], ebend[:, c:c + 1], pD[sl][:, 0:128], ALU.mult, ALU.add, stream_self=True)
                        cp(Sbf[(idx + 1) % 4][:], S32[(idx + 1) % 2][:], eng='act')
                    while pend:
                        emit_inter(pend.pop(0))
                    while pending:
                        pending.pop(0)()
                gcol = hg_g[:, l:l + 1]
                for bi, (b0, bs) in enumerate(BLOCKS):
                    if last and b0 < LC:
                        continue
                    act(sq[:, :bs], osum[:, b0:b0 + bs], AF.Square)
                    px = pOa if bi % 2 == 0 else pOb
                    mm(px[:, :bs], on_bf[:], sq[:, :bs])
                    tsc(rs[:, :bs], px[:, :bs], 1.0 / 128, EPS, ALU.mult, ALU.add)
                    act(rs[:, :bs], rs[:, :bs], AF.Sqrt)
                    recip(rs[:, :bs], rs[:, :bs])
                    tt(on_[:, :bs], osum[:, b0:b0 + bs], rs[:, :bs], ALU.mult)
                    g__ = gt[bi % 2]
                    S.load(g__[:, :bs], HGATE[h * 128:(h + 1) * 128, b0:b0 + bs])
                    tt(on_[:, :bs], on_[:, :bs], g__[:, :bs], ALU.mult)
                    o = ob[bi % 2]
                    act(o[:, :bs], on_[:, :bs], AF.Identity, scale=gcol)
                    S.store(BR[1024 + h * 128:1024 + (h + 1) * 128, b0:b0 + bs], o[:, :bs])
            S.barrier()
        if stop_after == ('gla', l):
            raise _Stop()

        PADT = T + 8
        OFFC, OFFL = 2, 2 + 256 + 4

        def poff(b0):
            return (OFFC + b0) if b0 < LC else (OFFL + (b0 - LC))

        with ExitStack() as ph:
            up = [sb(ph, f"c_u{i}", [128, PADT], BF16) for i in range(3)]
            cw = sb(ph, "c_w", [128, 24, 5], F32)
            S.load(cw[:], conv_wT[l])
            dw = sb(ph, "c_dw", [128, 24, 5, 128], BF16)
            brow_f = sb(ph, "c_brf", [1, 3072], F32)
            brow = sb(ph, "c_br", [1, 3072], BF16)
            S.load(brow_f[:], bass.AP(conv_b.tensor, l * 3072, [[0, 1], [1, 3072]]))
            cp(brow[:], brow_f[:])
            onerow = on_bf[0:1, :]
            stT = [sb(ph, f"c_sT{i}", [128, 512], BF16) for i in range(3)]
            stF = [sb(ph, f"c_sF{i}", [128, 512], BF16) for i in range(3)]
            sgm = [sb(ph, f"c_sg{i}", [128, 512], F32) for i in range(4)]
            pc = [ps(ph, f"c_p{i}", [128, 512], F32) for i in range(4)]
            for i in range(3):
                memset(up[i][:], 0.0)
            for cc in range(24):
                for k in range(5):
                    tsc(dw[:, cc, k, :], id_bf[:], cw[:, cc, k:k + 1], None, ALU.mult)
            n_p = 0
            n_s = 0
            for cc in range(24):
                u = up[cc % 3]
                S.load(u[:, OFFC:OFFC + LC], XBC[cc * 128:(cc + 1) * 128, 0:LC])
                S.load(u[:, OFFL:OFFL + L], XBC[cc * 128:(cc + 1) * 128, LC:T])
                if cc < 20:
                    for tix in range(NT):
                        t0_ = tix * 128
                        p = pc[n_p % 4]
                        n_p += 1
                        po = poff(t0_)
                        for k in range(5):
                            mm(p[:, 0:128], u[:, po + k - 2:po + k - 2 + 128], dw[:, cc, k, :], start=(k == 0), stop=False)
                        mm(p[:, 0:128], onerow, brow[0:1, cc * 128:(cc + 1) * 128], start=False, stop=True)
                        sg_ = sgm[n_s % 4]
                        st = stT[n_s % 3]
                        n_s += 1
                        act(sg_[:, 0:128], p[:, 0:128], AF.Sigmoid)
                        if cc < 16:
                            tt(sg_[:, 0:128], p[:, 0:128], sg_[:, 0:128], ALU.mult)
                            S.store(XS[t0_:t0_ + 128, cc * 128:(cc + 1) * 128], sg_[:, 0:128])
                            continue
                        tt(st[:, 0:128], p[:, 0:128], sg_[:, 0:128], ALU.mult)
                        if cc < 16:
                            S.store(XS[t0_:t0_ + 128, cc * 128:(cc + 1) * 128], st[:, 0:128])
                        else:
                            S.store(BTOK[t0_:t0_ + 128, (cc - 16) * 128:(cc - 15) * 128], st[:, 0:128])
                if cc >= 16:
                    dst = BTT if cc < 20 else CTT
                    r0_ = (cc - 16) * 128 if cc < 20 else (cc - 20) * 128
                    for (b0, bs) in BLOCKS:
                        p = pc[n_p % 4]
                        n_p += 1
                        po = poff(b0)
                        for k in range(5):
                            mm(p[:, :bs], dw[:, cc, k, :], u[:, po + k - 2:po + k - 2 + bs], start=(k == 0), stop=(k == 4))
                        sg_ = sgm[n_s % 4]
                        st = stF[n_s % 3]
                        n_s += 1
                        act(sg_[:, :bs], p[:, :bs], AF.Sigmoid, bias=cbT[:, l, cc:cc + 1])
                        stt(st[:, :bs], p[:, :bs], cbT[:, l, cc:cc + 1], sg_[:, :bs], ALU.add, ALU.mult)
                        S.store(dst[r0_:r0_ + 128, b0:b0 + bs], st[:, :bs])
            S.barrier()
        if stop_after == ('conv', l):
            raise _Stop()

        with ExitStack() as ph:
            ST32 = sb(ph, "s_ST", [128, 2048], F32)
            STb = sb(ph, "s_STb", [128, 2048], BF16)
            dtb = sb(ph, "s_dtb", [128, 64], F32)
            S.load(dtb[:], bass.AP(dt_bias.tensor, l * 64, [[0, 128], [1, 64]]))
            Ab = sb(ph, "s_A", [128, 64], F32)
            S.load(Ab[:], bass.AP(a_log.tensor, l * 64, [[0, 128], [1, 64]]))
            act(Ab[:], Ab[:], AF.Exp)
            tsc(Ab[:], Ab[:], -1.0, None, ALU.mult)
            dsk = sb(ph, "s_dsk", [128, 32], F32)
            S.load(dsk[:], bass.AP(ssm_d.tensor, l * 32, [[0, 128], [1, 32]]))
            xs_ = [sb(ph, f"s_xs{i}", [128, 2048], F32) for i in range(2)]
            btk = [sb(ph, f"s_bt{i}", [128, 512], BF16) for i in range(2)]
            bT_ = [sb(ph, f"s_bT{i}", [128, 4, 128], BF16) for i in range(2)]
            cT_ = [sb(ph, f"s_cT{i}", [128, 4, 128], BF16) for i in range(2)]
            dtr = [sb(ph, f"s_dt{i}", [128, 64], F32) for i in range(2)]
            dtv = [sb(ph, f"s_dtv{i}", [128, 32], F32) for i in range(2)]
            aav = [sb(ph, f"s_a{i}", [128, 32], F32) for i in range(2)]
            eacv = [sb(ph, f"s_eac{i}", [128, 32], F32) for i in range(2)]
            wendv = [sb(ph, f"s_wend{i}", [128, 32], F32) for i in range(2)]
            dendv = [sb(ph, f"s_dend{i}", [128, 32], F32) for i in range(2)]
            xdtv = [sb(ph, f"s_xdt{i}", [128, 2048], BF16) for i in range(2)]
            xdwv = [sb(ph, f"s_xdw{i}", [128, 2048], BF16) for i in range(2)]
            Am = [sb(ph, f"s_Am{i}", [128, 8, 128], F32) for i in range(2)]
            LT = [sb(ph, f"s_LT{i}", [128, 8, 128], F32) for i in range(2)]
            MT = [sb(ph, f"s_MT{i}", [128, 8, 128], BF16) for i in range(2)]
            CBm = [sb(ph, f"s_CB{i}", [128, 128], F32) for i in range(2)]
            ytmp = sb(ph, "s_yt", [128, 512], F32)
            yo = [sb(ph, f"s_yo{i}", [128, 2048], F32) for i in range(2)]
            yfl = [sb(ph, f"s_yf{i}", [128, 2048], F32) for i in range(2)]
            zt = [sb(ph, f"s_z{i}", [128, 2048], F32) for i in range(2)]
            stmp = sb(ph, "s_st", [128, 512], F32)
            junk = sb(ph, "s_junk", [128, 2048], BF16)
            ssq = sb(ph, "s_ssq", [128, 1], F32)
            ynb = sb(ph, "s_ynb", [128, 2048], F32)
            obt = [sb(ph, f"s_ob{i}", [128, 4, 128], BF16) for i in range(2)]
            p_ac = ps(ph, "s_pac", [128, 64], F32)
            p_cb = ps(ph, "s_pcb", [128, 128], F32)
            p_df = [ps(ph, f"s_pdf{i}", [128, 512], F32) for i in range(2)]
            p_y = ps(ph, "s_py", [128, 512], F32)
            p_yi = ps(ph, "s_pyi", [128, 512], F32)
            p_st = ps(ph, "s_pst", [128, 512], F32)
            p_tr = ps(ph, "s_ptr", [128, 4, 128], F32)
            nit = 0
            for d in range(2):
                if d == 1:
                    S.barrier()
                memset(ST32[:], 0.0)
                memset(STb[:], 0.0)
                order = [0, 1] + list(range(2, NT)) if d == 0 else [1, 0] + list(range(NT - 1, 1, -1))
                Ucum = U_f if d == 0 else Lo_f
                Mlhs = Ms_f if d == 0 else Ml_f
                Mcb = U_f if d == 0 else Lo_f
                def aside(tix, sl):
                    t0_ = tix * 128
                    xs = xs_[sl]
                    S.load(xs[:], XS[t0_:t0_ + 128, :])
                    S.load(btk[sl][:], BTOK[t0_:t0_ + 128, :])
                    S.load(bT_[sl][:], BTT[:, t0_:t0_ + 128].rearrange("(g p) t -> p g t", p=128))
                    S.load(cT_[sl][:], CTT[:, t0_:t0_ + 128].rearrange("(g p) t -> p g t", p=128))
                    S.load(dtr[sl][:], DTR[t0_:t0_ + 128, :])
                    if d == 1:
                        S.load(yfl[sl][:], YF[t0_:t0_ + 128, :])
                        S.load(zt[sl][:], ZZ[t0_:t0_ + 128, :])
                    dt_, aa_, eac_, wend_, dend_ = dtv[sl], aav[sl], eacv[sl], wendv[sl], dendv[sl]
                    tt(dt_[:], dtr[sl][:, d * 32:(d + 1) * 32], dtb[:, d * 32:(d + 1) * 32], ALU.add)
                    act(dt_[:], dt_[:], AF.Exp)
                    act(dt_[:], dt_[:], AF.Ln, bias=1.0)
                    tt(aa_[:], dt_[:], Ab[:, d * 32:(d + 1) * 32], ALU.mult)
                    mm(p_ac[:, 0:32], Ucum, aa_[:])
                    mm(p_ac[:, 32:64], ON_f, aa_[:])
                    act(eac_[:], p_ac[:, 0:32], AF.Exp)
                    act(dend_[:], p_ac[:, 32:64], AF.Exp)
                    cp(wend_[:], p_ac[:, 0:32], eng='dve')
                    tt(wend_[:], p_ac[:, 32:64], wend_[:], ALU.subtract)
                    act(wend_[:], wend_[:], AF.Exp)
                    xs3 = xs[:].rearrange("p (h q) -> p h q", q=64)
                    tt(xdtv[sl][:].rearrange("p (h q) -> p h q", q=64), xs3, _bc_last(dt_[:], 64), ALU.mult, eng='pool')
                    tt(wend_[:], wend_[:], dt_[:], ALU.mult)
                    tt(xdwv[sl][:].rearrange("p (h q) -> p h q", q=64), xs3, _bc_last(wend_[:], 64), ALU.mult)

                aside(order[0], nit % 2)
                for oi, tix in enumerate(order):
                    t0_ = tix * 128
                    sl = nit % 2
                    nit += 1
                    if oi + 1 < len(order):
                        aside(order[oi + 1], nit % 2)
                    xs = xs_[sl]
                    xs3 = xs[:].rearrange("p (h q) -> p h q", q=64)
                    aa_, eac, dend, xdt, xdw = aav[sl], eacv[sl], dendv[sl], xdtv[sl], xdwv[sl]
                    yout = yo[sl]
                    for g in range(4):
                        gs = (nit * 4 + g) % 2
                        mm(p_cb[:, 0:128], bT_[sl][:, g, :], cT_[sl][:, g, :])
                        tt(CBm[gs][:], p_cb[:, 0:128], Mcb, ALU.mult)
                        tt(Am[gs][:], _bc_mid(Ucum, 8), _bc_last(aa_[:, g * 8:(g + 1) * 8], 128), ALU.mult, eng='pool')
                        for half in range(2):
                            mm(p_df[half][:, 0:512], Mlhs, Am[gs][:, half * 4:(half + 1) * 4, :].rearrange("p h t -> p (h t)"))
                        act(LT[gs][:, 0:4, :], p_df[0][:].rearrange("p (h t) -> p h t", t=128), AF.Exp)
                        act(LT[gs][:, 4:8, :], p_df[1][:].rearrange("p (h t) -> p h t", t=128), AF.Exp)
                        tt(MT[gs][:], LT[gs][:], _bc_mid(CBm[gs][:], 8), ALU.mult)
                        dbgstop(3)
                        for hh in range(8):
                            hg = g * 8 + hh
                            mm(p_y[:, hh * 64:(hh + 1) * 64], MT[gs][:, hh, :], xdt[:, hg * 64:(hg + 1) * 64])
                        mm(p_yi[:], cT_[sl][:, g, :], STb[:, g * 512:(g + 1) * 512])
                        tt(ytmp[:].rearrange("p (h q) -> p h q", q=64), p_yi[:].rearrange("p (h q) -> p h q", q=64),
                           _bc_last(eac[:, g * 8:(g + 1) * 8], 64), ALU.mult)
                        tt(yout[:, g * 512:(g + 1) * 512], ytmp[:], p_y[:], ALU.add)
                        dbgstop(4)
                        mm(p_st[:], btk[sl][:, g * 128:(g + 1) * 128], xdw[:, g * 512:(g + 1) * 512])
                        tt(stmp[:].rearrange("p (h q) -> p h q", q=64),
                           ST32[:, g * 512:(g + 1) * 512].rearrange("p (h q) -> p h q", q=64),
                           _bc_last(dend[:, g * 8:(g + 1) * 8], 64), ALU.mult)
                        tt(ST32[:, g * 512:(g + 1) * 512], stmp[:], p_st[:], ALU.add)
                        cp(STb[:, g * 512:(g + 1) * 512], ST32[:, g * 512:(g + 1) * 512], eng='act')
                        dbgstop(5)
                    if d == 0:
                        S.store(YF[t0_:t0_ + 128, :], yout[:])
                        dbgstop(6)
                        if tix == 33:
                            dbgstop(7)
                    else:
                        if last and tix < 2:
                            continue
                        tt(yout[:], yout[:], yfl[sl][:], ALU.add)
                        tt(yfl[sl][:].rearrange("p (h q) -> p h q", q=64), xs3, _bc_last(dsk[:], 64), ALU.mult)
                        tt(yout[:], yout[:], yfl[sl][:], ALU.add)
                        tt(yout[:], yout[:], zt[sl][:], ALU.mult)
                        act(junk[:], yout[:], AF.Square, accum=ssq[:])
                        tsc(ssq[:], ssq[:], 1.0 / 2048, EPS, ALU.mult, ALU.add)
                        act(ssq[:], ssq[:], AF.Sqrt)
                        recip(ssq[:], ssq[:])
                        tsc(ynb[:], yout[:], ssq[:, 0:1], None, ALU.mult)
                        for q4 in range(4):
                            for j in range(4):
                                tr(p_tr[:, j * 128:(j + 1) * 128], ynb[:, (q4 * 4 + j) * 128:(q4 * 4 + j + 1) * 128], ID_f)
                            o = obt[q4 % 2]
                            for j in range(4):
                                act(o[:, j, :], p_tr[:, j * 128:(j + 1) * 128], AF.Identity, scale=ssm_g[:, l, q4 * 4 + j:q4 * 4 + j + 1])
                            S.store(BR[2048 + q4 * 512:2048 + (q4 + 1) * 512, t0_:t0_ + 128].rearrange("(j p) t -> p j t", p=128), o[:])
                S.barrier()
        if stop_after == ('ssd', l):
            raise _Stop()

        with ExitStack() as ph:
            wall = sb(ph, "m_w", [128, 40, 1024], BF16)
            with ExitStack() as pw:
                wstg = [sb(pw, f"m_ws{i}", [128, 4, 1024], F32) for i in range(2)]
                srcs = [(w_b_att[l], 8), (w_b_hg[l], 8), (w_b_ssm[l], 16), (w_out[l], 8)]
                kc0 = 0
                i = 0
                for (src, nk_) in srcs:
                    for k4 in range(0, nk_, 4):
                        f = wstg[i % 2]
                        S.load(f[:], src[k4 * 128:(k4 + 4) * 128, :].rearrange("(k p) c -> p k c", p=128))
                        for kk_ in range(4):
                            cp(wall[:, kc0 + k4 + kk_, :], f[:, kk_, :], eng=('dve' if (kk_ + i) % 2 == 0 else 'act'),
                               sub={wall.name: kc0 + k4 + kk_})
                        i += 1
                    kc0 += nk_
                S.barrier()
            MB = 256
            brb = [sb(ph, f"m_br{i}", [128, 32, MB], BF16) for i in range(2)]
            gb = [sb(ph, f"m_g{i}", [128, 24, MB], F32) for i in range(2)]
            xb = [sb(ph, f"m_x{i}", [128, 8, MB], F32) for i in range(2)]
            macc = sb(ph, "m_acc", [128, MB], F32)
            mtmp = sb(ph, "m_tmp", [128, MB], F32)
            mT = sb(ph, "m_mT", [128, 8, MB], BF16)
            pm = [ps(ph, f"m_p{i}", [128, MB], F32) for i in range(4)]
            npm = 0
            bi = 0
            for b0 in range(0, T, MB):
                if last and b0 < LC:
                    continue
                bs = MB
                s = strm(b0)
                br = brb[bi % 2]
                g_ = gb[bi % 2]
                x_ = xb[bi % 2]
                bi += 1
                S.load(br[:], BR[:, b0:b0 + bs].rearrange("(k p) t -> p k t", p=128))
                S.load(g_[:], GATES[:, b0:b0 + bs].rearrange("(k p) t -> p k t", p=128))
                S.load(x_[:], xsrc[:, b0:b0 + bs].rearrange("(k p) t -> p k t", p=128))
                for oc in range(8):
                    for bri, (k0, nk_) in enumerate(((0, 8), (8, 8), (16, 16))):
                        p = pm[npm % 4]
                        npm += 1
                        for k in range(nk_):
                            mm(p[:, :MB], wall[:, k0 + k, oc * 128:(oc + 1) * 128], br[:, k0 + k, :], start=(k == 0), stop=(k == nk_ - 1))
                        if bri == 0:
                            tt(macc[:], p[:, :MB], g_[:, oc, :], ALU.mult)
                        else:
                            tt(mtmp[:], p[:, :MB], g_[:, bri * 8 + oc, :], ALU.mult)
                            if bri == 1:
                                tt(macc[:], macc[:], mtmp[:], ALU.add)
                            else:
                                tt(mT[:, oc, :], macc[:], mtmp[:], ALU.add)
                for oc in range(8):
                    p = pm[npm % 4]
                    npm += 1
                    for k in range(8):
                        mm(p[:, :MB], wall[:, 32 + k, oc * 128:(oc + 1) * 128], mT[:, k, :], start=(k == 0), stop=(k == 7))
                    stt(x_[:, oc, :], p[:, :MB], modT[:, l, 16 + oc, s:s + 1], x_[:, oc, :], ALU.mult, ALU.add)
                S.store(XT[:, b0:b0 + bs].rearrange("(k p) t -> p k t", p=128), x_[:])
            S.barrier()
        if stop_after == ('merge', l):
            raise _Stop()

        with ExitStack() as ph:
            w1b = sb(ph, "f_w1", [128, 8, FFH], BF16)
            w3b = sb(ph, "f_w3", [128, 8, FFH], BF16)
            w2b = sb(ph, "f_w2", [128, 22, 1024], BF16)
            with ExitStack() as pw:
                wstg = [sb(pw, f"f_ws{i}", [128, FFH], F32) for i in range(2)]
                i = 0
                for (src, dstw) in ((ffn_w1[l], w1b), (ffn_w3[l], w3b)):
                    for k in range(8):
                        f = wstg[i % 2]
                        S.load(f[:], src[k * 128:(k + 1) * 128, :])
                        cp(dstw[:, k, :], f[:], eng=('dve' if i % 2 == 0 else 'act'))
                        i += 1
                for k in range(22):
                    f = wstg[i % 2]
                    S.load(f[:, 0:1024], ffn_w2[l][k * 128:(k + 1) * 128, :])
                    cp(w2b[:, k, :], f[:, 0:1024], eng=('dve' if i % 2 == 0 else 'act'))
                    i += 1
                S.barrier()
            FB = 256
            xb = [sb(ph, f"f_x{i}", [128, 8, FB], F32) for i in range(2)]
            sq = sb(ph, "f_sq", [128, 8, FB], BF16)
            rs = sb(ph, "f_rs", [128, FB], F32)
            tmp = sb(ph, "f_tmp", [128, 8, FB], F32)
            h2 = sb(ph, "f_h2", [128, 8, FB], BF16)
            uT = sb(ph, "f_u", [128, 22, FB], BF16)
            sg_ = [sb(ph, f"f_sg{i}", [128, FB], F32) for i in range(2)]
            s1 = [sb(ph, f"f_s1{i}", [128, FB], F32) for i in range(2)]
            yo_ = [sb(ph, f"f_yo{i}", [128, 8, FB], F32) for i in range(1)]
            pst = ps(ph, "f_pst", [128, FB], F32)
            pf = [ps(ph, f"f_p{i}", [128, FB], F32) for i in range(6)]
            npf = 0
            nblk = 0
            for b0 in range(0, T, FB):
                if last and b0 < LC:
                    continue
                bs = FB
                s = strm(b0)
                x_ = xb[nblk % 2]
                nblk += 1
                S.load(x_[:], XT[:, b0:b0 + bs].rearrange("(k p) t -> p k t", p=128))
                norm_block((sq, pst, rs, tmp), x_, bs,
                           lambda k: A2[:, l, k, s:s + 1], lambda k: modT[:, l, 24 + k, s:s + 1],
                           lambda k: h2[:, k, :])
                for hc in range(22):
                    pa = pf[npf % 6]
                    pb_ = pf[(npf + 1) % 6]
                    npf += 2
                    for k in range(8):
                        mm(pa[:, :FB], w1b[:, k, hc * 128:(hc + 1) * 128], h2[:, k, :], start=(k == 0), stop=(k == 7))
                    for k in range(8):
                        mm(pb_[:, :FB], w3b[:, k, hc * 128:(hc + 1) * 128], h2[:, k, :], start=(k == 0), stop=(k == 7))
                    sg = sg_[hc % 2]
                    s1_ = s1[hc % 2]
                    act(sg[:], pa[:, :FB], AF.Sigmoid)
                    tt(s1_[:], pa[:, :FB], sg[:], ALU.mult)
                    tt(uT[:, hc, :], pb_[:, :FB], s1_[:], ALU.mult)
                for oc in range(8):
                    p = pf[npf % 6]
                    npf += 1
                    for k in range(22):
                        mm(p[:, :FB], w2b[:, k, oc * 128:(oc + 1) * 128], uT[:, k, :], start=(k == 0), stop=(k == 21))
                    stt(x_[:, oc, :], p[:, :FB], modT[:, l, 40 + oc, s:s + 1], x_[:, oc, :], ALU.mult, ALU.add)
                if not last:
                    S.store(XT[:, b0:b0 + bs].rearrange("(k p) t -> p k t", p=128), x_[:])
                else:
                    yo = yo_[0]
                    norm_block((sq, pst, rs, tmp), x_, bs,
                               lambda k: gfs[:, k:k + 1], lambda k: None,
                               lambda k: yo[:, k, :])
                    S.store(yT[:, b0 - LC:b0 - LC + bs].rearrange("(k p) t -> p k t", p=128), yo[:])
            S.barrier()


    stopped = False
    try:
        for l in range(DEPTH):
            layer_body(l)
    except _Stop:
        stopped = True
    S.barrier()
    if not stopped:
        per.close()
        es.close()
    return nc, S


_CACHE = {}


def _consts():
    j = np.arange(128)[:, None]
    t = np.arange(128)[None, :]
    c = np.zeros((128, 6 * 128 + 64 + 4 + 256), np.float32)
    c[:, 0:128] = (j <= t)
    c[:, 128:256] = (j >= t)
    c[:, 256:384] = (j > t)
    c[:, 384:512] = (j < t)
    c[:, 512:640] = np.eye(128)
    c[:, 640:768] = 1.0
    s = np.arange(32)[:, None]
    tt_ = np.arange(32)[None, :]
    c[0:32, 768:800] = (s <= tt_)
    c[0:32, 800:832] = (s >= tt_)
    for q in range(4):
        c[32 * q:32 * q + 32, 832 + q] = 1.0
    same = (j // 32) == (t // 32)
    c[:, 836:964] = same & (j <= t)
    c[:, 964:1092] = same & (j >= t)
    return c


def _m0():
    m0 = np.ones(T, np.float32)
    m0[::32] = 0.0
    return np.ascontiguousarray(np.broadcast_to(m0[None, :], (128, T))).astype(ml_dtypes.bfloat16)


def _rope_tables():
    cos = np.ones((128, T), np.float64)
    sin = np.zeros((128, T), np.float64)
    tl = np.arange(L)
    row = (tl // 64).astype(np.float64)
    col = (tl % 64).astype(np.float64)
    for f in range(128):
        r = f % 32
        half_sel = (f % 64) // 32
        fi = r % 16
        freq = np.float32(10000.0) ** (-np.float32(fi) / np.float32(16))
        pos = row if half_sel == 0 else col
        ang = (pos.astype(np.float32) * np.float32(freq)).astype(np.float64)
        cos[f, LC:] = np.cos(ang)
        sgn = -1.0 if r < 16 else 1.0
        sin[f, LC:] = sgn * np.sin(ang)
    return cos.astype(np.float32), sin.astype(np.float32)


def _rot_perm():
    p = np.arange(1024)
    r = p % 32
    return np.where(r < 16, p + 16, p - 16)


def _prep_shared(inp):
    f = np.float32
    A = lambda a: np.ascontiguousarray(a, dtype=f)
    perm = _rot_perm()
    w_in = inp['w_in']
    w_rot = np.concatenate([w_in[:, :, O_AK:O_AK + 1024][:, :, perm], w_in[:, :, O_AQ:O_AQ + 1024][:, :, perm]], axis=2)
    cosT, sinT = _rope_tables()
    col8 = lambda g: A(g.reshape(8, 128).T)
    sh = {
        'w_ada': A(inp['w_ada']),
        'b_adaT': A(inp['b_ada'].reshape(DEPTH, 48, 128).transpose(0, 2, 1)),
        'g1T': A(inp['norm1_g'].reshape(DEPTH, 8, 128).transpose(0, 2, 1)),
        'g2T': A(inp['norm2_g'].reshape(DEPTH, 8, 128).transpose(0, 2, 1)),
        'gfT': col8(inp['final_g']),
        'w_in': A(w_in),
        'w_rot': A(w_rot),
        'cosT': cosT, 'sinT': sinT,
        'att_lam': A(inp['att_lambda'].reshape(DEPTH, 256)),
        'att_gT': A(inp['att_norm_g'].T),
        'hg_lbT': A(inp['hg_lb_logits'].reshape(2, DEPTH, 8, 128).transpose(3, 0, 1, 2)),
        'hg_gT': A(inp['hg_norm_g'].T),
        'conv_wT': A(inp['ssm_conv_w'].reshape(DEPTH, 5, 24, 128).transpose(0, 3, 2, 1)),
        'conv_bT': A(inp['ssm_conv_b'].reshape(DEPTH, 24, 128).transpose(0, 2, 1)),
        'conv_b': A(inp['ssm_conv_b']),
        'dt_bias': A(inp['ssm_dt_bias'].reshape(DEPTH, 64)),
        'a_log': A(inp['ssm_a_log'].reshape(DEPTH, 64)),
        'ssm_d': A(inp['ssm_d']),
        'ssm_gT': A(inp['ssm_norm_g'].reshape(DEPTH, 16, 128).transpose(0, 2, 1)),
        'w_b_att': A(inp['w_branch_att']), 'w_b_hg': A(inp['w_branch_hg']), 'w_b_ssm': A(inp['w_branch_ssm']),
        'w_out': A(inp['w_out']),
        'ffn_w1': A(inp['ffn_w1']), 'ffn_w3': A(inp['ffn_w3']), 'ffn_w2': A(inp['ffn_w2']),
        'cst': _consts(),
        'm0': _m0(),
    }
    return sh


def kernel(**inputs):
    inp = {k: np.asarray(v) for k, v in inputs.items()}
    if 'nc' not in _CACHE:
        _CACHE['nc'] = build_program()[0]
    nc = _CACHE['nc']
    sh = _prep_shared(inp)
    in_maps = []
    for b in range(8):
        m = dict(sh)
        m['xT'] = np.ascontiguousarray(np.concatenate([inp['ctx'][b], inp['x'][b]], axis=0).T, dtype=np.float32)
        m['c2'] = np.ascontiguousarray(np.stack([inp['c_ctx'], inp['c'][b]], axis=1), dtype=np.float32)
        in_maps.append(m)
    res = run_bass_kernel_spmd(nc, in_maps, core_ids=list(range(8)))
    out = np.stack([np.ascontiguousarray(res.results[b]['yT'].T) for b in range(8)], axis=0)
    return out.astype(np.float32)
```

```python
import math
from contextlib import ExitStack
import numpy as np
import ml_dtypes
import concourse.bass as bass
import concourse.mybir as mybir
from concourse.bass_utils import run_bass_kernel_spmd

F32 = mybir.dt.float32
BF16 = mybir.dt.bfloat16
AF = mybir.ActivationFunctionType
ALU = mybir.AluOpType

D = 1024
LC = 256
L = 4096
T = LC + L
NT = T // 128
NCH = T // 32
DEPTH = 2
EPS = 1e-6
NIN = 16448
FFH = 2816
BLOCKS = [(0, 256)] + [(256 + 512 * i, 512) for i in range(8)]
O_AK, O_AV, O_HFF, O_HFB, O_HI, O_XBC, O_DTF, O_DTB, O_AQ, O_HQ, O_HGATE, O_Z, O_GATES = (
    0, 1024, 2048, 3072, 4096, 5120, 8192, 8224, 8256, 9280, 10304, 11328, 13376)


class _Stop(Exception):
    pass


class Sched:
    def __init__(self, nc, es):
        self.nc = nc
        self.eng = {'pe': nc.tensor, 'act': nc.scalar, 'dve': nc.vector, 'pool': nc.gpsimd, 'sp': nc.sync}
        self.semh = {}
        self.cnt = {}
        for k in ['pe', 'act', 'dve', 'pool']:
            self.semh[k] = es.enter_context(nc.semaphore('s_' + k))
            self.cnt[k] = 0
        self.dpool = {'sp': [], 'pool': []}
        for q, n in (('sp', 24), ('pool', 16)):
            for i in range(n):
                nm = f'd_{q}{i}'
                self.semh[nm] = es.enter_context(nc.semaphore(nm))
                self.cnt[nm] = 0
                self.dpool[q].append(nm)
        self.drr = {'sp': 0, 'pool': 0}
        self.seen = {k: {} for k in self.eng}
        self.res = {}
        self.nops = 0
        self.psum = set()

    def _keys(self, ap, sub):
        nm = ap.tensor.name
        if nm in self.psum:
            return [(nm, None)]
        if sub is not None and nm in sub:
            v = sub[nm]
            if isinstance(v, (list, tuple)):
                return [(nm, x) for x in v]
            return [(nm, v)]
        return [(nm, None)]

    def _deps(self, rkeys, wkeys, eng=None):
        deps = []
        for k in rkeys:
            st = self.res.get(k)
            if st is not None and st[0] is not None:
                deps.append(st[0])
        for k in wkeys:
            st = self.res.get(k)
            if st is not None:
                if st[0] is not None and st[0][0] != eng:
                    deps.append(st[0])
                deps.extend(e for e in st[1] if e[0] != eng)
        return deps

    def _wait(self, eng, deps):
        need = {}
        for (k, v) in deps:
            if eng == 'pe' and k == 'pe':
                continue
            if self.seen[eng].get(k, 0) >= v:
                continue
            if need.get(k, 0) < v:
                need[k] = v
        for k, v in need.items():
            self.eng[eng].wait_ge(self.semh[k], v)
            self.seen[eng][k] = v

    def _record(self, ev, rkeys, wkeys):
        for k in rkeys:
            st = self.res.setdefault(k, [None, []])
            st[1] = [e for e in st[1] if e[0] != ev[0]] + [ev]
        for k in wkeys:
            self.res[k] = [ev, []]

    def op(self, eng, fn, ins, outs, sub=None, stream_self=False):
        rkeys = [k for a in ins if a is not None and hasattr(a, 'tensor') for k in self._keys(a, sub)]
        wkeys = [k for a in outs for k in self._keys(a, sub)]
        wkeys += [k for k in rkeys if k[0] in self.psum and k not in wkeys]
        deps = self._deps(rkeys, wkeys, eng)
        if stream_self:
            deps = [d_ for d_ in deps if d_[0] != eng]
        self._wait(eng, deps)
        inst = fn(self.eng[eng])
        self.cnt[eng] += 1
        inst.then_inc(self.semh[eng], 1)
        self._record((eng, self.cnt[eng]), rkeys, wkeys)
        self.nops += 1

    def dma(self, q, out, in_, sub=None, track_out=True, track_in=True):
        rkeys = self._keys(in_, sub) if track_in else []
        wkeys = self._keys(out, sub) if track_out else []
        pool = self.dpool[q]
        nm = pool[self.drr[q] % len(pool)]
        self.drr[q] += 1
        deps = self._deps(rkeys, wkeys)
        if self.cnt[nm] > 0:
            deps.append((nm, self.cnt[nm]))
        self._wait(q, deps)
        self.eng[q].dma_start(out=out, in_=in_).then_inc(self.semh[nm], 16)
        self.cnt[nm] += 16
        self._record((nm, self.cnt[nm]), rkeys, wkeys)
        self.nops += 1

    def load(self, out, in_, sub=None):
        self.dma('sp', out, in_, sub=sub, track_in=False)

    def store(self, out, in_, sub=None):
        self.dma('pool', out, in_, sub=sub, track_out=False)

    def barrier(self):
        allev = [(k, v) for k, v in self.cnt.items() if v > 0]
        for e in self.eng:
            self._wait(e, allev)
        self.res = {}


def _bc_mid(a, n):
    ap = [list(x) for x in a.ap]
    return bass.AP(a.tensor, a.offset, [ap[0], [0, n]] + ap[1:])


def _bc_last(a, n):
    ap = [list(x) for x in a.ap]
    return bass.AP(a.tensor, a.offset, ap + [[0, n]])


def _rev(a):
    ap = [list(x) for x in a.ap]
    assert len(ap) == 2 and ap[1][0] == 1
    return bass.AP(a.tensor, a.offset + ap[1][1] - 1, [ap[0], [-1, ap[1][1]]])


def _strided(a, start, step, n):
    ap = [list(x) for x in a.ap]
    return bass.AP(a.tensor, a.offset + start, [ap[0], [step, n]])


def _pbcast(dram_ap_1d_offset_tensor, offset, n):
    return bass.AP(dram_ap_1d_offset_tensor, offset, [[0, 128], [1, n]])


def build_program(stop_after=None, dump=None):
    nc = bass.Bass("TRN2", target_bir_lowering=False)
    es = ExitStack()
    S = Sched(nc, es)

    def din(name, shape, dt=F32):
        return nc.dram_tensor(name, list(shape), dt, kind="ExternalInput").ap()

    def dscr(name, shape, dt):
        if dump is not None and name in dump:
            return nc.dram_tensor(name, list(shape), dt, kind="ExternalOutput").ap()
        return nc.dram_tensor(name, list(shape), dt).ap()

    xT_in = din("xT", [D, T])
    c2_in = din("c2", [D, 2])
    w_ada = din("w_ada", [DEPTH, D, 6 * D])
    b_adaT = din("b_adaT", [DEPTH, 128, 48])
    g1T = din("g1T", [DEPTH, 128, 8])
    g2T = din("g2T", [DEPTH, 128, 8])
    gfT = din("gfT", [128, 8])
    w_in = din("w_in", [DEPTH, D, NIN])
    w_rot = din("w_rot", [DEPTH, D, 2048])
    cosT = din("cosT", [128, T])
    sinT = din("sinT", [128, T])
    att_lam = din("att_lam", [DEPTH, 256])
    att_gT = din("att_gT", [128, DEPTH])
    hg_lbT = din("hg_lbT", [128, 2, DEPTH, 8])
    hg_gT = din("hg_gT", [128, DEPTH])
    conv_wT = din("conv_wT", [DEPTH, 128, 24, 5])
    conv_bT = din("conv_bT", [DEPTH, 128, 24])
    conv_b = din("conv_b", [DEPTH, 3072])
    dt_bias = din("dt_bias", [DEPTH, 64])
    a_log = din("a_log", [DEPTH, 64])
    ssm_d = din("ssm_d", [DEPTH, 32])
    ssm_gT = din("ssm_gT", [DEPTH, 128, 16])
    w_b_att = din("w_b_att", [DEPTH, 1024, D])
    w_b_hg = din("w_b_hg", [DEPTH, 1024, D])
    w_b_ssm = din("w_b_ssm", [DEPTH, 2048, D])
    w_out = din("w_out", [DEPTH, D, D])
    ffn_w1 = din("ffn_w1", [DEPTH, D, FFH])
    ffn_w3 = din("ffn_w3", [DEPTH, D, FFH])
    ffn_w2 = din("ffn_w2", [DEPTH, FFH, D])
    cst_in = din("cst", [128, 6 * 128 + 64 + 4 + 256])
    m0_in = din("m0", [128, T], BF16)
    yT = nc.dram_tensor("yT", [D, L], F32, kind="ExternalOutput").ap()

    XT = dscr("XT", [D, T], F32)
    QT = dscr("QT", [1024, T], BF16)
    KT = dscr("KT", [1024, T], BF16)
    AV = dscr("AV", [T, 1024], BF16)
    HFF = dscr("HFF", [1024, T], F32)
    HFB = dscr("HFB", [1024, T], F32)
    HQ = dscr("HQ", [1024, T], F32)
    HI = dscr("HI", [T, 1024], BF16)
    HGATE = dscr("HGATE", [1024, T], F32)
    XBC = dscr("XBC", [3072, T], BF16)
    DTR = dscr("DTR", [T, 64], F32)
    ZZ = dscr("ZZ", [T, 2048], F32)
    GATES = dscr("GATES", [3072, T], F32)
    XS = dscr("XS", [T, 2048], F32)
    BTOK = dscr("BTOK", [T, 512], BF16)
    BTT = dscr("BTT", [512, T], BF16)
    CTT = dscr("CTT", [512, T], BF16)
    YF = dscr("YF", [T, 2048], F32)
    BR = dscr("BR", [4096, T], BF16)

    uid = [0]

    def sb(ctx, name, shape, dt):
        uid[0] += 1
        return ctx.enter_context(nc.sbuf_tensor(f"{name}_{uid[0]}", list(shape), dt))

    def ps(ctx, name, shape, dt=F32):
        uid[0] += 1
        t = ctx.enter_context(nc.psum_tensor(f"{name}_{uid[0]}", [128, 512] if dt == F32 else [128, 1024], dt))
        S.psum.add(t.name)
        return t

    def mm(out, lhsT, rhs, start=True, stop=True, sub=None):
        S.op('pe', lambda e: e.matmul(out, lhsT=lhsT, rhs=rhs, start=start, stop=stop), [lhsT, rhs], [out], sub)

    def tr(out, in_, ident, sub=None):
        S.op('pe', lambda e: e.transpose(out, in_, ident), [in_, ident], [out], sub)

    def act(out, in_, func, bias=None, scale=None, accum=None, sub=None, eng='act'):
        kw = {}
        if bias is not None:
            kw['bias'] = bias
        if scale is not None:
            kw['scale'] = scale
        if accum is not None:
            kw['accum_out'] = accum
        ins = [in_] + [x for x in (bias, scale) if hasattr(x, 'tensor')]
        outs = [out] + ([accum] if accum is not None else [])
        S.op('act', lambda e: e.activation(out=out, in_=in_, func=func, **kw), ins, outs, sub)

    def tt(out, in0, in1, op, sub=None, eng='dve'):
        S.op(eng, lambda e: e.tensor_tensor(out=out, in0=in0, in1=in1, op=op), [in0, in1], [out], sub)

    def tsc(out, in0, s1, s2, op0, op1=None, sub=None, eng='dve'):
        ins = [in0] + [x for x in (s1, s2) if hasattr(x, 'tensor')]
        if op1 is None:
            S.op(eng, lambda e: e.tensor_scalar(out=out, in0=in0, scalar1=s1, scalar2=None, op0=op0), ins, [out], sub)
        else:
            S.op(eng, lambda e: e.tensor_scalar(out=out, in0=in0, scalar1=s1, scalar2=s2, op0=op0, op1=op1), ins, [out], sub)

    def stt(out, in0, scalar, in1, op0, op1, sub=None, stream_self=False):
        ins = [in0, in1] + ([scalar] if hasattr(scalar, 'tensor') else [])
        S.op('dve', lambda e: e.scalar_tensor_tensor(out=out, in0=in0, scalar=scalar, in1=in1, op0=op0, op1=op1), ins, [out], sub,
             stream_self=stream_self)

    def cp(out, in_, sub=None, eng='dve'):
        if eng == 'act':
            act(out, in_, AF.Copy, sub=sub)
        else:
            S.op(eng, lambda e: e.tensor_copy(out=out, in_=in_), [in_], [out], sub)

    def recip(out, in_, sub=None):
        S.op('dve', lambda e: e.reciprocal(out=out, in_=in_), [in_], [out], sub)

    def memset(ap, val, eng='dve', sub=None):
        S.op(eng, lambda e: e.memset(ap, val), [], [ap], sub)

    def scan(out, d0, d1, sub=None):
        S.op('dve', lambda e: e.tensor_tensor_scan(out=out, data0=d0, data1=d1, initial=0.0, op0=ALU.mult, op1=ALU.add),
             [d0, d1], [out], sub)

    per = ExitStack()
    cst = sb(per, "cst", [128, 6 * 128 + 64 + 4 + 256], F32)
    S.load(cst[:], cst_in[:, :])
    U_f = cst[:, 0:128]
    Lo_f = cst[:, 128:256]
    Ms_f = cst[:, 256:384]
    Ml_f = cst[:, 384:512]
    ID_f = cst[:, 512:640]
    ON_f = cst[:, 640:768]
    RM = cst[:, 832:836]
    BDf = cst[:, 836:964]
    BDb = cst[:, 964:1092]
    M32 = cst[:, 768:832]
    id_bf = sb(per, "id_bf", [128, 128], BF16)
    on_bf = sb(per, "on_bf", [128, 128], BF16)
    cp(id_bf[:], ID_f)
    cp(on_bf[:], ON_f)
    modT = sb(per, "modT", [128, DEPTH, 48, 2], F32)
    A1 = sb(per, "A1", [128, DEPTH, 8, 2], F32)
    A2 = sb(per, "A2", [128, DEPTH, 8, 2], F32)
    g1s = sb(per, "g1s", [128, DEPTH, 8], F32)
    g2s = sb(per, "g2s", [128, DEPTH, 8], F32)
    gfs = sb(per, "gfs", [128, 8], F32)
    S.load(g1s[:], g1T.rearrange("l p k -> p l k"))
    S.load(g2s[:], g2T.rearrange("l p k -> p l k"))
    S.load(gfs[:], gfT[:, :])
    att_g = sb(per, "att_g", [128, DEPTH], F32)
    hg_g = sb(per, "hg_g", [128, DEPTH], F32)
    S.load(att_g[:], att_gT[:, :])
    S.load(hg_g[:], hg_gT[:, :])
    lbx = sb(per, "lbx", [128, 2, DEPTH, 8], F32)
    S.load(lbx[:], hg_lbT[:, :, :, :])
    lb1 = sb(per, "lb1", [128, 2, 8], F32)
    oml1 = sb(per, "oml1", [128, 2, 8], F32)
    neglam = sb(per, "neglam", [128, DEPTH], F32)
    ssm_g = sb(per, "ssm_g", [128, DEPTH, 16], F32)
    S.load(ssm_g[:], ssm_gT.rearrange("l p k -> p l k"))
    cbT = sb(per, "cbT", [128, DEPTH, 24], F32)
    S.load(cbT[:], conv_bT.rearrange("l p k -> p l k"))

    with ExitStack() as ph:
        sc = sb(ph, "p0_sc", [128, 8, 2], F32)
        S.load(sc[:], c2_in.rearrange("(k p) s -> p k s", p=128))
        sg = sb(ph, "p0_sg", [128, 8, 2], F32)
        act(sg[:], sc[:], AF.Sigmoid)
        tt(sc[:], sc[:], sg[:], ALU.mult)
        badd = sb(ph, "p0_b", [128, DEPTH, 48], F32)
        S.load(badd[:], b_adaT.rearrange("l p k -> p l k"))
        wst = [sb(ph, f"p0_w{i}", [128, 8, 512], F32) for i in range(2)]
        pm = ps(ph, "p0_pm", [128, 96], F32)
        it = 0
        for l in range(DEPTH):
            for cg in range(12):
                w = wst[it % 2]
                it += 1
                S.load(w[:], w_ada[l, :, cg * 512:(cg + 1) * 512].rearrange("(k p) c -> p k c", p=128))
                for j in range(4):
                    ch = cg * 4 + j
                    for k in range(8):
                        mm(pm[:, ch * 2:ch * 2 + 2], w[:, k, j * 128:(j + 1) * 128], sc[:, k, :], start=(k == 0), stop=(k == 7))
            tt(modT[:, l, :, :], pm[:, 0:96].rearrange("p (c s) -> p c s", s=2), _bc_last(badd[:, l, :], 2), ALU.add)
            tsc(A1[:, l, :, :], modT[:, l, 8:16, :], 1.0, None, ALU.add)
            tt(A1[:, l, :, :], A1[:, l, :, :], _bc_last(g1s[:, l, :], 2), ALU.mult)
            tsc(A2[:, l, :, :], modT[:, l, 32:40, :], 1.0, None, ALU.add)
            tt(A2[:, l, :, :], A2[:, l, :, :], _bc_last(g2s[:, l, :], 2), ALU.mult)
        lamt = sb(ph, "p0_lam", [128, DEPTH, 4, 64], F32)
        S.load(lamt[:].rearrange("p a b c -> p (a b c)"), bass.AP(att_lam.tensor, 0, [[0, 128], [1, DEPTH * 256]]))
        pr = sb(ph, "p0_pr", [128, 64], F32)
        sm = sb(ph, "p0_sm", [128, 4], F32)
        for l in range(DEPTH):
            for j in range(2):
                tt(pr[:], lamt[:, l, 2 * j, :], lamt[:, l, 2 * j + 1, :], ALU.mult)
                act(pr[:], pr[:], AF.Copy, accum=sm[:, 2 * l + j:2 * l + j + 1])
        act(sm[:], sm[:], AF.Exp)
        for l in range(DEPTH):
            lam_init = 0.8 - 0.6 * math.exp(-0.3 * l)
            tt(neglam[:, l:l + 1], sm[:, 2 * l + 1:2 * l + 2], sm[:, 2 * l:2 * l + 1], ALU.subtract)
            tsc(neglam[:, l:l + 1], neglam[:, l:l + 1], -lam_init, None, ALU.add)
        tt(lb1[:], lbx[:, :, 1, :], lbx[:, :, 0, :], ALU.subtract)
        act(lb1[:], lb1[:], AF.Sigmoid)
        tsc(oml1[:], lb1[:], -1.0, 1.0, ALU.mult, ALU.add)
        S.barrier()

    def norm_block(ph_tiles, xb, bs, A_of_k, B_of_k, out_of_k):
        sq, pst, rs, tmp = ph_tiles
        for k in range(8):
            act(sq[:, k, :bs], xb[:, k, :bs], AF.Square)
        for k in range(8):
            mm(pst[:, :bs], on_bf[:], sq[:, k, :bs], start=(k == 0), stop=(k == 7))
        tsc(rs[:, :bs], pst[:, :bs], 1.0 / D, EPS, ALU.mult, ALU.add)
        act(rs[:, :bs], rs[:, :bs], AF.Sqrt)
        recip(rs[:, :bs], rs[:, :bs])
        for k in range(8):
            tt(tmp[:, k, :bs], xb[:, k, :bs], rs[:, :bs], ALU.mult)
            b = B_of_k(k)
            if b is None:
                act(out_of_k(k), tmp[:, k, :bs], AF.Identity, scale=A_of_k(k))
            else:
                act(out_of_k(k), tmp[:, k, :bs], AF.Identity, scale=A_of_k(k), bias=b)

    def strm(b0):
        return 0 if b0 < LC else 1

    import os
    SSD_DBG = int(os.environ.get('SSD_DBG', '0'))

    def dbgstop(n):
        if SSD_DBG == n:
            raise _Stop()

    def layer_body(l):
        last = (l == DEPTH - 1)
        xsrc = xT_in if l == 0 else XT
        with ExitStack() as ph:
            hT = sb(ph, "hT", [128, 8, T], BF16)
            with ExitStack() as p1:
                xb2 = [sb(p1, f"p1_x{i}", [128, 8, 512], F32) for i in range(2)]
                sq = sb(p1, "p1_sq", [128, 8, 512], BF16)
                rs = sb(p1, "p1_rs", [128, 512], F32)
                tmp = sb(p1, "p1_tmp", [128, 8, 512], F32)
                pst = ps(p1, "p1_ps", [128, 512], F32)
                for bi, (b0, bs) in enumerate(BLOCKS):
                    xb = xb2[bi % 2]
                    S.load(xb[:, :, :bs], xsrc[:, b0:b0 + bs].rearrange("(k p) t -> p k t", p=128))
                    s = strm(b0)
                    norm_block((sq, pst, rs, tmp), xb, bs,
                               lambda k: A1[:, l, k, s:s + 1], lambda k: modT[:, l, k, s:s + 1],
                               lambda k: hT[:, k, b0:b0 + bs])
                S.barrier()
            with ExitStack() as p2:
                cs = sb(p2, "p2_cos", [128, T], F32)
                sn = sb(p2, "p2_sin", [128, T], F32)
                S.load(cs[:], cosT[:, :])
                S.load(sn[:], sinT[:, :])
                wf = [sb(p2, f"p2_wf{i}", [128, 8, 512], F32) for i in range(2)]
                wb = [sb(p2, f"p2_wb{i}", [128, 8, 512], BF16) for i in range(3)]
                stg = [sb(p2, f"p2_st{i}", [128, 512], F32) for i in range(4)]
                stb = [sb(p2, f"p2_sb{i}", [128, 512], BF16) for i in range(4)]
                r1 = sb(p2, "p2_r1", [128, 512], F32)
                r2 = sb(p2, "p2_r2", [128, 512], F32)
                pp = [ps(p2, f"p2_p{i}", [128, 512], F32) for i in range(6)]
                cnt = {'w': 0, 'p': 0, 's': 0, 'c': 0}

                def load_w(src, c0, n):
                    i = cnt['w']
                    cnt['w'] += 1
                    f = wf[i % 2]
                    b = wb[i % 3]
                    S.load(f[:, :, :n], src[:, c0:c0 + n].rearrange("(k p) c -> p k c", p=128))
                    for k in range(8):
                        if (k + i) % 2 == 0:
                            cp(b[:, k, :n], f[:, k, :n], eng='dve')
                        else:
                            cp(b[:, k, :n], f[:, k, :n], eng='act')
                    return b

                def newp():
                    p = pp[cnt['p'] % 6]
                    cnt['p'] += 1
                    return p

                def proj_F(wt, j, b0, bs):
                    p = newp()
                    for k in range(8):
                        mm(p[:, :bs], wt[:, k, j * 128:(j + 1) * 128], hT[:, k, b0:b0 + bs], start=(k == 0), stop=(k == 7))
                    return p

                def fgroup(src, c0, ncols, evac):
                    for g0 in range(0, ncols, 512):
                        n = min(512, ncols - g0)
                        wt = load_w(src, c0 + g0, n)
                        for j in range(n // 128):
                            for (b0, bs) in BLOCKS:
                                p = proj_F(wt, j, b0, bs)
                                evac(p, g0 + j * 128, b0, bs)

                def ev_simple(dst, func, dt):
                    def f(p, gc, b0, bs):
                        i = cnt['s']
                        cnt['s'] += 1
                        st = (stb if dt == BF16 else stg)[i % 4]
                        if func is None:
                            if i % 2 == 0:
                                cp(st[:, :bs], p[:, :bs], eng='dve')
                            else:
                                cp(st[:, :bs], p[:, :bs], eng='act')
                        else:
                            act(st[:, :bs], p[:, :bs], func)
                        S.store(dst[gc:gc + 128, b0:b0 + bs], st[:, :bs])
                    return f

                def ev_silu(dst):
                    def f(p, gc, b0, bs):
                        i = cnt['s']
                        cnt['s'] += 1
                        st = stg[i % 4]
                        act(st[:, :bs], p[:, :bs], AF.Sigmoid)
                        tt(st[:, :bs], p[:, :bs], st[:, :bs], ALU.mult)
                        S.store(dst[gc:gc + 128, b0:b0 + bs], st[:, :bs])
                    return f

                def rope_group(c_orig, c_rot, dst):
                    for g0 in range(0, 1024, 512):
                        wo = load_w(w_in[l], c_orig + g0, 512)
                        wr = load_w(w_rot[l], c_rot + g0, 512)
                        for j in range(4):
                            for (b0, bs) in BLOCKS:
                                p1_ = proj_F(wo, j, b0, bs)
                                p2_ = proj_F(wr, j, b0, bs)
                                i = cnt['s']
                                cnt['s'] += 1
                                st = stb[i % 4]
                                tt(r1[:, :bs], p1_[:, :bs], cs[:, b0:b0 + bs], ALU.mult)
                                tt(r2[:, :bs], p2_[:, :bs], sn[:, b0:b0 + bs], ALU.mult)
                                tt(st[:, :bs], r1[:, :bs], r2[:, :bs], ALU.add)
                                gc = g0 + j * 128
                                S.store(dst[gc:gc + 128, b0:b0 + bs], st[:, :bs])

                def tgroup(c0, ncols, dst, dcol0, func, dt):
                    for g0 in range(0, ncols, 512):
                        n = min(512, ncols - g0)
                        wt = load_w(w_in[l], c0 + g0, n)
                        for tix in range(NT):
                            p = newp()
                            for k in range(8):
                                mm(p[:, :n], hT[:, k, tix * 128:(tix + 1) * 128], wt[:, k, :n], start=(k == 0), stop=(k == 7))
                            i = cnt['s']
                            cnt['s'] += 1
                            st = (stb if dt == BF16 else stg)[i % 4]
                            if func == 'silu':
                                act(st[:, :n], p[:, :n], AF.Sigmoid)
                                tt(st[:, :n], p[:, :n], st[:, :n], ALU.mult)
                            elif i % 2 == 0:
                                cp(st[:, :n], p[:, :n], eng='dve')
                            else:
                                cp(st[:, :n], p[:, :n], eng='act')
                            S.store(dst[tix * 128:(tix + 1) * 128, dcol0 + g0:dcol0 + g0 + n], st[:, :n])

                rope_group(O_AK, 0, KT)
                rope_group(O_AQ, 1024, QT)
                tgroup(O_AV, 1024, AV, 0, None, BF16)
                fgroup(w_in[l], O_HFF, 1024, ev_simple(HFF, None, F32))
                fgroup(w_in[l], O_HFB, 1024, ev_simple(HFB, None, F32))
                tgroup(O_HI, 1024, HI, 0, None, BF16)
                fgroup(w_in[l], O_XBC, 3072, ev_simple(XBC, None, BF16))
                tgroup(O_DTF, 64, DTR, 0, None, F32)
                fgroup(w_in[l], O_HQ, 1024, ev_silu(HQ))
                fgroup(w_in[l], O_HGATE, 1024, ev_silu(HGATE))
                tgroup(O_Z, 2048, ZZ, 0, 'silu', F32)
                fgroup(w_in[l], O_GATES, 3072, ev_simple(GATES, AF.Sigmoid, F32))
                S.barrier()
        if stop_after == ('p2', l):
            raise _Stop()

        with ExitStack() as ph:
          if not os.environ.get('SKIP_AG'):
            kT2 = [sb(ph, f"a_kT{i}", [128, T], BF16) for i in range(2)]
            qT2 = [sb(ph, f"a_qT{i}", [128, T], BF16) for i in range(2)]
            vt2 = [sb(ph, f"a_v{i}", [128, NT, 128], BF16) for i in range(2)]
            qm2 = [[sb(ph, f"a_qm{i}_{c}", [128, T], BF16) for c in range(2)] for i in range(2)]
            pb = [sb(ph, f"a_p{i}", [128, 512], BF16) for i in range(6)]
            psm = [sb(ph, f"a_psm{i}", [128, 512], BF16) for i in range(4)]
            r0 = sb(ph, "a_r0", [128, 512], F32)
            r1 = sb(ph, "a_r1", [128, 512], F32)
            t0 = sb(ph, "a_t0", [128, 512], F32)
            t1 = sb(ph, "a_t1", [128, 512], F32)
            sq = sb(ph, "a_sq", [128, 512], BF16)
            ob = [sb(ph, f"a_ob{i}", [128, 512], BF16) for i in range(2)]
            gsc = sb(ph, "a_g", [128, 1], F32)
            lam_init = 0.8 - 0.6 * math.exp(-0.3 * l)
            tsc(gsc[:], att_g[:, l:l + 1], 1.0 - lam_init, None, ALU.mult)
            pS = [ps(ph, f"a_S{i}", [128, 512], F32) for i in range(4)]
            pO = [ps(ph, f"a_O{i}", [128, 512], F32) for i in range(2)]
            pL = [ps(ph, f"a_L{i}", [128, 512], F32) for i in range(2)]
            for h in range(8):
                kTt, qTt, vt = kT2[h % 2], qT2[h % 2], vt2[h % 2]
                S.load(kTt[:], KT[h * 128:(h + 1) * 128, :])
                S.load(qTt[:], QT[h * 128:(h + 1) * 128, :])
                S.load(vt[:], AV[:, h * 128:(h + 1) * 128].rearrange("(n p) c -> p n c", p=128))
                qm = qm2[h % 2]
                tsc(qm[0][:], qTt[:], U_f[:, 63:64], None, ALU.mult)
                tsc(qm[1][:], qTt[:], Lo_f[:, 64:65], None, ALU.mult)
                items = []
                for bi, (b0, bs) in enumerate(BLOCKS):
                    if last and b0 < LC:
                        continue
                    nk = 2 if b0 < LC else NT
                    for kt in range(nk):
                        for c in range(2):
                            items.append((bi, b0, bs, kt, c, nk))

                def emit_S(j):
                    (bi_, b0_, bs_, kt_i, c_, nk_) = items[j]
                    mm(pS[j % 4][:, :bs_], kTt[:, kt_i * 128:(kt_i + 1) * 128], qm[c_][:, b0_:b0_ + bs_])

                LOOK = 3
                for j in range(min(LOOK, len(items))):
                    emit_S(j)
                for j, (bi, b0, bs, kt, c, nk) in enumerate(items):
                    if j + LOOK < len(items):
                        emit_S(j + LOOK)
                    P = pb[j % 6]
                    act(P[:, :bs], pS[j % 4][:, :bs], AF.Exp, scale=0.125)
                    mm(pO[c][:, :bs], vt[:, kt, :], P[:, :bs], start=(kt == 0), stop=(kt == nk - 1))
                    if kt % 2 == 1:
                        sm_ = psm[((kt // 2) * 2 + c) % 4]
                        tt(sm_[:, :bs], pb[(j - 2) % 6][:, :bs], P[:, :bs], ALU.add)
                        mm(pL[c][:, :bs], on_bf[:], sm_[:, :bs], start=(kt == 1), stop=(kt == nk - 1))
                    if not (kt == nk - 1 and c == 1):
                        continue
                    recip(r0[:, :bs], pL[0][:, :bs])
                    recip(r1[:, :bs], pL[1][:, :bs])
                    tt(t0[:, :bs], pO[0][:, :bs], r0[:, :bs], ALU.mult)
                    tt(t1[:, :bs], pO[1][:, :bs], r1[:, :bs], ALU.mult)
                    stt(t0[:, :bs], t1[:, :bs], neglam[:, l:l + 1], t0[:, :bs], ALU.mult, ALU.add)
                    act(sq[:, :bs], t0[:, :bs], AF.Square)
                    pX = pL[0]
                    mm(pX[:, :bs], on_bf[:], sq[:, :bs])
                    tsc(r0[:, :bs], pX[:, :bs], 1.0 / 128, EPS, ALU.mult, ALU.add)
                    act(r0[:, :bs], r0[:, :bs], AF.Sqrt)
                    recip(r0[:, :bs], r0[:, :bs])
                    tt(t0[:, :bs], t0[:, :bs], r0[:, :bs], ALU.mult)
                    o = ob[bi % 2]
                    act(o[:, :bs], t0[:, :bs], AF.Identity, scale=gsc[:, 0:1])
                    S.store(BR[h * 128:(h + 1) * 128, b0:b0 + bs], o[:, :bs])
            S.barrier()
        if stop_after == ('att', l):
            raise _Stop()

        with ExitStack() as ph:
          if not os.environ.get('SKIP_AG'):
            fA = sb(ph, "g_f", [128, T], F32)
            bB = sb(ph, "g_b", [128, T], F32)
            kK = sb(ph, "g_k", [128, T], F32)
            tM = sb(ph, "g_t", [128, T], F32)
            hq = sb(ph, "g_hq", [128, T], F32)
            gt = [sb(ph, f"g_gate{i}", [128, 512], F32) for i in range(2)]
            qtL = [sb(ph, f"g_qt{i}", [128, T], BF16) for i in range(2)]
            ktL = [sb(ph, f"g_kt{i}", [128, T], BF16) for i in range(2)]
            khL = [sb(ph, f"g_kh{i}", [128, T], BF16) for i in range(2)]
            m0 = sb(ph, "g_m0", [128, T], BF16)
            osum = sb(ph, "g_os", [128, T], F32)
            v128 = sb(ph, "g_v128", [128, NT, 128], BF16)
            ebL = [sb(ph, f"g_eb{i}", [128, NCH], F32) for i in range(2)]
            S32 = [sb(ph, f"g_S32_{i}", [128, 128], F32) for i in range(2)]
            Sbf = [sb(ph, f"g_Sbf{i}", [128, 128], BF16) for i in range(4)]
            khTm = [[sb(ph, f"g_khm{i}_{j}", [128, 128], BF16) for j in range(4)] for i in range(2)]
            kh128 = [sb(ph, f"g_kh128_{i}", [128, 128], BF16) for i in range(2)]
            At = [sb(ph, f"g_At{i}", [128, 128], BF16) for i in range(2)]
            sq = sb(ph, "g_sq", [128, 512], BF16)
            rs = sb(ph, "g_rs", [128, 512], F32)
            on_ = sb(ph, "g_on", [128, 512], F32)
            ob = [sb(ph, f"g_ob{i}", [128, 512], BF16) for i in range(2)]
            lbz = sb(ph, "g_lbz", [128, 2], F32)
            pT = [ps(ph, "g_pT0", [32, 128], BF16)]
            pSc = [ps(ph, "g_pS0", [32, 32], F32)]
            pD = [ps(ph, f"g_pD{i}", [128, 128], F32) for i in range(4)]
            pOa = ps(ph, "g_pOa", [128, 512], F32)
            pOb = ps(ph, "g_pOb", [128, 512], F32)
            S.load(m0[:], m0_in[:, :])
            memset(lbz[:, 0:1], 0.0)
            memset(lbz[:, 1:2], 1.0)
            def prep_thunks(h, d):
                qt, kt_, kh, ebend = qtL[d], ktL[d], khL[d], ebL[d]
                if l == 0:
                    lb_ap, oml_ap = lbz[:, 0:1], lbz[:, 1:2]
                else:
                    lb_ap, oml_ap = lb1[:, d, h:h + 1], oml1[:, d, h:h + 1]
                bend = _strided(bB[:], 31, 32, NCH) if d == 0 else _strided(bB[:], 0, 32, NCH)
                th = []
                if d == 0:
                    th.append(lambda: S.load(hq[:], HQ[h * 128:(h + 1) * 128, :]))
                th.append(lambda: S.load(fA[:], (HFF if d == 0 else HFB)[h * 128:(h + 1) * 128, :]))
                th.append(lambda: act(fA[:], fA[:], AF.Sigmoid))
                th.append(lambda: tsc(fA[:], fA[:], oml_ap, lb_ap, ALU.mult, ALU.add))
                th.append(lambda: tsc(kK[:], fA[:], -1.0, 1.0, ALU.mult, ALU.add))
                th.append(lambda: act(fA[:], fA[:], AF.Ln))
                if d == 0:
                    th.append(lambda: scan(bB[:], m0[:], fA[:]))
                else:
                    th.append(lambda: scan(_rev(bB[:]), m0[:], _rev(fA[:])))
                th.append(lambda: act(tM[:], bB[:], AF.Exp))
                th.append(lambda: tt(qt[:], hq[:], tM[:], ALU.mult))
                th.append(lambda: act(tM[:], bB[:], AF.Exp, scale=-1.0))
                th.append(lambda: tt(kt_[:], kK[:], tM[:], ALU.mult))
                th.append(lambda: tt(tM[:].rearrange("p (c j) -> p c j", j=32), _bc_last(bend, 32),
                                     bB[:].rearrange("p (c j) -> p c j", j=32), ALU.subtract))
                th.append(lambda: act(tM[:], tM[:], AF.Exp))
                th.append(lambda: tt(kh[:], kK[:], tM[:], ALU.mult))
                th.append(lambda: act(ebend[:], bend, AF.Exp))
                return th

            pairs = [(h_, d_) for h_ in range(8) for d_ in range(2)]
            for t_ in prep_thunks(0, 0):
                t_()
            for h in range(8):
                S.load(v128[:], HI[:, h * 128:(h + 1) * 128].rearrange("(n p) c -> p n c", p=128))
                for d in range(2):
                    qt, kt_, kh, ebend = qtL[d], ktL[d], khL[d], ebL[d]
                    pi_ = h * 2 + d
                    pending = prep_thunks(*pairs[pi_ + 1]) if pi_ + 1 < len(pairs) else []
                    memset(S32[0][:], 0.0)
                    memset(Sbf[0][:], 0.0)
                    order = list(range(8)) + list(range(8, NCH)) if d == 0 else list(range(7, -1, -1)) + list(range(NCH - 1, 7, -1))
                    tiles = [0, 1] + list(range(2, NT)) if d == 0 else [1, 0] + list(range(NT - 1, 1, -1))
                    BD = BDf if d == 0 else BDb
                    LAG = 2
                    pend = []

                    def blk_of(c):
                        if c < 8:
                            return 0, 256
                        return BLOCKS[1 + (c - 8) // 16]

                    def emit_tile(ti):
                        tix = tiles[ti]
                        t0_ = tix * 128
                        ts = ti % 2
                        tr(pT[0][:, 0:128], kh[:, t0_:t0_ + 128], id_bf[:])
                        cp(kh128[ts][:], pT[0][:, 0:128], eng='act')
                        for j in range(4):
                            if j < 2:
                                tsc(khTm[ts][j][:], kh128[ts][:], RM[:, j:j + 1], 1.0, ALU.mult, ALU.mult, eng='pool')
                            elif j == 2:
                                tsc(khTm[ts][j][:], kh128[ts][:], RM[:, j:j + 1], None, ALU.mult)
                            else:
                                act(khTm[ts][j][:], kh128[ts][:], AF.Identity, scale=RM[:, j:j + 1])
                        mm(pSc[0][:, 0:128], kt_[:, t0_:t0_ + 128], qt[:, t0_:t0_ + 128])
                        tt(At[ts][:], pSc[0][:, 0:128], BD, ALU.mult)
                        bstart_, bsz_ = blk_of(tix * 4)
                        oc_ = t0_ - bstart_
                        mm(pOa[:, oc_:oc_ + 128], v128[:, tix, :], At[ts][:])
                        last_tile = (t0_ + 128 == bstart_ + bsz_) if d == 0 else (t0_ == bstart_)
                        if last_tile:
                            if d == 0:
                                cp(osum[:, bstart_:bstart_ + bsz_], pOa[:, :bsz_], eng='act')
                            else:
                                tt(osum[:, bstart_:bstart_ + bsz_], osum[:, bstart_:bstart_ + bsz_], pOa[:, :bsz_], ALU.add)

                    def emit_inter(item):
                        (i_, c0_, bstart_, bsz_, lastb_) = item
                        oc_ = c0_ - bstart_
                        mm(pOb[:, oc_:oc_ + 32], Sbf[i_ % 4][:], qt[:, c0_:c0_ + 32])
                        if lastb_:
                            tt(osum[:, bstart_:bstart_ + bsz_], osum[:, bstart_:bstart_ + bsz_], pOb[:, :bsz_], ALU.add)

                    emit_tile(0)
                    for idx, c in enumerate(order):
                        c0 = c * 32
                        sl = idx % 4
                        ti = idx // 4
                        tix = tiles[ti]
                        assert tix == c // 4
                        if idx % 4 == 1 and ti + 1 < len(tiles):
                            emit_tile(ti + 1)
                        if pending and idx % 8 == 3:
                            pending.pop(0)()
                        bstart, bsz = blk_of(c)
                        last_in_blk = (c0 + 32 == bstart + bsz) if d == 0 else (c0 == bstart)
                        mm(pD[sl][:, 0:128], khTm[ti % 2][c % 4][:], v128[:, tix, :])
                        pend.append((idx, c0, bstart, bsz, last_in_blk))
                        if len(pend) > LAG:
                            emit_inter(pend.pop(0))
                        stt(S32[(idx + 1) % 2][:], S32[idx % 2][:], ebend[:, c:c + 1], pD[sl][:, 0:128], ALU.mult, ALU.add, stream_self=True)
                        cp(Sbf[(idx + 1) % 4][:], S32[(idx + 1) % 2][:], eng='act')
                    while pend:
                        emit_inter(pend.pop(0))
                    while pending:
                        pending.pop(0)()
                gcol = hg_g[:, l:l + 1]
                for bi, (b0, bs) in enumerate(BLOCKS):
                    if last and b0 < LC:
                        continue
                    act(sq[:, :bs], osum[:, b0:b0 + bs], AF.Square)
                    px = pOa if bi % 2 == 0 else pOb
                    mm(px[:, :bs], on_bf[:], sq[:, :bs])
                    tsc(rs[:, :bs], px[:, :bs], 1.0 / 128, EPS, ALU.mult, ALU.add)
                    act(rs[:, :bs], rs[:, :bs], AF.Sqrt)
                    recip(rs[:, :bs], rs[:, :bs])
                    tt(on_[:, :bs], osum[:, b0:b0 + bs], rs[:, :bs], ALU.mult)
                    g__ = gt[bi % 2]
                    S.load(g__[:, :bs], HGATE[h * 128:(h + 1) * 128, b0:b0 + bs])
                    tt(on_[:, :bs], on_[:, :bs], g__[:, :bs], ALU.mult)
                    o = ob[bi % 2]
                    act(o[:, :bs], on_[:, :bs], AF.Identity, scale=gcol)
                    S.store(BR[1024 + h * 128:1024 + (h + 1) * 128, b0:b0 + bs], o[:, :bs])
            S.barrier()
        if stop_after == ('gla', l):
            raise _Stop()

        PADT = T + 8
        OFFC, OFFL = 2, 2 + 256 + 4

        def poff(b0):
            return (OFFC + b0) if b0 < LC else (OFFL + (b0 - LC))

        with ExitStack() as ph:
            up = [sb(ph, f"c_u{i}", [128, PADT], BF16) for i in range(3)]
            cw = sb(ph, "c_w", [128, 24, 5], F32)
            S.load(cw[:], conv_wT[l])
            dw = sb(ph, "c_dw", [128, 24, 5, 128], BF16)
            brow_f = sb(ph, "c_brf", [1, 3072], F32)
            brow = sb(ph, "c_br", [1, 3072], BF16)
            S.load(brow_f[:], bass.AP(conv_b.tensor, l * 3072, [[0, 1], [1, 3072]]))
            cp(brow[:], brow_f[:])
            onerow = on_bf[0:1, :]
            stT = [sb(ph, f"c_sT{i}", [128, 512], BF16) for i in range(3)]
            stF = [sb(ph, f"c_sF{i}", [128, 512], BF16) for i in range(3)]
            sgm = [sb(ph, f"c_sg{i}", [128, 512], F32) for i in range(4)]
            pc = [ps(ph, f"c_p{i}", [128, 512], F32) for i in range(4)]
            for i in range(3):
                memset(up[i][:], 0.0)
            for cc in range(24):
                for k in range(5):
                    tsc(dw[:, cc, k, :], id_bf[:], cw[:, cc, k:k + 1], None, ALU.mult)
            n_p = 0
            n_s = 0
            for cc in range(24):
                u = up[cc % 3]
                S.load(u[:, OFFC:OFFC + LC], XBC[cc * 128:(cc + 1) * 128, 0:LC])
                S.load(u[:, OFFL:OFFL + L], XBC[cc * 128:(cc + 1) * 128, LC:T])
                if cc < 20:
                    for tix in range(NT):
                        t0_ = tix * 128
                        p = pc[n_p % 4]
                        n_p += 1
                        po = poff(t0_)
                        for k in range(5):
                            mm(p[:, 0:128], u[:, po + k - 2:po + k - 2 + 128], dw[:, cc, k, :], start=(k == 0), stop=False)
                        mm(p[:, 0:128], onerow, brow[0:1, cc * 128:(cc + 1) * 128], start=False, stop=True)
                        sg_ = sgm[n_s % 4]
                        st = stT[n_s % 3]
                        n_s += 1
                        act(sg_[:, 0:128], p[:, 0:128], AF.Sigmoid)
                        if cc < 16:
                            tt(sg_[:, 0:128], p[:, 0:128], sg_[:, 0:128], ALU.mult)
                            S.store(XS[t0_:t0_ + 128, cc * 128:(cc + 1) * 128], sg_[:, 0:128])
                            continue
                        tt(st[:, 0:128], p[:, 0:128], sg_[:, 0:128], ALU.mult)
                        if cc < 16:
                            S.store(XS[t0_:t0_ + 128, cc * 128:(cc + 1) * 128], st[:, 0:128])
                        else:
                            S.store(BTOK[t0_:t0_ + 128, (cc - 16) * 128:(cc - 15) * 128], st[:, 0:128])
                if cc >= 16:
                    dst = BTT if cc < 20 else CTT
                    r0_ = (cc - 16) * 128 if cc < 20 else (cc - 20) * 128
                    for (b0, bs) in BLOCKS:
                        p = pc[n_p % 4]
                        n_p += 1
                        po = poff(b0)
                        for k in range(5):
                            mm(p[:, :bs], dw[:, cc, k, :], u[:, po + k - 2:po + k - 2 + bs], start=(k == 0), stop=(k == 4))
                        sg_ = sgm[n_s % 4]
                        st = stF[n_s % 3]
                        n_s += 1
                        act(sg_[:, :bs], p[:, :bs], AF.Sigmoid, bias=cbT[:, l, cc:cc + 1])
                        stt(st[:, :bs], p[:, :bs], cbT[:, l, cc:cc + 1], sg_[:, :bs], ALU.add, ALU.mult)
                        S.store(dst[r0_:r0_ + 128, b0:b0 + bs], st[:, :bs])
            S.barrier()
        if stop_after == ('conv', l):
            raise _Stop()

        with ExitStack() as ph:
            ST32 = sb(ph, "s_ST", [128, 2048], F32)
            STb = sb(ph, "s_STb", [128, 2048], BF16)
            dtb = sb(ph, "s_dtb", [128, 64], F32)
            S.load(dtb[:], bass.AP(dt_bias.tensor, l * 64, [[0, 128], [1, 64]]))
            Ab = sb(ph, "s_A", [128, 64], F32)
            S.load(Ab[:], bass.AP(a_log.tensor, l * 64, [[0, 128], [1, 64]]))
            act(Ab[:], Ab[:], AF.Exp)
            tsc(Ab[:], Ab[:], -1.0, None, ALU.mult)
            dsk = sb(ph, "s_dsk", [128, 32], F32)
            S.load(dsk[:], bass.AP(ssm_d.tensor, l * 32, [[0, 128], [1, 32]]))
            xs_ = [sb(ph, f"s_xs{i}", [128, 2048], F32) for i in range(2)]
            btk = [sb(ph, f"s_bt{i}", [128, 512], BF16) for i in range(2)]
            bT_ = [sb(ph, f"s_bT{i}", [128, 4, 128], BF16) for i in range(2)]
            cT_ = [sb(ph, f"s_cT{i}", [128, 4, 128], BF16) for i in range(2)]
            dtr = [sb(ph, f"s_dt{i}", [128, 64], F32) for i in range(2)]
            dtv = [sb(ph, f"s_dtv{i}", [128, 32], F32) for i in range(2)]
            aav = [sb(ph, f"s_a{i}", [128, 32], F32) for i in range(2)]
            eacv = [sb(ph, f"s_eac{i}", [128, 32], F32) for i in range(2)]
            wendv = [sb(ph, f"s_wend{i}", [128, 32], F32) for i in range(2)]
            dendv = [sb(ph, f"s_dend{i}", [128, 32], F32) for i in range(2)]
            xdtv = [sb(ph, f"s_xdt{i}", [128, 2048], BF16) for i in range(2)]
            xdwv = [sb(ph, f"s_xdw{i}", [128, 2048], BF16) for i in range(2)]
            Am = [sb(ph, f"s_Am{i}", [128, 8, 128], F32) for i in range(2)]
            LT = [sb(ph, f"s_LT{i}", [128, 8, 128], BF16) for i in range(2)]
            MT = [sb(ph, f"s_MT{i}", [128, 8, 128], BF16) for i in range(2)]
            CBm = [sb(ph, f"s_CB{i}", [128, 128], BF16) for i in range(2)]
            ytmp = sb(ph, "s_yt", [128, 512], F32)
            yo = [sb(ph, f"s_yo{i}", [128, 2048], F32) for i in range(2)]
            yfl = [sb(ph, f"s_yf{i}", [128, 2048], F32) for i in range(2)]
            zt = [sb(ph, f"s_z{i}", [128, 2048], F32) for i in range(2)]
            stmp = sb(ph, "s_st", [128, 512], F32)
            junk = sb(ph, "s_junk", [128, 2048], BF16)
            ssq = sb(ph, "s_ssq", [128, 1], F32)
            ynb = sb(ph, "s_ynb", [128, 2048], F32)
            obt = [sb(ph, f"s_ob{i}", [128, 4, 128], BF16) for i in range(2)]
            p_ac = ps(ph, "s_pac", [128, 64], F32)
            p_cb = ps(ph, "s_pcb", [128, 128], F32)
            p_df = [ps(ph, f"s_pdf{i}", [128, 512], F32) for i in range(2)]
            p_y = ps(ph, "s_py", [128, 512], F32)
            p_yi = ps(ph, "s_pyi", [128, 512], F32)
            p_st = ps(ph, "s_pst", [128, 512], F32)
            p_tr = ps(ph, "s_ptr", [128, 4, 128], F32)
            nit = 0
            for d in range(2):
                if d == 1:
                    S.barrier()
                memset(ST32[:], 0.0)
                memset(STb[:], 0.0)
                order = [0, 1] + list(range(2, NT)) if d == 0 else [1, 0] + list(range(NT - 1, 1, -1))
                Ucum = U_f if d == 0 else Lo_f
                Mlhs = Ms_f if d == 0 else Ml_f
                Mcb = U_f if d == 0 else Lo_f
                def aside(tix, sl):
                    t0_ = tix * 128
                    xs = xs_[sl]
                    S.load(xs[:], XS[t0_:t0_ + 128, :])
                    S.load(btk[sl][:], BTOK[t0_:t0_ + 128, :])
                    S.load(bT_[sl][:], BTT[:, t0_:t0_ + 128].rearrange("(g p) t -> p g t", p=128))
                    S.load(cT_[sl][:], CTT[:, t0_:t0_ + 128].rearrange("(g p) t -> p g t", p=128))
                    S.load(dtr[sl][:], DTR[t0_:t0_ + 128, :])
                    if d == 1:
                        S.load(yfl[sl][:], YF[t0_:t0_ + 128, :])
                        S.load(zt[sl][:], ZZ[t0_:t0_ + 128, :])
                    dt_, aa_, eac_, wend_, dend_ = dtv[sl], aav[sl], eacv[sl], wendv[sl], dendv[sl]
                    tt(dt_[:], dtr[sl][:, d * 32:(d + 1) * 32], dtb[:, d * 32:(d + 1) * 32], ALU.add)
                    act(dt_[:], dt_[:], AF.Exp)
                    act(dt_[:], dt_[:], AF.Ln, bias=1.0)
                    tt(aa_[:], dt_[:], Ab[:, d * 32:(d + 1) * 32], ALU.mult)
                    mm(p_ac[:, 0:32], Ucum, aa_[:])
                    mm(p_ac[:, 32:64], ON_f, aa_[:])
                    act(eac_[:], p_ac[:, 0:32], AF.Exp)
                    act(dend_[:], p_ac[:, 32:64], AF.Exp)
                    cp(wend_[:], p_ac[:, 0:32], eng='dve')
                    tt(wend_[:], p_ac[:, 32:64], wend_[:], ALU.subtract)
                    act(wend_[:], wend_[:], AF.Exp)
                    xs3 = xs[:].rearrange("p (h q) -> p h q", q=64)
                    tt(xdtv[sl][:].rearrange("p (h q) -> p h q", q=64), xs3, _bc_last(dt_[:], 64), ALU.mult, eng='pool')
                    tt(wend_[:], wend_[:], dt_[:], ALU.mult)
                    tt(xdwv[sl][:].rearrange("p (h q) -> p h q", q=64), xs3, _bc_last(wend_[:], 64), ALU.mult)

                aside(order[0], nit % 2)
                for oi, tix in enumerate(order):
                    t0_ = tix * 128
                    sl = nit % 2
                    nit += 1
                    if oi + 1 < len(order):
                        aside(order[oi + 1], nit % 2)
                    xs = xs_[sl]
                    xs3 = xs[:].rearrange("p (h q) -> p h q", q=64)
                    aa_, eac, dend, xdt, xdw = aav[sl], eacv[sl], dendv[sl], xdtv[sl], xdwv[sl]
                    yout = yo[sl]
                    for g in range(4):
                        gs = (nit * 4 + g) % 2
                        mm(p_cb[:, 0:128], bT_[sl][:, g, :], cT_[sl][:, g, :])
                        tt(CBm[gs][:], p_cb[:, 0:128], Mcb, ALU.mult)
                        tt(Am[gs][:], _bc_mid(Ucum, 8), _bc_last(aa_[:, g * 8:(g + 1) * 8], 128), ALU.mult, eng='pool')
                        for half in range(2):
                            mm(p_df[half][:, 0:512], Mlhs, Am[gs][:, half * 4:(half + 1) * 4, :].rearrange("p h t -> p (h t)"))
                        act(LT[gs][:, 0:4, :], p_df[0][:].rearrange("p (h t) -> p h t", t=128), AF.Exp)
                        act(LT[gs][:, 4:8, :], p_df[1][:].rearrange("p (h t) -> p h t", t=128), AF.Exp)
                        tt(MT[gs][:], LT[gs][:], _bc_mid(CBm[gs][:], 8), ALU.mult)
                        dbgstop(3)
                        for hh in range(8):
                            hg = g * 8 + hh
                            mm(p_y[:, hh * 64:(hh + 1) * 64], MT[gs][:, hh, :], xdt[:, hg * 64:(hg + 1) * 64])
                        mm(p_yi[:], cT_[sl][:, g, :], STb[:, g * 512:(g + 1) * 512])
                        tt(ytmp[:].rearrange("p (h q) -> p h q", q=64), p_yi[:].rearrange("p (h q) -> p h q", q=64),
                           _bc_last(eac[:, g * 8:(g + 1) * 8], 64), ALU.mult)
                        tt(yout[:, g * 512:(g + 1) * 512], ytmp[:], p_y[:], ALU.add)
                        dbgstop(4)
                        mm(p_st[:], btk[sl][:, g * 128:(g + 1) * 128], xdw[:, g * 512:(g + 1) * 512])
                        tt(stmp[:].rearrange("p (h q) -> p h q", q=64),
                           ST32[:, g * 512:(g + 1) * 512].rearrange("p (h q) -> p h q", q=64),
                           _bc_last(dend[:, g * 8:(g + 1) * 8], 64), ALU.mult)
                        tt(ST32[:, g * 512:(g + 1) * 512], stmp[:], p_st[:], ALU.add)
                        cp(STb[:, g * 512:(g + 1) * 512], ST32[:, g * 512:(g + 1) * 512], eng='act')
                        dbgstop(5)
                    if d == 0:
                        S.store(YF[t0_:t0_ + 128, :], yout[:])
                        dbgstop(6)
                        if tix == 33:
                            dbgstop(7)
                    else:
                        if last and tix < 2:
                            continue
                        tt(yout[:], yout[:], yfl[sl][:], ALU.add)
                        tt(yfl[sl][:].rearrange("p (h q) -> p h q", q=64), xs3, _bc_last(dsk[:], 64), ALU.mult)
                        tt(yout[:], yout[:], yfl[sl][:], ALU.add)
                        tt(yout[:], yout[:], zt[sl][:], ALU.mult)
                        act(junk[:], yout[:], AF.Square, accum=ssq[:])
                        tsc(ssq[:], ssq[:], 1.0 / 2048, EPS, ALU.mult, ALU.add)
                        act(ssq[:], ssq[:], AF.Sqrt)
                        recip(ssq[:], ssq[:])
                        tsc(ynb[:], yout[:], ssq[:, 0:1], None, ALU.mult)
                        for q4 in range(4):
                            for j in range(4):
                                tr(p_tr[:, j * 128:(j + 1) * 128], ynb[:, (q4 * 4 + j) * 128:(q4 * 4 + j + 1) * 128], ID_f)
                            o = obt[q4 % 2]
                            for j in range(4):
                                act(o[:, j, :], p_tr[:, j * 128:(j + 1) * 128], AF.Identity, scale=ssm_g[:, l, q4 * 4 + j:q4 * 4 + j + 1])
                            S.store(BR[2048 + q4 * 512:2048 + (q4 + 1) * 512, t0_:t0_ + 128].rearrange("(j p) t -> p j t", p=128), o[:])
                S.barrier()
        if stop_after == ('ssd', l):
            raise _Stop()

        with ExitStack() as ph:
            wall = sb(ph, "m_w", [128, 40, 1024], BF16)
            with ExitStack() as pw:
                wstg = [sb(pw, f"m_ws{i}", [128, 4, 1024], F32) for i in range(2)]
                srcs = [(w_b_att[l], 8), (w_b_hg[l], 8), (w_b_ssm[l], 16), (w_out[l], 8)]
                kc0 = 0
                i = 0
                for (src, nk_) in srcs:
                    for k4 in range(0, nk_, 4):
                        f = wstg[i % 2]
                        S.load(f[:], src[k4 * 128:(k4 + 4) * 128, :].rearrange("(k p) c -> p k c", p=128))
                        for kk_ in range(4):
                            cp(wall[:, kc0 + k4 + kk_, :], f[:, kk_, :], eng=('dve' if (kk_ + i) % 2 == 0 else 'act'),
                               sub={wall.name: kc0 + k4 + kk_})
                        i += 1
                    kc0 += nk_
                S.barrier()
            MB = 256
            brb = [sb(ph, f"m_br{i}", [128, 32, MB], BF16) for i in range(2)]
            gb = [sb(ph, f"m_g{i}", [128, 24, MB], F32) for i in range(2)]
            xb = [sb(ph, f"m_x{i}", [128, 8, MB], F32) for i in range(2)]
            macc = sb(ph, "m_acc", [128, MB], F32)
            mtmp = sb(ph, "m_tmp", [128, MB], F32)
            mT = sb(ph, "m_mT", [128, 8, MB], BF16)
            pm = [ps(ph, f"m_p{i}", [128, MB], F32) for i in range(4)]
            npm = 0
            bi = 0
            for b0 in range(0, T, MB):
                if last and b0 < LC:
                    continue
                bs = MB
                s = strm(b0)
                br = brb[bi % 2]
                g_ = gb[bi % 2]
                x_ = xb[bi % 2]
                bi += 1
                S.load(br[:], BR[:, b0:b0 + bs].rearrange("(k p) t -> p k t", p=128))
                S.load(g_[:], GATES[:, b0:b0 + bs].rearrange("(k p) t -> p k t", p=128))
                S.load(x_[:], xsrc[:, b0:b0 + bs].rearrange("(k p) t -> p k t", p=128))
                for oc in range(8):
                    for bri, (k0, nk_) in enumerate(((0, 8), (8, 8), (16, 16))):
                        p = pm[npm % 4]
                        npm += 1
                        for k in range(nk_):
                            mm(p[:, :MB], wall[:, k0 + k, oc * 128:(oc + 1) * 128], br[:, k0 + k, :], start=(k == 0), stop=(k == nk_ - 1))
                        if bri == 0:
                            tt(macc[:], p[:, :MB], g_[:, oc, :], ALU.mult)
                        else:
                            tt(mtmp[:], p[:, :MB], g_[:, bri * 8 + oc, :], ALU.mult)
                            if bri == 1:
                                tt(macc[:], macc[:], mtmp[:], ALU.add)
                            else:
                                tt(mT[:, oc, :], macc[:], mtmp[:], ALU.add)
                for oc in range(8):
                    p = pm[npm % 4]
                    npm += 1
                    for k in range(8):
                        mm(p[:, :MB], wall[:, 32 + k, oc * 128:(oc + 1) * 128], mT[:, k, :], start=(k == 0), stop=(k == 7))
                    stt(x_[:, oc, :], p[:, :MB], modT[:, l, 16 + oc, s:s + 1], x_[:, oc, :], ALU.mult, ALU.add)
                S.store(XT[:, b0:b0 + bs].rearrange("(k p) t -> p k t", p=128), x_[:])
            S.barrier()
        if stop_after == ('merge', l):
            raise _Stop()

        with ExitStack() as ph:
            w1b = sb(ph, "f_w1", [128, 8, FFH], BF16)
            w3b = sb(ph, "f_w3", [128, 8, FFH], BF16)
            w2b = sb(ph, "f_w2", [128, 22, 1024], BF16)
            with ExitStack() as pw:
                wstg = [sb(pw, f"f_ws{i}", [128, FFH], F32) for i in range(2)]
                i = 0
                for (src, dstw) in ((ffn_w1[l], w1b), (ffn_w3[l], w3b)):
                    for k in range(8):
                        f = wstg[i % 2]
                        S.load(f[:], src[k * 128:(k + 1) * 128, :])
                        cp(dstw[:, k, :], f[:], eng=('dve' if i % 2 == 0 else 'act'))
                        i += 1
                for k in range(22):
                    f = wstg[i % 2]
                    S.load(f[:, 0:1024], ffn_w2[l][k * 128:(k + 1) * 128, :])
                    cp(w2b[:, k, :], f[:, 0:1024], eng=('dve' if i % 2 == 0 else 'act'))
                    i += 1
                S.barrier()
            FB = 256
            xb = [sb(ph, f"f_x{i}", [128, 8, FB], F32) for i in range(2)]
            sq = sb(ph, "f_sq", [128, 8, FB], BF16)
            rs = sb(ph, "f_rs", [128, FB], F32)
            tmp = sb(ph, "f_tmp", [128, 8, FB], F32)
            h2 = sb(ph, "f_h2", [128, 8, FB], BF16)
            uT = sb(ph, "f_u", [128, 22, FB], BF16)
            sg_ = [sb(ph, f"f_sg{i}", [128, FB], F32) for i in range(2)]
            s1 = [sb(ph, f"f_s1{i}", [128, FB], F32) for i in range(2)]
            yo_ = [sb(ph, f"f_yo{i}", [128, 8, FB], F32) for i in range(1)]
            pst = ps(ph, "f_pst", [128, FB], F32)
            pf = [ps(ph, f"f_p{i}", [128, FB], F32) for i in range(6)]
            npf = 0
            nblk = 0
            for b0 in range(0, T, FB):
                if last and b0 < LC:
                    continue
                bs = FB
                s = strm(b0)
                x_ = xb[nblk % 2]
                nblk += 1
                S.load(x_[:], XT[:, b0:b0 + bs].rearrange("(k p) t -> p k t", p=128))
                norm_block((sq, pst, rs, tmp), x_, bs,
                           lambda k: A2[:, l, k, s:s + 1], lambda k: modT[:, l, 24 + k, s:s + 1],
                           lambda k: h2[:, k, :])
                for hc in range(22):
                    pa = pf[npf % 6]
                    pb_ = pf[(npf + 1) % 6]
                    npf += 2
                    for k in range(8):
                        mm(pa[:, :FB], w1b[:, k, hc * 128:(hc + 1) * 128], h2[:, k, :], start=(k == 0), stop=(k == 7))
                    for k in range(8):
                        mm(pb_[:, :FB], w3b[:, k, hc * 128:(hc + 1) * 128], h2[:, k, :], start=(k == 0), stop=(k == 7))
                    sg = sg_[hc % 2]
                    s1_ = s1[hc % 2]
                    act(sg[:], pa[:, :FB], AF.Sigmoid)
                    tt(s1_[:], pa[:, :FB], sg[:], ALU.mult)
                    tt(uT[:, hc, :], pb_[:, :FB], s1_[:], ALU.mult)
                for oc in range(8):
                    p = pf[npf % 6]
                    npf += 1
                    for k in range(22):
                        mm(p[:, :FB], w2b[:, k, oc * 128:(oc + 1) * 128], uT[:, k, :], start=(k == 0), stop=(k == 21))
                    stt(x_[:, oc, :], p[:, :FB], modT[:, l, 40 + oc, s:s + 1], x_[:, oc, :], ALU.mult, ALU.add)
                if not last:
                    S.store(XT[:, b0:b0 + bs].rearrange("(k p) t -> p k t", p=128), x_[:])
                else:
                    yo = yo_[0]
                    norm_block((sq, pst, rs, tmp), x_, bs,
                               lambda k: gfs[:, k:k + 1], lambda k: None,
                               lambda k: yo[:, k, :])
                    S.store(yT[:, b0 - LC:b0 - LC + bs].rearrange("(k p) t -> p k t", p=128), yo[:])
            S.barrier()


    stopped = False
    try:
        for l in range(DEPTH):
            layer_body(l)
    except _Stop:
        stopped = True
    S.barrier()
    if not stopped:
        per.close()
        es.close()
    return nc, S


_CACHE = {}


def _consts():
    j = np.arange(128)[:, None]
    t = np.arange(128)[None, :]
    c = np.zeros((128, 6 * 128 + 64 + 4 + 256), np.float32)
    c[:, 0:128] = (j <= t)
    c[:, 128:256] = (j >= t)
    c[:, 256:384] = (j > t)
    c[:, 384:512] = (j < t)
    c[:, 512:640] = np.eye(128)
    c[:, 640:768] = 1.0
    s = np.arange(32)[:, None]
    tt_ = np.arange(32)[None, :]
    c[0:32, 768:800] = (s <= tt_)
    c[0:32, 800:832] = (s >= tt_)
    for q in range(4):
        c[32 * q:32 * q + 32, 832 + q] = 1.0
    same = (j // 32) == (t // 32)
    c[:, 836:964] = same & (j <= t)
    c[:, 964:1092] = same & (j >= t)
    return c


def _m0():
    m0 = np.ones(T, np.float32)
    m0[::32] = 0.0
    return np.ascontiguousarray(np.broadcast_to(m0[None, :], (128, T))).astype(ml_dtypes.bfloat16)


def _rope_tables():
    cos = np.ones((128, T), np.float64)
    sin = np.zeros((128, T), np.float64)
    tl = np.arange(L)
    row = (tl // 64).astype(np.float64)
    col = (tl % 64).astype(np.float64)
    for f in range(128):
        r = f % 32
        half_sel = (f % 64) // 32
        fi = r % 16
        freq = np.float32(10000.0) ** (-np.float32(fi) / np.float32(16))
        pos = row if half_sel == 0 else col
        ang = (pos.astype(np.float32) * np.float32(freq)).astype(np.float64)
        cos[f, LC:] = np.cos(ang)
        sgn = -1.0 if r < 16 else 1.0
        sin[f, LC:] = sgn * np.sin(ang)
    return cos.astype(np.float32), sin.astype(np.float32)


def _rot_perm():
    p = np.arange(1024)
    r = p % 32
    return np.where(r < 16, p + 16, p - 16)


def _prep_shared(inp):
    f = np.float32
    A = lambda a: np.ascontiguousarray(a, dtype=f)
    perm = _rot_perm()
    w_in = inp['w_in']
    w_rot = np.concatenate([w_in[:, :, O_AK:O_AK + 1024][:, :, perm], w_in[:, :, O_AQ:O_AQ + 1024][:, :, perm]], axis=2)
    cosT, sinT = _rope_tables()
    col8 = lambda g: A(g.reshape(8, 128).T)
    sh = {
        'w_ada': A(inp['w_ada']),
        'b_adaT': A(inp['b_ada'].reshape(DEPTH, 48, 128).transpose(0, 2, 1)),
        'g1T': A(inp['norm1_g'].reshape(DEPTH, 8, 128).transpose(0, 2, 1)),
        'g2T': A(inp['norm2_g'].reshape(DEPTH, 8, 128).transpose(0, 2, 1)),
        'gfT': col8(inp['final_g']),
        'w_in': A(w_in),
        'w_rot': A(w_rot),
        'cosT': cosT, 'sinT': sinT,
        'att_lam': A(inp['att_lambda'].reshape(DEPTH, 256)),
        'att_gT': A(inp['att_norm_g'].T),
        'hg_lbT': A(inp['hg_lb_logits'].reshape(2, DEPTH, 8, 128).transpose(3, 0, 1, 2)),
        'hg_gT': A(inp['hg_norm_g'].T),
        'conv_wT': A(inp['ssm_conv_w'].reshape(DEPTH, 5, 24, 128).transpose(0, 3, 2, 1)),
        'conv_bT': A(inp['ssm_conv_b'].reshape(DEPTH, 24, 128).transpose(0, 2, 1)),
        'conv_b': A(inp['ssm_conv_b']),
        'dt_bias': A(inp['ssm_dt_bias'].reshape(DEPTH, 64)),
        'a_log': A(inp['ssm_a_log'].reshape(DEPTH, 64)),
        'ssm_d': A(inp['ssm_d']),
        'ssm_gT': A(inp['ssm_norm_g'].reshape(DEPTH, 16, 128).transpose(0, 2, 1)),
        'w_b_att': A(inp['w_branch_att']), 'w_b_hg': A(inp['w_branch_hg']), 'w_b_ssm': A(inp['w_branch_ssm']),
        'w_out': A(inp['w_out']),
        'ffn_w1': A(inp['ffn_w1']), 'ffn_w3': A(inp['ffn_w3']), 'ffn_w2': A(inp['ffn_w2']),
        'cst': _consts(),
        'm0': _m0(),
    }
    return sh


def kernel(**inputs):
    inp = {k: np.asarray(v) for k, v in inputs.items()}
    if 'nc' not in _CACHE:
        _CACHE['nc'] = build_program()[0]
    nc = _CACHE['nc']
    sh = _prep_shared(inp)
    in_maps = []
    for b in range(8):
        m = dict(sh)
        m['xT'] = np.ascontiguousarray(np.concatenate([inp['ctx'][b], inp['x'][b]], axis=0).T, dtype=np.float32)
        m['c2'] = np.ascontiguousarray(np.stack([inp['c_ctx'], inp['c'][b]], axis=1), dtype=np.float32)
        in_maps.append(m)
    res = run_bass_kernel_spmd(nc, in_maps, core_ids=list(range(8)))
    out = np.stack([np.ascontiguousarray(res.results[b]['yT'].T) for b in range(8)], axis=0)
    return out.astype(np.float32)
```
